# Optimizing a Trainium2 kernel written in Bass

```python
import jax, jax.numpy as jnp
from jax import lax
import numpy as np

D_MODEL = 2048
BATCH = 2
SEQ = 8192
DEPTH = 1

CHUNK = 64
M_HEADS = 4
M_HEAD_DIM = 256
M_WIDTH = M_HEADS * M_HEAD_DIM
CONV_WIDTH = 4
FORGET_BIAS_LO = 3.0
FORGET_BIAS_HI = 6.0
A_HEADS = 8
A_HEAD_DIM = 128
A_WIDTH = A_HEADS * A_HEAD_DIM
LEFT_CHUNKS = 8
BAND = (LEFT_CHUNKS + 1) * CHUNK
MAX_REL = 128
N_REL = MAX_REL + CHUNK
N_BRANCH = 2
IN_COLS = 5 * M_WIDTH + 2 * M_HEADS + 4 * A_WIDTH + N_BRANCH * D_MODEL
EPS = 1e-6

kernel_name = "hybrid_mlstm_chunkattn_gated_block"


def rmsnorm(x, g):
    xf = x.astype(jnp.float32)
    y = xf * lax.rsqrt(jnp.mean(xf * xf, axis=-1, keepdims=True) + EPS)
    return (y * g).astype(x.dtype)


def to_heads(t, n_heads):
    b, s, _ = t.shape
    return t.reshape(b, s, n_heads, -1).transpose(0, 2, 1, 3)


def from_heads(t):
    b, h, s, dh = t.shape
    return t.transpose(0, 2, 1, 3).reshape(b, s, h * dh)


def split_columns(proj):
    sizes = [M_WIDTH] * 5 + [M_HEADS] * 2 + [A_WIDTH] * 4 + [D_MODEL] * N_BRANCH
    idx = [int(v) for v in np.cumsum(sizes)[:-1]]
    return jnp.split(proj, idx, axis=-1)


def causal_depthwise_conv(u, w, b):
    s = u.shape[1]
    up = jnp.pad(u, ((0, 0), (CONV_WIDTH - 1, 0), (0, 0)))
    out = b
    for tap in range(CONV_WIDTH):
        out = out + up[:, tap:tap + s] * w[tap]
    return out


def mlstm_cell(q, k, v, ig, lf):
    b_, h_, s_, dh = q.shape
    nc = s_ // CHUNK

    def chunks(t):
        return jnp.moveaxis(t.reshape(b_, h_, nc, CHUNK, *t.shape[3:]), 2, 0)

    tril = jnp.tril(jnp.ones((CHUNK, CHUNK), dtype=bool))

    def step(carry, inp):
        C, n, m = carry
        qc, kc, vc, ic, fc = inp
        bcum = jnp.cumsum(fc, axis=-1)
        d = jnp.where(tril, bcum[..., :, None] - bcum[..., None, :] + ic[..., None, :], -jnp.inf)
        inter = bcum + m[..., None]
        m_comb = jnp.maximum(jnp.max(d, axis=-1), inter)
        w_intra = jnp.exp(d - m_comb[..., None])
        w_inter = jnp.exp(inter - m_comb)
        s = jnp.einsum('bhtd,bhsd->bhts', qc, kc) * w_intra
        num = (jnp.einsum('bhts,bhsd->bhtd', s, vc)
               + w_inter[..., None] * jnp.einsum('bhvk,bhtk->bhtv', C, qc))
        den = jnp.sum(s, axis=-1) + w_inter * jnp.einsum('bhk,bhtk->bht', n, qc)
        h = num / jnp.maximum(jnp.abs(den), jnp.exp(-m_comb))[..., None]
        b_last = bcum[..., -1]
        g = b_last[..., None] - bcum + ic
        m_new = jnp.maximum(b_last + m, jnp.max(g, axis=-1))
        w_k = jnp.exp(g - m_new[..., None])
        w_c = jnp.exp(b_last + m - m_new)
        C_new = w_c[..., None, None] * C + jnp.einsum('bhs,bhsv,bhsk->bhvk', w_k, vc, kc)
        n_new = w_c[..., None] * n + jnp.einsum('bhs,bhsk->bhk', w_k, kc)
        return (C_new, n_new, m_new), h

    init = (jnp.zeros((b_, h_, dh, dh), jnp.float32),
            jnp.zeros((b_, h_, dh), jnp.float32),
            jnp.zeros((b_, h_), jnp.float32))
    _, hs = lax.scan(step, init, (chunks(q), chunks(k), chunks(v), chunks(ig), chunks(lf)))
    return jnp.moveaxis(hs, 0, 2).reshape(b_, h_, s_, dh)


def mlstm_branch(mq, mk, mv, mo, mz, mi, mf, b_if, conv_w, conv_b, mh_norm_g):
    dt = mv.dtype
    qk = jax.nn.silu(causal_depthwise_conv(jnp.concatenate([mq, mk], axis=-1), conv_w, conv_b))
    q_in, k_in = jnp.split(qk, 2, axis=-1)
    q = to_heads(q_in, M_HEADS).astype(jnp.float32)
    k = to_heads(k_in, M_HEADS).astype(jnp.float32) * (M_HEAD_DIM ** -0.5)
    v = to_heads(mv, M_HEADS).astype(jnp.float32)
    ig = (mi + b_if[:M_HEADS]).astype(jnp.float32).transpose(0, 2, 1)
    lf = jax.nn.log_sigmoid((mf + b_if[M_HEADS:]).astype(jnp.float32)).transpose(0, 2, 1)
    h = mlstm_cell(q, k, v, ig, lf)
    h = h * lax.rsqrt(jnp.mean(h * h, axis=-1, keepdims=True) + EPS)
    h = from_heads(h) * mh_norm_g
    h = h * jax.nn.sigmoid(mo.astype(jnp.float32))
    return (h * jax.nn.silu(mz.astype(jnp.float32))).astype(dt)


def chunk_attention_branch(aq, ak, av, az, rel_bias):
    b_, s_, _ = aq.shape
    nc = s_ // CHUNK
    q = to_heads(aq, A_HEADS) * (A_HEAD_DIM ** -0.5)
    k = to_heads(ak, A_HEADS)
    v = to_heads(av, A_HEADS)
    pad = ((0, 0), (0, 0), (LEFT_CHUNKS * CHUNK, 0), (0, 0))
    k_pad = jnp.pad(k, pad)
    v_pad = jnp.pad(v, pad)
    qi = jnp.arange(CHUNK)[:, None]
    kj = jnp.arange(BAND)[None, :]
    rel = jnp.clip(qi + LEFT_CHUNKS * CHUNK - kj, -(CHUNK - 1), MAX_REL) + (CHUNK - 1)
    bias = rel_bias[:, rel].astype(jnp.float32)

    def attend(j):
        qj = lax.dynamic_slice_in_dim(q, j * CHUNK, CHUNK, axis=2)
        kb = lax.dynamic_slice_in_dim(k_pad, j * CHUNK, BAND, axis=2)
        vb = lax.dynamic_slice_in_dim(v_pad, j * CHUNK, BAND, axis=2)
        valid = ((j - LEFT_CHUNKS) * CHUNK + jnp.arange(BAND)) >= 0
        s = jnp.einsum('bhqd,bhkd->bhqk', qj, kb).astype(jnp.float32) + bias
        s = jnp.where(valid, s, -jnp.inf)
        p = jax.nn.softmax(s, axis=-1)
        return jnp.einsum('bhqk,bhkd->bhqd', p.astype(vb.dtype), vb)

    out = lax.map(attend, jnp.arange(nc))
    out = out.transpose(1, 2, 0, 3, 4).reshape(b_, A_HEADS, s_, A_HEAD_DIM)
    return from_heads(out) * jax.nn.silu(az)


def hybrid_layer(x, c, w_ada, b_ada, norm_g, w_in, b_if, conv_w, conv_b, mh_norm_g,
                 rel_bias, w_proj_m, w_proj_a, w_out):
    mod = jax.nn.silu(c) @ w_ada + b_ada
    shift, scale, gate = jnp.split(mod, 3, axis=-1)
    h = rmsnorm(x, norm_g) * (1.0 + scale[:, None]) + shift[:, None]
    proj = h @ w_in
    (mq, mk, mv, mo, mz, mi, mf, aq, ak, av, az, g_m, g_a) = split_columns(proj)
    y_m = mlstm_branch(mq, mk, mv, mo, mz, mi, mf, b_if, conv_w, conv_b, mh_norm_g)
    y_a = chunk_attention_branch(aq, ak, av, az, rel_bias)
    merged = jax.nn.sigmoid(g_m) * (y_m @ w_proj_m) + jax.nn.sigmoid(g_a) * (y_a @ w_proj_a)
    return x + gate[:, None] * (merged @ w_out)


def setup_inputs(seed: int = 0) -> dict:
    key = jax.random.key(seed)
    ks = jax.random.split(key, 16)
    f32 = jnp.float32
    nrm = lambda k, shape: jax.random.normal(k, shape, f32)
    forget_b = jnp.linspace(FORGET_BIAS_LO, FORGET_BIAS_HI, M_HEADS).astype(f32)
    b_if = jnp.concatenate([0.1 * nrm(ks[6], (DEPTH, M_HEADS)),
                            forget_b[None] + 0.1 * nrm(ks[7], (DEPTH, M_HEADS))], axis=-1)
    return {
        "x": nrm(ks[0], (BATCH, SEQ, D_MODEL)),
        "c": nrm(ks[1], (BATCH, D_MODEL)),
        "w_ada": 0.5 * D_MODEL ** -0.5 * nrm(ks[2], (DEPTH, D_MODEL, 3 * D_MODEL)),
        "b_ada": 0.02 * nrm(ks[3], (DEPTH, 3 * D_MODEL)),
        "norm_g": 1.0 + 0.02 * nrm(ks[4], (DEPTH, D_MODEL)),
        "w_in": D_MODEL ** -0.5 * nrm(ks[5], (DEPTH, D_MODEL, IN_COLS)),
        "b_if": b_if,
        "conv_w": CONV_WIDTH ** -0.5 * nrm(ks[8], (DEPTH, CONV_WIDTH, 2 * M_WIDTH)),
        "conv_b": 0.02 * nrm(ks[9], (DEPTH, 2 * M_WIDTH)),
        "mh_norm_g": 1.0 + 0.02 * nrm(ks[10], (DEPTH, M_WIDTH)),
        "rel_bias": 0.1 * nrm(ks[11], (DEPTH, A_HEADS, N_REL)),
        "w_proj_m": M_WIDTH ** -0.5 * nrm(ks[12], (DEPTH, M_WIDTH, D_MODEL)),
        "w_proj_a": A_WIDTH ** -0.5 * nrm(ks[13], (DEPTH, A_WIDTH, D_MODEL)),
        "w_out": D_MODEL ** -0.5 * nrm(ks[14], (DEPTH, D_MODEL, D_MODEL)),
        "final_norm_g": 1.0 + 0.02 * nrm(ks[15], (D_MODEL,)),
    }


def reference(x, c, w_ada, b_ada, norm_g, w_in, b_if, conv_w, conv_b, mh_norm_g,
              rel_bias, w_proj_m, w_proj_a, w_out, final_norm_g):
    for layer in range(DEPTH):
        x = hybrid_layer(x, c, w_ada[layer], b_ada[layer], norm_g[layer], w_in[layer],
                         b_if[layer], conv_w[layer], conv_b[layer], mh_norm_g[layer],
                         rel_bias[layer], w_proj_m[layer], w_proj_a[layer], w_out[layer])
    return rmsnorm(x, final_norm_g)
```

```python
import os
import numpy as np
import ml_dtypes
from contextlib import ExitStack
import concourse.bass as bass
import concourse.mybir as mybir
from concourse.bass_utils import run_bass_kernel_spmd

F32 = mybir.dt.float32
BF16 = mybir.dt.bfloat16
ALU = mybir.AluOpType
AF = mybir.ActivationFunctionType
AX = mybir.AxisListType

D = 2048
NT = 64
NPRE = 48
NOWN = 16
EPS = 1e-6
C_MQ, C_MK, C_MV, C_MO, C_MZ, C_MI, C_MF = 0, 1024, 2048, 3072, 4096, 5120, 5124
C_AQ, C_AK, C_AV, C_AZ, C_GM, C_GA = 5128, 6152, 7176, 8200, 9224, 11272
IN_COLS = 13320
NEG = -30000.0
ENGS = ("sp", "act", "dve", "pool", "pe")
KSTOP = os.environ.get("KSTOP")


class _Stop(Exception):
    pass


class Op:
    __slots__ = ("eng", "fn", "deps", "dma", "sig", "sem", "val")

    def __init__(self, eng, fn, dma):
        self.eng, self.fn, self.dma = eng, fn, dma
        self.deps, self.sig, self.sem, self.val = [], dma, None, 0


class Sched:
    ND = 40

    def __init__(self, nc, es):
        self.nc = nc
        self.eops = {e: [] for e in ENGS}
        self.lastw, self.readers = {}, {}
        self.csem = {e: es.enter_context(nc.semaphore("cs_" + e)) for e in ENGS}
        self.dsem = [es.enter_context(nc.semaphore("ds%d" % i)) for i in range(self.ND)]
        self.dlast = [None] * self.ND
        self.duse = [0] * self.ND
        self.dn = 0

    @staticmethod
    def _is_psum(k):
        return k == "psS" or (isinstance(k, tuple) and len(k) > 0 and k[0] in ("ps", "psb", "psS"))

    def add(self, eng, fn, r=(), w=(), dma=False):
        w = tuple(w) + tuple(k for k in r if self._is_psum(k))
        r = tuple(k for k in r if not self._is_psum(k))
        op = Op(eng, fn, dma)
        deps = []
        for k in r:
            if k in self.lastw:
                deps.append(self.lastw[k])
        for k in w:
            if k in self.lastw:
                deps.append(self.lastw[k])
            deps.extend(self.readers.get(k, ()))
        if dma:
            i = self.dn % self.ND
            self.dn += 1
            if self.dlast[i] is not None:
                deps.append(self.dlast[i])
            self.duse[i] += 1
            op.sem, op.val = self.dsem[i], 16 * self.duse[i]
            self.dlast[i] = op
        seen = set()
        for d in deps:
            if d is op or id(d) in seen:
                continue
            seen.add(id(d))
            if d.eng == "pe" and eng == "pe" and not d.dma:
                continue
            d.sig = True
            op.deps.append(d)
        for k in r:
            lst = self.readers.setdefault(k, [])
            if not dma:
                lst[:] = [o_ for o_ in lst if o_.dma or o_.eng != eng]
            lst.append(op)
        for k in w:
            self.lastw[k] = op
            self.readers[k] = []
        self.eops[eng].append(op)
        return op

    def barrier(self):
        lasts = []
        for e in ENGS:
            for o in reversed(self.eops[e]):
                if not o.dma and o.fn is not None:
                    lasts.append(o)
                    break
        pend = [d for d in self.dlast if d is not None]
        for e in ENGS:
            op = Op(e, None, False)
            for d in lasts + pend:
                if d.eng == e and not d.dma:
                    continue
                d.sig = True
                op.deps.append(d)
            self.eops[e].append(op)
        self.lastw, self.readers = {}, {}

    def finish(self, keys):
        op = Op("sp", None, False)
        for d in self.dlast:
            if d is not None:
                op.deps.append(d)
        self.eops["sp"].append(op)

    def emit(self):
        nc = self.nc
        for e in ENGS:
            c = 0
            for o in self.eops[e]:
                if o.dma or o.fn is None:
                    continue
                if o.sig:
                    c += 1
                    o.sem, o.val = self.csem[e], c

        def run(ename, eng):
            known = {}
            for o in self.eops[ename]:
                for d in o.deps:
                    key = id(d.sem)
                    if known.get(key, 0) >= d.val:
                        continue
                    eng.wait_ge(d.sem, d.val)
                    known[key] = d.val
                if o.fn is None:
                    continue
                ins = o.fn(eng)
                if o.sig:
                    ins.then_inc(o.sem, 16 if o.dma else 1)

        with nc.Block() as block:
            @block.sync
            def _(e):
                run("sp", e)

            @block.scalar
            def _(e):
                run("act", e)

            @block.vector
            def _(e):
                run("dve", e)

            @block.gpsimd
            def _(e):
                run("pool", e)

            @block.tensor
            def _(e):
                run("pe", e)


def build_nc():
    nc = bass.Bass("TRN2", target_bir_lowering=False)

    def din(name, shape, dt=F32):
        return nc.dram_tensor(name, list(shape), dt, kind="ExternalInput").ap()

    xl = din("xl", [NT * 128, D])
    cT = din("cT", [128, 16])
    w_ada = din("w_ada", [D, 3 * D])
    b_ada = din("b_ada", [1, 3 * D])
    ngT = din("ngT", [128, 16])
    w_in = din("w_in", [D, IN_COLS])
    bif_bc = din("bif_bc", [128, 8])
    convw = din("convw", [128, 16, 4])
    convb = din("convb", [128, 16])
    mhg_bc = din("mhg_bc", [128, 1024])
    relmat = din("relmat", [8, 128, 640])
    amask = din("amask", [128, 640])
    w_pm = din("w_pm", [1024, D])
    w_pa = din("w_pa", [1024, D])
    w_out = din("w_out", [D, D])
    fg_bc = din("fg_bc", [128, D])
    padneg = din("padneg", [128, NT])
    tilevalid = din("tilevalid", [128, NT])
    negtile = din("negtile", [128, NT])
    ident_d = din("ident", [128, 128], BF16)
    tri_d = din("tri", [128, 128])
    ones_d = din("ones", [128, 128])
    mst_d = din("maskst", [128, 128])
    y_out = nc.dram_tensor("y", [NOWN * 128, D], F32, kind="ExternalOutput").ap()
    dbg = nc.dram_tensor("dbg", [128, 8192], F32, kind="ExternalOutput").ap() if KSTOP else None
    yT_d = nc.dram_tensor("yT_d", [16, 128, NOWN * 128], BF16, kind="Internal").ap()
    mT_d = nc.dram_tensor("mT_d", [16, 128, NOWN * 128], BF16, kind="Internal").ap()

    es = ExitStack()
    with es, nc.allow_low_precision("bf16 matmul operands, fp32 accumulation"), \
            nc.allow_non_contiguous_dma("column-sliced weight loads"):
        S = Sched(nc, es)

        def sb(name, shape, dt=F32):
            return es.enter_context(nc.sbuf_tensor("s_" + name, list(shape), dt))

        def pst(name, shape, dt=F32):
            return es.enter_context(nc.psum_tensor(name, list(shape), dt))

        ident = sb("ident", [128, 128], BF16)
        tri = sb("tri", [128, 128])
        ones = sb("ones", [128, 128])
        gs = sb("gs", [128, 16])
        shift = sb("shift", [128, 16])
        ngt_s = sb("ngt_s", [128, 16])
        gate_bc = sb("gate_bc", [128, D])
        cw = sb("cw", [128, 16, 4])
        cb_ = sb("cb_", [128, 16])
        bif = sb("bif", [128, 8])
        mhg = sb("mhg", [128, 1024])
        pneg = sb("pneg", [128, NT])
        tval = sb("tval", [128, NT])
        ntile = sb("ntile", [128, NT])
        amask_s = sb("amask_s", [128, 640])
        wif = sb("wif", [128, 16, 8], BF16)
        wifs = sb("wifs", [128, 16, 8])
        state = [[sb("st%d%d" % (h, b), [128, 257]) for b in range(2)] for h in range(4)]
        ctbf = [[sb("ct%d%d" % (h, b), [128, 257], BF16) for b in range(2)] for h in range(4)]
        hT_halo = sb("hT_halo", [128, 16, 512], BF16)
        wk_o = sb("wk_o", [128, NOWN, 4])
        wa_o = sb("wa_o", [128, NOWN, 4])
        eB_o = sb("eB_o", [128, NOWN, 4])
        ebc_o = sb("ebc_o", [128, NOWN, 4])
        sm = sb("sm", [128, 64])
        ARENA_COLS = 40000
        arena = sb("arena", [128, ARENA_COLS])

        ps = [pst("ps%d" % i, [128, 512]) for i in range(4)]
        psS = pst("psS", [128, 1024])
        psb = [pst("psb%d" % i, [128, 1024], BF16) for i in range(2)]
        rot = {"ps": 0, "psb": 0, "cast": 0, "q": 0}

        def getps():
            i = rot["ps"] % 4
            rot["ps"] += 1
            return ps[i], ("ps", i)

        def getpsb():
            i = rot["psb"] % 2
            rot["psb"] += 1
            return psb[i], ("psb", i)

        aoff = [0]

        def areset():
            aoff[0] = 0

        def af32(cols):
            o = aoff[0]
            aoff[0] += cols
            assert aoff[0] <= ARENA_COLS, aoff[0]
            return arena[:, o:o + cols]

        def abf(cols):
            n = (cols + 1) // 2
            return af32(n).bitcast(BF16)[:, 0:cols]

        def dma(q, out, in_, r, w):
            S.add(q, lambda e, o=out, i=in_: e.dma_start(out=o, in_=i), r=r, w=w, dma=True)

        def dmaq():
            rot["q"] += 1
            return "sp" if rot["q"] % 2 else "pool"

        def act(out, in_, func, r, w, bias=None, scale=None, accum=None):
            kw = {}
            if bias is not None:
                kw["bias"] = bias
            if scale is not None:
                kw["scale"] = scale
            if accum is not None:
                kw["accum_out"] = accum
            S.add("act", lambda e: e.activation(out, in_, func, **kw), r=r, w=w)

        def ts(eng, out, in0, s1, s2, op0, op1, r, w):
            if s2 is None:
                s2, op1 = (1.0, ALU.mult) if op0 == ALU.add else (0.0, ALU.add)
            S.add(eng, lambda e: e.tensor_scalar(out, in0, s1, s2, op0, op1), r=r, w=w)

        def rsqrt_col(col, key):
            S.add("act", lambda e: e.activation(col, col, AF.Sqrt), r=(key,), w=(key,))
            S.add("dve", lambda e: e.reciprocal(col, col), r=(key,), w=(key,))

        def tt(eng, out, in0, in1, op, r, w):
            S.add(eng, lambda e: e.tensor_tensor(out, in0, in1, op), r=r, w=w)

        def stt(eng, out, in0, sc, in1, op0, op1, r, w):
            S.add(eng, lambda e: e.scalar_tensor_tensor(out, in0, sc, in1, op0, op1), r=r, w=w)

        def mm(out, lhsT, rhs, start, stop, r, w):
            S.add("pe", lambda e: e.matmul(out, lhsT, rhs, start=start, stop=stop), r=r, w=w)

        def tr(out, in_, r, w):
            S.add("pe", lambda e: e.transpose(out, in_, ident[:]), r=tuple(r) + ("ident",), w=w)

        def cast(out, in_, r, w, eng=None):
            if eng is None:
                rot["cast"] += 1
                eng = ("act", "pool")[rot["cast"] % 2]
            if eng == "act":
                S.add("act", lambda e: e.copy(out, in_), r=r, w=w)
            else:
                S.add(eng, lambda e: e.tensor_copy(out, in_), r=r, w=w)

        def memset(eng, ap, v, w):
            S.add(eng, lambda e: e.memset(ap, v), w=w)

        def stop(tag, dumps=()):
            if KSTOP != tag:
                return
            S.barrier()
            off = 0
            for ap_, n_ in dumps:
                dma("sp", dbg[:, off:off + n_], ap_, r=(), w=(("dbg", off),))
                off += n_
            raise _Stop()

        def body():
            for (t_, d_, k_) in ((ident, ident_d, "ident"), (tri, tri_d, "tri"), (ones, ones_d, "ones"),
                                 (ngt_s, ngT, "ngt"), (cw, convw, "cw"),
                                 (cb_, convb, "cb"), (bif, bif_bc, "bif"), (mhg, mhg_bc, "mhg"),
                                 (pneg, padneg, "pneg"), (tval, tilevalid, "tval"),
                                 (ntile, negtile, "ntile"), (amask_s, amask, "amask")):
                dma("sp", t_[:], d_, r=(), w=(k_,))
            for k4 in range(4):
                dma("sp", wifs[:, 4 * k4:4 * k4 + 4, :],
                    w_in[k4 * 512:(k4 + 1) * 512, C_MI:C_MI + 8].rearrange("(k p) c -> p k c", p=128), r=(), w=("wifs",))
            cast(wif[:], wifs[:], r=("wifs",), w=("wif",), eng="dve")
            for h in range(4):
                for b in range(2):
                    memset("dve", state[h][b][:], 0.0, w=(("st", h, b),))

            areset()
            sc_in = af32(16)
            scv = af32(16)
            modrow = af32(3 * D)[0:1, :]
            badar = af32(3 * D)[0:1, :]
            stgA = [af32(3072) for _ in range(2)]
            dma("sp", sc_in, cT, r=(), w=("sc_in",))
            dma("sp", badar, b_ada, r=(), w=("badar",))
            act(scv, sc_in, AF.Silu, r=("sc_in",), w=("scv",))
            banks = [(ps[0], ("ps", 0), 0), (ps[1], ("ps", 1), 0), (ps[2], ("ps", 2), 0),
                     (ps[3], ("ps", 3), 0), (psS, ("psS",), 0), (psS, ("psS",), 512)]
            for half in range(2):
                for k in range(16):
                    st_ = stgA[k % 2]
                    key = ("stgA", k % 2)
                    dma(dmaq(), st_, w_ada[k * 128:(k + 1) * 128, half * 3072:(half + 1) * 3072], r=(), w=(key,))
                    for cbk in range(6):
                        t_, pk, off = banks[cbk]
                        mm(t_[0:1, off:off + 512], scv[:, k:k + 1], st_[:, cbk * 512:(cbk + 1) * 512],
                           k == 0, k == 15, r=(key, "scv"), w=(pk,))
                for cbk in range(6):
                    t_, pk, off = banks[cbk]
                    c0 = half * 3072 + cbk * 512
                    tt("dve", modrow[:, c0:c0 + 512], t_[0:1, off:off + 512], badar[:, c0:c0 + 512], ALU.add,
                       r=(pk, "badar"), w=("modrow",))
            pcol, pck = getps()
            for c in range(32):
                mm(pcol[:, c:c + 1], modrow[0:1, c * 128:(c + 1) * 128], ones[0:1, 0:1], True, True,
                   r=("modrow", "ones"), w=(pck,))
            cast(shift[:], pcol[:, 0:16], r=(pck,), w=("shift",), eng="dve")
            stt("dve", gs[:], pcol[:, 16:32], 1.0, ngt_s[:], ALU.add, ALU.mult, r=(pck, "ngt"), w=("gs",))
            for cbk in range(4):
                t_, pk = getps()
                mm(t_[:, 0:512], ones[0:1, :], modrow[0:1, 2 * D + cbk * 512:2 * D + (cbk + 1) * 512], True, True,
                   r=("modrow", "ones"), w=(pk,))
                cast(gate_bc[:, cbk * 512:(cbk + 1) * 512], t_[:, 0:512], r=(pk,), w=("gate_bc",), eng="act")
            S.barrier()
            stop("S1", [(gs[:], 16), (shift[:], 16), (gate_bc[:, 0:64], 64)])

            def frontend(tau, dstf, dkey, xbufs, xn, i2):
                xt = xbufs[i2 % len(xbufs)]
                xk = ("xt", i2 % len(xbufs))
                dma(dmaq(), xt, xl[tau * 128:(tau + 1) * 128, :], r=(), w=(xk,))
                memset("dve", sm[:, 0:1], 0.0, w=("ss",))
                act(xn, xt, AF.Square, r=(xk, "ss"), w=("xn", "ss"), accum=sm[:, 0:1])
                ts("dve", sm[:, 1:2], sm[:, 0:1], 1.0 / D, EPS, ALU.mult, ALU.add, r=("ss",), w=("rstd",))
                rsqrt_col(sm[:, 1:2], "rstd")
                act(xn, xt, AF.Identity, r=(xk, "rstd"), w=("xn",), scale=sm[:, 1:2])
                if tau == 0:
                    stop("F1", [(sm[:], 64)])
                for hh in range(2):
                    pt, ptk = getpsb()
                    for kk in range(8):
                        k = hh * 8 + kk
                        tr(pt[:, kk * 128:(kk + 1) * 128], xn[:, k * 128:(k + 1) * 128], r=("xn",), w=(ptk,))
                    if tau == 0 and hh == 0:
                        stop("F2", [(sm[:], 64)])
                    for kk in range(8):
                        k = hh * 8 + kk
                        if hh == 0:
                            act(dstf(k), pt[:, kk * 128:(kk + 1) * 128], AF.Identity, r=(ptk, "gs", "shift"),
                                w=(dkey[0],), bias=shift[:, k:k + 1], scale=gs[:, k:k + 1])
                        else:
                            ts("dve", dstf(k), pt[:, kk * 128:(kk + 1) * 128], gs[:, k:k + 1], shift[:, k:k + 1],
                               ALU.mult, ALU.add, r=(ptk, "gs", "shift"), w=(dkey[1],))

            def load_w(dst, src_rows, nk, stgs, skey, dkey):
                for k in range(nk):
                    st_ = stgs[k % len(stgs)]
                    key = (skey, k % len(stgs))
                    dma(dmaq(), st_, src_rows(k), r=(), w=(key,))
                    cast(dst(k), st_, r=(key,), w=(dkey,))

            def gates(tau, hsrc, hkey, own_i):
                pg, pgk = getps()
                for k in range(16):
                    mm(pg[:, 0:8], hsrc(k), wif[:, k, :], k == 0, k == 15, r=tuple(hkey) + ("wif",), w=(pgk,))
                gl = sm[:, 8:16]
                tt("dve", gl, pg[:, 0:8], bif[:], ALU.add, r=(pgk, "bif"), w=("gl",))
                ts("dve", sm[:, 16:20], gl[:, 0:4], pneg[:, tau:tau + 1], None, ALU.add, None, r=("gl", "pneg"), w=("ig",))
                act(sm[:, 20:24], gl[:, 4:8], AF.Exp, r=("gl",), w=("e1",), scale=-1.0)
                ts("dve", sm[:, 20:24], sm[:, 20:24], 1.0, None, ALU.add, None, r=("e1",), w=("e1",))
                act(sm[:, 24:28], sm[:, 20:24], AF.Ln, r=("e1",), w=("lf",))
                ts("dve", sm[:, 24:28], sm[:, 24:28], -1.0, None, ALU.mult, None, r=("lf",), w=("lf",))
                pc, pckk = getps()
                mm(pc[:, 0:4], tri[:], sm[:, 24:28], True, True, r=("lf", "tri"), w=(pckk,))
                mm(pc[:, 4:8], ones[:], sm[:, 24:28], True, True, r=("lf", "ones"), w=(pckk,))
                tt("dve", sm[:, 28:32], sm[:, 16:20], pc[:, 0:4], ALU.subtract, r=("ig", pckk), w=("t1",))
                tt("dve", sm[:, 32:36], sm[:, 28:32], pc[:, 4:8], ALU.add, r=("t1", pckk), w=("t2",))
                if own_i is None:
                    wk, eB = sm[:, 36:40], sm[:, 40:44]
                    wkk, eBk = "wk", "eB"
                else:
                    wk, eB = wk_o[:, own_i, :], eB_o[:, own_i, :]
                    wkk = eBk = ("gown", own_i)
                act(wk, sm[:, 32:36], AF.Exp, r=("t2",), w=(wkk,))
                ts("dve", wk, wk, 0.0625, None, ALU.mult, None, r=(wkk,), w=(wkk,))
                act(eB, pc[:, 4:8], AF.Exp, r=(pckk,), w=(eBk,))
                if own_i is not None:
                    act(wa_o[:, own_i, :], sm[:, 28:32], AF.Exp, r=("t1",), w=(wkk,))
                    ts("dve", wa_o[:, own_i, :], wa_o[:, own_i, :], 0.0625, None, ALU.mult, None, r=(wkk,), w=(wkk,))
                    act(ebc_o[:, own_i, :], pc[:, 0:4], AF.Exp, r=(pckk,), w=(wkk,))
                return wk, eB, wkk, eBk

            def conv_silu(pre, prek, cblk, accb, acck, out, outk):
                ts("dve", accb, pre[:, 3:515], cw[:, cblk, 3:4], cb_[:, cblk:cblk + 1], ALU.mult, ALU.add,
                   r=(prek, "cw", "cb"), w=(acck,))
                for tap in range(3):
                    stt("dve", accb, pre[:, tap:tap + 512], cw[:, cblk, tap:tap + 1], accb, ALU.mult, ALU.add,
                        r=(prek, acck, "cw"), w=(acck,))
                act(out, accb, AF.Silu, r=(acck,), w=(outk,))

            def state_update(h, kT_blk, kTk, vaug, vk, wk_col, wkk, eB_col, eBk, kpp, kppk, refresh_bf):
                pt, ptk = getpsb()
                for blk in range(2):
                    tr(pt[:, blk * 128:(blk + 1) * 128], kT_blk(blk), r=(kTk[blk],), w=(ptk,))
                ts("dve", kpp, pt[:, 0:256], wk_col, None, ALU.mult, None, r=(ptk, wkk), w=(kppk,))
                for blk in range(2):
                    p_, pk = getps()
                    mm(p_[:, 0:257], kpp[:, blk * 128:(blk + 1) * 128], vaug, True, True, r=(kppk, vk), w=(pk,))
                    stt("dve", state[h][blk][:], state[h][blk][:], eB_col, p_[:, 0:257], ALU.mult, ALU.add,
                        r=(pk, eBk, ("st", h, blk)), w=(("st", h, blk),))
                    if refresh_bf:
                        cast(ctbf[h][blk][:], state[h][blk][:], r=(("st", h, blk),), w=(("ct", h, blk),), eng="act")

            areset()
            WA = abf(16 * 2048).rearrange("p (k c) -> p k c", k=16)
            hTgA = abf(16 * 512).rearrange("p (k c) -> p k c", k=16)
            xn = abf(D)
            kT = abf(8 * 512).rearrange("p (b c) -> p b c", b=8)
            vaugs = [[abf(258)[:, 0:257] for _ in range(4)] for _ in range(2)]
            kpps = [abf(256) for _ in range(2)]
            stgs = [af32(2048) for _ in range(2)]
            xbufs = stgs
            kpre = af32(8 * 515).rearrange("p (b c) -> p b c", b=8)
            accb = af32(512)
            load_w(lambda k: WA[:, k, :], lambda k: w_in[k * 128:(k + 1) * 128, C_MK:C_MK + 2048], 16, stgs, "xt", "WA")
            stop("A00", [(sm[:], 64)])
            memset("pool", kpre[:, :, 0:3], 0.0, w=tuple(("kpre", c) for c in range(8)))
            for vb in range(2):
                for h in range(4):
                    memset("pool", vaugs[vb][h][:, 256:257], 1.0, w=(("vaug", vb, h),))
            fe_i = 0
            for g in range(NPRE // 4):
                hT = hT_halo if g == NPRE // 4 - 1 else hTgA
                hgk = tuple(("hTg", i_, p_) for i_ in range(4) for p_ in range(2))
                for i in range(4):
                    frontend(4 * g + i, lambda k, i=i, hT=hT: hT[:, k, i * 128:(i + 1) * 128], (("hTg", i, 0), ("hTg", i, 1)),
                             xbufs, xn, fe_i)
                    fe_i += 1
                if g == 0:
                    stop("A0", [(sm[:], 64)])
                for cbk in range(8):
                    p_, pk = getps()
                    for k in range(16):
                        mm(p_[:, 0:512], WA[:, k, cbk * 128:(cbk + 1) * 128], hT[:, k, :], k == 0, k == 15,
                           r=("WA",) + hgk, w=(pk,))
                    cast(kpre[:, cbk, 3:515], p_[:, 0:512], r=(pk,), w=(("kpre", cbk),), eng="act")
                    conv_silu(kpre[:, cbk, :], ("kpre", cbk), 8 + cbk, accb, "accb", kT[:, cbk, :], ("kT", cbk))
                    ts("dve", kpre[:, cbk, 0:3], kpre[:, cbk, 512:515], tval[:, 4 * g + 3:4 * g + 4], None, ALU.mult, None,
                       r=(("kpre", cbk), "tval"), w=(("kpre", cbk),))
                if g == 0:
                    stop("A1", [(kpre[:, 0, :], 515), (kpre[:, 7, :], 515), (accb, 512)])
                for i in range(4):
                    tau = 4 * g + i
                    vb = tau % 2
                    for half in range(2):
                        p_, pk = getps()
                        for k in range(16):
                            mm(p_[:, 0:512], hT[:, k, i * 128:(i + 1) * 128], WA[:, k, 1024 + half * 512:1024 + (half + 1) * 512],
                               k == 0, k == 15, r=("WA",) + hgk, w=(pk,))
                        for hh in range(2):
                            h = 2 * half + hh
                            cast(vaugs[vb][h][:, 0:256], p_[:, hh * 256:(hh + 1) * 256], r=(pk,), w=(("vaug", vb, h),),
                                 eng=("act", "dve")[hh])
                    wk, eB, wkk, eBk = gates(tau, lambda k, i=i, hT=hT: hT[:, k, i * 128:(i + 1) * 128],
                                             (("hTg", i, 0), ("hTg", i, 1)), None)
                    if tau == 0:
                        stop("A2", [(sm[:], 64)])
                    for h in range(4):
                        state_update(h, lambda blk, h=h, i=i: kT[:, 2 * h + blk, i * 128:(i + 1) * 128],
                                     (("kT", 2 * h), ("kT", 2 * h + 1)), vaugs[vb][h], ("vaug", vb, h), wk[:, h:h + 1], wkk, eB[:, h:h + 1], eBk,
                                     kpps[h % 2], ("kpp", h % 2), False)
                    if tau == 0:
                        stop("A3", [(state[0][0][:], 257), (state[3][1][:], 257), (sm[:], 64)])
            S.barrier()
            stop("A", [(state[0][0][:], 257), (state[3][1][:], 257), (sm[:], 64)])

            areset()
            hT_own = abf(16 * 2048).rearrange("p (k c) -> p k c", k=16)
            mark_b = aoff[0]
            xn = abf(D)
            xbufs = [af32(D) for _ in range(2)]
            for i in range(NOWN):
                frontend(NPRE + i, lambda k, i=i: hT_own[:, k, i * 128:(i + 1) * 128], (("hTo", i, 0), ("hTo", i, 1)), xbufs, xn, i)
                gates(NPRE + i, lambda k, i=i: hT_own[:, k, i * 128:(i + 1) * 128], (("hTo", i, 0), ("hTo", i, 1)), i)
            for h in range(4):
                for b in range(2):
                    cast(ctbf[h][b][:], state[h][b][:], r=(("st", h, b),), w=(("ct", h, b),), eng="act")
            S.barrier()
            stop("B0", [(wk_o[:].rearrange("p a b -> p (a b)"), 64), (wa_o[:].rearrange("p a b -> p (a b)"), 64), (eB_o[:].rearrange("p a b -> p (a b)"), 64), (ebc_o[:].rearrange("p a b -> p (a b)"), 64), (state[0][0][:], 257)])

            aoff[0] = mark_b
            WG = abf(16 * 1280).rearrange("p (k c) -> p k c", k=16)
            qkT = abf(4 * 512).rearrange("p (b c) -> p b c", b=4)
            vaug1 = [abf(258)[:, 0:257] for _ in range(2)]
            kpps = [abf(256) for _ in range(2)]
            STb = [abf(128) for _ in range(2)]
            ybf = [abf(256) for _ in range(2)]
            yTt = [abf(256).rearrange("p (b c) -> p b c", b=2) for _ in range(2)]
            stg1 = [af32(1280) for _ in range(2)]
            qkpre = af32(4 * 515).rearrange("p (b c) -> p b c", b=4)
            accb = af32(512)
            sgo = af32(256)
            slz = af32(256)
            Gt = af32(256)
            junk = af32(256)
            w5 = w_in[:, 0:5120].rearrange("r (s c) -> r s c", c=1024)
            for vb in range(2):
                memset("pool", vaug1[vb][:, 256:257], 1.0, w=(("vaug1", vb),))
            allh = tuple(() for i in range(NOWN))
            for h in range(4):
                load_w(lambda k: WG[:, k, :].rearrange("p (s c) -> p s c", c=256),
                       lambda k, h=h: w5[k * 128:(k + 1) * 128, :, h * 256:(h + 1) * 256],
                       16, [s_.rearrange("p (s c) -> p s c", c=256) for s_ in stg1], "stg1", "WG")
                for blk in range(4):
                    p_, pk = getps()
                    for k in range(16):
                        mm(p_[:, 0:3], WG[:, k, blk * 128:(blk + 1) * 128], hT_halo[:, k, 509:512], k == 0, k == 15,
                           r=("WG",), w=(pk,))
                    ts("dve", qkpre[:, blk, 0:3], p_[:, 0:3], tval[:, NPRE - 1:NPRE], None, ALU.mult, None,
                       r=(pk, "tval"), w=(("qkpre", blk),))
                for g in range(4):
                    for blk in range(4):
                        p_, pk = getps()
                        for k in range(16):
                            mm(p_[:, 0:512], WG[:, k, blk * 128:(blk + 1) * 128], hT_own[:, k, g * 512:(g + 1) * 512],
                               k == 0, k == 15, r=("WG",), w=(pk,))
                        cast(qkpre[:, blk, 3:515], p_[:, 0:512], r=(pk,), w=(("qkpre", blk),), eng="act")
                        cidx = (2 * h + blk) if blk < 2 else (8 + 2 * h + blk - 2)
                        conv_silu(qkpre[:, blk, :], ("qkpre", blk), cidx, accb, "accb", qkT[:, blk, :], ("qkT", blk))
                        cast(qkpre[:, blk, 0:3], qkpre[:, blk, 512:515], r=(("qkpre", blk),), w=(("qkpre", blk),), eng="dve")
                    for i in range(4):
                        t = 4 * g + i
                        vb = t % 2
                        tsl = slice(i * 128, (i + 1) * 128)
                        pA, pAk = getps()
                        for k in range(16):
                            mm(pA[:, 0:512], hT_own[:, k, t * 128:(t + 1) * 128], WG[:, k, 512:1024], k == 0, k == 15,
                               r=("WG",), w=(pAk,))
                        pB, pBk = getps()
                        for k in range(16):
                            mm(pB[:, 0:256], hT_own[:, k, t * 128:(t + 1) * 128], WG[:, k, 1024:1280], k == 0, k == 15,
                               r=("WG",), w=(pBk,))
                        vk = ("vaug1", vb)
                        cast(vaug1[vb][:, 0:256], pA[:, 0:256], r=(pAk,), w=(vk,), eng="act")
                        act(sgo, pA[:, 256:512], AF.Sigmoid, r=(pAk,), w=("sgo",))
                        act(slz, pB[:, 0:256], AF.Silu, r=(pBk,), w=("slz",))
                        tt("dve", Gt, sgo, slz, ALU.mult, r=("sgo", "slz"), w=("Gt",))
                        tt("dve", Gt, Gt, mhg[:, h * 256:(h + 1) * 256], ALU.mult, r=("Gt", "mhg"), w=("Gt",))
                        pS, pSk = getps()
                        for blk in range(2):
                            mm(pS[:, 0:128], qkT[:, 2 + blk, tsl], qkT[:, blk, tsl], blk == 0, blk == 1,
                               r=(("qkT", blk), ("qkT", 2 + blk)), w=(pSk,))
                        gk = ("gown", t)
                        stt("dve", STb[vb], pS[:, 0:128], wa_o[:, t, h:h + 1], tri[:], ALU.mult, ALU.mult,
                            r=(pSk, gk, "tri"), w=(("STb", vb),))
                        pN, pNk = getps()
                        mm(pN[:, 0:257], STb[vb], vaug1[vb], True, False, r=(("STb", vb), vk), w=(pNk,))
                        for blk in range(2):
                            mm(pN[:, 0:257], qkT[:, blk, tsl], ctbf[h][blk][:], False, blk == 1,
                               r=(("qkT", blk), ("ct", h, blk)), w=(pNk,))
                        tt("dve", sm[:, 44:45], pN[:, 256:257], ebc_o[:, t, h:h + 1], ALU.mult, r=(pNk, gk), w=("d1",))
                        ts("dve", sm[:, 54:55], sm[:, 44:45], -1.0, None, ALU.mult, None, r=("d1",), w=("d1n",))
                        tt("dve", sm[:, 44:45], sm[:, 44:45], sm[:, 54:55], ALU.max, r=("d1", "d1n"), w=("d1",))
                        ts("dve", sm[:, 44:45], sm[:, 44:45], 1.0, 1.0, ALU.max, ALU.mult, r=("d1",), w=("d1",))
                        S.add("dve", lambda e: e.reciprocal(sm[:, 44:45], sm[:, 44:45]), r=("d1",), w=("d1",))
                        tt("dve", sm[:, 45:46], ebc_o[:, t, h:h + 1], sm[:, 44:45], ALU.mult, r=("d1", gk), w=("rr",))
                        memset("dve", sm[:, 46:47], 0.0, w=("ss2",))
                        act(junk, pN[:, 0:256], AF.Square, r=(pNk, "rr", "ss2"), w=("junk", "ss2"), scale=sm[:, 45:46],
                            accum=sm[:, 46:47])
                        ts("dve", sm[:, 47:48], sm[:, 46:47], 1.0 / 256, EPS, ALU.mult, ALU.add, r=("ss2",), w=("r2",))
                        rsqrt_col(sm[:, 47:48], "r2")
                        tt("dve", sm[:, 47:48], sm[:, 47:48], sm[:, 45:46], ALU.mult, r=("r2", "rr"), w=("r2",))
                        stt("dve", ybf[vb], pN[:, 0:256], sm[:, 47:48], Gt, ALU.mult, ALU.mult, r=(pNk, "r2", "Gt"),
                            w=(("ybf", vb),))
                        pt, ptk = getpsb()
                        for blk in range(2):
                            tr(pt[:, blk * 128:(blk + 1) * 128], ybf[vb][:, blk * 128:(blk + 1) * 128], r=(("ybf", vb),), w=(ptk,))
                        cast(yTt[vb].rearrange("p b c -> p (b c)"), pt[:, 0:256], r=(ptk,), w=(("yTt", vb),), eng="act")
                        dma(dmaq(), yT_d[2 * h:2 * h + 2, :, t * 128:(t + 1) * 128].rearrange("b p t -> p b t"), yTt[vb],
                            r=(("yTt", vb),), w=(("yTd", 2 * h, t),))
                        state_update(h, lambda blk, tsl=tsl: qkT[:, 2 + blk, tsl], (("qkT", 2), ("qkT", 3)), vaug1[vb], vk,
                                     wk_o[:, t, h:h + 1], gk, eB_o[:, t, h:h + 1], gk, kpps[vb], ("kpp", vb), True)
            S.barrier()
            stop("B1", [(state[0][0][:], 257)])

            aoff[0] = mark_b
            WG2 = abf(16 * 512).rearrange("p (k c) -> p k c", k=16)
            akT = abf(2560)
            aqT = abf(2048)
            Vt = abf(20 * 128).rearrange("p (t c) -> p t c", t=20)
            slzA = abf(16 * 128).rearrange("p (t c) -> p t c", t=16)
            Pb = [abf(640) for _ in range(2)]
            PT = [abf(640) for _ in range(2)]
            yab = [abf(128) for _ in range(2)]
            yaT = [abf(128) for _ in range(2)]
            stg2 = [af32(512) for _ in range(2)]
            rb = af32(640)
            bm = af32(640)
            sbuf_s = [af32(640) for _ in range(2)]
            wA = w_in[:, C_AQ:C_AQ + 4096].rearrange("r (s c) -> r s c", c=1024)
            for h in range(8):
                load_w(lambda k: WG2[:, k, :].rearrange("p (s c) -> p s c", c=128),
                       lambda k, h=h: wA[k * 128:(k + 1) * 128, :, h * 128:(h + 1) * 128],
                       16, [s_.rearrange("p (s c) -> p s c", c=128) for s_ in stg2], "stg2", "WG2")
                dma("sp", rb, relmat[h], r=(), w=("rb",))
                tt("pool", bm, rb, amask_s[:], ALU.add, r=("rb", "amask"), w=("bm",))
                for g5 in range(5):
                    src = hT_halo if g5 == 0 else hT_own[:, :, (g5 - 1) * 512:g5 * 512]
                    srk = ()
                    p_, pk = getps()
                    for k in range(16):
                        mm(p_[:, 0:512], WG2[:, k, 128:256], src[:, k, :], k == 0, k == 15, r=("WG2",) + srk, w=(pk,))
                    cast(akT[:, g5 * 512:(g5 + 1) * 512], p_[:, 0:512], r=(pk,), w=(("akT", g5),), eng="act")
                    if g5 > 0:
                        p_, pk = getps()
                        for k in range(16):
                            mm(p_[:, 0:512], WG2[:, k, 0:128], src[:, k, :], k == 0, k == 15, r=("WG2",) + srk, w=(pk,))
                        act(aqT[:, (g5 - 1) * 512:g5 * 512], p_[:, 0:512], AF.Copy, r=(pk,), w=(("aqT", g5 - 1),),
                            scale=float(128 ** -0.5))
                    for i in range(4):
                        ttile = g5 * 4 + i
                        p_, pk = getps()
                        for k in range(16):
                            mm(p_[:, 0:256], src[:, k, i * 128:(i + 1) * 128], WG2[:, k, 256:512], k == 0, k == 15,
                               r=("WG2",) + srk, w=(pk,))
                        cast(Vt[:, ttile, :], p_[:, 0:128], r=(pk,), w=(("Vt", ttile),), eng="act")
                        if g5 > 0:
                            act(slzA[:, ttile - 4, :], p_[:, 128:256], AF.Silu, r=(pk,), w=(("slzA", ttile - 4),))
                for t in range(NOWN):
                    vb = t % 2
                    gq = ("aqT", t // 4)
                    kk0 = tuple(("akT", x) for x in sorted({t // 4, (t + 3) // 4, (t + 4) // 4}))
                    mm(psS[:, 0:512], aqT[:, t * 128:(t + 1) * 128], akT[:, t * 128:t * 128 + 512], True, True,
                       r=(gq,) + kk0, w=("psS",))
                    mm(psS[:, 512:640], aqT[:, t * 128:(t + 1) * 128], akT[:, t * 128 + 512:t * 128 + 640], True, True,
                       r=(gq,) + kk0, w=("psS",))
                    sbt = sbuf_s[vb]
                    sk = ("sbt", vb)
                    tt("dve", sbt, psS[:, 0:640], bm, ALU.add, r=("psS", "bm"), w=(sk,))
                    for kb in range(max(0, 4 - t)):
                        lt = NPRE - 4 + t + kb
                        ts("dve", sbt[:, kb * 128:(kb + 1) * 128], sbt[:, kb * 128:(kb + 1) * 128], ntile[:, lt:lt + 1], None,
                           ALU.add, None, r=(sk, "ntile"), w=(sk,))
                    S.add("dve", lambda e, sbt=sbt: e.reduce_max(sm[:, 48:49], sbt, AX.X), r=(sk,), w=("mx",))
                    ts("dve", sm[:, 48:49], sm[:, 48:49], -1.0, None, ALU.mult, None, r=("mx",), w=("mx",))
                    memset("dve", sm[:, 49:50], 0.0, w=("rsum",))
                    act(Pb[vb], sbt, AF.Exp, r=(sk, "mx", "rsum"), w=(("Pb", vb), "rsum"), bias=sm[:, 48:49],
                        accum=sm[:, 49:50])
                    pt, ptk = getpsb()
                    for kb in range(5):
                        tr(pt[:, kb * 128:(kb + 1) * 128], Pb[vb][:, kb * 128:(kb + 1) * 128], r=(("Pb", vb),), w=(ptk,))
                    cast(PT[vb], pt[:, 0:640], r=(ptk,), w=(("PT", vb),), eng="act")
                    pO, pOk = getps()
                    for kb in range(5):
                        mm(pO[:, 0:128], PT[vb][:, kb * 128:(kb + 1) * 128], Vt[:, t + kb, :], kb == 0, kb == 4,
                           r=(("PT", vb), ("Vt", t + kb)), w=(pOk,))
                    S.add("dve", lambda e: e.reciprocal(sm[:, 50:51], sm[:, 49:50]), r=("rsum",), w=("rrs",))
                    stt("dve", yab[vb], pO[:, 0:128], sm[:, 50:51], slzA[:, t, :], ALU.mult, ALU.mult,
                        r=(pOk, "rrs", ("slzA", t)), w=(("yab", vb),))
                    pt2, pt2k = getpsb()
                    tr(pt2[:, 0:128], yab[vb], r=(("yab", vb),), w=(pt2k,))
                    cast(yaT[vb], pt2[:, 0:128], r=(pt2k,), w=(("yaT", vb),), eng="act")
                    dma(dmaq(), yT_d[8 + h, :, t * 128:(t + 1) * 128], yaT[vb], r=(("yaT", vb),), w=(("yTd", 8 + h, t),))
            S.barrier()
            stop("B2", [(sm[:], 64)])

            aoff[0] = mark_b
            yT_res = abf(16 * 2048).rearrange("p (b c) -> p b c", b=16)
            wgm = [abf(16 * 128).rearrange("p (k c) -> p k c", k=16) for _ in range(1)]
            wga = [abf(16 * 128).rearrange("p (k c) -> p k c", k=16) for _ in range(1)]
            wpm = [abf(8 * 128).rearrange("p (k c) -> p k c", k=8) for _ in range(1)]
            wpa = [abf(8 * 128).rearrange("p (k c) -> p k c", k=8) for _ in range(1)]
            mTb = [abf(512) for _ in range(2)]
            stg3 = [af32(16 * 128).rearrange("p (k c) -> p k c", k=16) for _ in range(1)]
            sgm = af32(512)
            sga = af32(512)
            t1b = af32(512)
            for b in range(16):
                dma(dmaq(), yT_res[:, b, :], yT_d[b], r=(), w=("yT_res",))
            si = 0
            for c in range(16):
                cbuf = 0
                for (dst, src, nk, nm) in ((wgm, w_in[:, C_GM + c * 128:C_GM + (c + 1) * 128], 16, "wgm"),
                                           (wga, w_in[:, C_GA + c * 128:C_GA + (c + 1) * 128], 16, "wga"),
                                           (wpm, w_pm[:, c * 128:(c + 1) * 128], 8, "wpm"),
                                           (wpa, w_pa[:, c * 128:(c + 1) * 128], 8, "wpa")):
                    st_ = stg3[0]
                    sk = ("stg3", 0)
                    si += 1
                    for k4 in range(nk // 4):
                        dma(dmaq(), st_[:, 4 * k4:4 * k4 + 4, :],
                            src[k4 * 512:(k4 + 1) * 512, :].rearrange("(k p) c -> p k c", p=128), r=(), w=(sk,))
                    cast(dst[cbuf][:], st_[:, 0:nk, :], r=(sk,), w=((nm, cbuf),))
                for g in range(4):
                    gsl = slice(g * 512, (g + 1) * 512)
                    p1, p1k = getps()
                    for k in range(16):
                        mm(p1[:, 0:512], wgm[cbuf][:, k, :], hT_own[:, k, gsl], k == 0, k == 15, r=(("wgm", cbuf),), w=(p1k,))
                    act(sgm, p1[:, 0:512], AF.Sigmoid, r=(p1k,), w=("sgm",))
                    p2, p2k = getps()
                    for k in range(16):
                        mm(p2[:, 0:512], wga[cbuf][:, k, :], hT_own[:, k, gsl], k == 0, k == 15, r=(("wga", cbuf),), w=(p2k,))
                    act(sga, p2[:, 0:512], AF.Sigmoid, r=(p2k,), w=("sga",))
                    p3, p3k = getps()
                    for k in range(8):
                        mm(p3[:, 0:512], wpm[cbuf][:, k, :], yT_res[:, k, gsl], k == 0, k == 7,
                           r=(("wpm", cbuf), "yT_res"), w=(p3k,))
                    tt("dve", t1b, sgm, p3[:, 0:512], ALU.mult, r=("sgm", p3k), w=("t1b",))
                    p4, p4k = getps()
                    for k in range(8):
                        mm(p4[:, 0:512], wpa[cbuf][:, k, :], yT_res[:, 8 + k, gsl], k == 0, k == 7,
                           r=(("wpa", cbuf), "yT_res"), w=(p4k,))
                    tt("dve", sga, sga, p4[:, 0:512], ALU.mult, r=("sga", p4k), w=("sga",))
                    mb = mTb[(c * 4 + g) % 2]
                    mk_ = ("mTb", (c * 4 + g) % 2)
                    tt("dve", mb, t1b, sga, ALU.add, r=("t1b", "sga"), w=(mk_,))
                    dma(dmaq(), mT_d[c, :, gsl], mb, r=(mk_,), w=(("mTd", c, g),))
            S.barrier()
            stop("B3", [(sm[:], 64)])

            areset()
            Wout = abf(16 * 2048).rearrange("p (k c) -> p k c", k=16)
            mTt = [abf(16 * 128).rearrange("p (c t) -> p c t", c=16) for _ in range(2)]
            fgb = af32(D)
            stgC = [af32(D) for _ in range(2)]
            xts = [af32(D) for _ in range(2)]
            obuf = [af32(D) for _ in range(2)]
            junkC = af32(D)
            dma("sp", fgb, fg_bc, r=(), w=("fgb",))
            for k in range(16):
                st_ = stgC[k % 2]
                sk = ("stgC", k % 2)
                dma(dmaq(), st_, w_out[k * 128:(k + 1) * 128, :], r=(), w=(sk,))
                tt(("dve", "pool")[k % 2], Wout[:, k, :], st_, gate_bc[:], ALU.mult, r=(sk, "gate_bc"), w=("Wout",))
            outkeys = []
            for t in range(NOWN):
                vb = t % 2
                for c4 in range(4):
                    dma(dmaq(), mTt[vb][:, 4 * c4:4 * c4 + 4, :],
                        mT_d[4 * c4:4 * c4 + 4, :, t * 128:(t + 1) * 128].rearrange("c p t -> p c t"), r=(), w=(("mTt", vb),))
                dma(dmaq(), xts[vb], xl[(NPRE + t) * 128:(NPRE + t + 1) * 128, :], r=(), w=(("xts", vb),))
                ob = obuf[vb]
                ok = ("ob", vb)
                for cbk in range(4):
                    p_, pk = getps()
                    for c in range(16):
                        mm(p_[:, 0:512], mTt[vb][:, c, :], Wout[:, c, cbk * 512:(cbk + 1) * 512], c == 0, c == 15,
                           r=(("mTt", vb), "Wout"), w=(pk,))
                    tt("dve", ob[:, cbk * 512:(cbk + 1) * 512], p_[:, 0:512], xts[vb][:, cbk * 512:(cbk + 1) * 512], ALU.add,
                       r=(pk, ("xts", vb)), w=(ok,))
                memset("dve", sm[:, 52:53], 0.0, w=("ssC",))
                act(junkC, ob, AF.Square, r=(ok, "ssC"), w=("junkC", "ssC"), accum=sm[:, 52:53])
                ts("dve", sm[:, 53:54], sm[:, 52:53], 1.0 / D, EPS, ALU.mult, ALU.add, r=("ssC",), w=("rC",))
                rsqrt_col(sm[:, 53:54], "rC")
                stt("dve", ob, ob, sm[:, 53:54], fgb, ALU.mult, ALU.mult, r=(ok, "rC", "fgb"), w=(ok,))
                dma(dmaq(), y_out[t * 128:(t + 1) * 128, :], ob, r=(ok,), w=(("yout", t),))
                outkeys.append(("yout", t))
        try:
            body()
        except _Stop:
            pass
        S.finish(())
        S.emit()
    return nc


_NC_CACHE = {}


def _consts():
    ident = np.eye(128, dtype=np.float32).astype(ml_dtypes.bfloat16)
    s = np.arange(128)[:, None]
    t = np.arange(128)[None, :]
    tri = (s <= t).astype(np.float32)
    ones = np.ones((128, 128), np.float32)
    q = np.arange(128)[:, None]
    kap = np.arange(640)[None, :]
    cq, ck = q // 64, kap // 64
    allowed = (ck >= cq) & (ck <= cq + 8)
    amask = np.where(allowed, 0.0, NEG).astype(np.float32)
    relidx = np.clip(q + 512 - kap, -63, 128) + 63
    return ident, tri, ones, tri.copy(), amask, relidx


def kernel(x, c, w_ada, b_ada, norm_g, w_in, b_if, conv_w, conv_b, mh_norm_g, rel_bias,
           w_proj_m, w_proj_a, w_out, final_norm_g):
    f = np.float32
    x = np.asarray(x, f)
    ident, tri, ones, mst, amask, relidx = _consts()
    if "nc" not in _NC_CACHE:
        _NC_CACHE["nc"] = build_nc()
    nc = _NC_CACHE["nc"]
    rep = lambda v, n=128: np.ascontiguousarray(np.broadcast_to(np.asarray(v, f).reshape(1, -1), (n, np.asarray(v).size)))
    colT = lambda v: np.ascontiguousarray(np.asarray(v, f).reshape(16, 128).T)
    cwl = np.ascontiguousarray(np.asarray(conv_w[0], f).T.reshape(16, 128, 4).transpose(1, 0, 2))
    shared = {
        "w_ada": np.ascontiguousarray(w_ada[0], f), "b_ada": np.ascontiguousarray(b_ada[0], f).reshape(1, -1),
        "ngT": colT(norm_g[0]), "w_in": np.ascontiguousarray(w_in[0], f), "bif_bc": rep(b_if[0]),
        "convw": cwl, "convb": colT(conv_b[0]), "mhg_bc": rep(mh_norm_g[0]),
        "relmat": np.ascontiguousarray(np.asarray(rel_bias[0], f)[:, relidx]), "amask": amask,
        "w_pm": np.ascontiguousarray(w_proj_m[0], f), "w_pa": np.ascontiguousarray(w_proj_a[0], f),
        "w_out": np.ascontiguousarray(w_out[0], f), "fg_bc": rep(final_norm_g),
        "ident": ident, "tri": tri, "ones": ones, "maskst": mst,
    }
    in_maps = []
    for core in range(8):
        b, j = core // 4, core % 4
        npad = (3 - j) * 16
        xl = np.zeros((NT * 128, D), f)
        xl[npad * 128:] = x[b, 0:(j + 1) * 2048]
        valid = (np.arange(NT) >= npad).astype(f)
        m = dict(shared)
        m["xl"] = xl
        m["cT"] = colT(c[b])
        m["padneg"] = rep(np.where(valid > 0, 0.0, NEG))
        m["tilevalid"] = rep(valid)
        m["negtile"] = rep(np.where(valid > 0, 0.0, NEG))
        in_maps.append(m)
    res = run_bass_kernel_spmd(nc, in_maps, core_ids=list(range(8)))
    out = np.empty((2, 8192, D), f)
    for core in range(8):
        b, j = core // 4, core % 4
        out[b, j * 2048:(j + 1) * 2048] = res.results[core]["y"]
    return out
```

```python
import os
import numpy as np
import ml_dtypes
from contextlib import ExitStack
import concourse.bass as bass
import concourse.mybir as mybir
from concourse.bass_utils import run_bass_kernel_spmd

F32 = mybir.dt.float32
BF16 = mybir.dt.bfloat16
ALU = mybir.AluOpType
AF = mybir.ActivationFunctionType
AX = mybir.AxisListType

D = 2048
NT = 64
NPRE = 48
NOWN = 16
EPS = 1e-6
C_MQ, C_MK, C_MV, C_MO, C_MZ, C_MI, C_MF = 0, 1024, 2048, 3072, 4096, 5120, 5124
C_AQ, C_AK, C_AV, C_AZ, C_GM, C_GA = 5128, 6152, 7176, 8200, 9224, 11272
IN_COLS = 13320
NEG = -30000.0
ENGS = ("sp", "act", "dve", "pool", "pe")
KSTOP = os.environ.get("KSTOP")


class _Stop(Exception):
    pass


class Op:
    __slots__ = ("eng", "fn", "deps", "dma", "sig", "sem", "val")

    def __init__(self, eng, fn, dma):
        self.eng, self.fn, self.dma = eng, fn, dma
        self.deps, self.sig, self.sem, self.val = [], dma, None, 0


class Sched:
    ND = 40

    def __init__(self, nc, es):
        self.nc = nc
        self.eops = {e: [] for e in ENGS}
        self.lastw, self.readers = {}, {}
        self.csem = {e: es.enter_context(nc.semaphore("cs_" + e)) for e in ENGS}
        self.dsem = [es.enter_context(nc.semaphore("ds%d" % i)) for i in range(self.ND)]
        self.dlast = [None] * self.ND
        self.duse = [0] * self.ND
        self.dn = 0

    @staticmethod
    def _is_psum(k):
        return k == "psS" or (isinstance(k, tuple) and len(k) > 0 and k[0] in ("ps", "psb", "psS"))

    def add(self, eng, fn, r=(), w=(), dma=False):
        w = tuple(w) + tuple(k for k in r if self._is_psum(k))
        r = tuple(k for k in r if not self._is_psum(k))
        op = Op(eng, fn, dma)
        deps = []
        for k in r:
            if k in self.lastw:
                deps.append(self.lastw[k])
        for k in w:
            if k in self.lastw:
                deps.append(self.lastw[k])
            deps.extend(self.readers.get(k, ()))
        if dma:
            i = self.dn % self.ND
            self.dn += 1
            if self.dlast[i] is not None:
                deps.append(self.dlast[i])
            self.duse[i] += 1
            op.sem, op.val = self.dsem[i], 16 * self.duse[i]
            self.dlast[i] = op
        seen = set()
        for d in deps:
            if d is op or id(d) in seen:
                continue
            seen.add(id(d))
            if d.eng == "pe" and eng == "pe" and not d.dma:
                continue
            d.sig = True
            op.deps.append(d)
        for k in r:
            lst = self.readers.setdefault(k, [])
            if not dma:
                lst[:] = [o_ for o_ in lst if o_.dma or o_.eng != eng]
            lst.append(op)
        for k in w:
            self.lastw[k] = op
            self.readers[k] = []
        self.eops[eng].append(op)
        return op

    def barrier(self):
        lasts = []
        for e in ENGS:
            for o in reversed(self.eops[e]):
                if not o.dma and o.fn is not None:
                    lasts.append(o)
                    break
        pend = [d for d in self.dlast if d is not None]
        for e in ENGS:
            op = Op(e, None, False)
            for d in lasts + pend:
                if d.eng == e and not d.dma:
                    continue
                d.sig = True
                op.deps.append(d)
            self.eops[e].append(op)
        self.lastw, self.readers = {}, {}

    def finish(self, keys):
        op = Op("sp", None, False)
        for d in self.dlast:
            if d is not None:
                op.deps.append(d)
        self.eops["sp"].append(op)

    def emit(self):
        nc = self.nc
        for e in ENGS:
            c = 0
            for o in self.eops[e]:
                if o.dma or o.fn is None:
                    continue
                if o.sig:
                    c += 1
                    o.sem, o.val = self.csem[e], c

        def run(ename, eng):
            known = {}
            for o in self.eops[ename]:
                for d in o.deps:
                    key = id(d.sem)
                    if known.get(key, 0) >= d.val:
                        continue
                    eng.wait_ge(d.sem, d.val)
                    known[key] = d.val
                if o.fn is None:
                    continue
                ins = o.fn(eng)
                if o.sig:
                    ins.then_inc(o.sem, 16 if o.dma else 1)

        with nc.Block() as block:
            @block.sync
            def _(e):
                run("sp", e)

            @block.scalar
            def _(e):
                run("act", e)

            @block.vector
            def _(e):
                run("dve", e)

            @block.gpsimd
            def _(e):
                run("pool", e)

            @block.tensor
            def _(e):
                run("pe", e)


def build_nc():
    nc = bass.Bass("TRN2", target_bir_lowering=False)

    def din(name, shape, dt=F32):
        return nc.dram_tensor(name, list(shape), dt, kind="ExternalInput").ap()

    xl = din("xl", [NT * 128, D])
    cT = din("cT", [128, 16])
    w_ada = din("w_ada", [D, 3 * D])
    b_ada = din("b_ada", [1, 3 * D])
    ngT = din("ngT", [128, 16])
    w_in = din("w_in", [D, IN_COLS])
    bif_bc = din("bif_bc", [128, 8])
    convw = din("convw", [128, 16, 4])
    convb = din("convb", [128, 16])
    mhg_bc = din("mhg_bc", [128, 1024])
    relmat = din("relmat", [8, 128, 640])
    amask = din("amask", [128, 640])
    w_pm = din("w_pm", [1024, D])
    w_pa = din("w_pa", [1024, D])
    w_out = din("w_out", [D, D])
    fg_bc = din("fg_bc", [128, D])
    padneg = din("padneg", [128, NT])
    tilevalid = din("tilevalid", [128, NT])
    negtile = din("negtile", [128, NT])
    ident_d = din("ident", [128, 128], BF16)
    tri_d = din("tri", [128, 128])
    ones_d = din("ones", [128, 128])
    mst_d = din("maskst", [128, 128])
    y_out = nc.dram_tensor("y", [NOWN * 128, D], F32, kind="ExternalOutput").ap()
    dbg = nc.dram_tensor("dbg", [128, 8192], F32, kind="ExternalOutput").ap() if KSTOP else None
    yT_d = nc.dram_tensor("yT_d", [16, 128, NOWN * 128], BF16, kind="Internal").ap()
    mT_d = nc.dram_tensor("mT_d", [16, 128, NOWN * 128], BF16, kind="Internal").ap()

    es = ExitStack()
    with es, nc.allow_low_precision("bf16 matmul operands, fp32 accumulation"), \
            nc.allow_non_contiguous_dma("column-sliced weight loads"):
        S = Sched(nc, es)

        def sb(name, shape, dt=F32):
            return es.enter_context(nc.sbuf_tensor("s_" + name, list(shape), dt))

        def pst(name, shape, dt=F32):
            return es.enter_context(nc.psum_tensor(name, list(shape), dt))

        ident = sb("ident", [128, 128], BF16)
        tri = sb("tri", [128, 128])
        ones = sb("ones", [128, 128])
        gs = sb("gs", [128, 16])
        shift = sb("shift", [128, 16])
        ngt_s = sb("ngt_s", [128, 16])
        gate_bc = sb("gate_bc", [128, D])
        cw = sb("cw", [128, 16, 4])
        cb_ = sb("cb_", [128, 16])
        bif = sb("bif", [128, 8])
        mhg = sb("mhg", [128, 1024])
        pneg = sb("pneg", [128, NT])
        tval = sb("tval", [128, NT])
        ntile = sb("ntile", [128, NT])
        amask_s = sb("amask_s", [128, 640])
        wif = sb("wif", [128, 16, 8], BF16)
        wifs = sb("wifs", [128, 16, 8])
        state = [[sb("st%d%d" % (h, b), [128, 257]) for b in range(2)] for h in range(4)]
        ctbf = [[sb("ct%d%d" % (h, b), [128, 257], BF16) for b in range(2)] for h in range(4)]
        hT_halo = sb("hT_halo", [128, 16, 512], BF16)
        wk_o = sb("wk_o", [128, NOWN, 4])
        wa_o = sb("wa_o", [128, NOWN, 4])
        eB_o = sb("eB_o", [128, NOWN, 4])
        ebc_o = sb("ebc_o", [128, NOWN, 4])
        sm = sb("sm", [128, 64])
        ARENA_COLS = 40000
        arena = sb("arena", [128, ARENA_COLS])

        ps = [pst("ps%d" % i, [128, 512]) for i in range(4)]
        psS = pst("psS", [128, 1024])
        psb = [pst("psb%d" % i, [128, 1024], BF16) for i in range(2)]
        rot = {"ps": 0, "psb": 0, "cast": 0, "q": 0}

        def getps():
            i = rot["ps"] % 4
            rot["ps"] += 1
            return ps[i], ("ps", i)

        def getpsb():
            i = rot["psb"] % 2
            rot["psb"] += 1
            return psb[i], ("psb", i)

        aoff = [0]

        def areset():
            aoff[0] = 0

        def af32(cols):
            o = aoff[0]
            aoff[0] += cols
            assert aoff[0] <= ARENA_COLS, aoff[0]
            return arena[:, o:o + cols]

        def abf(cols):
            n = (cols + 1) // 2
            return af32(n).bitcast(BF16)[:, 0:cols]

        def dma(q, out, in_, r, w):
            S.add(q, lambda e, o=out, i=in_: e.dma_start(out=o, in_=i), r=r, w=w, dma=True)

        def dmaq():
            rot["q"] += 1
            return "sp" if rot["q"] % 2 else "pool"

        def act(out, in_, func, r, w, bias=None, scale=None, accum=None):
            kw = {}
            if bias is not None:
                kw["bias"] = bias
            if scale is not None:
                kw["scale"] = scale
            if accum is not None:
                kw["accum_out"] = accum
            S.add("act", lambda e: e.activation(out, in_, func, **kw), r=r, w=w)

        def ts(eng, out, in0, s1, s2, op0, op1, r, w):
            if s2 is None:
                s2, op1 = (1.0, ALU.mult) if op0 == ALU.add else (0.0, ALU.add)
            S.add(eng, lambda e: e.tensor_scalar(out, in0, s1, s2, op0, op1), r=r, w=w)

        def rsqrt_col(col, key):
            S.add("act", lambda e: e.activation(col, col, AF.Sqrt), r=(key,), w=(key,))
            S.add("dve", lambda e: e.reciprocal(col, col), r=(key,), w=(key,))

        def tt(eng, out, in0, in1, op, r, w):
            S.add(eng, lambda e: e.tensor_tensor(out, in0, in1, op), r=r, w=w)

        def stt(eng, out, in0, sc, in1, op0, op1, r, w):
            S.add(eng, lambda e: e.scalar_tensor_tensor(out, in0, sc, in1, op0, op1), r=r, w=w)

        def mm(out, lhsT, rhs, start, stop, r, w):
            S.add("pe", lambda e: e.matmul(out, lhsT, rhs, start=start, stop=stop), r=r, w=w)

        def tr(out, in_, r, w):
            S.add("pe", lambda e: e.transpose(out, in_, ident[:]), r=tuple(r) + ("ident",), w=w)

        def cast(out, in_, r, w, eng=None):
            if eng is None:
                rot["cast"] += 1
                eng = ("act", "pool")[rot["cast"] % 2]
            if eng == "act":
                S.add("act", lambda e: e.copy(out, in_), r=r, w=w)
            else:
                S.add(eng, lambda e: e.tensor_copy(out, in_), r=r, w=w)

        def memset(eng, ap, v, w):
            S.add(eng, lambda e: e.memset(ap, v), w=w)

        def stop(tag, dumps=()):
            if KSTOP != tag:
                return
            S.barrier()
            off = 0
            for ap_, n_ in dumps:
                dma("sp", dbg[:, off:off + n_], ap_, r=(), w=(("dbg", off),))
                off += n_
            raise _Stop()

        def body():
            for (t_, d_, k_) in ((ident, ident_d, "ident"), (tri, tri_d, "tri"), (ones, ones_d, "ones"),
                                 (ngt_s, ngT, "ngt"), (cw, convw, "cw"),
                                 (cb_, convb, "cb"), (bif, bif_bc, "bif"), (mhg, mhg_bc, "mhg"),
                                 (pneg, padneg, "pneg"), (tval, tilevalid, "tval"),
                                 (ntile, negtile, "ntile"), (amask_s, amask, "amask")):
                dma("sp", t_[:], d_, r=(), w=(k_,))
            for k4 in range(4):
                dma("sp", wifs[:, 4 * k4:4 * k4 + 4, :],
                    w_in[k4 * 512:(k4 + 1) * 512, C_MI:C_MI + 8].rearrange("(k p) c -> p k c", p=128), r=(), w=("wifs",))
            cast(wif[:], wifs[:], r=("wifs",), w=("wif",), eng="dve")
            for h in range(4):
                for b in range(2):
                    memset("dve", state[h][b][:], 0.0, w=(("st", h, b),))

            areset()
            sc_in = af32(16)
            scv = af32(16)
            modrow = af32(3 * D)[0:1, :]
            badar = af32(3 * D)[0:1, :]
            stgA = [af32(3072) for _ in range(2)]
            dma("sp", sc_in, cT, r=(), w=("sc_in",))
            dma("sp", badar, b_ada, r=(), w=("badar",))
            act(scv, sc_in, AF.Silu, r=("sc_in",), w=("scv",))
            banks = [(ps[0], ("ps", 0), 0), (ps[1], ("ps", 1), 0), (ps[2], ("ps", 2), 0),
                     (ps[3], ("ps", 3), 0), (psS, ("psS",), 0), (psS, ("psS",), 512)]
            for half in range(2):
                for k in range(16):
                    st_ = stgA[k % 2]
                    key = ("stgA", k % 2)
                    dma(dmaq(), st_, w_ada[k * 128:(k + 1) * 128, half * 3072:(half + 1) * 3072], r=(), w=(key,))
                    for cbk in range(6):
                        t_, pk, off = banks[cbk]
                        mm(t_[0:1, off:off + 512], scv[:, k:k + 1], st_[:, cbk * 512:(cbk + 1) * 512],
                           k == 0, k == 15, r=(key, "scv"), w=(pk,))
                for cbk in range(6):
                    t_, pk, off = banks[cbk]
                    c0 = half * 3072 + cbk * 512
                    tt("dve", modrow[:, c0:c0 + 512], t_[0:1, off:off + 512], badar[:, c0:c0 + 512], ALU.add,
                       r=(pk, "badar"), w=("modrow",))
            pcol, pck = getps()
            for c in range(32):
                mm(pcol[:, c:c + 1], modrow[0:1, c * 128:(c + 1) * 128], ones[0:1, 0:1], True, True,
                   r=("modrow", "ones"), w=(pck,))
            cast(shift[:], pcol[:, 0:16], r=(pck,), w=("shift",), eng="dve")
            stt("dve", gs[:], pcol[:, 16:32], 1.0, ngt_s[:], ALU.add, ALU.mult, r=(pck, "ngt"), w=("gs",))
            for cbk in range(4):
                t_, pk = getps()
                mm(t_[:, 0:512], ones[0:1, :], modrow[0:1, 2 * D + cbk * 512:2 * D + (cbk + 1) * 512], True, True,
                   r=("modrow", "ones"), w=(pk,))
                cast(gate_bc[:, cbk * 512:(cbk + 1) * 512], t_[:, 0:512], r=(pk,), w=("gate_bc",), eng="act")
            S.barrier()
            stop("S1", [(gs[:], 16), (shift[:], 16), (gate_bc[:, 0:64], 64)])

            def frontend(tau, dstf, dkey, xbufs, xn, i2):
                xt = xbufs[i2 % len(xbufs)]
                xk = ("xt", i2 % len(xbufs))
                dma(dmaq(), xt, xl[tau * 128:(tau + 1) * 128, :], r=(), w=(xk,))
                memset("dve", sm[:, 0:1], 0.0, w=("ss",))
                act(xn, xt, AF.Square, r=(xk, "ss"), w=("xn", "ss"), accum=sm[:, 0:1])
                ts("dve", sm[:, 1:2], sm[:, 0:1], 1.0 / D, EPS, ALU.mult, ALU.add, r=("ss",), w=("rstd",))
                rsqrt_col(sm[:, 1:2], "rstd")
                act(xn, xt, AF.Identity, r=(xk, "rstd"), w=("xn",), scale=sm[:, 1:2])
                if tau == 0:
                    stop("F1", [(sm[:], 64)])
                for hh in range(2):
                    pt, ptk = getpsb()
                    for kk in range(8):
                        k = hh * 8 + kk
                        tr(pt[:, kk * 128:(kk + 1) * 128], xn[:, k * 128:(k + 1) * 128], r=("xn",), w=(ptk,))
                    if tau == 0 and hh == 0:
                        stop("F2", [(sm[:], 64)])
                    for kk in range(8):
                        k = hh * 8 + kk
                        if hh == 0:
                            act(dstf(k), pt[:, kk * 128:(kk + 1) * 128], AF.Identity, r=(ptk, "gs", "shift"),
                                w=(dkey[0],), bias=shift[:, k:k + 1], scale=gs[:, k:k + 1])
                        else:
                            ts("dve", dstf(k), pt[:, kk * 128:(kk + 1) * 128], gs[:, k:k + 1], shift[:, k:k + 1],
                               ALU.mult, ALU.add, r=(ptk, "gs", "shift"), w=(dkey[1],))

            def load_w(dst, src_rows, nk, stgs, skey, dkey):
                for k in range(nk):
                    st_ = stgs[k % len(stgs)]
                    key = (skey, k % len(stgs))
                    dma(dmaq(), st_, src_rows(k), r=(), w=(key,))
                    cast(dst(k), st_, r=(key,), w=(dkey,))

            def gates(tau, hsrc, hkey, own_i):
                pg, pgk = getps()
                for k in range(16):
                    mm(pg[:, 0:8], hsrc(k), wif[:, k, :], k == 0, k == 15, r=tuple(hkey) + ("wif",), w=(pgk,))
                gl = sm[:, 8:16]
                tt("dve", gl, pg[:, 0:8], bif[:], ALU.add, r=(pgk, "bif"), w=("gl",))
                ts("dve", sm[:, 16:20], gl[:, 0:4], pneg[:, tau:tau + 1], None, ALU.add, None, r=("gl", "pneg"), w=("ig",))
                act(sm[:, 20:24], gl[:, 4:8], AF.Exp, r=("gl",), w=("e1",), scale=-1.0)
                ts("dve", sm[:, 20:24], sm[:, 20:24], 1.0, None, ALU.add, None, r=("e1",), w=("e1",))
                act(sm[:, 24:28], sm[:, 20:24], AF.Ln, r=("e1",), w=("lf",))
                ts("dve", sm[:, 24:28], sm[:, 24:28], -1.0, None, ALU.mult, None, r=("lf",), w=("lf",))
                pc, pckk = getps()
                mm(pc[:, 0:4], tri[:], sm[:, 24:28], True, True, r=("lf", "tri"), w=(pckk,))
                mm(pc[:, 4:8], ones[:], sm[:, 24:28], True, True, r=("lf", "ones"), w=(pckk,))
                tt("dve", sm[:, 28:32], sm[:, 16:20], pc[:, 0:4], ALU.subtract, r=("ig", pckk), w=("t1",))
                tt("dve", sm[:, 32:36], sm[:, 28:32], pc[:, 4:8], ALU.add, r=("t1", pckk), w=("t2",))
                if own_i is None:
                    wk, eB = sm[:, 36:40], sm[:, 40:44]
                    wkk, eBk = "wk", "eB"
                else:
                    wk, eB = wk_o[:, own_i, :], eB_o[:, own_i, :]
                    wkk = eBk = ("gown", own_i)
                act(wk, sm[:, 32:36], AF.Exp, r=("t2",), w=(wkk,))
                ts("dve", wk, wk, 0.0625, None, ALU.mult, None, r=(wkk,), w=(wkk,))
                act(eB, pc[:, 4:8], AF.Exp, r=(pckk,), w=(eBk,))
                if own_i is not None:
                    act(wa_o[:, own_i, :], sm[:, 28:32], AF.Exp, r=("t1",), w=(wkk,))
                    ts("dve", wa_o[:, own_i, :], wa_o[:, own_i, :], 0.0625, None, ALU.mult, None, r=(wkk,), w=(wkk,))
                    act(ebc_o[:, own_i, :], pc[:, 0:4], AF.Exp, r=(pckk,), w=(wkk,))
                return wk, eB, wkk, eBk

            def conv_silu(pre, prek, cblk, accb, acck, out, outk):
                ts("dve", accb, pre[:, 3:515], cw[:, cblk, 3:4], cb_[:, cblk:cblk + 1], ALU.mult, ALU.add,
                   r=(prek, "cw", "cb"), w=(acck,))
                for tap in range(3):
                    stt("dve", accb, pre[:, tap:tap + 512], cw[:, cblk, tap:tap + 1], accb, ALU.mult, ALU.add,
                        r=(prek, acck, "cw"), w=(acck,))
                act(out, accb, AF.Silu, r=(acck,), w=(outk,))

            def state_update(h, kT_blk, kTk, vaug, vk, wk_col, wkk, eB_col, eBk, kpp, kppk, refresh_bf):
                pt, ptk = getpsb()
                for blk in range(2):
                    tr(pt[:, blk * 128:(blk + 1) * 128], kT_blk(blk), r=(kTk[blk],), w=(ptk,))
                ts("dve", kpp, pt[:, 0:256], wk_col, None, ALU.mult, None, r=(ptk, wkk), w=(kppk,))
                for blk in range(2):
                    p_, pk = getps()
                    mm(p_[:, 0:257], kpp[:, blk * 128:(blk + 1) * 128], vaug, True, True, r=(kppk, vk), w=(pk,))
                    stt("dve", state[h][blk][:], state[h][blk][:], eB_col, p_[:, 0:257], ALU.mult, ALU.add,
                        r=(pk, eBk, ("st", h, blk)), w=(("st", h, blk),))
                    if refresh_bf:
                        cast(ctbf[h][blk][:], state[h][blk][:], r=(("st", h, blk),), w=(("ct", h, blk),), eng="act")

            areset()
            WA = abf(16 * 2048).rearrange("p (k c) -> p k c", k=16)
            hTgA = abf(16 * 512).rearrange("p (k c) -> p k c", k=16)
            xn = abf(D)
            kT = abf(8 * 512).rearrange("p (b c) -> p b c", b=8)
            vaugs = [[abf(258)[:, 0:257] for _ in range(4)] for _ in range(2)]
            kpps = [abf(256) for _ in range(2)]
            stgs = [af32(2048) for _ in range(2)]
            xbufs = stgs
            kpre = af32(8 * 515).rearrange("p (b c) -> p b c", b=8)
            accb = af32(512)
            load_w(lambda k: WA[:, k, :], lambda k: w_in[k * 128:(k + 1) * 128, C_MK:C_MK + 2048], 16, stgs, "xt", "WA")
            stop("A00", [(sm[:], 64)])
            memset("pool", kpre[:, :, 0:3], 0.0, w=tuple(("kpre", c) for c in range(8)))
            for vb in range(2):
                for h in range(4):
                    memset("pool", vaugs[vb][h][:, 256:257], 1.0, w=(("vaug", vb, h),))
            fe_i = 0
            for g in range(NPRE // 4):
                hT = hT_halo if g == NPRE // 4 - 1 else hTgA
                hgk = tuple(("hTg", i_, p_) for i_ in range(4) for p_ in range(2))
                for i in range(4):
                    frontend(4 * g + i, lambda k, i=i, hT=hT: hT[:, k, i * 128:(i + 1) * 128], (("hTg", i, 0), ("hTg", i, 1)),
                             xbufs, xn, fe_i)
                    fe_i += 1
                if g == 0:
                    stop("A0", [(sm[:], 64)])
                for cbk in range(8):
                    p_, pk = getps()
                    for k in range(16):
                        mm(p_[:, 0:512], WA[:, k, cbk * 128:(cbk + 1) * 128], hT[:, k, :], k == 0, k == 15,
                           r=("WA",) + hgk, w=(pk,))
                    cast(kpre[:, cbk, 3:515], p_[:, 0:512], r=(pk,), w=(("kpre", cbk),), eng="act")
                    conv_silu(kpre[:, cbk, :], ("kpre", cbk), 8 + cbk, accb, "accb", kT[:, cbk, :], ("kT", cbk))
                    ts("dve", kpre[:, cbk, 0:3], kpre[:, cbk, 512:515], tval[:, 4 * g + 3:4 * g + 4], None, ALU.mult, None,
                       r=(("kpre", cbk), "tval"), w=(("kpre", cbk),))
                if g == 0:
                    stop("A1", [(kpre[:, 0, :], 515), (kpre[:, 7, :], 515), (accb, 512)])
                for i in range(4):
                    tau = 4 * g + i
                    vb = tau % 2
                    for half in range(2):
                        p_, pk = getps()
                        for k in range(16):
                            mm(p_[:, 0:512], hT[:, k, i * 128:(i + 1) * 128], WA[:, k, 1024 + half * 512:1024 + (half + 1) * 512],
                               k == 0, k == 15, r=("WA",) + hgk, w=(pk,))
                        for hh in range(2):
                            h = 2 * half + hh
                            cast(vaugs[vb][h][:, 0:256], p_[:, hh * 256:(hh + 1) * 256], r=(pk,), w=(("vaug", vb, h),),
                                 eng=("act", "dve")[hh])
                    wk, eB, wkk, eBk = gates(tau, lambda k, i=i, hT=hT: hT[:, k, i * 128:(i + 1) * 128],
                                             (("hTg", i, 0), ("hTg", i, 1)), None)
                    if tau == 0:
                        stop("A2", [(sm[:], 64)])
                    for h in range(4):
                        state_update(h, lambda blk, h=h, i=i: kT[:, 2 * h + blk, i * 128:(i + 1) * 128],
                                     (("kT", 2 * h), ("kT", 2 * h + 1)), vaugs[vb][h], ("vaug", vb, h), wk[:, h:h + 1], wkk, eB[:, h:h + 1], eBk,
                                     kpps[h % 2], ("kpp", h % 2), False)
                    if tau == 0:
                        stop("A3", [(state[0][0][:], 257), (state[3][1][:], 257), (sm[:], 64)])
            S.barrier()
            stop("A", [(state[0][0][:], 257), (state[3][1][:], 257), (sm[:], 64)])

            areset()
            hT_own = abf(16 * 2048).rearrange("p (k c) -> p k c", k=16)
            mark_b = aoff[0]
            xn = abf(D)
            xbufs = [af32(D) for _ in range(2)]
            for i in range(NOWN):
                frontend(NPRE + i, lambda k, i=i: hT_own[:, k, i * 128:(i + 1) * 128], (("hTo", i, 0), ("hTo", i, 1)), xbufs, xn, i)
                gates(NPRE + i, lambda k, i=i: hT_own[:, k, i * 128:(i + 1) * 128], (("hTo", i, 0), ("hTo", i, 1)), i)
            for h in range(4):
                for b in range(2):
                    cast(ctbf[h][b][:], state[h][b][:], r=(("st", h, b),), w=(("ct", h, b),), eng="act")
            S.barrier()
            stop("B0", [(wk_o[:].rearrange("p a b -> p (a b)"), 64), (wa_o[:].rearrange("p a b -> p (a b)"), 64), (eB_o[:].rearrange("p a b -> p (a b)"), 64), (ebc_o[:].rearrange("p a b -> p (a b)"), 64), (state[0][0][:], 257)])

            aoff[0] = mark_b
            WG = abf(16 * 1280).rearrange("p (k c) -> p k c", k=16)
            qkT = abf(4 * 512).rearrange("p (b c) -> p b c", b=4)
            vaug1 = [abf(258)[:, 0:257] for _ in range(2)]
            kpps = [abf(256) for _ in range(2)]
            STb = [abf(128) for _ in range(2)]
            ybf = [abf(256) for _ in range(2)]
            yTt = [abf(256).rearrange("p (b c) -> p b c", b=2) for _ in range(2)]
            stg1 = [af32(1280) for _ in range(2)]
            qkpre = af32(4 * 515).rearrange("p (b c) -> p b c", b=4)
            accb = af32(512)
            sgo = af32(256)
            slz = af32(256)
            Gt = af32(256)
            junk = af32(256)
            w5 = w_in[:, 0:5120].rearrange("r (s c) -> r s c", c=1024)
            for vb in range(2):
                memset("pool", vaug1[vb][:, 256:257], 1.0, w=(("vaug1", vb),))
            allh = tuple(() for i in range(NOWN))
            for h in range(4):
                load_w(lambda k: WG[:, k, :].rearrange("p (s c) -> p s c", c=256),
                       lambda k, h=h: w5[k * 128:(k + 1) * 128, :, h * 256:(h + 1) * 256],
                       16, [s_.rearrange("p (s c) -> p s c", c=256) for s_ in stg1], "stg1", "WG")
                for blk in range(4):
                    p_, pk = getps()
                    for k in range(16):
                        mm(p_[:, 0:3], WG[:, k, blk * 128:(blk + 1) * 128], hT_halo[:, k, 509:512], k == 0, k == 15,
                           r=("WG",), w=(pk,))
                    ts("dve", qkpre[:, blk, 0:3], p_[:, 0:3], tval[:, NPRE - 1:NPRE], None, ALU.mult, None,
                       r=(pk, "tval"), w=(("qkpre", blk),))
                for g in range(4):
                    for blk in range(4):
                        p_, pk = getps()
                        for k in range(16):
                            mm(p_[:, 0:512], WG[:, k, blk * 128:(blk + 1) * 128], hT_own[:, k, g * 512:(g + 1) * 512],
                               k == 0, k == 15, r=("WG",), w=(pk,))
                        cast(qkpre[:, blk, 3:515], p_[:, 0:512], r=(pk,), w=(("qkpre", blk),), eng="act")
                        cidx = (2 * h + blk) if blk < 2 else (8 + 2 * h + blk - 2)
                        conv_silu(qkpre[:, blk, :], ("qkpre", blk), cidx, accb, "accb", qkT[:, blk, :], ("qkT", blk))
                        cast(qkpre[:, blk, 0:3], qkpre[:, blk, 512:515], r=(("qkpre", blk),), w=(("qkpre", blk),), eng="dve")
                    for i in range(4):
                        t = 4 * g + i
                        vb = t % 2
                        tsl = slice(i * 128, (i + 1) * 128)
                        pA, pAk = getps()
                        for k in range(16):
                            mm(pA[:, 0:512], hT_own[:, k, t * 128:(t + 1) * 128], WG[:, k, 512:1024], k == 0, k == 15,
                               r=("WG",), w=(pAk,))
                        pB, pBk = getps()
                        for k in range(16):
                            mm(pB[:, 0:256], hT_own[:, k, t * 128:(t + 1) * 128], WG[:, k, 1024:1280], k == 0, k == 15,
                               r=("WG",), w=(pBk,))
                        vk = ("vaug1", vb)
                        cast(vaug1[vb][:, 0:256], pA[:, 0:256], r=(pAk,), w=(vk,), eng="act")
                        act(sgo, pA[:, 256:512], AF.Sigmoid, r=(pAk,), w=("sgo",))
                        act(slz, pB[:, 0:256], AF.Silu, r=(pBk,), w=("slz",))
                        tt("dve", Gt, sgo, slz, ALU.mult, r=("sgo", "slz"), w=("Gt",))
                        tt("dve", Gt, Gt, mhg[:, h * 256:(h + 1) * 256], ALU.mult, r=("Gt", "mhg"), w=("Gt",))
                        pS, pSk = getps()
                        for blk in range(2):
                            mm(pS[:, 0:128], qkT[:, 2 + blk, tsl], qkT[:, blk, tsl], blk == 0, blk == 1,
                               r=(("qkT", blk), ("qkT", 2 + blk)), w=(pSk,))
                        gk = ("gown", t)
                        stt("dve", STb[vb], pS[:, 0:128], wa_o[:, t, h:h + 1], tri[:], ALU.mult, ALU.mult,
                            r=(pSk, gk, "tri"), w=(("STb", vb),))
                        pN, pNk = getps()
                        mm(pN[:, 0:257], STb[vb], vaug1[vb], True, False, r=(("STb", vb), vk), w=(pNk,))
                        for blk in range(2):
                            mm(pN[:, 0:257], qkT[:, blk, tsl], ctbf[h][blk][:], False, blk == 1,
                               r=(("qkT", blk), ("ct", h, blk)), w=(pNk,))
                        state_update(h, lambda blk, tsl=tsl: qkT[:, 2 + blk, tsl], (("qkT", 2), ("qkT", 3)), vaug1[vb], vk,
                                     wk_o[:, t, h:h + 1], gk, eB_o[:, t, h:h + 1], gk, kpps[vb], ("kpp", vb), True)
                        tt("dve", sm[:, 44:45], pN[:, 256:257], ebc_o[:, t, h:h + 1], ALU.mult, r=(pNk, gk), w=("d1",))
                        ts("dve", sm[:, 54:55], sm[:, 44:45], -1.0, None, ALU.mult, None, r=("d1",), w=("d1n",))
                        tt("dve", sm[:, 44:45], sm[:, 44:45], sm[:, 54:55], ALU.max, r=("d1", "d1n"), w=("d1",))
                        ts("dve", sm[:, 44:45], sm[:, 44:45], 1.0, 1.0, ALU.max, ALU.mult, r=("d1",), w=("d1",))
                        S.add("dve", lambda e: e.reciprocal(sm[:, 44:45], sm[:, 44:45]), r=("d1",), w=("d1",))
                        tt("dve", sm[:, 45:46], ebc_o[:, t, h:h + 1], sm[:, 44:45], ALU.mult, r=("d1", gk), w=("rr",))
                        memset("dve", sm[:, 46:47], 0.0, w=("ss2",))
                        act(junk, pN[:, 0:256], AF.Square, r=(pNk, "rr", "ss2"), w=("junk", "ss2"), scale=sm[:, 45:46],
                            accum=sm[:, 46:47])
                        ts("dve", sm[:, 47:48], sm[:, 46:47], 1.0 / 256, EPS, ALU.mult, ALU.add, r=("ss2",), w=("r2",))
                        rsqrt_col(sm[:, 47:48], "r2")
                        tt("dve", sm[:, 47:48], sm[:, 47:48], sm[:, 45:46], ALU.mult, r=("r2", "rr"), w=("r2",))
                        stt("dve", ybf[vb], pN[:, 0:256], sm[:, 47:48], Gt, ALU.mult, ALU.mult, r=(pNk, "r2", "Gt"),
                            w=(("ybf", vb),))
                        pt, ptk = getpsb()
                        for blk in range(2):
                            tr(pt[:, blk * 128:(blk + 1) * 128], ybf[vb][:, blk * 128:(blk + 1) * 128], r=(("ybf", vb),), w=(ptk,))
                        cast(yTt[vb].rearrange("p b c -> p (b c)"), pt[:, 0:256], r=(ptk,), w=(("yTt", vb),), eng="act")
                        dma(dmaq(), yT_d[2 * h:2 * h + 2, :, t * 128:(t + 1) * 128].rearrange("b p t -> p b t"), yTt[vb],
                            r=(("yTt", vb),), w=(("yTd", 2 * h, t),))
            S.barrier()
            stop("B1", [(state[0][0][:], 257)])

            aoff[0] = mark_b
            WG2 = abf(16 * 512).rearrange("p (k c) -> p k c", k=16)
            akT = abf(2560)
            aqT = abf(2048)
            Vt = abf(20 * 128).rearrange("p (t c) -> p t c", t=20)
            slzA = abf(16 * 128).rearrange("p (t c) -> p t c", t=16)
            Pb = [abf(640) for _ in range(2)]
            PT = [abf(640) for _ in range(2)]
            yab = [abf(128) for _ in range(2)]
            yaT = [abf(128) for _ in range(2)]
            stg2 = [af32(512) for _ in range(2)]
            rb = af32(640)
            bm = af32(640)
            sbuf_s = [af32(640) for _ in range(2)]
            wA = w_in[:, C_AQ:C_AQ + 4096].rearrange("r (s c) -> r s c", c=1024)
            for h in range(8):
                load_w(lambda k: WG2[:, k, :].rearrange("p (s c) -> p s c", c=128),
                       lambda k, h=h: wA[k * 128:(k + 1) * 128, :, h * 128:(h + 1) * 128],
                       16, [s_.rearrange("p (s c) -> p s c", c=128) for s_ in stg2], "stg2", "WG2")
                dma("sp", rb, relmat[h], r=(), w=("rb",))
                tt("pool", bm, rb, amask_s[:], ALU.add, r=("rb", "amask"), w=("bm",))
                for g5 in range(5):
                    src = hT_halo if g5 == 0 else hT_own[:, :, (g5 - 1) * 512:g5 * 512]
                    srk = ()
                    p_, pk = getps()
                    for k in range(16):
                        mm(p_[:, 0:512], WG2[:, k, 128:256], src[:, k, :], k == 0, k == 15, r=("WG2",) + srk, w=(pk,))
                    cast(akT[:, g5 * 512:(g5 + 1) * 512], p_[:, 0:512], r=(pk,), w=(("akT", g5),), eng="act")
                    if g5 > 0:
                        p_, pk = getps()
                        for k in range(16):
                            mm(p_[:, 0:512], WG2[:, k, 0:128], src[:, k, :], k == 0, k == 15, r=("WG2",) + srk, w=(pk,))
                        act(aqT[:, (g5 - 1) * 512:g5 * 512], p_[:, 0:512], AF.Copy, r=(pk,), w=(("aqT", g5 - 1),),
                            scale=float(128 ** -0.5))
                    for i in range(4):
                        ttile = g5 * 4 + i
                        p_, pk = getps()
                        for k in range(16):
                            mm(p_[:, 0:256], src[:, k, i * 128:(i + 1) * 128], WG2[:, k, 256:512], k == 0, k == 15,
                               r=("WG2",) + srk, w=(pk,))
                        cast(Vt[:, ttile, :], p_[:, 0:128], r=(pk,), w=(("Vt", ttile),), eng="act")
                        if g5 > 0:
                            act(slzA[:, ttile - 4, :], p_[:, 128:256], AF.Silu, r=(pk,), w=(("slzA", ttile - 4),))
                def att_s1(t, h=h):
                    vb = t % 2
                    gq = ("aqT", t // 4)
                    kk0 = tuple(("akT", x) for x in sorted({t // 4, (t + 3) // 4, (t + 4) // 4}))
                    mm(psS[:, 0:512], aqT[:, t * 128:(t + 1) * 128], akT[:, t * 128:t * 128 + 512], True, True,
                       r=(gq,) + kk0, w=("psS",))
                    mm(psS[:, 512:640], aqT[:, t * 128:(t + 1) * 128], akT[:, t * 128 + 512:t * 128 + 640], True, True,
                       r=(gq,) + kk0, w=("psS",))
                    sbt = sbuf_s[vb]
                    sk = ("sbt", vb)
                    tt("dve", sbt, psS[:, 0:640], bm, ALU.add, r=("psS", "bm"), w=(sk,))
                    for kb in range(max(0, 4 - t)):
                        lt = NPRE - 4 + t + kb
                        ts("dve", sbt[:, kb * 128:(kb + 1) * 128], sbt[:, kb * 128:(kb + 1) * 128], ntile[:, lt:lt + 1], None,
                           ALU.add, None, r=(sk, "ntile"), w=(sk,))
                    S.add("dve", lambda e, sbt=sbt: e.reduce_max(sm[:, 48:49], sbt, AX.X), r=(sk,), w=("mx",))
                    ts("dve", sm[:, 48:49], sm[:, 48:49], -1.0, None, ALU.mult, None, r=("mx",), w=("mx",))
                    rsc = sm[:, 56 + vb:57 + vb]
                    memset("dve", rsc, 0.0, w=(("rsum", vb),))
                    act(Pb[vb], sbt, AF.Exp, r=(sk, "mx", ("rsum", vb)), w=(("Pb", vb), ("rsum", vb)), bias=sm[:, 48:49],
                        accum=rsc)

                def att_s2(t, h=h):
                    vb = t % 2
                    rsc = sm[:, 56 + vb:57 + vb]
                    rrc = sm[:, 58 + vb:59 + vb]
                    pt, ptk = getpsb()
                    for kb in range(5):
                        tr(pt[:, kb * 128:(kb + 1) * 128], Pb[vb][:, kb * 128:(kb + 1) * 128], r=(("Pb", vb),), w=(ptk,))
                    cast(PT[vb], pt[:, 0:640], r=(ptk,), w=(("PT", vb),), eng="act")
                    pO, pOk = getps()
                    for kb in range(5):
                        mm(pO[:, 0:128], PT[vb][:, kb * 128:(kb + 1) * 128], Vt[:, t + kb, :], kb == 0, kb == 4,
                           r=(("PT", vb), ("Vt", t + kb)), w=(pOk,))
                    S.add("dve", lambda e: e.reciprocal(rrc, rsc), r=(("rsum", vb),), w=(("rrs", vb),))
                    stt("dve", yab[vb], pO[:, 0:128], rrc, slzA[:, t, :], ALU.mult, ALU.mult,
                        r=(pOk, ("rrs", vb), ("slzA", t)), w=(("yab", vb),))
                    pt2, pt2k = getpsb()
                    tr(pt2[:, 0:128], yab[vb], r=(("yab", vb),), w=(pt2k,))
                    cast(yaT[vb], pt2[:, 0:128], r=(pt2k,), w=(("yaT", vb),), eng="act")
                    dma(dmaq(), yT_d[8 + h, :, t * 128:(t + 1) * 128], yaT[vb], r=(("yaT", vb),), w=(("yTd", 8 + h, t),))

                att_s1(0)
                for t in range(NOWN):
                    if t + 1 < NOWN:
                        att_s1(t + 1)
                    att_s2(t)
            S.barrier()
            stop("B2", [(sm[:], 64)])

            aoff[0] = mark_b
            yT_res = abf(16 * 2048).rearrange("p (b c) -> p b c", b=16)
            hh_flat = hT_halo[:].rearrange("p k c -> p (k c)")
            wgm = [abf(16 * 128).rearrange("p (k c) -> p k c", k=16), hh_flat[:, 0:2048].rearrange("p (k c) -> p k c", k=16)]
            wga = [abf(16 * 128).rearrange("p (k c) -> p k c", k=16), hh_flat[:, 2048:4096].rearrange("p (k c) -> p k c", k=16)]
            wpm = [abf(8 * 128).rearrange("p (k c) -> p k c", k=8), hh_flat[:, 4096:5120].rearrange("p (k c) -> p k c", k=8)]
            wpa = [abf(8 * 128).rearrange("p (k c) -> p k c", k=8), hh_flat[:, 5120:6144].rearrange("p (k c) -> p k c", k=8)]
            mTb = [abf(512) for _ in range(2)]
            stg3 = [af32(8 * 128).rearrange("p (k c) -> p k c", k=8) for _ in range(2)]
            sgm = af32(512)
            sga = af32(512)
            t1b = af32(512)
            for b in range(16):
                dma(dmaq(), yT_res[:, b, :], yT_d[b], r=(), w=("yT_res",))
            si3 = [0]

            def b3_load(c):
                cbuf = c % 2
                for (dst, src, nk, nm) in ((wgm, w_in[:, C_GM + c * 128:C_GM + (c + 1) * 128], 16, "wgm"),
                                           (wga, w_in[:, C_GA + c * 128:C_GA + (c + 1) * 128], 16, "wga"),
                                           (wpm, w_pm[:, c * 128:(c + 1) * 128], 8, "wpm"),
                                           (wpa, w_pa[:, c * 128:(c + 1) * 128], 8, "wpa")):
                    for k8 in range(nk // 8):
                        st_ = stg3[si3[0] % 2]
                        sk = ("stg3", si3[0] % 2)
                        si3[0] += 1
                        for k4 in range(2):
                            r0 = (k8 * 8 + k4 * 4) * 128
                            dma(dmaq(), st_[:, 4 * k4:4 * k4 + 4, :],
                                src[r0:r0 + 512, :].rearrange("(k p) c -> p k c", p=128), r=(), w=(sk,))
                        cast(dst[cbuf][:, k8 * 8:(k8 + 1) * 8, :], st_[:], r=(sk,), w=((nm, cbuf),))

            b3_load(0)
            for c in range(16):
                cbuf = c % 2
                if c + 1 < 16:
                    b3_load(c + 1)
                for g in range(4):
                    gsl = slice(g * 512, (g + 1) * 512)
                    p1, p1k = getps()
                    for k in range(16):
                        mm(p1[:, 0:512], wgm[cbuf][:, k, :], hT_own[:, k, gsl], k == 0, k == 15, r=(("wgm", cbuf),), w=(p1k,))
                    act(sgm, p1[:, 0:512], AF.Sigmoid, r=(p1k,), w=("sgm",))
                    p2, p2k = getps()
                    for k in range(16):
                        mm(p2[:, 0:512], wga[cbuf][:, k, :], hT_own[:, k, gsl], k == 0, k == 15, r=(("wga", cbuf),), w=(p2k,))
                    act(sga, p2[:, 0:512], AF.Sigmoid, r=(p2k,), w=("sga",))
                    p3, p3k = getps()
                    for k in range(8):
                        mm(p3[:, 0:512], wpm[cbuf][:, k, :], yT_res[:, k, gsl], k == 0, k == 7,
                           r=(("wpm", cbuf), "yT_res"), w=(p3k,))
                    tt("dve", t1b, sgm, p3[:, 0:512], ALU.mult, r=("sgm", p3k), w=("t1b",))
                    p4, p4k = getps()
                    for k in range(8):
                        mm(p4[:, 0:512], wpa[cbuf][:, k, :], yT_res[:, 8 + k, gsl], k == 0, k == 7,
                           r=(("wpa", cbuf), "yT_res"), w=(p4k,))
                    tt("dve", sga, sga, p4[:, 0:512], ALU.mult, r=("sga", p4k), w=("sga",))
                    mb = mTb[(c * 4 + g) % 2]
                    mk_ = ("mTb", (c * 4 + g) % 2)
                    tt("dve", mb, t1b, sga, ALU.add, r=("t1b", "sga"), w=(mk_,))
                    dma(dmaq(), mT_d[c, :, gsl], mb, r=(mk_,), w=(("mTd", c, g),))
            S.barrier()
            stop("B3", [(sm[:], 64)])

            areset()
            Wout = abf(16 * 2048).rearrange("p (k c) -> p k c", k=16)
            mTt = [abf(16 * 128).rearrange("p (c t) -> p c t", c=16) for _ in range(2)]
            fgb = af32(D)
            stgC = [af32(D) for _ in range(2)]
            xts = [af32(D) for _ in range(2)]
            obuf = [af32(D) for _ in range(2)]
            junkC = af32(D)
            dma("sp", fgb, fg_bc, r=(), w=("fgb",))
            for k in range(16):
                st_ = stgC[k % 2]
                sk = ("stgC", k % 2)
                dma(dmaq(), st_, w_out[k * 128:(k + 1) * 128, :], r=(), w=(sk,))
                tt(("dve", "pool")[k % 2], Wout[:, k, :], st_, gate_bc[:], ALU.mult, r=(sk, "gate_bc"), w=("Wout",))
            outkeys = []
            for t in range(NOWN):
                vb = t % 2
                for c4 in range(4):
                    dma(dmaq(), mTt[vb][:, 4 * c4:4 * c4 + 4, :],
                        mT_d[4 * c4:4 * c4 + 4, :, t * 128:(t + 1) * 128].rearrange("c p t -> p c t"), r=(), w=(("mTt", vb),))
                dma(dmaq(), xts[vb], xl[(NPRE + t) * 128:(NPRE + t + 1) * 128, :], r=(), w=(("xts", vb),))
                ob = obuf[vb]
                ok = ("ob", vb)
                for cbk in range(4):
                    p_, pk = getps()
                    for c in range(16):
                        mm(p_[:, 0:512], mTt[vb][:, c, :], Wout[:, c, cbk * 512:(cbk + 1) * 512], c == 0, c == 15,
                           r=(("mTt", vb), "Wout"), w=(pk,))
                    tt("dve", ob[:, cbk * 512:(cbk + 1) * 512], p_[:, 0:512], xts[vb][:, cbk * 512:(cbk + 1) * 512], ALU.add,
                       r=(pk, ("xts", vb)), w=(ok,))
                memset("dve", sm[:, 52:53], 0.0, w=("ssC",))
                act(junkC, ob, AF.Square, r=(ok, "ssC"), w=("junkC", "ssC"), accum=sm[:, 52:53])
                ts("dve", sm[:, 53:54], sm[:, 52:53], 1.0 / D, EPS, ALU.mult, ALU.add, r=("ssC",), w=("rC",))
                rsqrt_col(sm[:, 53:54], "rC")
                stt("dve", ob, ob, sm[:, 53:54], fgb, ALU.mult, ALU.mult, r=(ok, "rC", "fgb"), w=(ok,))
                dma(dmaq(), y_out[t * 128:(t + 1) * 128, :], ob, r=(ok,), w=(("yout", t),))
                outkeys.append(("yout", t))
        try:
            body()
        except _Stop:
            pass
        S.finish(())
        S.emit()
    return nc


_NC_CACHE = {}


def _consts():
    ident = np.eye(128, dtype=np.float32).astype(ml_dtypes.bfloat16)
    s = np.arange(128)[:, None]
    t = np.arange(128)[None, :]
    tri = (s <= t).astype(np.float32)
    ones = np.ones((128, 128), np.float32)
    q = np.arange(128)[:, None]
    kap = np.arange(640)[None, :]
    cq, ck = q // 64, kap // 64
    allowed = (ck >= cq) & (ck <= cq + 8)
    amask = np.where(allowed, 0.0, NEG).astype(np.float32)
    relidx = np.clip(q + 512 - kap, -63, 128) + 63
    return ident, tri, ones, tri.copy(), amask, relidx


def kernel(x, c, w_ada, b_ada, norm_g, w_in, b_if, conv_w, conv_b, mh_norm_g, rel_bias,
           w_proj_m, w_proj_a, w_out, final_norm_g):
    f = np.float32
    x = np.asarray(x, f)
    ident, tri, ones, mst, amask, relidx = _consts()
    if "nc" not in _NC_CACHE:
        _NC_CACHE["nc"] = build_nc()
    nc = _NC_CACHE["nc"]
    rep = lambda v, n=128: np.ascontiguousarray(np.broadcast_to(np.asarray(v, f).reshape(1, -1), (n, np.asarray(v).size)))
    colT = lambda v: np.ascontiguousarray(np.asarray(v, f).reshape(16, 128).T)
    cwl = np.ascontiguousarray(np.asarray(conv_w[0], f).T.reshape(16, 128, 4).transpose(1, 0, 2))
    shared = {
        "w_ada": np.ascontiguousarray(w_ada[0], f), "b_ada": np.ascontiguousarray(b_ada[0], f).reshape(1, -1),
        "ngT": colT(norm_g[0]), "w_in": np.ascontiguousarray(w_in[0], f), "bif_bc": rep(b_if[0]),
        "convw": cwl, "convb": colT(conv_b[0]), "mhg_bc": rep(mh_norm_g[0]),
        "relmat": np.ascontiguousarray(np.asarray(rel_bias[0], f)[:, relidx]), "amask": amask,
        "w_pm": np.ascontiguousarray(w_proj_m[0], f), "w_pa": np.ascontiguousarray(w_proj_a[0], f),
        "w_out": np.ascontiguousarray(w_out[0], f), "fg_bc": rep(final_norm_g),
        "ident": ident, "tri": tri, "ones": ones, "maskst": mst,
    }
    in_maps = []
    for core in range(8):
        b, j = core // 4, core % 4
        npad = (3 - j) * 16
        xl = np.zeros((NT * 128, D), f)
        xl[npad * 128:] = x[b, 0:(j + 1) * 2048]
        valid = (np.arange(NT) >= npad).astype(f)
        m = dict(shared)
        m["xl"] = xl
        m["cT"] = colT(c[b])
        m["padneg"] = rep(np.where(valid > 0, 0.0, NEG))
        m["tilevalid"] = rep(valid)
        m["negtile"] = rep(np.where(valid > 0, 0.0, NEG))
        in_maps.append(m)
    res = run_bass_kernel_spmd(nc, in_maps, core_ids=list(range(8)))
    out = np.empty((2, 8192, D), f)
    for core in range(8):
        b, j = core // 4, core % 4
        out[b, j * 2048:(j + 1) * 2048] = res.results[core]["y"]
    return out
```

```python
import os
import numpy as np
import ml_dtypes
from contextlib import ExitStack
import concourse.bass as bass
import concourse.mybir as mybir
from concourse.bass_utils import run_bass_kernel_spmd

F32 = mybir.dt.float32
BF16 = mybir.dt.bfloat16
ALU = mybir.AluOpType
AF = mybir.ActivationFunctionType
AX = mybir.AxisListType

D = 2048
NT = 64
NPRE = 48
NOWN = 16
EPS = 1e-6
C_MQ, C_MK, C_MV, C_MO, C_MZ, C_MI, C_MF = 0, 1024, 2048, 3072, 4096, 5120, 5124
C_AQ, C_AK, C_AV, C_AZ, C_GM, C_GA = 5128, 6152, 7176, 8200, 9224, 11272
IN_COLS = 13320
NEG = -30000.0
ENGS = ("sp", "act", "dve", "pool", "pe")
KSTOP = os.environ.get("KSTOP")


class _Stop(Exception):
    pass


class Op:
    __slots__ = ("eng", "fn", "deps", "dma", "sig", "sem", "val")

    def __init__(self, eng, fn, dma):
        self.eng, self.fn, self.dma = eng, fn, dma
        self.deps, self.sig, self.sem, self.val = [], dma, None, 0


class Sched:
    ND = 40

    def __init__(self, nc, es):
        self.nc = nc
        self.eops = {e: [] for e in ENGS}
        self.lastw, self.readers = {}, {}
        self.csem = {e: es.enter_context(nc.semaphore("cs_" + e)) for e in ENGS}
        self.dsem = [es.enter_context(nc.semaphore("ds%d" % i)) for i in range(self.ND)]
        self.dlast = [None] * self.ND
        self.duse = [0] * self.ND
        self.dn = 0

    @staticmethod
    def _is_psum(k):
        return k == "psS" or (isinstance(k, tuple) and len(k) > 0 and k[0] in ("ps", "psb", "psS"))

    def add(self, eng, fn, r=(), w=(), dma=False):
        w = tuple(w) + tuple(k for k in r if self._is_psum(k))
        r = tuple(k for k in r if not self._is_psum(k))
        op = Op(eng, fn, dma)
        deps = []
        for k in r:
            if k in self.lastw:
                deps.append(self.lastw[k])
        for k in w:
            if k in self.lastw:
                deps.append(self.lastw[k])
            deps.extend(self.readers.get(k, ()))
        if dma:
            i = self.dn % self.ND
            self.dn += 1
            if self.dlast[i] is not None:
                deps.append(self.dlast[i])
            self.duse[i] += 1
            op.sem, op.val = self.dsem[i], 16 * self.duse[i]
            self.dlast[i] = op
        seen = set()
        for d in deps:
            if d is op or id(d) in seen:
                continue
            seen.add(id(d))
            if d.eng == "pe" and eng == "pe" and not d.dma:
                continue
            d.sig = True
            op.deps.append(d)
        for k in r:
            lst = self.readers.setdefault(k, [])
            if not dma:
                lst[:] = [o_ for o_ in lst if o_.dma or o_.eng != eng]
            lst.append(op)
        for k in w:
            self.lastw[k] = op
            self.readers[k] = []
        self.eops[eng].append(op)
        return op

    def barrier(self):
        lasts = []
        for e in ENGS:
            for o in reversed(self.eops[e]):
                if not o.dma and o.fn is not None:
                    lasts.append(o)
                    break
        pend = [d for d in self.dlast if d is not None]
        for e in ENGS:
            op = Op(e, None, False)
            for d in lasts + pend:
                if d.eng == e and not d.dma:
                    continue
                d.sig = True
                op.deps.append(d)
            self.eops[e].append(op)
        self.lastw, self.readers = {}, {}

    def finish(self, keys):
        op = Op("sp", None, False)
        for d in self.dlast:
            if d is not None:
                op.deps.append(d)
        self.eops["sp"].append(op)

    def emit(self):
        nc = self.nc
        for e in ENGS:
            c = 0
            for o in self.eops[e]:
                if o.dma or o.fn is None:
                    continue
                if o.sig:
                    c += 1
                    o.sem, o.val = self.csem[e], c

        def run(ename, eng):
            known = {}
            for o in self.eops[ename]:
                for d in o.deps:
                    key = id(d.sem)
                    if known.get(key, 0) >= d.val:
                        continue
                    eng.wait_ge(d.sem, d.val)
                    known[key] = d.val
                if o.fn is None:
                    continue
                ins = o.fn(eng)
                if o.sig:
                    ins.then_inc(o.sem, 16 if o.dma else 1)

        with nc.Block() as block:
            @block.sync
            def _(e):
                run("sp", e)

            @block.scalar
            def _(e):
                run("act", e)

            @block.vector
            def _(e):
                run("dve", e)

            @block.gpsimd
            def _(e):
                run("pool", e)

            @block.tensor
            def _(e):
                run("pe", e)


def build_nc():
    nc = bass.Bass("TRN2", target_bir_lowering=False)

    def din(name, shape, dt=F32):
        return nc.dram_tensor(name, list(shape), dt, kind="ExternalInput").ap()

    xl = din("xl", [NT * 128, D])
    cT = din("cT", [128, 16])
    w_ada = din("w_ada", [D, 3 * D])
    b_ada = din("b_ada", [1, 3 * D])
    ngT = din("ngT", [128, 16])
    w_in = din("w_in", [D, IN_COLS])
    bif_bc = din("bif_bc", [128, 8])
    convw = din("convw", [128, 16, 4])
    convb = din("convb", [128, 16])
    mhg_bc = din("mhg_bc", [128, 1024])
    relmat = din("relmat", [8, 128, 640])
    amask = din("amask", [128, 640])
    w_pm = din("w_pm", [1024, D])
    w_pa = din("w_pa", [1024, D])
    w_out = din("w_out", [D, D])
    fg_bc = din("fg_bc", [128, D])
    padneg = din("padneg", [128, NT])
    tilevalid = din("tilevalid", [128, NT])
    negtile = din("negtile", [128, NT])
    ident_d = din("ident", [128, 128], BF16)
    tri_d = din("tri", [128, 128])
    ones_d = din("ones", [128, 128])
    mst_d = din("maskst", [128, 128])
    y_out = nc.dram_tensor("y", [NOWN * 128, D], F32, kind="ExternalOutput").ap()
    dbg = nc.dram_tensor("dbg", [128, 8192], F32, kind="ExternalOutput").ap() if KSTOP else None
    yT_d = nc.dram_tensor("yT_d", [16, 128, NOWN * 128], BF16, kind="Internal").ap()
    mT_d = nc.dram_tensor("mT_d", [16, 128, NOWN * 128], BF16, kind="Internal").ap()

    es = ExitStack()
    with es, nc.allow_low_precision("bf16 matmul operands, fp32 accumulation"), \
            nc.allow_non_contiguous_dma("column-sliced weight loads"):
        S = Sched(nc, es)

        def sb(name, shape, dt=F32):
            return es.enter_context(nc.sbuf_tensor("s_" + name, list(shape), dt))

        def pst(name, shape, dt=F32):
            return es.enter_context(nc.psum_tensor(name, list(shape), dt))

        ident = sb("ident", [128, 128], BF16)
        tri = sb("tri", [128, 128])
        ones = sb("ones", [128, 128])
        gs = sb("gs", [128, 16])
        shift = sb("shift", [128, 16])
        ngt_s = sb("ngt_s", [128, 16])
        gate_bc = sb("gate_bc", [128, D])
        cw = sb("cw", [128, 16, 4])
        cb_ = sb("cb_", [128, 16])
        bif = sb("bif", [128, 8])
        mhg = sb("mhg", [128, 1024])
        pneg = sb("pneg", [128, NT])
        tval = sb("tval", [128, NT])
        ntile = sb("ntile", [128, NT])
        amask_s = sb("amask_s", [128, 640])
        wif = sb("wif", [128, 16, 8], BF16)
        wifs = sb("wifs", [128, 16, 8])
        state = [[sb("st%d%d" % (h, b), [128, 257]) for b in range(2)] for h in range(4)]
        ctbf = [[sb("ct%d%d" % (h, b), [128, 257], BF16) for b in range(2)] for h in range(4)]
        hT_halo = sb("hT_halo", [128, 16, 512], BF16)
        wk_o = sb("wk_o", [128, NOWN, 4])
        wa_o = sb("wa_o", [128, NOWN, 4])
        eB_o = sb("eB_o", [128, NOWN, 4])
        ebc_o = sb("ebc_o", [128, NOWN, 4])
        sm = sb("sm", [128, 64])
        ARENA_COLS = 40000
        arena = sb("arena", [128, ARENA_COLS])

        ps = [pst("ps%d" % i, [128, 512]) for i in range(4)]
        psS = pst("psS", [128, 1024])
        psb = [pst("psb%d" % i, [128, 1024], BF16) for i in range(2)]
        rot = {"ps": 0, "psb": 0, "cast": 0, "q": 0}

        def getps():
            i = rot["ps"] % 4
            rot["ps"] += 1
            return ps[i], ("ps", i)

        def getpsb():
            i = rot["psb"] % 2
            rot["psb"] += 1
            return psb[i], ("psb", i)

        aoff = [0]

        def areset():
            aoff[0] = 0

        def af32(cols):
            o = aoff[0]
            aoff[0] += cols
            assert aoff[0] <= ARENA_COLS, aoff[0]
            return arena[:, o:o + cols]

        def abf(cols):
            n = (cols + 1) // 2
            return af32(n).bitcast(BF16)[:, 0:cols]

        def dma(q, out, in_, r, w):
            S.add(q, lambda e, o=out, i=in_: e.dma_start(out=o, in_=i), r=r, w=w, dma=True)

        def dmaq():
            rot["q"] += 1
            return "sp" if rot["q"] % 2 else "pool"

        def act(out, in_, func, r, w, bias=None, scale=None, accum=None):
            kw = {}
            if bias is not None:
                kw["bias"] = bias
            if scale is not None:
                kw["scale"] = scale
            if accum is not None:
                kw["accum_out"] = accum
            S.add("act", lambda e: e.activation(out, in_, func, **kw), r=r, w=w)

        def ts(eng, out, in0, s1, s2, op0, op1, r, w):
            if s2 is None:
                s2, op1 = (1.0, ALU.mult) if op0 == ALU.add else (0.0, ALU.add)
            S.add(eng, lambda e: e.tensor_scalar(out, in0, s1, s2, op0, op1), r=r, w=w)

        def rsqrt_col(col, key):
            S.add("act", lambda e: e.activation(col, col, AF.Sqrt), r=(key,), w=(key,))
            S.add("dve", lambda e: e.reciprocal(col, col), r=(key,), w=(key,))

        def tt(eng, out, in0, in1, op, r, w):
            S.add(eng, lambda e: e.tensor_tensor(out, in0, in1, op), r=r, w=w)

        def stt(eng, out, in0, sc, in1, op0, op1, r, w):
            S.add(eng, lambda e: e.scalar_tensor_tensor(out, in0, sc, in1, op0, op1), r=r, w=w)

        def mm(out, lhsT, rhs, start, stop, r, w):
            S.add("pe", lambda e: e.matmul(out, lhsT, rhs, start=start, stop=stop), r=r, w=w)

        def tr(out, in_, r, w):
            S.add("pe", lambda e: e.transpose(out, in_, ident[:]), r=tuple(r) + ("ident",), w=w)

        def cast(out, in_, r, w, eng=None):
            if eng is None:
                rot["cast"] += 1
                eng = ("act", "pool")[rot["cast"] % 2]
            if eng == "act":
                S.add("act", lambda e: e.copy(out, in_), r=r, w=w)
            else:
                S.add(eng, lambda e: e.tensor_copy(out, in_), r=r, w=w)

        def memset(eng, ap, v, w):
            S.add(eng, lambda e: e.memset(ap, v), w=w)

        def stop(tag, dumps=()):
            if KSTOP != tag:
                return
            S.barrier()
            off = 0
            for ap_, n_ in dumps:
                dma("sp", dbg[:, off:off + n_], ap_, r=(), w=(("dbg", off),))
                off += n_
            raise _Stop()

        def body():
            for (t_, d_, k_) in ((ident, ident_d, "ident"), (tri, tri_d, "tri"), (ones, ones_d, "ones"),
                                 (ngt_s, ngT, "ngt"), (cw, convw, "cw"),
                                 (cb_, convb, "cb"), (bif, bif_bc, "bif"), (mhg, mhg_bc, "mhg"),
                                 (pneg, padneg, "pneg"), (tval, tilevalid, "tval"),
                                 (ntile, negtile, "ntile"), (amask_s, amask, "amask")):
                dma("sp", t_[:], d_, r=(), w=(k_,))
            for k4 in range(4):
                dma("sp", wifs[:, 4 * k4:4 * k4 + 4, :],
                    w_in[k4 * 512:(k4 + 1) * 512, C_MI:C_MI + 8].rearrange("(k p) c -> p k c", p=128), r=(), w=("wifs",))
            cast(wif[:], wifs[:], r=("wifs",), w=("wif",), eng="dve")
            for h in range(4):
                for b in range(2):
                    memset("dve", state[h][b][:], 0.0, w=(("st", h, b),))

            areset()
            sc_in = af32(16)
            scv = af32(16)
            modrow = af32(3 * D)[0:1, :]
            badar = af32(3 * D)[0:1, :]
            stgA = [af32(3072) for _ in range(2)]
            dma("sp", sc_in, cT, r=(), w=("sc_in",))
            dma("sp", badar, b_ada, r=(), w=("badar",))
            act(scv, sc_in, AF.Silu, r=("sc_in",), w=("scv",))
            banks = [(ps[0], ("ps", 0), 0), (ps[1], ("ps", 1), 0), (ps[2], ("ps", 2), 0),
                     (ps[3], ("ps", 3), 0), (psS, ("psS",), 0), (psS, ("psS",), 512)]
            for half in range(2):
                for k in range(16):
                    st_ = stgA[k % 2]
                    key = ("stgA", k % 2)
                    dma(dmaq(), st_, w_ada[k * 128:(k + 1) * 128, half * 3072:(half + 1) * 3072], r=(), w=(key,))
                    for cbk in range(6):
                        t_, pk, off = banks[cbk]
                        mm(t_[0:1, off:off + 512], scv[:, k:k + 1], st_[:, cbk * 512:(cbk + 1) * 512],
                           k == 0, k == 15, r=(key, "scv"), w=(pk,))
                for cbk in range(6):
                    t_, pk, off = banks[cbk]
                    c0 = half * 3072 + cbk * 512
                    tt("dve", modrow[:, c0:c0 + 512], t_[0:1, off:off + 512], badar[:, c0:c0 + 512], ALU.add,
                       r=(pk, "badar"), w=("modrow",))
            pcol, pck = getps()
            for c in range(32):
                mm(pcol[:, c:c + 1], modrow[0:1, c * 128:(c + 1) * 128], ones[0:1, 0:1], True, True,
                   r=("modrow", "ones"), w=(pck,))
            cast(shift[:], pcol[:, 0:16], r=(pck,), w=("shift",), eng="dve")
            stt("dve", gs[:], pcol[:, 16:32], 1.0, ngt_s[:], ALU.add, ALU.mult, r=(pck, "ngt"), w=("gs",))
            for cbk in range(4):
                t_, pk = getps()
                mm(t_[:, 0:512], ones[0:1, :], modrow[0:1, 2 * D + cbk * 512:2 * D + (cbk + 1) * 512], True, True,
                   r=("modrow", "ones"), w=(pk,))
                cast(gate_bc[:, cbk * 512:(cbk + 1) * 512], t_[:, 0:512], r=(pk,), w=("gate_bc",), eng="act")
            S.barrier()
            stop("S1", [(gs[:], 16), (shift[:], 16), (gate_bc[:, 0:64], 64)])

            def frontend(tau, dstf, dkey, xbufs, xn, i2):
                xt = xbufs[i2 % len(xbufs)]
                xk = ("xt", i2 % len(xbufs))
                dma(dmaq(), xt, xl[tau * 128:(tau + 1) * 128, :], r=(), w=(xk,))
                memset("dve", sm[:, 0:1], 0.0, w=("ss",))
                act(xn, xt, AF.Square, r=(xk, "ss"), w=("xn", "ss"), accum=sm[:, 0:1])
                ts("dve", sm[:, 1:2], sm[:, 0:1], 1.0 / D, EPS, ALU.mult, ALU.add, r=("ss",), w=("rstd",))
                rsqrt_col(sm[:, 1:2], "rstd")
                act(xn, xt, AF.Identity, r=(xk, "rstd"), w=("xn",), scale=sm[:, 1:2])
                if tau == 0:
                    stop("F1", [(sm[:], 64)])
                for hh in range(2):
                    pt, ptk = getpsb()
                    for kk in range(8):
                        k = hh * 8 + kk
                        tr(pt[:, kk * 128:(kk + 1) * 128], xn[:, k * 128:(k + 1) * 128], r=("xn",), w=(ptk,))
                    if tau == 0 and hh == 0:
                        stop("F2", [(sm[:], 64)])
                    for kk in range(8):
                        k = hh * 8 + kk
                        if hh == 0:
                            act(dstf(k), pt[:, kk * 128:(kk + 1) * 128], AF.Identity, r=(ptk, "gs", "shift"),
                                w=(dkey[0],), bias=shift[:, k:k + 1], scale=gs[:, k:k + 1])
                        else:
                            ts("dve", dstf(k), pt[:, kk * 128:(kk + 1) * 128], gs[:, k:k + 1], shift[:, k:k + 1],
                               ALU.mult, ALU.add, r=(ptk, "gs", "shift"), w=(dkey[1],))

            def load_w(dst, src_rows, nk, stgs, skey, dkey):
                for k in range(nk):
                    st_ = stgs[k % len(stgs)]
                    key = (skey, k % len(stgs))
                    dma(dmaq(), st_, src_rows(k), r=(), w=(key,))
                    cast(dst(k), st_, r=(key,), w=(dkey,))

            def gates(tau, hsrc, hkey, own_i):
                pg, pgk = getps()
                for k in range(16):
                    mm(pg[:, 0:8], hsrc(k), wif[:, k, :], k == 0, k == 15, r=tuple(hkey) + ("wif",), w=(pgk,))
                gl = sm[:, 8:16]
                tt("dve", gl, pg[:, 0:8], bif[:], ALU.add, r=(pgk, "bif"), w=("gl",))
                ts("dve", sm[:, 16:20], gl[:, 0:4], pneg[:, tau:tau + 1], None, ALU.add, None, r=("gl", "pneg"), w=("ig",))
                act(sm[:, 20:24], gl[:, 4:8], AF.Exp, r=("gl",), w=("e1",), scale=-1.0)
                ts("dve", sm[:, 20:24], sm[:, 20:24], 1.0, None, ALU.add, None, r=("e1",), w=("e1",))
                act(sm[:, 24:28], sm[:, 20:24], AF.Ln, r=("e1",), w=("lf",))
                ts("dve", sm[:, 24:28], sm[:, 24:28], -1.0, None, ALU.mult, None, r=("lf",), w=("lf",))
                pc, pckk = getps()
                mm(pc[:, 0:4], tri[:], sm[:, 24:28], True, True, r=("lf", "tri"), w=(pckk,))
                mm(pc[:, 4:8], ones[:], sm[:, 24:28], True, True, r=("lf", "ones"), w=(pckk,))
                tt("dve", sm[:, 28:32], sm[:, 16:20], pc[:, 0:4], ALU.subtract, r=("ig", pckk), w=("t1",))
                tt("dve", sm[:, 32:36], sm[:, 28:32], pc[:, 4:8], ALU.add, r=("t1", pckk), w=("t2",))
                if own_i is None:
                    wk, eB = sm[:, 36:40], sm[:, 40:44]
                    wkk, eBk = "wk", "eB"
                else:
                    wk, eB = wk_o[:, own_i, :], eB_o[:, own_i, :]
                    wkk = eBk = ("gown", own_i)
                act(wk, sm[:, 32:36], AF.Exp, r=("t2",), w=(wkk,))
                ts("dve", wk, wk, 0.0625, None, ALU.mult, None, r=(wkk,), w=(wkk,))
                act(eB, pc[:, 4:8], AF.Exp, r=(pckk,), w=(eBk,))
                if own_i is not None:
                    act(wa_o[:, own_i, :], sm[:, 28:32], AF.Exp, r=("t1",), w=(wkk,))
                    ts("dve", wa_o[:, own_i, :], wa_o[:, own_i, :], 0.0625, None, ALU.mult, None, r=(wkk,), w=(wkk,))
                    act(ebc_o[:, own_i, :], pc[:, 0:4], AF.Exp, r=(pckk,), w=(wkk,))
                return wk, eB, wkk, eBk

            def conv_silu(pre, prek, cblk, accb, acck, out, outk):
                ts("dve", accb, pre[:, 3:515], cw[:, cblk, 3:4], cb_[:, cblk:cblk + 1], ALU.mult, ALU.add,
                   r=(prek, "cw", "cb"), w=(acck,))
                for tap in range(3):
                    stt("dve", accb, pre[:, tap:tap + 512], cw[:, cblk, tap:tap + 1], accb, ALU.mult, ALU.add,
                        r=(prek, acck, "cw"), w=(acck,))
                act(out, accb, AF.Silu, r=(acck,), w=(outk,))

            def state_update(h, kT_blk, kTk, vaug, vk, wk_col, wkk, eB_col, eBk, kpp, kppk, refresh_bf):
                pt, ptk = getpsb()
                for blk in range(2):
                    tr(pt[:, blk * 128:(blk + 1) * 128], kT_blk(blk), r=(kTk[blk],), w=(ptk,))
                ts("dve", kpp, pt[:, 0:256], wk_col, None, ALU.mult, None, r=(ptk, wkk), w=(kppk,))
                for blk in range(2):
                    p_, pk = getps()
                    mm(p_[:, 0:257], kpp[:, blk * 128:(blk + 1) * 128], vaug, True, True, r=(kppk, vk), w=(pk,))
                    stt("dve", state[h][blk][:], state[h][blk][:], eB_col, p_[:, 0:257], ALU.mult, ALU.add,
                        r=(pk, eBk, ("st", h, blk)), w=(("st", h, blk),))
                    if refresh_bf:
                        cast(ctbf[h][blk][:], state[h][blk][:], r=(("st", h, blk),), w=(("ct", h, blk),), eng="act")

            areset()
            WA = abf(16 * 2048).rearrange("p (k c) -> p k c", k=16)
            hTgA = abf(16 * 512).rearrange("p (k c) -> p k c", k=16)
            xn = abf(D)
            kT = abf(8 * 512).rearrange("p (b c) -> p b c", b=8)
            vaugs = [[abf(258)[:, 0:257] for _ in range(4)] for _ in range(2)]
            kpps = [abf(256) for _ in range(2)]
            stgs = [af32(2048) for _ in range(2)]
            xbufs = stgs
            kpre = af32(8 * 515).rearrange("p (b c) -> p b c", b=8)
            accb = af32(512)
            load_w(lambda k: WA[:, k, :], lambda k: w_in[k * 128:(k + 1) * 128, C_MK:C_MK + 2048], 16, stgs, "xt", "WA")
            stop("A00", [(sm[:], 64)])
            memset("pool", kpre[:, :, 0:3], 0.0, w=tuple(("kpre", c) for c in range(8)))
            for vb in range(2):
                for h in range(4):
                    memset("pool", vaugs[vb][h][:, 256:257], 1.0, w=(("vaug", vb, h),))
            fe_i = 0
            for g in range(NPRE // 4):
                hT = hT_halo if g == NPRE // 4 - 1 else hTgA
                hgk = tuple(("hTg", i_, p_) for i_ in range(4) for p_ in range(2))
                for i in range(4):
                    frontend(4 * g + i, lambda k, i=i, hT=hT: hT[:, k, i * 128:(i + 1) * 128], (("hTg", i, 0), ("hTg", i, 1)),
                             xbufs, xn, fe_i)
                    fe_i += 1
                if g == 0:
                    stop("A0", [(sm[:], 64)])
                for cbk in range(8):
                    p_, pk = getps()
                    for k in range(16):
                        mm(p_[:, 0:512], WA[:, k, cbk * 128:(cbk + 1) * 128], hT[:, k, :], k == 0, k == 15,
                           r=("WA",) + hgk, w=(pk,))
                    cast(kpre[:, cbk, 3:515], p_[:, 0:512], r=(pk,), w=(("kpre", cbk),), eng="act")
                    conv_silu(kpre[:, cbk, :], ("kpre", cbk), 8 + cbk, accb, "accb", kT[:, cbk, :], ("kT", cbk))
                    ts("dve", kpre[:, cbk, 0:3], kpre[:, cbk, 512:515], tval[:, 4 * g + 3:4 * g + 4], None, ALU.mult, None,
                       r=(("kpre", cbk), "tval"), w=(("kpre", cbk),))
                if g == 0:
                    stop("A1", [(kpre[:, 0, :], 515), (kpre[:, 7, :], 515), (accb, 512)])
                for i in range(4):
                    tau = 4 * g + i
                    vb = tau % 2
                    for half in range(2):
                        p_, pk = getps()
                        for k in range(16):
                            mm(p_[:, 0:512], hT[:, k, i * 128:(i + 1) * 128], WA[:, k, 1024 + half * 512:1024 + (half + 1) * 512],
                               k == 0, k == 15, r=("WA",) + hgk, w=(pk,))
                        for hh in range(2):
                            h = 2 * half + hh
                            cast(vaugs[vb][h][:, 0:256], p_[:, hh * 256:(hh + 1) * 256], r=(pk,), w=(("vaug", vb, h),),
                                 eng=("act", "dve")[hh])
                    wk, eB, wkk, eBk = gates(tau, lambda k, i=i, hT=hT: hT[:, k, i * 128:(i + 1) * 128],
                                             (("hTg", i, 0), ("hTg", i, 1)), None)
                    if tau == 0:
                        stop("A2", [(sm[:], 64)])
                    for h in range(4):
                        state_update(h, lambda blk, h=h, i=i: kT[:, 2 * h + blk, i * 128:(i + 1) * 128],
                                     (("kT", 2 * h), ("kT", 2 * h + 1)), vaugs[vb][h], ("vaug", vb, h), wk[:, h:h + 1], wkk, eB[:, h:h + 1], eBk,
                                     kpps[h % 2], ("kpp", h % 2), False)
                    if tau == 0:
                        stop("A3", [(state[0][0][:], 257), (state[3][1][:], 257), (sm[:], 64)])
            S.barrier()
            stop("A", [(state[0][0][:], 257), (state[3][1][:], 257), (sm[:], 64)])

            areset()
            hT_own = abf(16 * 2048).rearrange("p (k c) -> p k c", k=16)
            mark_b = aoff[0]
            xn = abf(D)
            xbufs = [af32(D) for _ in range(2)]
            for i in range(NOWN):
                frontend(NPRE + i, lambda k, i=i: hT_own[:, k, i * 128:(i + 1) * 128], (("hTo", i, 0), ("hTo", i, 1)), xbufs, xn, i)
                gates(NPRE + i, lambda k, i=i: hT_own[:, k, i * 128:(i + 1) * 128], (("hTo", i, 0), ("hTo", i, 1)), i)
            for h in range(4):
                for b in range(2):
                    cast(ctbf[h][b][:], state[h][b][:], r=(("st", h, b),), w=(("ct", h, b),), eng="act")
            S.barrier()
            stop("B0", [(wk_o[:].rearrange("p a b -> p (a b)"), 64), (wa_o[:].rearrange("p a b -> p (a b)"), 64), (eB_o[:].rearrange("p a b -> p (a b)"), 64), (ebc_o[:].rearrange("p a b -> p (a b)"), 64), (state[0][0][:], 257)])

            aoff[0] = mark_b
            WG = abf(16 * 1280).rearrange("p (k c) -> p k c", k=16)
            qkT = [abf(4 * 512).rearrange("p (b c) -> p b c", b=4) for _ in range(2)]
            vaug1 = [abf(258)[:, 0:257] for _ in range(2)]
            kpps = [abf(256) for _ in range(2)]
            STb = [abf(128) for _ in range(2)]
            ybf = [abf(256) for _ in range(2)]
            yTt = [abf(256).rearrange("p (b c) -> p b c", b=2) for _ in range(2)]
            stg1 = [af32(1280) for _ in range(2)]
            qkpre = af32(4 * 515).rearrange("p (b c) -> p b c", b=4)
            accb = af32(512)
            sgo = af32(256)
            slz = af32(256)
            Gt = [af32(256) for _ in range(2)]
            junk = af32(256)
            w5 = w_in[:, 0:5120].rearrange("r (s c) -> r s c", c=1024)
            for vb in range(2):
                memset("pool", vaug1[vb][:, 256:257], 1.0, w=(("vaug1", vb),))
            ts("dve", mhg[:], mhg[:], 0.5, None, ALU.mult, None, r=("mhg",), w=("mhg",))
            PS0, PS1, PS2, PS3 = (ps[0], ("ps", 0)), (ps[1], ("ps", 1)), (ps[2], ("ps", 2)), (ps[3], ("ps", 3))
            for h in range(4):
                load_w(lambda k: WG[:, k, :].rearrange("p (s c) -> p s c", c=256),
                       lambda k, h=h: w5[k * 128:(k + 1) * 128, :, h * 256:(h + 1) * 256],
                       16, [s_.rearrange("p (s c) -> p s c", c=256) for s_ in stg1], "stg1", "WG")
                for blk in range(4):
                    p_, pk = (PS0, PS1)[blk % 2]
                    for k in range(16):
                        mm(p_[:, 0:3], WG[:, k, blk * 128:(blk + 1) * 128], hT_halo[:, k, 509:512], k == 0, k == 15,
                           r=("WG",), w=(pk,))
                    ts("dve", qkpre[:, blk, 0:3], p_[:, 0:3], tval[:, NPRE - 1:NPRE], None, ALU.mult, None,
                       r=(pk, "tval"), w=(("qkpre", blk),))

                def b1_proj(g, h=h):
                    qk = qkT[g % 2]
                    for blk in range(4):
                        p_, pk = (PS0, PS1)[blk % 2]
                        for k in range(16):
                            mm(p_[:, 0:512], WG[:, k, blk * 128:(blk + 1) * 128], hT_own[:, k, g * 512:(g + 1) * 512],
                               k == 0, k == 15, r=("WG",), w=(pk,))
                        cast(qkpre[:, blk, 3:515], p_[:, 0:512], r=(pk,), w=(("qkpre", blk),), eng="act")
                        cidx = (2 * h + blk) if blk < 2 else (8 + 2 * h + blk - 2)
                        conv_silu(qkpre[:, blk, :], ("qkpre", blk), cidx, accb, "accb", qk[:, blk, :], ("qkT", g % 2, blk))
                        cast(qkpre[:, blk, 0:3], qkpre[:, blk, 512:515], r=(("qkpre", blk),), w=(("qkpre", blk),), eng="dve")

                def b1_P(t, h=h):
                    vb = t % 2
                    pA, pAk = PS0
                    for k in range(16):
                        mm(pA[:, 0:512], hT_own[:, k, t * 128:(t + 1) * 128], WG[:, k, 512:1024], k == 0, k == 15,
                           r=("WG",), w=(pAk,))
                    pB, pBk = PS1
                    for k in range(16):
                        mm(pB[:, 0:256], hT_own[:, k, t * 128:(t + 1) * 128], WG[:, k, 1024:1280], k == 0, k == 15,
                           r=("WG",), w=(pBk,))
                    cast(vaug1[vb][:, 0:256], pA[:, 0:256], r=(pAk,), w=(("vaug1", vb),), eng="act")
                    act(sgo, pA[:, 256:512], AF.Tanh, r=(pAk,), w=("sgo",), scale=0.5)
                    act(slz, pB[:, 0:256], AF.Silu, r=(pBk,), w=("slz",))
                    stt("dve", Gt[vb], sgo, 1.0, slz, ALU.add, ALU.mult, r=("sgo", "slz"), w=(("Gt", vb),))
                    tt("dve", Gt[vb], Gt[vb], mhg[:, h * 256:(h + 1) * 256], ALU.mult, r=(("Gt", vb), "mhg"), w=(("Gt", vb),))

                def b1_S(t, h=h):
                    vb = t % 2
                    gp = (t // 4) % 2
                    tsl = slice((t % 4) * 128, (t % 4 + 1) * 128)
                    pS, pSk = PS2
                    for blk in range(2):
                        mm(pS[:, 0:128], qkT[gp][:, 2 + blk, tsl], qkT[gp][:, blk, tsl], blk == 0, blk == 1,
                           r=(("qkT", gp, blk), ("qkT", gp, 2 + blk)), w=(pSk,))
                    stt("dve", STb[vb], pS[:, 0:128], wa_o[:, t, h:h + 1], tri[:], ALU.mult, ALU.mult,
                        r=(pSk, ("gown", t), "tri"), w=(("STb", vb),))

                def b1_N(t, h=h):
                    vb = t % 2
                    gp = (t // 4) % 2
                    tsl = slice((t % 4) * 128, (t % 4 + 1) * 128)
                    pN, pNk = PS3
                    mm(pN[:, 0:257], STb[vb], vaug1[vb], True, False, r=(("STb", vb), ("vaug1", vb)), w=(pNk,))
                    for blk in range(2):
                        mm(pN[:, 0:257], qkT[gp][:, blk, tsl], ctbf[h][blk][:], False, blk == 1,
                           r=(("qkT", gp, blk), ("ct", h, blk)), w=(pNk,))

                def b1_Utr(t, h=h):
                    vb = t % 2
                    gp = (t // 4) % 2
                    tsl = slice((t % 4) * 128, (t % 4 + 1) * 128)
                    pt, ptk = psb[0], ("psb", 0)
                    for blk in range(2):
                        tr(pt[:, blk * 128:(blk + 1) * 128], qkT[gp][:, 2 + blk, tsl], r=(("qkT", gp, 2 + blk),), w=(ptk,))
                    ts("dve", kpps[vb], pt[:, 0:256], wk_o[:, t, h:h + 1], None, ALU.mult, None,
                       r=(ptk, ("gown", t)), w=(("kpp", vb),))

                def b1_Umm(t, h=h):
                    vb = t % 2
                    for blk in range(2):
                        pk = ("psS", blk)
                        p_ = psS[:, blk * 512:blk * 512 + 257]
                        mm(p_, kpps[vb][:, blk * 128:(blk + 1) * 128], vaug1[vb], True, True,
                           r=(("kpp", vb), ("vaug1", vb)), w=(pk,))
                        stt("dve", state[h][blk][:], state[h][blk][:], eB_o[:, t, h:h + 1], p_, ALU.mult, ALU.add,
                            r=(pk, ("gown", t), ("st", h, blk)), w=(("st", h, blk),))
                        cast(ctbf[h][blk][:], state[h][blk][:], r=(("st", h, blk),), w=(("ct", h, blk),), eng="act")

                def b1_Ychain(t, h=h):
                    vb = t % 2
                    pN, pNk = PS3
                    gk = ("gown", t)
                    tt("dve", sm[:, 44:45], pN[:, 256:257], ebc_o[:, t, h:h + 1], ALU.mult, r=(pNk, gk), w=("d1",))
                    ts("dve", sm[:, 54:55], sm[:, 44:45], -1.0, None, ALU.mult, None, r=("d1",), w=("d1n",))
                    tt("dve", sm[:, 44:45], sm[:, 44:45], sm[:, 54:55], ALU.max, r=("d1", "d1n"), w=("d1",))
                    ts("dve", sm[:, 44:45], sm[:, 44:45], 1.0, 1.0, ALU.max, ALU.mult, r=("d1",), w=("d1",))
                    S.add("dve", lambda e: e.reciprocal(sm[:, 44:45], sm[:, 44:45]), r=("d1",), w=("d1",))
                    tt("dve", sm[:, 45:46], ebc_o[:, t, h:h + 1], sm[:, 44:45], ALU.mult, r=("d1", gk), w=("rr",))
                    memset("dve", sm[:, 46:47], 0.0, w=("ss2",))
                    act(junk, pN[:, 0:256], AF.Square, r=(pNk, "rr", "ss2"), w=("junk", "ss2"), scale=sm[:, 45:46],
                        accum=sm[:, 46:47])
                    ts("dve", sm[:, 47:48], sm[:, 46:47], 1.0 / 256, EPS, ALU.mult, ALU.add, r=("ss2",), w=("r2",))
                    rsqrt_col(sm[:, 47:48], "r2")
                    tt("dve", sm[:, 47:48], sm[:, 47:48], sm[:, 45:46], ALU.mult, r=("r2", "rr"), w=("r2",))
                    stt("dve", ybf[vb], pN[:, 0:256], sm[:, 47:48], Gt[vb], ALU.mult, ALU.mult,
                        r=(pNk, "r2", ("Gt", vb)), w=(("ybf", vb),))

                def b1_Ytr(t, h=h):
                    vb = t % 2
                    pt, ptk = psb[1], ("psb", 1)
                    for blk in range(2):
                        tr(pt[:, blk * 128:(blk + 1) * 128], ybf[vb][:, blk * 128:(blk + 1) * 128], r=(("ybf", vb),), w=(ptk,))
                    cast(yTt[vb].rearrange("p b c -> p (b c)"), pt[:, 0:256], r=(ptk,), w=(("yTt", vb),), eng="act")
                    dma(dmaq(), yT_d[2 * h:2 * h + 2, :, t * 128:(t + 1) * 128].rearrange("b p t -> p b t"), yTt[vb],
                        r=(("yTt", vb),), w=(("yTd", 2 * h, t),))

                b1_proj(0)
                b1_P(0)
                b1_S(0)
                for t in range(NOWN):
                    if t + 1 < NOWN:
                        if (t + 1) % 4 == 0:
                            b1_proj((t + 1) // 4)
                        b1_P(t + 1)
                        b1_S(t + 1)
                    b1_N(t)
                    b1_Utr(t)
                    if t > 0:
                        b1_Ytr(t - 1)
                    b1_Umm(t)
                    b1_Ychain(t)
                b1_Ytr(NOWN - 1)
            S.barrier()
            stop("B1", [(state[0][0][:], 257)])

            aoff[0] = mark_b
            WG2 = abf(16 * 512).rearrange("p (k c) -> p k c", k=16)
            akT = abf(2560)
            aqT = abf(2048)
            Vt = abf(20 * 128).rearrange("p (t c) -> p t c", t=20)
            slzA = abf(16 * 128).rearrange("p (t c) -> p t c", t=16)
            Pb = [abf(640) for _ in range(2)]
            PT = [abf(640) for _ in range(2)]
            yab = [abf(128) for _ in range(2)]
            yaT = [abf(128) for _ in range(2)]
            stg2 = [af32(512) for _ in range(2)]
            rb = af32(640)
            bm = af32(640)
            sbuf_s = [af32(640) for _ in range(2)]
            wA = w_in[:, C_AQ:C_AQ + 4096].rearrange("r (s c) -> r s c", c=1024)
            for h in range(8):
                load_w(lambda k: WG2[:, k, :].rearrange("p (s c) -> p s c", c=128),
                       lambda k, h=h: wA[k * 128:(k + 1) * 128, :, h * 128:(h + 1) * 128],
                       16, [s_.rearrange("p (s c) -> p s c", c=128) for s_ in stg2], "stg2", "WG2")
                dma("sp", rb, relmat[h], r=(), w=("rb",))
                tt("pool", bm, rb, amask_s[:], ALU.add, r=("rb", "amask"), w=("bm",))
                for g5 in range(5):
                    src = hT_halo if g5 == 0 else hT_own[:, :, (g5 - 1) * 512:g5 * 512]
                    srk = ()
                    p_, pk = getps()
                    for k in range(16):
                        mm(p_[:, 0:512], WG2[:, k, 128:256], src[:, k, :], k == 0, k == 15, r=("WG2",) + srk, w=(pk,))
                    cast(akT[:, g5 * 512:(g5 + 1) * 512], p_[:, 0:512], r=(pk,), w=(("akT", g5),), eng="act")
                    if g5 > 0:
                        p_, pk = getps()
                        for k in range(16):
                            mm(p_[:, 0:512], WG2[:, k, 0:128], src[:, k, :], k == 0, k == 15, r=("WG2",) + srk, w=(pk,))
                        act(aqT[:, (g5 - 1) * 512:g5 * 512], p_[:, 0:512], AF.Copy, r=(pk,), w=(("aqT", g5 - 1),),
                            scale=float(128 ** -0.5))
                    for i in range(4):
                        ttile = g5 * 4 + i
                        p_, pk = getps()
                        for k in range(16):
                            mm(p_[:, 0:256], src[:, k, i * 128:(i + 1) * 128], WG2[:, k, 256:512], k == 0, k == 15,
                               r=("WG2",) + srk, w=(pk,))
                        cast(Vt[:, ttile, :], p_[:, 0:128], r=(pk,), w=(("Vt", ttile),), eng="act")
                        if g5 > 0:
                            act(slzA[:, ttile - 4, :], p_[:, 128:256], AF.Silu, r=(pk,), w=(("slzA", ttile - 4),))
                def att_s1(t, h=h):
                    vb = t % 2
                    gq = ("aqT", t // 4)
                    kk0 = tuple(("akT", x) for x in sorted({t // 4, (t + 3) // 4, (t + 4) // 4}))
                    mm(psS[:, 0:512], aqT[:, t * 128:(t + 1) * 128], akT[:, t * 128:t * 128 + 512], True, True,
                       r=(gq,) + kk0, w=("psS",))
                    mm(psS[:, 512:640], aqT[:, t * 128:(t + 1) * 128], akT[:, t * 128 + 512:t * 128 + 640], True, True,
                       r=(gq,) + kk0, w=("psS",))
                    sbt = sbuf_s[vb]
                    sk = ("sbt", vb)
                    tt("dve", sbt, psS[:, 0:640], bm, ALU.add, r=("psS", "bm"), w=(sk,))
                    for kb in range(max(0, 4 - t)):
                        lt = NPRE - 4 + t + kb
                        ts("dve", sbt[:, kb * 128:(kb + 1) * 128], sbt[:, kb * 128:(kb + 1) * 128], ntile[:, lt:lt + 1], None,
                           ALU.add, None, r=(sk, "ntile"), w=(sk,))
                    S.add("dve", lambda e, sbt=sbt: e.reduce_max(sm[:, 48:49], sbt, AX.X), r=(sk,), w=("mx",))
                    ts("dve", sm[:, 48:49], sm[:, 48:49], -1.0, None, ALU.mult, None, r=("mx",), w=("mx",))
                    rsc = sm[:, 56 + vb:57 + vb]
                    memset("dve", rsc, 0.0, w=(("rsum", vb),))
                    act(Pb[vb], sbt, AF.Exp, r=(sk, "mx", ("rsum", vb)), w=(("Pb", vb), ("rsum", vb)), bias=sm[:, 48:49],
                        accum=rsc)

                def att_s2(t, h=h):
                    vb = t % 2
                    rsc = sm[:, 56 + vb:57 + vb]
                    rrc = sm[:, 58 + vb:59 + vb]
                    pt, ptk = getpsb()
                    for kb in range(5):
                        tr(pt[:, kb * 128:(kb + 1) * 128], Pb[vb][:, kb * 128:(kb + 1) * 128], r=(("Pb", vb),), w=(ptk,))
                    cast(PT[vb], pt[:, 0:640], r=(ptk,), w=(("PT", vb),), eng="act")
                    pO, pOk = getps()
                    for kb in range(5):
                        mm(pO[:, 0:128], PT[vb][:, kb * 128:(kb + 1) * 128], Vt[:, t + kb, :], kb == 0, kb == 4,
                           r=(("PT", vb), ("Vt", t + kb)), w=(pOk,))
                    S.add("dve", lambda e: e.reciprocal(rrc, rsc), r=(("rsum", vb),), w=(("rrs", vb),))
                    stt("dve", yab[vb], pO[:, 0:128], rrc, slzA[:, t, :], ALU.mult, ALU.mult,
                        r=(pOk, ("rrs", vb), ("slzA", t)), w=(("yab", vb),))
                    pt2, pt2k = getpsb()
                    tr(pt2[:, 0:128], yab[vb], r=(("yab", vb),), w=(pt2k,))
                    cast(yaT[vb], pt2[:, 0:128], r=(pt2k,), w=(("yaT", vb),), eng="act")
                    dma(dmaq(), yT_d[8 + h, :, t * 128:(t + 1) * 128], yaT[vb], r=(("yaT", vb),), w=(("yTd", 8 + h, t),))

                att_s1(0)
                for t in range(NOWN):
                    if t + 1 < NOWN:
                        att_s1(t + 1)
                    att_s2(t)
            S.barrier()
            stop("B2", [(sm[:], 64)])

            aoff[0] = mark_b
            yT_res = abf(16 * 2048).rearrange("p (b c) -> p b c", b=16)
            hh_flat = hT_halo[:].rearrange("p k c -> p (k c)")
            wgm = [abf(16 * 128).rearrange("p (k c) -> p k c", k=16), hh_flat[:, 0:2048].rearrange("p (k c) -> p k c", k=16)]
            wga = [abf(16 * 128).rearrange("p (k c) -> p k c", k=16), hh_flat[:, 2048:4096].rearrange("p (k c) -> p k c", k=16)]
            wpm = [abf(8 * 128).rearrange("p (k c) -> p k c", k=8), hh_flat[:, 4096:5120].rearrange("p (k c) -> p k c", k=8)]
            wpa = [abf(8 * 128).rearrange("p (k c) -> p k c", k=8), hh_flat[:, 5120:6144].rearrange("p (k c) -> p k c", k=8)]
            mTb = [abf(512) for _ in range(2)]
            stg3 = [af32(8 * 128).rearrange("p (k c) -> p k c", k=8) for _ in range(2)]
            sgm = af32(512)
            sga = af32(512)
            t1b = af32(512)
            for b in range(16):
                dma(dmaq(), yT_res[:, b, :], yT_d[b], r=(), w=("yT_res",))
            si3 = [0]

            def b3_load(c):
                cbuf = c % 2
                for (dst, src, nk, nm) in ((wgm, w_in[:, C_GM + c * 128:C_GM + (c + 1) * 128], 16, "wgm"),
                                           (wga, w_in[:, C_GA + c * 128:C_GA + (c + 1) * 128], 16, "wga"),
                                           (wpm, w_pm[:, c * 128:(c + 1) * 128], 8, "wpm"),
                                           (wpa, w_pa[:, c * 128:(c + 1) * 128], 8, "wpa")):
                    for k8 in range(nk // 8):
                        st_ = stg3[si3[0] % 2]
                        sk = ("stg3", si3[0] % 2)
                        si3[0] += 1
                        for k4 in range(2):
                            r0 = (k8 * 8 + k4 * 4) * 128
                            dma(dmaq(), st_[:, 4 * k4:4 * k4 + 4, :],
                                src[r0:r0 + 512, :].rearrange("(k p) c -> p k c", p=128), r=(), w=(sk,))
                        cast(dst[cbuf][:, k8 * 8:(k8 + 1) * 8, :], st_[:], r=(sk,), w=((nm, cbuf),))

            b3_load(0)
            for c in range(16):
                cbuf = c % 2
                if c + 1 < 16:
                    b3_load(c + 1)
                for g in range(4):
                    gsl = slice(g * 512, (g + 1) * 512)
                    p1, p1k = getps()
                    for k in range(16):
                        mm(p1[:, 0:512], wgm[cbuf][:, k, :], hT_own[:, k, gsl], k == 0, k == 15, r=(("wgm", cbuf),), w=(p1k,))
                    act(sgm, p1[:, 0:512], AF.Sigmoid, r=(p1k,), w=("sgm",))
                    p2, p2k = getps()
                    for k in range(16):
                        mm(p2[:, 0:512], wga[cbuf][:, k, :], hT_own[:, k, gsl], k == 0, k == 15, r=(("wga", cbuf),), w=(p2k,))
                    act(sga, p2[:, 0:512], AF.Sigmoid, r=(p2k,), w=("sga",))
                    p3, p3k = getps()
                    for k in range(8):
                        mm(p3[:, 0:512], wpm[cbuf][:, k, :], yT_res[:, k, gsl], k == 0, k == 7,
                           r=(("wpm", cbuf), "yT_res"), w=(p3k,))
                    tt("dve", t1b, sgm, p3[:, 0:512], ALU.mult, r=("sgm", p3k), w=("t1b",))
                    p4, p4k = getps()
                    for k in range(8):
                        mm(p4[:, 0:512], wpa[cbuf][:, k, :], yT_res[:, 8 + k, gsl], k == 0, k == 7,
                           r=(("wpa", cbuf), "yT_res"), w=(p4k,))
                    tt("dve", sga, sga, p4[:, 0:512], ALU.mult, r=("sga", p4k), w=("sga",))
                    mb = mTb[(c * 4 + g) % 2]
                    mk_ = ("mTb", (c * 4 + g) % 2)
                    tt("dve", mb, t1b, sga, ALU.add, r=("t1b", "sga"), w=(mk_,))
                    dma(dmaq(), mT_d[c, :, gsl], mb, r=(mk_,), w=(("mTd", c, g),))
            S.barrier()
            stop("B3", [(sm[:], 64)])

            areset()
            Wout = abf(16 * 2048).rearrange("p (k c) -> p k c", k=16)
            mTt = [abf(16 * 128).rearrange("p (c t) -> p c t", c=16) for _ in range(2)]
            fgb = af32(D)
            stgC = [af32(D) for _ in range(2)]
            xts = [af32(D) for _ in range(2)]
            obuf = [af32(D) for _ in range(2)]
            junkC = af32(D)
            dma("sp", fgb, fg_bc, r=(), w=("fgb",))
            for k in range(16):
                st_ = stgC[k % 2]
                sk = ("stgC", k % 2)
                dma(dmaq(), st_, w_out[k * 128:(k + 1) * 128, :], r=(), w=(sk,))
                tt(("dve", "pool")[k % 2], Wout[:, k, :], st_, gate_bc[:], ALU.mult, r=(sk, "gate_bc"), w=("Wout",))
            outkeys = []
            for t in range(NOWN):
                vb = t % 2
                for c4 in range(4):
                    dma(dmaq(), mTt[vb][:, 4 * c4:4 * c4 + 4, :],
                        mT_d[4 * c4:4 * c4 + 4, :, t * 128:(t + 1) * 128].rearrange("c p t -> p c t"), r=(), w=(("mTt", vb),))
                dma(dmaq(), xts[vb], xl[(NPRE + t) * 128:(NPRE + t + 1) * 128, :], r=(), w=(("xts", vb),))
                ob = obuf[vb]
                ok = ("ob", vb)
                for cbk in range(4):
                    p_, pk = getps()
                    for c in range(16):
                        mm(p_[:, 0:512], mTt[vb][:, c, :], Wout[:, c, cbk * 512:(cbk + 1) * 512], c == 0, c == 15,
                           r=(("mTt", vb), "Wout"), w=(pk,))
                    tt("dve", ob[:, cbk * 512:(cbk + 1) * 512], p_[:, 0:512], xts[vb][:, cbk * 512:(cbk + 1) * 512], ALU.add,
                       r=(pk, ("xts", vb)), w=(ok,))
                memset("dve", sm[:, 52:53], 0.0, w=("ssC",))
                act(junkC, ob, AF.Square, r=(ok, "ssC"), w=("junkC", "ssC"), accum=sm[:, 52:53])
                ts("dve", sm[:, 53:54], sm[:, 52:53], 1.0 / D, EPS, ALU.mult, ALU.add, r=("ssC",), w=("rC",))
                rsqrt_col(sm[:, 53:54], "rC")
                stt("dve", ob, ob, sm[:, 53:54], fgb, ALU.mult, ALU.mult, r=(ok, "rC", "fgb"), w=(ok,))
                dma(dmaq(), y_out[t * 128:(t + 1) * 128, :], ob, r=(ok,), w=(("yout", t),))
                outkeys.append(("yout", t))
        try:
            body()
        except _Stop:
            pass
        S.finish(())
        S.emit()
    return nc


_NC_CACHE = {}


def _consts():
    ident = np.eye(128, dtype=np.float32).astype(ml_dtypes.bfloat16)
    s = np.arange(128)[:, None]
    t = np.arange(128)[None, :]
    tri = (s <= t).astype(np.float32)
    ones = np.ones((128, 128), np.float32)
    q = np.arange(128)[:, None]
    kap = np.arange(640)[None, :]
    cq, ck = q // 64, kap // 64
    allowed = (ck >= cq) & (ck <= cq + 8)
    amask = np.where(allowed, 0.0, NEG).astype(np.float32)
    relidx = np.clip(q + 512 - kap, -63, 128) + 63
    return ident, tri, ones, tri.copy(), amask, relidx


def kernel(x, c, w_ada, b_ada, norm_g, w_in, b_if, conv_w, conv_b, mh_norm_g, rel_bias,
           w_proj_m, w_proj_a, w_out, final_norm_g):
    f = np.float32
    x = np.asarray(x, f)
    ident, tri, ones, mst, amask, relidx = _consts()
    if "nc" not in _NC_CACHE:
        _NC_CACHE["nc"] = build_nc()
    nc = _NC_CACHE["nc"]
    rep = lambda v, n=128: np.ascontiguousarray(np.broadcast_to(np.asarray(v, f).reshape(1, -1), (n, np.asarray(v).size)))
    colT = lambda v: np.ascontiguousarray(np.asarray(v, f).reshape(16, 128).T)
    cwl = np.ascontiguousarray(np.asarray(conv_w[0], f).T.reshape(16, 128, 4).transpose(1, 0, 2))
    shared = {
        "w_ada": np.ascontiguousarray(w_ada[0], f), "b_ada": np.ascontiguousarray(b_ada[0], f).reshape(1, -1),
        "ngT": colT(norm_g[0]), "w_in": np.ascontiguousarray(w_in[0], f), "bif_bc": rep(b_if[0]),
        "convw": cwl, "convb": colT(conv_b[0]), "mhg_bc": rep(mh_norm_g[0]),
        "relmat": np.ascontiguousarray(np.asarray(rel_bias[0], f)[:, relidx]), "amask": amask,
        "w_pm": np.ascontiguousarray(w_proj_m[0], f), "w_pa": np.ascontiguousarray(w_proj_a[0], f),
        "w_out": np.ascontiguousarray(w_out[0], f), "fg_bc": rep(final_norm_g),
        "ident": ident, "tri": tri, "ones": ones, "maskst": mst,
    }
    in_maps = []
    for core in range(8):
        b, j = core // 4, core % 4
        npad = (3 - j) * 16
        xl = np.zeros((NT * 128, D), f)
        xl[npad * 128:] = x[b, 0:(j + 1) * 2048]
        valid = (np.arange(NT) >= npad).astype(f)
        m = dict(shared)
        m["xl"] = xl
        m["cT"] = colT(c[b])
        m["padneg"] = rep(np.where(valid > 0, 0.0, NEG))
        m["tilevalid"] = rep(valid)
        m["negtile"] = rep(np.where(valid > 0, 0.0, NEG))
        in_maps.append(m)
    res = run_bass_kernel_spmd(nc, in_maps, core_ids=list(range(8)))
    out = np.empty((2, 8192, D), f)
    for core in range(8):
        b, j = core // 4, core % 4
        out[b, j * 2048:(j + 1) * 2048] = res.results[core]["y"]
    return out
```

```python
import os
import numpy as np
import ml_dtypes
from contextlib import ExitStack
import concourse.bass as bass
import concourse.mybir as mybir
from concourse.bass_utils import run_bass_kernel_spmd

F32 = mybir.dt.float32
BF16 = mybir.dt.bfloat16
ALU = mybir.AluOpType
AF = mybir.ActivationFunctionType
AX = mybir.AxisListType

D = 2048
NT = 64
NPRE = 48
NOWN = 16
EPS = 1e-6
C_MQ, C_MK, C_MV, C_MO, C_MZ, C_MI, C_MF = 0, 1024, 2048, 3072, 4096, 5120, 5124
C_AQ, C_AK, C_AV, C_AZ, C_GM, C_GA = 5128, 6152, 7176, 8200, 9224, 11272
IN_COLS = 13320
NEG = -30000.0
ENGS = ("sp", "act", "dve", "pool", "pe")
KSTOP = os.environ.get("KSTOP")


class _Stop(Exception):
    pass


class Op:
    __slots__ = ("eng", "fn", "deps", "dma", "sig", "sem", "val")

    def __init__(self, eng, fn, dma):
        self.eng, self.fn, self.dma = eng, fn, dma
        self.deps, self.sig, self.sem, self.val = [], dma, None, 0


class Sched:
    ND = 40

    def __init__(self, nc, es):
        self.nc = nc
        self.eops = {e: [] for e in ENGS}
        self.lastw, self.readers = {}, {}
        self.csem = {e: es.enter_context(nc.semaphore("cs_" + e)) for e in ENGS}
        self.dsem = [es.enter_context(nc.semaphore("ds%d" % i)) for i in range(self.ND)]
        self.dlast = [None] * self.ND
        self.duse = [0] * self.ND
        self.dn = 0

    @staticmethod
    def _is_psum(k):
        return k == "psS" or (isinstance(k, tuple) and len(k) > 0 and k[0] in ("ps", "psb", "psS"))

    def add(self, eng, fn, r=(), w=(), dma=False):
        w = tuple(w) + tuple(k for k in r if self._is_psum(k))
        r = tuple(k for k in r if not self._is_psum(k))
        op = Op(eng, fn, dma)
        deps = []
        for k in r:
            if k in self.lastw:
                deps.append(self.lastw[k])
        for k in w:
            if k in self.lastw:
                deps.append(self.lastw[k])
            deps.extend(self.readers.get(k, ()))
        if dma:
            i = self.dn % self.ND
            self.dn += 1
            if self.dlast[i] is not None:
                deps.append(self.dlast[i])
            self.duse[i] += 1
            op.sem, op.val = self.dsem[i], 16 * self.duse[i]
            self.dlast[i] = op
        seen = set()
        for d in deps:
            if d is op or id(d) in seen:
                continue
            seen.add(id(d))
            if d.eng == "pe" and eng == "pe" and not d.dma:
                continue
            d.sig = True
            op.deps.append(d)
        for k in r:
            lst = self.readers.setdefault(k, [])
            if not dma:
                lst[:] = [o_ for o_ in lst if o_.dma or o_.eng != eng]
            lst.append(op)
        for k in w:
            self.lastw[k] = op
            self.readers[k] = []
        self.eops[eng].append(op)
        return op

    def barrier(self):
        lasts = []
        for e in ENGS:
            for o in reversed(self.eops[e]):
                if not o.dma and o.fn is not None:
                    lasts.append(o)
                    break
        pend = [d for d in self.dlast if d is not None]
        for e in ENGS:
            op = Op(e, None, False)
            for d in lasts + pend:
                if d.eng == e and not d.dma:
                    continue
                d.sig = True
                op.deps.append(d)
            self.eops[e].append(op)
        self.lastw, self.readers = {}, {}

    def finish(self, keys):
        op = Op("sp", None, False)
        for d in self.dlast:
            if d is not None:
                op.deps.append(d)
        self.eops["sp"].append(op)

    def emit(self):
        nc = self.nc
        for e in ENGS:
            c = 0
            for o in self.eops[e]:
                if o.dma or o.fn is None:
                    continue
                if o.sig:
                    c += 1
                    o.sem, o.val = self.csem[e], c

        def run(ename, eng):
            known = {}
            for o in self.eops[ename]:
                for d in o.deps:
                    key = id(d.sem)
                    if known.get(key, 0) >= d.val:
                        continue
                    eng.wait_ge(d.sem, d.val)
                    known[key] = d.val
                if o.fn is None:
                    continue
                ins = o.fn(eng)
                if o.sig:
                    ins.then_inc(o.sem, 16 if o.dma else 1)

        with nc.Block() as block:
            @block.sync
            def _(e):
                run("sp", e)

            @block.scalar
            def _(e):
                run("act", e)

            @block.vector
            def _(e):
                run("dve", e)

            @block.gpsimd
            def _(e):
                run("pool", e)

            @block.tensor
            def _(e):
                run("pe", e)


def build_nc():
    nc = bass.Bass("TRN2", target_bir_lowering=False)

    def din(name, shape, dt=F32):
        return nc.dram_tensor(name, list(shape), dt, kind="ExternalInput").ap()

    xl = din("xl", [NT * 128, D])
    cT = din("cT", [128, 16])
    w_ada = din("w_ada", [D, 3 * D])
    b_ada = din("b_ada", [1, 3 * D])
    ngT = din("ngT", [128, 16])
    w_in = din("w_in", [D, IN_COLS])
    bif_bc = din("bif_bc", [128, 8])
    convw = din("convw", [128, 16, 4])
    convb = din("convb", [128, 16])
    mhg_bc = din("mhg_bc", [128, 1024])
    relmat = din("relmat", [8, 128, 640])
    amask = din("amask", [128, 640])
    w_pm = din("w_pm", [1024, D])
    w_pa = din("w_pa", [1024, D])
    w_out = din("w_out", [D, D])
    fg_bc = din("fg_bc", [128, D])
    padneg = din("padneg", [128, NT])
    tilevalid = din("tilevalid", [128, NT])
    negtile = din("negtile", [128, NT])
    ident_d = din("ident", [128, 128], BF16)
    tri_d = din("tri", [128, 128])
    ones_d = din("ones", [128, 128])
    mst_d = din("maskst", [128, 128])
    y_out = nc.dram_tensor("y", [NOWN * 128, D], F32, kind="ExternalOutput").ap()
    dbg = nc.dram_tensor("dbg", [128, 8192], F32, kind="ExternalOutput").ap() if KSTOP else None
    yT_d = nc.dram_tensor("yT_d", [16, 128, NOWN * 128], BF16, kind="Internal").ap()
    mT_d = nc.dram_tensor("mT_d", [16, 128, NOWN * 128], BF16, kind="Internal").ap()

    es = ExitStack()
    with es, nc.allow_low_precision("bf16 matmul operands, fp32 accumulation"), \
            nc.allow_non_contiguous_dma("column-sliced weight loads"):
        S = Sched(nc, es)

        def sb(name, shape, dt=F32):
            return es.enter_context(nc.sbuf_tensor("s_" + name, list(shape), dt))

        def pst(name, shape, dt=F32):
            return es.enter_context(nc.psum_tensor(name, list(shape), dt))

        ident = sb("ident", [128, 128], BF16)
        tri = sb("tri", [128, 128])
        ones = sb("ones", [128, 128])
        gs = sb("gs", [128, 16])
        shift = sb("shift", [128, 16])
        ngt_s = sb("ngt_s", [128, 16])
        gate_bc = sb("gate_bc", [128, D])
        cw = sb("cw", [128, 16, 4])
        cb_ = sb("cb_", [128, 16])
        bif = sb("bif", [128, 8])
        mhg = sb("mhg", [128, 1024])
        pneg = sb("pneg", [128, NT])
        tval = sb("tval", [128, NT])
        ntile = sb("ntile", [128, NT])
        amask_s = sb("amask_s", [128, 640])
        wif = sb("wif", [128, 16, 8], BF16)
        wifs = sb("wifs", [128, 16, 8])
        state = [[sb("st%d%d" % (h, b), [128, 257]) for b in range(2)] for h in range(4)]
        ctbf = [[sb("ct%d%d" % (h, b), [128, 257], BF16) for b in range(2)] for h in range(4)]
        hT_halo = sb("hT_halo", [128, 16, 512], BF16)
        wk_o = sb("wk_o", [128, NOWN, 4])
        wa_o = sb("wa_o", [128, NOWN, 4])
        eB_o = sb("eB_o", [128, NOWN, 4])
        ebc_o = sb("ebc_o", [128, NOWN, 4])
        sm = sb("sm", [128, 64])
        ARENA_COLS = 40000
        arena = sb("arena", [128, ARENA_COLS])

        ps = [pst("ps%d" % i, [128, 512]) for i in range(4)]
        psS = pst("psS", [128, 1024])
        psb = [pst("psb%d" % i, [128, 1024], BF16) for i in range(2)]
        rot = {"ps": 0, "psb": 0, "cast": 0, "q": 0}

        def getps():
            i = rot["ps"] % 4
            rot["ps"] += 1
            return ps[i], ("ps", i)

        def getpsb():
            i = rot["psb"] % 2
            rot["psb"] += 1
            return psb[i], ("psb", i)

        aoff = [0]

        def areset():
            aoff[0] = 0

        def af32(cols):
            o = aoff[0]
            aoff[0] += cols
            assert aoff[0] <= ARENA_COLS, aoff[0]
            return arena[:, o:o + cols]

        def abf(cols):
            n = (cols + 1) // 2
            return af32(n).bitcast(BF16)[:, 0:cols]

        def dma(q, out, in_, r, w):
            S.add(q, lambda e, o=out, i=in_: e.dma_start(out=o, in_=i), r=r, w=w, dma=True)

        def dmaq():
            rot["q"] += 1
            return "sp" if rot["q"] % 2 else "pool"

        def act(out, in_, func, r, w, bias=None, scale=None, accum=None):
            kw = {}
            if bias is not None:
                kw["bias"] = bias
            if scale is not None:
                kw["scale"] = scale
            if accum is not None:
                kw["accum_out"] = accum
            S.add("act", lambda e: e.activation(out, in_, func, **kw), r=r, w=w)

        def ts(eng, out, in0, s1, s2, op0, op1, r, w):
            if s2 is None:
                s2, op1 = (1.0, ALU.mult) if op0 == ALU.add else (0.0, ALU.add)
            S.add(eng, lambda e: e.tensor_scalar(out, in0, s1, s2, op0, op1), r=r, w=w)

        def rsqrt_col(col, key):
            S.add("act", lambda e: e.activation(col, col, AF.Sqrt), r=(key,), w=(key,))
            S.add("dve", lambda e: e.reciprocal(col, col), r=(key,), w=(key,))

        def tt(eng, out, in0, in1, op, r, w):
            S.add(eng, lambda e: e.tensor_tensor(out, in0, in1, op), r=r, w=w)

        def stt(eng, out, in0, sc, in1, op0, op1, r, w):
            S.add(eng, lambda e: e.scalar_tensor_tensor(out, in0, sc, in1, op0, op1), r=r, w=w)

        def mm(out, lhsT, rhs, start, stop, r, w):
            S.add("pe", lambda e: e.matmul(out, lhsT, rhs, start=start, stop=stop), r=r, w=w)

        def tr(out, in_, r, w):
            S.add("pe", lambda e: e.transpose(out, in_, ident[:]), r=tuple(r) + ("ident",), w=w)

        def cast(out, in_, r, w, eng=None):
            if eng is None:
                rot["cast"] += 1
                eng = ("act", "pool")[rot["cast"] % 2]
            if eng == "act":
                S.add("act", lambda e: e.copy(out, in_), r=r, w=w)
            else:
                S.add(eng, lambda e: e.tensor_copy(out, in_), r=r, w=w)

        def memset(eng, ap, v, w):
            S.add(eng, lambda e: e.memset(ap, v), w=w)

        def stop(tag, dumps=()):
            if KSTOP != tag:
                return
            S.barrier()
            off = 0
            for ap_, n_ in dumps:
                dma("sp", dbg[:, off:off + n_], ap_, r=(), w=(("dbg", off),))
                off += n_
            raise _Stop()

        def body():
            for (t_, d_, k_) in ((ident, ident_d, "ident"), (tri, tri_d, "tri"), (ones, ones_d, "ones"),
                                 (ngt_s, ngT, "ngt"), (cw, convw, "cw"),
                                 (cb_, convb, "cb"), (bif, bif_bc, "bif"), (mhg, mhg_bc, "mhg"),
                                 (pneg, padneg, "pneg"), (tval, tilevalid, "tval"),
                                 (ntile, negtile, "ntile"), (amask_s, amask, "amask")):
                dma("sp", t_[:], d_, r=(), w=(k_,))
            for k4 in range(4):
                dma("sp", wifs[:, 4 * k4:4 * k4 + 4, :],
                    w_in[k4 * 512:(k4 + 1) * 512, C_MI:C_MI + 8].rearrange("(k p) c -> p k c", p=128), r=(), w=("wifs",))
            cast(wif[:], wifs[:], r=("wifs",), w=("wif",), eng="dve")
            for h in range(4):
                for b in range(2):
                    memset("dve", state[h][b][:], 0.0, w=(("st", h, b),))

            areset()
            sc_in = af32(16)
            scv = af32(16)
            modrow = af32(3 * D)[0:1, :]
            badar = af32(3 * D)[0:1, :]
            stgA = [af32(3072) for _ in range(2)]
            dma("sp", sc_in, cT, r=(), w=("sc_in",))
            dma("sp", badar, b_ada, r=(), w=("badar",))
            act(scv, sc_in, AF.Silu, r=("sc_in",), w=("scv",))
            banks = [(ps[0], ("ps", 0), 0), (ps[1], ("ps", 1), 0), (ps[2], ("ps", 2), 0),
                     (ps[3], ("ps", 3), 0), (psS, ("psS",), 0), (psS, ("psS",), 512)]
            for half in range(2):
                for k in range(16):
                    st_ = stgA[k % 2]
                    key = ("stgA", k % 2)
                    dma(dmaq(), st_, w_ada[k * 128:(k + 1) * 128, half * 3072:(half + 1) * 3072], r=(), w=(key,))
                    for cbk in range(6):
                        t_, pk, off = banks[cbk]
                        mm(t_[0:1, off:off + 512], scv[:, k:k + 1], st_[:, cbk * 512:(cbk + 1) * 512],
                           k == 0, k == 15, r=(key, "scv"), w=(pk,))
                for cbk in range(6):
                    t_, pk, off = banks[cbk]
                    c0 = half * 3072 + cbk * 512
                    tt("dve", modrow[:, c0:c0 + 512], t_[0:1, off:off + 512], badar[:, c0:c0 + 512], ALU.add,
                       r=(pk, "badar"), w=("modrow",))
            pcol, pck = getps()
            for c in range(32):
                mm(pcol[:, c:c + 1], modrow[0:1, c * 128:(c + 1) * 128], ones[0:1, 0:1], True, True,
                   r=("modrow", "ones"), w=(pck,))
            cast(shift[:], pcol[:, 0:16], r=(pck,), w=("shift",), eng="dve")
            stt("dve", gs[:], pcol[:, 16:32], 1.0, ngt_s[:], ALU.add, ALU.mult, r=(pck, "ngt"), w=("gs",))
            for cbk in range(4):
                t_, pk = getps()
                mm(t_[:, 0:512], ones[0:1, :], modrow[0:1, 2 * D + cbk * 512:2 * D + (cbk + 1) * 512], True, True,
                   r=("modrow", "ones"), w=(pk,))
                cast(gate_bc[:, cbk * 512:(cbk + 1) * 512], t_[:, 0:512], r=(pk,), w=("gate_bc",), eng="act")
            S.barrier()
            stop("S1", [(gs[:], 16), (shift[:], 16), (gate_bc[:, 0:64], 64)])

            def frontend(tau, dstf, dkey, xbufs, xn, i2):
                xt = xbufs[i2 % len(xbufs)]
                xk = ("xt", i2 % len(xbufs))
                dma(dmaq(), xt, xl[tau * 128:(tau + 1) * 128, :], r=(), w=(xk,))
                memset("dve", sm[:, 0:1], 0.0, w=("ss",))
                act(xn, xt, AF.Square, r=(xk, "ss"), w=("xn", "ss"), accum=sm[:, 0:1])
                ts("dve", sm[:, 1:2], sm[:, 0:1], 1.0 / D, EPS, ALU.mult, ALU.add, r=("ss",), w=("rstd",))
                rsqrt_col(sm[:, 1:2], "rstd")
                act(xn, xt, AF.Identity, r=(xk, "rstd"), w=("xn",), scale=sm[:, 1:2])
                if tau == 0:
                    stop("F1", [(sm[:], 64)])
                for hh in range(2):
                    pt, ptk = getpsb()
                    for kk in range(8):
                        k = hh * 8 + kk
                        tr(pt[:, kk * 128:(kk + 1) * 128], xn[:, k * 128:(k + 1) * 128], r=("xn",), w=(ptk,))
                    if tau == 0 and hh == 0:
                        stop("F2", [(sm[:], 64)])
                    for kk in range(8):
                        k = hh * 8 + kk
                        if hh == 0:
                            act(dstf(k), pt[:, kk * 128:(kk + 1) * 128], AF.Identity, r=(ptk, "gs", "shift"),
                                w=(dkey[0],), bias=shift[:, k:k + 1], scale=gs[:, k:k + 1])
                        else:
                            ts("dve", dstf(k), pt[:, kk * 128:(kk + 1) * 128], gs[:, k:k + 1], shift[:, k:k + 1],
                               ALU.mult, ALU.add, r=(ptk, "gs", "shift"), w=(dkey[1],))

            def load_w(dst, src_rows, nk, stgs, skey, dkey):
                for k in range(nk):
                    st_ = stgs[k % len(stgs)]
                    key = (skey, k % len(stgs))
                    dma(dmaq(), st_, src_rows(k), r=(), w=(key,))
                    cast(dst(k), st_, r=(key,), w=(dkey,))

            def gates(tau, hsrc, hkey, own_i):
                pg, pgk = getps()
                for k in range(16):
                    mm(pg[:, 0:8], hsrc(k), wif[:, k, :], k == 0, k == 15, r=tuple(hkey) + ("wif",), w=(pgk,))
                gl = sm[:, 8:16]
                tt("dve", gl, pg[:, 0:8], bif[:], ALU.add, r=(pgk, "bif"), w=("gl",))
                ts("dve", sm[:, 16:20], gl[:, 0:4], pneg[:, tau:tau + 1], None, ALU.add, None, r=("gl", "pneg"), w=("ig",))
                act(sm[:, 20:24], gl[:, 4:8], AF.Exp, r=("gl",), w=("e1",), scale=-1.0)
                ts("dve", sm[:, 20:24], sm[:, 20:24], 1.0, None, ALU.add, None, r=("e1",), w=("e1",))
                act(sm[:, 24:28], sm[:, 20:24], AF.Ln, r=("e1",), w=("lf",))
                ts("dve", sm[:, 24:28], sm[:, 24:28], -1.0, None, ALU.mult, None, r=("lf",), w=("lf",))
                pc, pckk = getps()
                mm(pc[:, 0:4], tri[:], sm[:, 24:28], True, True, r=("lf", "tri"), w=(pckk,))
                mm(pc[:, 4:8], ones[:], sm[:, 24:28], True, True, r=("lf", "ones"), w=(pckk,))
                tt("dve", sm[:, 28:32], sm[:, 16:20], pc[:, 0:4], ALU.subtract, r=("ig", pckk), w=("t1",))
                tt("dve", sm[:, 32:36], sm[:, 28:32], pc[:, 4:8], ALU.add, r=("t1", pckk), w=("t2",))
                if own_i is None:
                    wk, eB = sm[:, 36:40], sm[:, 40:44]
                    wkk, eBk = "wk", "eB"
                else:
                    wk, eB = wk_o[:, own_i, :], eB_o[:, own_i, :]
                    wkk = eBk = ("gown", own_i)
                act(wk, sm[:, 32:36], AF.Exp, r=("t2",), w=(wkk,))
                ts("dve", wk, wk, 0.0625, None, ALU.mult, None, r=(wkk,), w=(wkk,))
                act(eB, pc[:, 4:8], AF.Exp, r=(pckk,), w=(eBk,))
                if own_i is not None:
                    act(wa_o[:, own_i, :], sm[:, 28:32], AF.Exp, r=("t1",), w=(wkk,))
                    ts("dve", wa_o[:, own_i, :], wa_o[:, own_i, :], 0.0625, None, ALU.mult, None, r=(wkk,), w=(wkk,))
                    act(ebc_o[:, own_i, :], pc[:, 0:4], AF.Exp, r=(pckk,), w=(wkk,))
                return wk, eB, wkk, eBk

            def conv_silu(pre, prek, cblk, accb, acck, out, outk):
                ts("dve", accb, pre[:, 3:515], cw[:, cblk, 3:4], cb_[:, cblk:cblk + 1], ALU.mult, ALU.add,
                   r=(prek, "cw", "cb"), w=(acck,))
                for tap in range(3):
                    stt("dve", accb, pre[:, tap:tap + 512], cw[:, cblk, tap:tap + 1], accb, ALU.mult, ALU.add,
                        r=(prek, acck, "cw"), w=(acck,))
                act(out, accb, AF.Silu, r=(acck,), w=(outk,))

            def state_update(h, kT_blk, kTk, vaug, vk, wk_col, wkk, eB_col, eBk, kpp, kppk, refresh_bf):
                pt, ptk = getpsb()
                for blk in range(2):
                    tr(pt[:, blk * 128:(blk + 1) * 128], kT_blk(blk), r=(kTk[blk],), w=(ptk,))
                ts("dve", kpp, pt[:, 0:256], wk_col, None, ALU.mult, None, r=(ptk, wkk), w=(kppk,))
                for blk in range(2):
                    p_, pk = getps()
                    mm(p_[:, 0:257], kpp[:, blk * 128:(blk + 1) * 128], vaug, True, True, r=(kppk, vk), w=(pk,))
                    stt("dve", state[h][blk][:], state[h][blk][:], eB_col, p_[:, 0:257], ALU.mult, ALU.add,
                        r=(pk, eBk, ("st", h, blk)), w=(("st", h, blk),))
                    if refresh_bf:
                        cast(ctbf[h][blk][:], state[h][blk][:], r=(("st", h, blk),), w=(("ct", h, blk),), eng="act")

            areset()
            WA = abf(16 * 2048).rearrange("p (k c) -> p k c", k=16)
            hTgA = [abf(16 * 512).rearrange("p (k c) -> p k c", k=16) for _ in range(2)]
            xn = abf(D)
            kT = abf(8 * 512).rearrange("p (b c) -> p b c", b=8)
            vaugs = [[abf(258)[:, 0:257] for _ in range(4)] for _ in range(4)]
            kpps = [abf(256) for _ in range(2)]
            stgs = [af32(2048) for _ in range(2)]
            xbufs = stgs
            kpre = af32(8 * 515).rearrange("p (b c) -> p b c", b=8)
            accb = af32(512)
            gtmp = af32(4 * 48).rearrange("p (i c) -> p i c", i=4)
            load_w(lambda k: WA[:, k, :], lambda k: w_in[k * 128:(k + 1) * 128, C_MK:C_MK + 2048], 16, stgs, "xt", "WA")
            memset("pool", kpre[:, :, 0:3], 0.0, w=tuple(("kpre", c) for c in range(8)))
            for i in range(4):
                for h in range(4):
                    memset("pool", vaugs[i][h][:, 256:257], 1.0, w=(("vaug", i, h),))
            NG = NPRE // 4
            fe_cnt = [0]

            def a_hT(g):
                return hT_halo if g == NG - 1 else hTgA[g % 2]

            def a_keys(g, i):
                return (("hTg", g % 2, i, 0), ("hTg", g % 2, i, 1))

            def a_fe(g, i):
                hT = a_hT(g)
                frontend(4 * g + i, lambda k: hT[:, k, i * 128:(i + 1) * 128], a_keys(g, i), xbufs, xn, fe_cnt[0])
                fe_cnt[0] += 1

            def a_gates1(g, i):
                tau = 4 * g + i
                hT = a_hT(g)
                gt = gtmp[:, i, :]
                pg, pgk = ps[2 + i % 2], ("ps", 2 + i % 2)
                for k in range(16):
                    mm(pg[:, 0:8], hT[:, k, i * 128:(i + 1) * 128], wif[:, k, :], k == 0, k == 15,
                       r=a_keys(g, i) + ("wif",), w=(pgk,))
                tt("dve", gt[:, 0:8], pg[:, 0:8], bif[:], ALU.add, r=(pgk, "bif"), w=(("g_gl", i),))
                ts("dve", gt[:, 8:12], gt[:, 0:4], pneg[:, tau:tau + 1], None, ALU.add, None, r=(("g_gl", i), "pneg"), w=(("g_ig", i),))
                act(gt[:, 12:16], gt[:, 4:8], AF.Exp, r=(("g_gl", i),), w=(("g_lf", i),), scale=-1.0)
                ts("dve", gt[:, 12:16], gt[:, 12:16], 1.0, None, ALU.add, None, r=(("g_lf", i),), w=(("g_lf", i),))
                act(gt[:, 12:16], gt[:, 12:16], AF.Ln, r=(("g_lf", i),), w=(("g_lf", i),))
                ts("dve", gt[:, 12:16], gt[:, 12:16], -1.0, None, ALU.mult, None, r=(("g_lf", i),), w=(("g_lf", i),))

            def a_gates2(g, i):
                gt = gtmp[:, i, :]
                pc, pck_ = psS[:, (i % 2) * 512:(i % 2) * 512 + 8], ("psS", i % 2)
                mm(pc[:, 0:4], tri[:], gt[:, 12:16], True, True, r=(("g_lf", i), "tri"), w=(pck_,))
                mm(pc[:, 4:8], ones[:], gt[:, 12:16], True, True, r=(("g_lf", i), "ones"), w=(pck_,))
                tt("dve", gt[:, 16:20], gt[:, 8:12], pc[:, 0:4], ALU.subtract, r=(("g_ig", i), pck_), w=(("g_t", i),))
                tt("dve", gt[:, 16:20], gt[:, 16:20], pc[:, 4:8], ALU.add, r=(("g_t", i), pck_), w=(("g_t", i),))
                act(gt[:, 20:24], gt[:, 16:20], AF.Exp, r=(("g_t", i),), w=(("g_wk", i),))
                ts("dve", gt[:, 20:24], gt[:, 20:24], 0.0625, None, ALU.mult, None, r=(("g_wk", i),), w=(("g_wk", i),))
                act(gt[:, 24:28], pc[:, 4:8], AF.Exp, r=(pck_,), w=(("g_eB", i),))

            def a_update(g, i):
                gt = gtmp[:, i, :]
                for h in range(4):
                    pt, ptk = getpsb()
                    for blk in range(2):
                        tr(pt[:, blk * 128:(blk + 1) * 128], kT[:, 2 * h + blk, i * 128:(i + 1) * 128],
                           r=(("kT", 2 * h + blk),), w=(ptk,))
                    kp, kpk = kpps[h % 2], ("kpp", h % 2)
                    ts("dve", kp, pt[:, 0:256], gt[:, 20 + h:21 + h], None, ALU.mult, None, r=(ptk, ("g_wk", i)), w=(kpk,))
                    for blk in range(2):
                        pk = ("psS", blk)
                        p_ = psS[:, blk * 512:blk * 512 + 257]
                        mm(p_, kp[:, blk * 128:(blk + 1) * 128], vaugs[i][h], True, True, r=(kpk, ("vaug", i, h)), w=(pk,))
                        stt("dve", state[h][blk][:], state[h][blk][:], gt[:, 24 + h:25 + h], p_, ALU.mult, ALU.add,
                            r=(pk, ("g_eB", i), ("st", h, blk)), w=(("st", h, blk),))

            for i in range(4):
                a_fe(0, i)
            for g in range(NG):
                hT = a_hT(g)
                hgk = tuple(k_ for i_ in range(4) for k_ in a_keys(g, i_))
                for cbk in range(8):
                    p_, pk = ps[cbk % 2], ("ps", cbk % 2)
                    for k in range(16):
                        mm(p_[:, 0:512], WA[:, k, cbk * 128:(cbk + 1) * 128], hT[:, k, :], k == 0, k == 15,
                           r=("WA",) + hgk, w=(pk,))
                    cast(kpre[:, cbk, 3:515], p_[:, 0:512], r=(pk,), w=(("kpre", cbk),), eng="act")
                    conv_silu(kpre[:, cbk, :], ("kpre", cbk), 8 + cbk, accb, "accb", kT[:, cbk, :], ("kT", cbk))
                    ts("dve", kpre[:, cbk, 0:3], kpre[:, cbk, 512:515], tval[:, 4 * g + 3:4 * g + 4], None, ALU.mult, None,
                       r=(("kpre", cbk), "tval"), w=(("kpre", cbk),))
                for i in range(4):
                    a_gates1(g, i)
                for i in range(4):
                    for half in range(2):
                        p_, pk = ps[2 + half], ("ps", 2 + half)
                        for k in range(16):
                            mm(p_[:, 0:512], hT[:, k, i * 128:(i + 1) * 128], WA[:, k, 1024 + half * 512:1024 + (half + 1) * 512],
                               k == 0, k == 15, r=("WA",) + a_keys(g, i), w=(pk,))
                        for hh in range(2):
                            h = 2 * half + hh
                            cast(vaugs[i][h][:, 0:256], p_[:, hh * 256:(hh + 1) * 256], r=(pk,), w=(("vaug", i, h),),
                                 eng=("act", "dve")[hh])
                for i in range(4):
                    a_gates2(g, i)
                for i in range(4):
                    if g + 1 < NG:
                        a_fe(g + 1, i)
                    a_update(g, i)
            S.barrier()
            stop("A", [(state[0][0][:], 257), (state[3][1][:], 257), (sm[:], 64)])

            areset()
            hT_own = abf(16 * 2048).rearrange("p (k c) -> p k c", k=16)
            mark_b = aoff[0]
            xn = abf(D)
            xbufs = [af32(D) for _ in range(2)]
            for i in range(NOWN):
                frontend(NPRE + i, lambda k, i=i: hT_own[:, k, i * 128:(i + 1) * 128], (("hTo", i, 0), ("hTo", i, 1)), xbufs, xn, i)
                gates(NPRE + i, lambda k, i=i: hT_own[:, k, i * 128:(i + 1) * 128], (("hTo", i, 0), ("hTo", i, 1)), i)
            for h in range(4):
                for b in range(2):
                    cast(ctbf[h][b][:], state[h][b][:], r=(("st", h, b),), w=(("ct", h, b),), eng="act")
            S.barrier()
            stop("B0", [(wk_o[:].rearrange("p a b -> p (a b)"), 64), (wa_o[:].rearrange("p a b -> p (a b)"), 64), (eB_o[:].rearrange("p a b -> p (a b)"), 64), (ebc_o[:].rearrange("p a b -> p (a b)"), 64), (state[0][0][:], 257)])

            aoff[0] = mark_b
            WG = abf(16 * 1280).rearrange("p (k c) -> p k c", k=16)
            qkT = [abf(4 * 512).rearrange("p (b c) -> p b c", b=4) for _ in range(2)]
            vaug1 = [abf(258)[:, 0:257] for _ in range(2)]
            kpps = [abf(256) for _ in range(2)]
            STb = [abf(128) for _ in range(2)]
            ybf = [abf(256) for _ in range(2)]
            yTt = [abf(256).rearrange("p (b c) -> p b c", b=2) for _ in range(2)]
            stg1 = [af32(1280) for _ in range(2)]
            qkpre = af32(4 * 515).rearrange("p (b c) -> p b c", b=4)
            accb = af32(512)
            sgo = af32(256)
            slz = af32(256)
            Gt = [af32(256) for _ in range(2)]
            junk = af32(256)
            w5 = w_in[:, 0:5120].rearrange("r (s c) -> r s c", c=1024)
            for vb in range(2):
                memset("pool", vaug1[vb][:, 256:257], 1.0, w=(("vaug1", vb),))
            ts("dve", mhg[:], mhg[:], 0.5, None, ALU.mult, None, r=("mhg",), w=("mhg",))
            PS0, PS1, PS2, PS3 = (ps[0], ("ps", 0)), (ps[1], ("ps", 1)), (ps[2], ("ps", 2)), (ps[3], ("ps", 3))
            for h in range(4):
                load_w(lambda k: WG[:, k, :].rearrange("p (s c) -> p s c", c=256),
                       lambda k, h=h: w5[k * 128:(k + 1) * 128, :, h * 256:(h + 1) * 256],
                       16, [s_.rearrange("p (s c) -> p s c", c=256) for s_ in stg1], "stg1", "WG")
                for blk in range(4):
                    p_, pk = (PS0, PS1)[blk % 2]
                    for k in range(16):
                        mm(p_[:, 0:3], WG[:, k, blk * 128:(blk + 1) * 128], hT_halo[:, k, 509:512], k == 0, k == 15,
                           r=("WG",), w=(pk,))
                    ts("dve", qkpre[:, blk, 0:3], p_[:, 0:3], tval[:, NPRE - 1:NPRE], None, ALU.mult, None,
                       r=(pk, "tval"), w=(("qkpre", blk),))

                def b1_proj(g, h=h):
                    qk = qkT[g % 2]
                    for blk in range(4):
                        p_, pk = (PS0, PS1)[blk % 2]
                        for k in range(16):
                            mm(p_[:, 0:512], WG[:, k, blk * 128:(blk + 1) * 128], hT_own[:, k, g * 512:(g + 1) * 512],
                               k == 0, k == 15, r=("WG",), w=(pk,))
                        cast(qkpre[:, blk, 3:515], p_[:, 0:512], r=(pk,), w=(("qkpre", blk),), eng="act")
                        cidx = (2 * h + blk) if blk < 2 else (8 + 2 * h + blk - 2)
                        conv_silu(qkpre[:, blk, :], ("qkpre", blk), cidx, accb, "accb", qk[:, blk, :], ("qkT", g % 2, blk))
                        cast(qkpre[:, blk, 0:3], qkpre[:, blk, 512:515], r=(("qkpre", blk),), w=(("qkpre", blk),), eng="dve")

                def b1_P(t, h=h):
                    vb = t % 2
                    pA, pAk = PS0
                    for k in range(16):
                        mm(pA[:, 0:512], hT_own[:, k, t * 128:(t + 1) * 128], WG[:, k, 512:1024], k == 0, k == 15,
                           r=("WG",), w=(pAk,))
                    pB, pBk = PS1
                    for k in range(16):
                        mm(pB[:, 0:256], hT_own[:, k, t * 128:(t + 1) * 128], WG[:, k, 1024:1280], k == 0, k == 15,
                           r=("WG",), w=(pBk,))
                    cast(vaug1[vb][:, 0:256], pA[:, 0:256], r=(pAk,), w=(("vaug1", vb),), eng="act")
                    act(sgo, pA[:, 256:512], AF.Tanh, r=(pAk,), w=("sgo",), scale=0.5)
                    act(slz, pB[:, 0:256], AF.Silu, r=(pBk,), w=("slz",))
                    stt("dve", Gt[vb], sgo, 1.0, slz, ALU.add, ALU.mult, r=("sgo", "slz"), w=(("Gt", vb),))
                    tt("dve", Gt[vb], Gt[vb], mhg[:, h * 256:(h + 1) * 256], ALU.mult, r=(("Gt", vb), "mhg"), w=(("Gt", vb),))

                def b1_S(t, h=h):
                    vb = t % 2
                    gp = (t // 4) % 2
                    tsl = slice((t % 4) * 128, (t % 4 + 1) * 128)
                    pS, pSk = PS2
                    for blk in range(2):
                        mm(pS[:, 0:128], qkT[gp][:, 2 + blk, tsl], qkT[gp][:, blk, tsl], blk == 0, blk == 1,
                           r=(("qkT", gp, blk), ("qkT", gp, 2 + blk)), w=(pSk,))
                    stt("dve", STb[vb], pS[:, 0:128], wa_o[:, t, h:h + 1], tri[:], ALU.mult, ALU.mult,
                        r=(pSk, ("gown", t), "tri"), w=(("STb", vb),))

                def b1_N(t, h=h):
                    vb = t % 2
                    gp = (t // 4) % 2
                    tsl = slice((t % 4) * 128, (t % 4 + 1) * 128)
                    pN, pNk = PS3
                    mm(pN[:, 0:257], STb[vb], vaug1[vb], True, False, r=(("STb", vb), ("vaug1", vb)), w=(pNk,))
                    for blk in range(2):
                        mm(pN[:, 0:257], qkT[gp][:, blk, tsl], ctbf[h][blk][:], False, blk == 1,
                           r=(("qkT", gp, blk), ("ct", h, blk)), w=(pNk,))

                def b1_Utr(t, h=h):
                    vb = t % 2
                    gp = (t // 4) % 2
                    tsl = slice((t % 4) * 128, (t % 4 + 1) * 128)
                    pt, ptk = psb[0], ("psb", 0)
                    for blk in range(2):
                        tr(pt[:, blk * 128:(blk + 1) * 128], qkT[gp][:, 2 + blk, tsl], r=(("qkT", gp, 2 + blk),), w=(ptk,))
                    ts("dve", kpps[vb], pt[:, 0:256], wk_o[:, t, h:h + 1], None, ALU.mult, None,
                       r=(ptk, ("gown", t)), w=(("kpp", vb),))

                def b1_Umm(t, h=h):
                    vb = t % 2
                    for blk in range(2):
                        pk = ("psS", blk)
                        p_ = psS[:, blk * 512:blk * 512 + 257]
                        mm(p_, kpps[vb][:, blk * 128:(blk + 1) * 128], vaug1[vb], True, True,
                           r=(("kpp", vb), ("vaug1", vb)), w=(pk,))
                        stt("dve", state[h][blk][:], state[h][blk][:], eB_o[:, t, h:h + 1], p_, ALU.mult, ALU.add,
                            r=(pk, ("gown", t), ("st", h, blk)), w=(("st", h, blk),))
                        cast(ctbf[h][blk][:], state[h][blk][:], r=(("st", h, blk),), w=(("ct", h, blk),), eng="act")

                def b1_Ychain(t, h=h):
                    vb = t % 2
                    pN, pNk = PS3
                    gk = ("gown", t)
                    tt("dve", sm[:, 44:45], pN[:, 256:257], ebc_o[:, t, h:h + 1], ALU.mult, r=(pNk, gk), w=("d1",))
                    ts("dve", sm[:, 54:55], sm[:, 44:45], -1.0, None, ALU.mult, None, r=("d1",), w=("d1n",))
                    tt("dve", sm[:, 44:45], sm[:, 44:45], sm[:, 54:55], ALU.max, r=("d1", "d1n"), w=("d1",))
                    ts("dve", sm[:, 44:45], sm[:, 44:45], 1.0, 1.0, ALU.max, ALU.mult, r=("d1",), w=("d1",))
                    S.add("dve", lambda e: e.reciprocal(sm[:, 44:45], sm[:, 44:45]), r=("d1",), w=("d1",))
                    tt("dve", sm[:, 45:46], ebc_o[:, t, h:h + 1], sm[:, 44:45], ALU.mult, r=("d1", gk), w=("rr",))
                    memset("dve", sm[:, 46:47], 0.0, w=("ss2",))
                    act(junk, pN[:, 0:256], AF.Square, r=(pNk, "rr", "ss2"), w=("junk", "ss2"), scale=sm[:, 45:46],
                        accum=sm[:, 46:47])
                    ts("dve", sm[:, 47:48], sm[:, 46:47], 1.0 / 256, EPS, ALU.mult, ALU.add, r=("ss2",), w=("r2",))
                    rsqrt_col(sm[:, 47:48], "r2")
                    tt("dve", sm[:, 47:48], sm[:, 47:48], sm[:, 45:46], ALU.mult, r=("r2", "rr"), w=("r2",))
                    stt("dve", ybf[vb], pN[:, 0:256], sm[:, 47:48], Gt[vb], ALU.mult, ALU.mult,
                        r=(pNk, "r2", ("Gt", vb)), w=(("ybf", vb),))

                def b1_Ytr(t, h=h):
                    vb = t % 2
                    pt, ptk = psb[1], ("psb", 1)
                    for blk in range(2):
                        tr(pt[:, blk * 128:(blk + 1) * 128], ybf[vb][:, blk * 128:(blk + 1) * 128], r=(("ybf", vb),), w=(ptk,))
                    cast(yTt[vb].rearrange("p b c -> p (b c)"), pt[:, 0:256], r=(ptk,), w=(("yTt", vb),), eng="act")
                    dma(dmaq(), yT_d[2 * h:2 * h + 2, :, t * 128:(t + 1) * 128].rearrange("b p t -> p b t"), yTt[vb],
                        r=(("yTt", vb),), w=(("yTd", 2 * h, t),))

                b1_proj(0)
                b1_P(0)
                b1_S(0)
                for t in range(NOWN):
                    if t + 1 < NOWN:
                        if (t + 1) % 4 == 0:
                            b1_proj((t + 1) // 4)
                        b1_P(t + 1)
                        b1_S(t + 1)
                    b1_N(t)
                    b1_Utr(t)
                    if t > 0:
                        b1_Ytr(t - 1)
                    b1_Umm(t)
                    b1_Ychain(t)
                b1_Ytr(NOWN - 1)
            S.barrier()
            stop("B1", [(state[0][0][:], 257)])

            aoff[0] = mark_b
            WG2 = abf(16 * 512).rearrange("p (k c) -> p k c", k=16)
            akT = abf(2560)
            aqT = abf(2048)
            Vt = abf(20 * 128).rearrange("p (t c) -> p t c", t=20)
            slzA = abf(16 * 128).rearrange("p (t c) -> p t c", t=16)
            Pb = [abf(640) for _ in range(2)]
            PT = [abf(640) for _ in range(2)]
            yab = [abf(128) for _ in range(2)]
            yaT = [abf(128) for _ in range(2)]
            stg2 = [af32(512) for _ in range(2)]
            rb = af32(640)
            bm = af32(640)
            sbuf_s = [af32(640) for _ in range(2)]
            wA = w_in[:, C_AQ:C_AQ + 4096].rearrange("r (s c) -> r s c", c=1024)
            for h in range(8):
                load_w(lambda k: WG2[:, k, :].rearrange("p (s c) -> p s c", c=128),
                       lambda k, h=h: wA[k * 128:(k + 1) * 128, :, h * 128:(h + 1) * 128],
                       16, [s_.rearrange("p (s c) -> p s c", c=128) for s_ in stg2], "stg2", "WG2")
                dma("sp", rb, relmat[h], r=(), w=("rb",))
                tt("pool", bm, rb, amask_s[:], ALU.add, r=("rb", "amask"), w=("bm",))
                for g5 in range(5):
                    src = hT_halo if g5 == 0 else hT_own[:, :, (g5 - 1) * 512:g5 * 512]
                    srk = ()
                    p_, pk = getps()
                    for k in range(16):
                        mm(p_[:, 0:512], WG2[:, k, 128:256], src[:, k, :], k == 0, k == 15, r=("WG2",) + srk, w=(pk,))
                    cast(akT[:, g5 * 512:(g5 + 1) * 512], p_[:, 0:512], r=(pk,), w=(("akT", g5),), eng="act")
                    if g5 > 0:
                        p_, pk = getps()
                        for k in range(16):
                            mm(p_[:, 0:512], WG2[:, k, 0:128], src[:, k, :], k == 0, k == 15, r=("WG2",) + srk, w=(pk,))
                        act(aqT[:, (g5 - 1) * 512:g5 * 512], p_[:, 0:512], AF.Copy, r=(pk,), w=(("aqT", g5 - 1),),
                            scale=float(128 ** -0.5))
                    for i in range(4):
                        ttile = g5 * 4 + i
                        p_, pk = getps()
                        for k in range(16):
                            mm(p_[:, 0:256], src[:, k, i * 128:(i + 1) * 128], WG2[:, k, 256:512], k == 0, k == 15,
                               r=("WG2",) + srk, w=(pk,))
                        cast(Vt[:, ttile, :], p_[:, 0:128], r=(pk,), w=(("Vt", ttile),), eng="act")
                        if g5 > 0:
                            act(slzA[:, ttile - 4, :], p_[:, 128:256], AF.Silu, r=(pk,), w=(("slzA", ttile - 4),))
                def att_s1(t, h=h):
                    vb = t % 2
                    gq = ("aqT", t // 4)
                    kk0 = tuple(("akT", x) for x in sorted({t // 4, (t + 3) // 4, (t + 4) // 4}))
                    mm(psS[:, 0:512], aqT[:, t * 128:(t + 1) * 128], akT[:, t * 128:t * 128 + 512], True, True,
                       r=(gq,) + kk0, w=("psS",))
                    mm(psS[:, 512:640], aqT[:, t * 128:(t + 1) * 128], akT[:, t * 128 + 512:t * 128 + 640], True, True,
                       r=(gq,) + kk0, w=("psS",))
                    sbt = sbuf_s[vb]
                    sk = ("sbt", vb)
                    tt("dve", sbt, psS[:, 0:640], bm, ALU.add, r=("psS", "bm"), w=(sk,))
                    for kb in range(max(0, 4 - t)):
                        lt = NPRE - 4 + t + kb
                        ts("dve", sbt[:, kb * 128:(kb + 1) * 128], sbt[:, kb * 128:(kb + 1) * 128], ntile[:, lt:lt + 1], None,
                           ALU.add, None, r=(sk, "ntile"), w=(sk,))
                    S.add("dve", lambda e, sbt=sbt: e.reduce_max(sm[:, 48:49], sbt, AX.X), r=(sk,), w=("mx",))
                    ts("dve", sm[:, 48:49], sm[:, 48:49], -1.0, None, ALU.mult, None, r=("mx",), w=("mx",))
                    rsc = sm[:, 56 + vb:57 + vb]
                    memset("dve", rsc, 0.0, w=(("rsum", vb),))
                    act(Pb[vb], sbt, AF.Exp, r=(sk, "mx", ("rsum", vb)), w=(("Pb", vb), ("rsum", vb)), bias=sm[:, 48:49],
                        accum=rsc)

                def att_s2(t, h=h):
                    vb = t % 2
                    rsc = sm[:, 56 + vb:57 + vb]
                    rrc = sm[:, 58 + vb:59 + vb]
                    pt, ptk = getpsb()
                    for kb in range(5):
                        tr(pt[:, kb * 128:(kb + 1) * 128], Pb[vb][:, kb * 128:(kb + 1) * 128], r=(("Pb", vb),), w=(ptk,))
                    cast(PT[vb], pt[:, 0:640], r=(ptk,), w=(("PT", vb),), eng="act")
                    pO, pOk = getps()
                    for kb in range(5):
                        mm(pO[:, 0:128], PT[vb][:, kb * 128:(kb + 1) * 128], Vt[:, t + kb, :], kb == 0, kb == 4,
                           r=(("PT", vb), ("Vt", t + kb)), w=(pOk,))
                    S.add("dve", lambda e: e.reciprocal(rrc, rsc), r=(("rsum", vb),), w=(("rrs", vb),))
                    stt("dve", yab[vb], pO[:, 0:128], rrc, slzA[:, t, :], ALU.mult, ALU.mult,
                        r=(pOk, ("rrs", vb), ("slzA", t)), w=(("yab", vb),))
                    pt2, pt2k = getpsb()
                    tr(pt2[:, 0:128], yab[vb], r=(("yab", vb),), w=(pt2k,))
                    cast(yaT[vb], pt2[:, 0:128], r=(pt2k,), w=(("yaT", vb),), eng="act")
                    dma(dmaq(), yT_d[8 + h, :, t * 128:(t + 1) * 128], yaT[vb], r=(("yaT", vb),), w=(("yTd", 8 + h, t),))

                att_s1(0)
                for t in range(NOWN):
                    if t + 1 < NOWN:
                        att_s1(t + 1)
                    att_s2(t)
            S.barrier()
            stop("B2", [(sm[:], 64)])

            aoff[0] = mark_b
            yT_res = abf(16 * 2048).rearrange("p (b c) -> p b c", b=16)
            hh_flat = hT_halo[:].rearrange("p k c -> p (k c)")
            wgm = [abf(16 * 128).rearrange("p (k c) -> p k c", k=16), hh_flat[:, 0:2048].rearrange("p (k c) -> p k c", k=16)]
            wga = [abf(16 * 128).rearrange("p (k c) -> p k c", k=16), hh_flat[:, 2048:4096].rearrange("p (k c) -> p k c", k=16)]
            wpm = [abf(8 * 128).rearrange("p (k c) -> p k c", k=8), hh_flat[:, 4096:5120].rearrange("p (k c) -> p k c", k=8)]
            wpa = [abf(8 * 128).rearrange("p (k c) -> p k c", k=8), hh_flat[:, 5120:6144].rearrange("p (k c) -> p k c", k=8)]
            mTb = [abf(512) for _ in range(2)]
            stg3 = [af32(8 * 128).rearrange("p (k c) -> p k c", k=8) for _ in range(2)]
            sgm = af32(512)
            sga = af32(512)
            t1b = af32(512)
            for b in range(16):
                dma(dmaq(), yT_res[:, b, :], yT_d[b], r=(), w=("yT_res",))
            si3 = [0]

            def b3_load(c):
                cbuf = c % 2
                for (dst, src, nk, nm) in ((wgm, w_in[:, C_GM + c * 128:C_GM + (c + 1) * 128], 16, "wgm"),
                                           (wga, w_in[:, C_GA + c * 128:C_GA + (c + 1) * 128], 16, "wga"),
                                           (wpm, w_pm[:, c * 128:(c + 1) * 128], 8, "wpm"),
                                           (wpa, w_pa[:, c * 128:(c + 1) * 128], 8, "wpa")):
                    for k8 in range(nk // 8):
                        st_ = stg3[si3[0] % 2]
                        sk = ("stg3", si3[0] % 2)
                        si3[0] += 1
                        for k4 in range(2):
                            r0 = (k8 * 8 + k4 * 4) * 128
                            dma(dmaq(), st_[:, 4 * k4:4 * k4 + 4, :],
                                src[r0:r0 + 512, :].rearrange("(k p) c -> p k c", p=128), r=(), w=(sk,))
                        cast(dst[cbuf][:, k8 * 8:(k8 + 1) * 8, :], st_[:], r=(sk,), w=((nm, cbuf),))

            b3_load(0)
            for c in range(16):
                cbuf = c % 2
                if c + 1 < 16:
                    b3_load(c + 1)
                for g in range(4):
                    gsl = slice(g * 512, (g + 1) * 512)
                    p1, p1k = getps()
                    for k in range(16):
                        mm(p1[:, 0:512], wgm[cbuf][:, k, :], hT_own[:, k, gsl], k == 0, k == 15, r=(("wgm", cbuf),), w=(p1k,))
                    act(sgm, p1[:, 0:512], AF.Sigmoid, r=(p1k,), w=("sgm",))
                    p2, p2k = getps()
                    for k in range(16):
                        mm(p2[:, 0:512], wga[cbuf][:, k, :], hT_own[:, k, gsl], k == 0, k == 15, r=(("wga", cbuf),), w=(p2k,))
                    act(sga, p2[:, 0:512], AF.Sigmoid, r=(p2k,), w=("sga",))
                    p3, p3k = getps()
                    for k in range(8):
                        mm(p3[:, 0:512], wpm[cbuf][:, k, :], yT_res[:, k, gsl], k == 0, k == 7,
                           r=(("wpm", cbuf), "yT_res"), w=(p3k,))
                    tt("dve", t1b, sgm, p3[:, 0:512], ALU.mult, r=("sgm", p3k), w=("t1b",))
                    p4, p4k = getps()
                    for k in range(8):
                        mm(p4[:, 0:512], wpa[cbuf][:, k, :], yT_res[:, 8 + k, gsl], k == 0, k == 7,
                           r=(("wpa", cbuf), "yT_res"), w=(p4k,))
                    tt("dve", sga, sga, p4[:, 0:512], ALU.mult, r=("sga", p4k), w=("sga",))
                    mb = mTb[(c * 4 + g) % 2]
                    mk_ = ("mTb", (c * 4 + g) % 2)
                    tt("dve", mb, t1b, sga, ALU.add, r=("t1b", "sga"), w=(mk_,))
                    dma(dmaq(), mT_d[c, :, gsl], mb, r=(mk_,), w=(("mTd", c, g),))
            S.barrier()
            stop("B3", [(sm[:], 64)])

            areset()
            Wout = abf(16 * 2048).rearrange("p (k c) -> p k c", k=16)
            mTt = [abf(16 * 128).rearrange("p (c t) -> p c t", c=16) for _ in range(2)]
            fgb = af32(D)
            stgC = [af32(D) for _ in range(2)]
            xts = [af32(D) for _ in range(2)]
            obuf = [af32(D) for _ in range(2)]
            junkC = af32(D)
            dma("sp", fgb, fg_bc, r=(), w=("fgb",))
            for k in range(16):
                st_ = stgC[k % 2]
                sk = ("stgC", k % 2)
                dma(dmaq(), st_, w_out[k * 128:(k + 1) * 128, :], r=(), w=(sk,))
                tt(("dve", "pool")[k % 2], Wout[:, k, :], st_, gate_bc[:], ALU.mult, r=(sk, "gate_bc"), w=("Wout",))
            outkeys = []
            for t in range(NOWN):
                vb = t % 2
                for c4 in range(4):
                    dma(dmaq(), mTt[vb][:, 4 * c4:4 * c4 + 4, :],
                        mT_d[4 * c4:4 * c4 + 4, :, t * 128:(t + 1) * 128].rearrange("c p t -> p c t"), r=(), w=(("mTt", vb),))
                dma(dmaq(), xts[vb], xl[(NPRE + t) * 128:(NPRE + t + 1) * 128, :], r=(), w=(("xts", vb),))
                ob = obuf[vb]
                ok = ("ob", vb)
                for cbk in range(4):
                    p_, pk = getps()
                    for c in range(16):
                        mm(p_[:, 0:512], mTt[vb][:, c, :], Wout[:, c, cbk * 512:(cbk + 1) * 512], c == 0, c == 15,
                           r=(("mTt", vb), "Wout"), w=(pk,))
                    tt("dve", ob[:, cbk * 512:(cbk + 1) * 512], p_[:, 0:512], xts[vb][:, cbk * 512:(cbk + 1) * 512], ALU.add,
                       r=(pk, ("xts", vb)), w=(ok,))
                memset("dve", sm[:, 52:53], 0.0, w=("ssC",))
                act(junkC, ob, AF.Square, r=(ok, "ssC"), w=("junkC", "ssC"), accum=sm[:, 52:53])
                ts("dve", sm[:, 53:54], sm[:, 52:53], 1.0 / D, EPS, ALU.mult, ALU.add, r=("ssC",), w=("rC",))
                rsqrt_col(sm[:, 53:54], "rC")
                stt("dve", ob, ob, sm[:, 53:54], fgb, ALU.mult, ALU.mult, r=(ok, "rC", "fgb"), w=(ok,))
                dma(dmaq(), y_out[t * 128:(t + 1) * 128, :], ob, r=(ok,), w=(("yout", t),))
                outkeys.append(("yout", t))
        try:
            body()
        except _Stop:
            pass
        S.finish(())
        S.emit()
    return nc


_NC_CACHE = {}


def _consts():
    ident = np.eye(128, dtype=np.float32).astype(ml_dtypes.bfloat16)
    s = np.arange(128)[:, None]
    t = np.arange(128)[None, :]
    tri = (s <= t).astype(np.float32)
    ones = np.ones((128, 128), np.float32)
    q = np.arange(128)[:, None]
    kap = np.arange(640)[None, :]
    cq, ck = q // 64, kap // 64
    allowed = (ck >= cq) & (ck <= cq + 8)
    amask = np.where(allowed, 0.0, NEG).astype(np.float32)
    relidx = np.clip(q + 512 - kap, -63, 128) + 63
    return ident, tri, ones, tri.copy(), amask, relidx


def kernel(x, c, w_ada, b_ada, norm_g, w_in, b_if, conv_w, conv_b, mh_norm_g, rel_bias,
           w_proj_m, w_proj_a, w_out, final_norm_g):
    f = np.float32
    x = np.asarray(x, f)
    ident, tri, ones, mst, amask, relidx = _consts()
    if "nc" not in _NC_CACHE:
        _NC_CACHE["nc"] = build_nc()
    nc = _NC_CACHE["nc"]
    rep = lambda v, n=128: np.ascontiguousarray(np.broadcast_to(np.asarray(v, f).reshape(1, -1), (n, np.asarray(v).size)))
    colT = lambda v: np.ascontiguousarray(np.asarray(v, f).reshape(16, 128).T)
    cwl = np.ascontiguousarray(np.asarray(conv_w[0], f).T.reshape(16, 128, 4).transpose(1, 0, 2))
    shared = {
        "w_ada": np.ascontiguousarray(w_ada[0], f), "b_ada": np.ascontiguousarray(b_ada[0], f).reshape(1, -1),
        "ngT": colT(norm_g[0]), "w_in": np.ascontiguousarray(w_in[0], f), "bif_bc": rep(b_if[0]),
        "convw": cwl, "convb": colT(conv_b[0]), "mhg_bc": rep(mh_norm_g[0]),
        "relmat": np.ascontiguousarray(np.asarray(rel_bias[0], f)[:, relidx]), "amask": amask,
        "w_pm": np.ascontiguousarray(w_proj_m[0], f), "w_pa": np.ascontiguousarray(w_proj_a[0], f),
        "w_out": np.ascontiguousarray(w_out[0], f), "fg_bc": rep(final_norm_g),
        "ident": ident, "tri": tri, "ones": ones, "maskst": mst,
    }
    in_maps = []
    for core in range(8):
        b, j = core // 4, core % 4
        npad = (3 - j) * 16
        xl = np.zeros((NT * 128, D), f)
        xl[npad * 128:] = x[b, 0:(j + 1) * 2048]
        valid = (np.arange(NT) >= npad).astype(f)
        m = dict(shared)
        m["xl"] = xl
        m["cT"] = colT(c[b])
        m["padneg"] = rep(np.where(valid > 0, 0.0, NEG))
        m["tilevalid"] = rep(valid)
        m["negtile"] = rep(np.where(valid > 0, 0.0, NEG))
        in_maps.append(m)
    res = run_bass_kernel_spmd(nc, in_maps, core_ids=list(range(8)))
    out = np.empty((2, 8192, D), f)
    for core in range(8):
        b, j = core // 4, core % 4
        out[b, j * 2048:(j + 1) * 2048] = res.results[core]["y"]
    return out
```

```python
import os
import numpy as np
import ml_dtypes
from contextlib import ExitStack
import concourse.bass as bass
import concourse.mybir as mybir
from concourse.bass_utils import run_bass_kernel_spmd

F32 = mybir.dt.float32
BF16 = mybir.dt.bfloat16
ALU = mybir.AluOpType
AF = mybir.ActivationFunctionType
AX = mybir.AxisListType

D = 2048
NT = 64
NPRE = 48
NOWN = 16
EPS = 1e-6
C_MQ, C_MK, C_MV, C_MO, C_MZ, C_MI, C_MF = 0, 1024, 2048, 3072, 4096, 5120, 5124
C_AQ, C_AK, C_AV, C_AZ, C_GM, C_GA = 5128, 6152, 7176, 8200, 9224, 11272
IN_COLS = 13320
NEG = -30000.0
ENGS = ("sp", "act", "dve", "pool", "pe")
KSTOP = os.environ.get("KSTOP")


class _Stop(Exception):
    pass


class Op:
    __slots__ = ("eng", "fn", "deps", "dma", "sig", "sem", "val")

    def __init__(self, eng, fn, dma):
        self.eng, self.fn, self.dma = eng, fn, dma
        self.deps, self.sig, self.sem, self.val = [], dma, None, 0


class Sched:
    ND = 40

    def __init__(self, nc, es):
        self.nc = nc
        self.eops = {e: [] for e in ENGS}
        self.lastw, self.readers = {}, {}
        self.csem = {e: es.enter_context(nc.semaphore("cs_" + e)) for e in ENGS}
        self.dsem = [es.enter_context(nc.semaphore("ds%d" % i)) for i in range(self.ND)]
        self.dlast = [None] * self.ND
        self.duse = [0] * self.ND
        self.dn = 0

    @staticmethod
    def _is_psum(k):
        return k == "psS" or (isinstance(k, tuple) and len(k) > 0 and k[0] in ("ps", "psb", "psS"))

    def add(self, eng, fn, r=(), w=(), dma=False):
        w = tuple(w) + tuple(k for k in r if self._is_psum(k))
        r = tuple(k for k in r if not self._is_psum(k))
        op = Op(eng, fn, dma)
        deps = []
        for k in r:
            if k in self.lastw:
                deps.append(self.lastw[k])
        for k in w:
            if k in self.lastw:
                deps.append(self.lastw[k])
            deps.extend(self.readers.get(k, ()))
        if dma:
            i = self.dn % self.ND
            self.dn += 1
            if self.dlast[i] is not None:
                deps.append(self.dlast[i])
            self.duse[i] += 1
            op.sem, op.val = self.dsem[i], 16 * self.duse[i]
            self.dlast[i] = op
        seen = set()
        for d in deps:
            if d is op or id(d) in seen:
                continue
            seen.add(id(d))
            if d.eng == "pe" and eng == "pe" and not d.dma:
                continue
            d.sig = True
            op.deps.append(d)
        for k in r:
            lst = self.readers.setdefault(k, [])
            if not dma:
                lst[:] = [o_ for o_ in lst if o_.dma or o_.eng != eng]
            lst.append(op)
        for k in w:
            self.lastw[k] = op
            self.readers[k] = []
        self.eops[eng].append(op)
        return op

    def barrier(self):
        lasts = []
        for e in ENGS:
            for o in reversed(self.eops[e]):
                if not o.dma and o.fn is not None:
                    lasts.append(o)
                    break
        pend = [d for d in self.dlast if d is not None]
        for e in ENGS:
            op = Op(e, None, False)
            for d in lasts + pend:
                if d.eng == e and not d.dma:
                    continue
                d.sig = True
                op.deps.append(d)
            self.eops[e].append(op)
        self.lastw, self.readers = {}, {}

    def finish(self, keys):
        op = Op("sp", None, False)
        for d in self.dlast:
            if d is not None:
                op.deps.append(d)
        self.eops["sp"].append(op)

    def emit(self):
        nc = self.nc
        for e in ENGS:
            c = 0
            for o in self.eops[e]:
                if o.dma or o.fn is None:
                    continue
                if o.sig:
                    c += 1
                    o.sem, o.val = self.csem[e], c

        def run(ename, eng):
            known = {}
            for o in self.eops[ename]:
                for d in o.deps:
                    key = id(d.sem)
                    if known.get(key, 0) >= d.val:
                        continue
                    eng.wait_ge(d.sem, d.val)
                    known[key] = d.val
                if o.fn is None:
                    continue
                ins = o.fn(eng)
                if o.sig:
                    ins.then_inc(o.sem, 16 if o.dma else 1)

        with nc.Block() as block:
            @block.sync
            def _(e):
                run("sp", e)

            @block.scalar
            def _(e):
                run("act", e)

            @block.vector
            def _(e):
                run("dve", e)

            @block.gpsimd
            def _(e):
                run("pool", e)

            @block.tensor
            def _(e):
                run("pe", e)


def build_nc():
    nc = bass.Bass("TRN2", target_bir_lowering=False)

    def din(name, shape, dt=F32):
        return nc.dram_tensor(name, list(shape), dt, kind="ExternalInput").ap()

    xl = din("xl", [NT * 128, D])
    cT = din("cT", [128, 16])
    w_ada = din("w_ada", [D, 3 * D])
    b_ada = din("b_ada", [1, 3 * D])
    ngT = din("ngT", [128, 16])
    w_in = din("w_in", [D, IN_COLS])
    bif_bc = din("bif_bc", [128, 8])
    convw = din("convw", [128, 16, 4])
    convb = din("convb", [128, 16])
    mhg_bc = din("mhg_bc", [128, 1024])
    relmat = din("relmat", [8, 128, 640])
    amask = din("amask", [128, 640])
    w_pm = din("w_pm", [1024, D])
    w_pa = din("w_pa", [1024, D])
    w_out = din("w_out", [D, D])
    fg_bc = din("fg_bc", [128, D])
    padneg = din("padneg", [128, NT])
    tilevalid = din("tilevalid", [128, NT])
    negtile = din("negtile", [128, NT])
    ident_d = din("ident", [128, 128], BF16)
    tri_d = din("tri", [128, 128])
    ones_d = din("ones", [128, 128])
    mst_d = din("maskst", [128, 128])
    y_out = nc.dram_tensor("y", [NOWN * 128, D], F32, kind="ExternalOutput").ap()
    dbg = nc.dram_tensor("dbg", [128, 8192], F32, kind="ExternalOutput").ap() if KSTOP else None
    yT_d = nc.dram_tensor("yT_d", [16, 128, NOWN * 128], BF16, kind="Internal").ap()
    mT_d = nc.dram_tensor("mT_d", [16, 128, NOWN * 128], BF16, kind="Internal").ap()

    es = ExitStack()
    with es, nc.allow_low_precision("bf16 matmul operands, fp32 accumulation"), \
            nc.allow_non_contiguous_dma("column-sliced weight loads"):
        S = Sched(nc, es)

        def sb(name, shape, dt=F32):
            return es.enter_context(nc.sbuf_tensor("s_" + name, list(shape), dt))

        def pst(name, shape, dt=F32):
            return es.enter_context(nc.psum_tensor(name, list(shape), dt))

        ident = sb("ident", [128, 128], BF16)
        tri = sb("tri", [128, 128])
        ones = sb("ones", [128, 128])
        gs = sb("gs", [128, 16])
        shift = sb("shift", [128, 16])
        ngt_s = sb("ngt_s", [128, 16])
        gate_bc = sb("gate_bc", [128, D])
        cw = sb("cw", [128, 16, 4])
        cb_ = sb("cb_", [128, 16])
        bif = sb("bif", [128, 8])
        mhg = sb("mhg", [128, 1024])
        pneg = sb("pneg", [128, NT])
        tval = sb("tval", [128, NT])
        ntile = sb("ntile", [128, NT])
        amask_s = sb("amask_s", [128, 640])
        wif = sb("wif", [128, 16, 8], BF16)
        wifs = sb("wifs", [128, 16, 8])
        state = [[sb("st%d%d" % (h, b), [128, 257]) for b in range(2)] for h in range(4)]
        ctbf = [[sb("ct%d%d" % (h, b), [128, 257], BF16) for b in range(2)] for h in range(4)]
        hT_halo = sb("hT_halo", [128, 16, 512], BF16)
        wk_o = sb("wk_o", [128, NOWN, 4])
        wa_o = sb("wa_o", [128, NOWN, 4])
        eB_o = sb("eB_o", [128, NOWN, 4])
        ebc_o = sb("ebc_o", [128, NOWN, 4])
        sm = sb("sm", [128, 64])
        ARENA_COLS = 40000
        arena = sb("arena", [128, ARENA_COLS])

        ps = [pst("ps%d" % i, [128, 512]) for i in range(4)]
        psS = pst("psS", [128, 1024])
        psb = [pst("psb%d" % i, [128, 1024], BF16) for i in range(2)]
        rot = {"ps": 0, "psb": 0, "cast": 0, "q": 0}

        def getps():
            i = rot["ps"] % 4
            rot["ps"] += 1
            return ps[i], ("ps", i)

        def getpsb():
            i = rot["psb"] % 2
            rot["psb"] += 1
            return psb[i], ("psb", i)

        aoff = [0]

        def areset():
            aoff[0] = 0

        def af32(cols):
            o = aoff[0]
            aoff[0] += cols
            assert aoff[0] <= ARENA_COLS, aoff[0]
            return arena[:, o:o + cols]

        def abf(cols):
            n = (cols + 1) // 2
            return af32(n).bitcast(BF16)[:, 0:cols]

        def dma(q, out, in_, r, w):
            S.add(q, lambda e, o=out, i=in_: e.dma_start(out=o, in_=i), r=r, w=w, dma=True)

        def dmaq():
            rot["q"] += 1
            return "sp" if rot["q"] % 2 else "pool"

        def act(out, in_, func, r, w, bias=None, scale=None, accum=None):
            kw = {}
            if bias is not None:
                kw["bias"] = bias
            if scale is not None:
                kw["scale"] = scale
            if accum is not None:
                kw["accum_out"] = accum
            S.add("act", lambda e: e.activation(out, in_, func, **kw), r=r, w=w)

        def ts(eng, out, in0, s1, s2, op0, op1, r, w):
            if s2 is None:
                s2, op1 = (1.0, ALU.mult) if op0 == ALU.add else (0.0, ALU.add)
            S.add(eng, lambda e: e.tensor_scalar(out, in0, s1, s2, op0, op1), r=r, w=w)

        def rsqrt_col(col, key):
            S.add("act", lambda e: e.activation(col, col, AF.Sqrt), r=(key,), w=(key,))
            S.add("dve", lambda e: e.reciprocal(col, col), r=(key,), w=(key,))

        def tt(eng, out, in0, in1, op, r, w):
            S.add(eng, lambda e: e.tensor_tensor(out, in0, in1, op), r=r, w=w)

        def stt(eng, out, in0, sc, in1, op0, op1, r, w):
            S.add(eng, lambda e: e.scalar_tensor_tensor(out, in0, sc, in1, op0, op1), r=r, w=w)

        def mm(out, lhsT, rhs, start, stop, r, w):
            S.add("pe", lambda e: e.matmul(out, lhsT, rhs, start=start, stop=stop), r=r, w=w)

        def tr(out, in_, r, w):
            S.add("pe", lambda e: e.transpose(out, in_, ident[:]), r=tuple(r) + ("ident",), w=w)

        def cast(out, in_, r, w, eng=None):
            if eng is None:
                rot["cast"] += 1
                eng = ("act", "pool")[rot["cast"] % 2]
            if eng == "act":
                S.add("act", lambda e: e.copy(out, in_), r=r, w=w)
            else:
                S.add(eng, lambda e: e.tensor_copy(out, in_), r=r, w=w)

        def memset(eng, ap, v, w):
            S.add(eng, lambda e: e.memset(ap, v), w=w)

        def stop(tag, dumps=()):
            if KSTOP != tag:
                return
            S.barrier()
            off = 0
            for ap_, n_ in dumps:
                dma("sp", dbg[:, off:off + n_], ap_, r=(), w=(("dbg", off),))
                off += n_
            raise _Stop()

        def body():
            for (t_, d_, k_) in ((ident, ident_d, "ident"), (tri, tri_d, "tri"), (ones, ones_d, "ones"),
                                 (ngt_s, ngT, "ngt"), (cw, convw, "cw"),
                                 (cb_, convb, "cb"), (bif, bif_bc, "bif"), (mhg, mhg_bc, "mhg"),
                                 (pneg, padneg, "pneg"), (tval, tilevalid, "tval"),
                                 (ntile, negtile, "ntile"), (amask_s, amask, "amask")):
                dma("sp", t_[:], d_, r=(), w=(k_,))
            for k4 in range(4):
                dma("sp", wifs[:, 4 * k4:4 * k4 + 4, :],
                    w_in[k4 * 512:(k4 + 1) * 512, C_MI:C_MI + 8].rearrange("(k p) c -> p k c", p=128), r=(), w=("wifs",))
            cast(wif[:], wifs[:], r=("wifs",), w=("wif",), eng="dve")
            for h in range(4):
                for b in range(2):
                    memset("dve", state[h][b][:], 0.0, w=(("st", h, b),))

            areset()
            sc_in = af32(16)
            scv = af32(16)
            modrow = af32(3 * D)[0:1, :]
            badar = af32(3 * D)[0:1, :]
            stgA = [af32(3072) for _ in range(2)]
            dma("sp", sc_in, cT, r=(), w=("sc_in",))
            dma("sp", badar, b_ada, r=(), w=("badar",))
            act(scv, sc_in, AF.Silu, r=("sc_in",), w=("scv",))
            banks = [(ps[0], ("ps", 0), 0), (ps[1], ("ps", 1), 0), (ps[2], ("ps", 2), 0),
                     (ps[3], ("ps", 3), 0), (psS, ("psS",), 0), (psS, ("psS",), 512)]
            for half in range(2):
                for k in range(16):
                    st_ = stgA[k % 2]
                    key = ("stgA", k % 2)
                    dma(dmaq(), st_, w_ada[k * 128:(k + 1) * 128, half * 3072:(half + 1) * 3072], r=(), w=(key,))
                    for cbk in range(6):
                        t_, pk, off = banks[cbk]
                        mm(t_[0:1, off:off + 512], scv[:, k:k + 1], st_[:, cbk * 512:(cbk + 1) * 512],
                           k == 0, k == 15, r=(key, "scv"), w=(pk,))
                for cbk in range(6):
                    t_, pk, off = banks[cbk]
                    c0 = half * 3072 + cbk * 512
                    tt("dve", modrow[:, c0:c0 + 512], t_[0:1, off:off + 512], badar[:, c0:c0 + 512], ALU.add,
                       r=(pk, "badar"), w=("modrow",))
            pcol, pck = getps()
            for c in range(32):
                mm(pcol[:, c:c + 1], modrow[0:1, c * 128:(c + 1) * 128], ones[0:1, 0:1], True, True,
                   r=("modrow", "ones"), w=(pck,))
            cast(shift[:], pcol[:, 0:16], r=(pck,), w=("shift",), eng="dve")
            stt("dve", gs[:], pcol[:, 16:32], 1.0, ngt_s[:], ALU.add, ALU.mult, r=(pck, "ngt"), w=("gs",))
            for cbk in range(4):
                t_, pk = getps()
                mm(t_[:, 0:512], ones[0:1, :], modrow[0:1, 2 * D + cbk * 512:2 * D + (cbk + 1) * 512], True, True,
                   r=("modrow", "ones"), w=(pk,))
                cast(gate_bc[:, cbk * 512:(cbk + 1) * 512], t_[:, 0:512], r=(pk,), w=("gate_bc",), eng="act")
            S.barrier()
            stop("S1", [(gs[:], 16), (shift[:], 16), (gate_bc[:, 0:64], 64)])

            def frontend(tau, dstf, dkey, xbufs, xn, i2):
                xt = xbufs[i2 % len(xbufs)]
                xk = ("xt", i2 % len(xbufs))
                dma(dmaq(), xt, xl[tau * 128:(tau + 1) * 128, :], r=(), w=(xk,))
                memset("dve", sm[:, 0:1], 0.0, w=("ss",))
                act(xn, xt, AF.Square, r=(xk, "ss"), w=("xn", "ss"), accum=sm[:, 0:1])
                ts("dve", sm[:, 1:2], sm[:, 0:1], 1.0 / D, EPS, ALU.mult, ALU.add, r=("ss",), w=("rstd",))
                rsqrt_col(sm[:, 1:2], "rstd")
                act(xn, xt, AF.Identity, r=(xk, "rstd"), w=("xn",), scale=sm[:, 1:2])
                if tau == 0:
                    stop("F1", [(sm[:], 64)])
                for hh in range(2):
                    pt, ptk = getpsb()
                    for kk in range(8):
                        k = hh * 8 + kk
                        tr(pt[:, kk * 128:(kk + 1) * 128], xn[:, k * 128:(k + 1) * 128], r=("xn",), w=(ptk,))
                    if tau == 0 and hh == 0:
                        stop("F2", [(sm[:], 64)])
                    for kk in range(8):
                        k = hh * 8 + kk
                        if hh == 0:
                            act(dstf(k), pt[:, kk * 128:(kk + 1) * 128], AF.Identity, r=(ptk, "gs", "shift"),
                                w=(dkey[0],), bias=shift[:, k:k + 1], scale=gs[:, k:k + 1])
                        else:
                            ts("dve", dstf(k), pt[:, kk * 128:(kk + 1) * 128], gs[:, k:k + 1], shift[:, k:k + 1],
                               ALU.mult, ALU.add, r=(ptk, "gs", "shift"), w=(dkey[1],))

            def load_w(dst, src_rows, nk, stgs, skey, dkey, cast_eng=None, q=None):
                for k in range(nk):
                    st_ = stgs[k % len(stgs)]
                    key = (skey, k % len(stgs))
                    dma(q or dmaq(), st_, src_rows(k), r=(), w=(key,))
                    cast(dst(k), st_, r=(key,), w=(dkey,), eng=cast_eng)

            def gates(tau, hsrc, hkey, own_i):
                pg, pgk = getps()
                for k in range(16):
                    mm(pg[:, 0:8], hsrc(k), wif[:, k, :], k == 0, k == 15, r=tuple(hkey) + ("wif",), w=(pgk,))
                gl = sm[:, 8:16]
                tt("dve", gl, pg[:, 0:8], bif[:], ALU.add, r=(pgk, "bif"), w=("gl",))
                ts("dve", sm[:, 16:20], gl[:, 0:4], pneg[:, tau:tau + 1], None, ALU.add, None, r=("gl", "pneg"), w=("ig",))
                act(sm[:, 20:24], gl[:, 4:8], AF.Exp, r=("gl",), w=("e1",), scale=-1.0)
                ts("dve", sm[:, 20:24], sm[:, 20:24], 1.0, None, ALU.add, None, r=("e1",), w=("e1",))
                act(sm[:, 24:28], sm[:, 20:24], AF.Ln, r=("e1",), w=("lf",))
                ts("dve", sm[:, 24:28], sm[:, 24:28], -1.0, None, ALU.mult, None, r=("lf",), w=("lf",))
                pc, pckk = getps()
                mm(pc[:, 0:4], tri[:], sm[:, 24:28], True, True, r=("lf", "tri"), w=(pckk,))
                mm(pc[:, 4:8], ones[:], sm[:, 24:28], True, True, r=("lf", "ones"), w=(pckk,))
                tt("dve", sm[:, 28:32], sm[:, 16:20], pc[:, 0:4], ALU.subtract, r=("ig", pckk), w=("t1",))
                tt("dve", sm[:, 32:36], sm[:, 28:32], pc[:, 4:8], ALU.add, r=("t1", pckk), w=("t2",))
                if own_i is None:
                    wk, eB = sm[:, 36:40], sm[:, 40:44]
                    wkk, eBk = "wk", "eB"
                else:
                    wk, eB = wk_o[:, own_i, :], eB_o[:, own_i, :]
                    wkk = eBk = ("gown", own_i)
                act(wk, sm[:, 32:36], AF.Exp, r=("t2",), w=(wkk,))
                ts("dve", wk, wk, 0.0625, None, ALU.mult, None, r=(wkk,), w=(wkk,))
                act(eB, pc[:, 4:8], AF.Exp, r=(pckk,), w=(eBk,))
                if own_i is not None:
                    act(wa_o[:, own_i, :], sm[:, 28:32], AF.Exp, r=("t1",), w=(wkk,))
                    ts("dve", wa_o[:, own_i, :], wa_o[:, own_i, :], 0.0625, None, ALU.mult, None, r=(wkk,), w=(wkk,))
                    act(ebc_o[:, own_i, :], pc[:, 0:4], AF.Exp, r=(pckk,), w=(wkk,))
                return wk, eB, wkk, eBk

            def conv_silu(pre, prek, cblk, accb, acck, out, outk):
                ts("dve", accb, pre[:, 3:515], cw[:, cblk, 3:4], cb_[:, cblk:cblk + 1], ALU.mult, ALU.add,
                   r=(prek, "cw", "cb"), w=(acck,))
                for tap in range(3):
                    stt("dve", accb, pre[:, tap:tap + 512], cw[:, cblk, tap:tap + 1], accb, ALU.mult, ALU.add,
                        r=(prek, acck, "cw"), w=(acck,))
                act(out, accb, AF.Silu, r=(acck,), w=(outk,))

            def state_update(h, kT_blk, kTk, vaug, vk, wk_col, wkk, eB_col, eBk, kpp, kppk, refresh_bf):
                pt, ptk = getpsb()
                for blk in range(2):
                    tr(pt[:, blk * 128:(blk + 1) * 128], kT_blk(blk), r=(kTk[blk],), w=(ptk,))
                ts("dve", kpp, pt[:, 0:256], wk_col, None, ALU.mult, None, r=(ptk, wkk), w=(kppk,))
                for blk in range(2):
                    p_, pk = getps()
                    mm(p_[:, 0:257], kpp[:, blk * 128:(blk + 1) * 128], vaug, True, True, r=(kppk, vk), w=(pk,))
                    stt("dve", state[h][blk][:], state[h][blk][:], eB_col, p_[:, 0:257], ALU.mult, ALU.add,
                        r=(pk, eBk, ("st", h, blk)), w=(("st", h, blk),))
                    if refresh_bf:
                        cast(ctbf[h][blk][:], state[h][blk][:], r=(("st", h, blk),), w=(("ct", h, blk),), eng="act")

            areset()
            WA = abf(16 * 2048).rearrange("p (k c) -> p k c", k=16)
            hTgA = [abf(16 * 512).rearrange("p (k c) -> p k c", k=16) for _ in range(2)]
            xn = abf(D)
            kT = abf(8 * 512).rearrange("p (b c) -> p b c", b=8)
            vaugs = [[abf(258)[:, 0:257] for _ in range(4)] for _ in range(4)]
            kpps = [abf(256) for _ in range(2)]
            stgs = [af32(2048) for _ in range(2)]
            xbufs = stgs
            kpre = af32(8 * 515).rearrange("p (b c) -> p b c", b=8)
            accb = af32(512)
            gtmp = af32(4 * 48).rearrange("p (i c) -> p i c", i=4)
            load_w(lambda k: WA[:, k, :], lambda k: w_in[k * 128:(k + 1) * 128, C_MK:C_MK + 2048], 16, stgs, "xt", "WA")
            memset("pool", kpre[:, :, 0:3], 0.0, w=tuple(("kpre", c) for c in range(8)))
            for i in range(4):
                for h in range(4):
                    memset("pool", vaugs[i][h][:, 256:257], 1.0, w=(("vaug", i, h),))
            NG = NPRE // 4
            fe_cnt = [0]

            def a_hT(g):
                return hT_halo if g == NG - 1 else hTgA[g % 2]

            def a_keys(g, i):
                return (("hTg", g % 2, i, 0), ("hTg", g % 2, i, 1))

            def a_fe(g, i):
                hT = a_hT(g)
                frontend(4 * g + i, lambda k: hT[:, k, i * 128:(i + 1) * 128], a_keys(g, i), xbufs, xn, fe_cnt[0])
                fe_cnt[0] += 1

            def a_gates1(g, i):
                tau = 4 * g + i
                hT = a_hT(g)
                gt = gtmp[:, i, :]
                pg, pgk = ps[2 + i % 2], ("ps", 2 + i % 2)
                for k in range(16):
                    mm(pg[:, 0:8], hT[:, k, i * 128:(i + 1) * 128], wif[:, k, :], k == 0, k == 15,
                       r=a_keys(g, i) + ("wif",), w=(pgk,))
                tt("dve", gt[:, 0:8], pg[:, 0:8], bif[:], ALU.add, r=(pgk, "bif"), w=(("g_gl", i),))
                ts("dve", gt[:, 8:12], gt[:, 0:4], pneg[:, tau:tau + 1], None, ALU.add, None, r=(("g_gl", i), "pneg"), w=(("g_ig", i),))
                act(gt[:, 12:16], gt[:, 4:8], AF.Exp, r=(("g_gl", i),), w=(("g_lf", i),), scale=-1.0)
                ts("dve", gt[:, 12:16], gt[:, 12:16], 1.0, None, ALU.add, None, r=(("g_lf", i),), w=(("g_lf", i),))
                act(gt[:, 12:16], gt[:, 12:16], AF.Ln, r=(("g_lf", i),), w=(("g_lf", i),))
                ts("dve", gt[:, 12:16], gt[:, 12:16], -1.0, None, ALU.mult, None, r=(("g_lf", i),), w=(("g_lf", i),))

            def a_gates2(g, i):
                gt = gtmp[:, i, :]
                pc, pck_ = psS[:, (i % 2) * 512:(i % 2) * 512 + 8], ("psS", i % 2)
                mm(pc[:, 0:4], tri[:], gt[:, 12:16], True, True, r=(("g_lf", i), "tri"), w=(pck_,))
                mm(pc[:, 4:8], ones[:], gt[:, 12:16], True, True, r=(("g_lf", i), "ones"), w=(pck_,))
                tt("dve", gt[:, 16:20], gt[:, 8:12], pc[:, 0:4], ALU.subtract, r=(("g_ig", i), pck_), w=(("g_t", i),))
                tt("dve", gt[:, 16:20], gt[:, 16:20], pc[:, 4:8], ALU.add, r=(("g_t", i), pck_), w=(("g_t", i),))
                act(gt[:, 20:24], gt[:, 16:20], AF.Exp, r=(("g_t", i),), w=(("g_wk", i),))
                ts("dve", gt[:, 20:24], gt[:, 20:24], 0.0625, None, ALU.mult, None, r=(("g_wk", i),), w=(("g_wk", i),))
                act(gt[:, 24:28], pc[:, 4:8], AF.Exp, r=(pck_,), w=(("g_eB", i),))

            def a_update(g, i):
                gt = gtmp[:, i, :]
                for h in range(4):
                    pt, ptk = getpsb()
                    for blk in range(2):
                        tr(pt[:, blk * 128:(blk + 1) * 128], kT[:, 2 * h + blk, i * 128:(i + 1) * 128],
                           r=(("kT", 2 * h + blk),), w=(ptk,))
                    kp, kpk = kpps[h % 2], ("kpp", h % 2)
                    ts("dve", kp, pt[:, 0:256], gt[:, 20 + h:21 + h], None, ALU.mult, None, r=(ptk, ("g_wk", i)), w=(kpk,))
                    for blk in range(2):
                        pk = ("psS", blk)
                        p_ = psS[:, blk * 512:blk * 512 + 257]
                        mm(p_, kp[:, blk * 128:(blk + 1) * 128], vaugs[i][h], True, True, r=(kpk, ("vaug", i, h)), w=(pk,))
                        stt("dve", state[h][blk][:], state[h][blk][:], gt[:, 24 + h:25 + h], p_, ALU.mult, ALU.add,
                            r=(pk, ("g_eB", i), ("st", h, blk)), w=(("st", h, blk),))

            for i in range(4):
                a_fe(0, i)
            for g in range(NG):
                hT = a_hT(g)
                hgk = tuple(k_ for i_ in range(4) for k_ in a_keys(g, i_))
                for cbk in range(8):
                    p_, pk = ps[cbk % 2], ("ps", cbk % 2)
                    for k in range(16):
                        mm(p_[:, 0:512], WA[:, k, cbk * 128:(cbk + 1) * 128], hT[:, k, :], k == 0, k == 15,
                           r=("WA",) + hgk, w=(pk,))
                    cast(kpre[:, cbk, 3:515], p_[:, 0:512], r=(pk,), w=(("kpre", cbk),), eng="act")
                    conv_silu(kpre[:, cbk, :], ("kpre", cbk), 8 + cbk, accb, "accb", kT[:, cbk, :], ("kT", cbk))
                    ts("dve", kpre[:, cbk, 0:3], kpre[:, cbk, 512:515], tval[:, 4 * g + 3:4 * g + 4], None, ALU.mult, None,
                       r=(("kpre", cbk), "tval"), w=(("kpre", cbk),))
                for i in range(4):
                    a_gates1(g, i)
                for i in range(4):
                    for half in range(2):
                        p_, pk = ps[2 + half], ("ps", 2 + half)
                        for k in range(16):
                            mm(p_[:, 0:512], hT[:, k, i * 128:(i + 1) * 128], WA[:, k, 1024 + half * 512:1024 + (half + 1) * 512],
                               k == 0, k == 15, r=("WA",) + a_keys(g, i), w=(pk,))
                        for hh in range(2):
                            h = 2 * half + hh
                            cast(vaugs[i][h][:, 0:256], p_[:, hh * 256:(hh + 1) * 256], r=(pk,), w=(("vaug", i, h),),
                                 eng=("act", "dve")[hh])
                for i in range(4):
                    a_gates2(g, i)
                for i in range(4):
                    if g + 1 < NG:
                        a_fe(g + 1, i)
                    a_update(g, i)
            S.barrier()
            stop("A", [(state[0][0][:], 257), (state[3][1][:], 257), (sm[:], 64)])

            areset()
            hT_own = abf(16 * 2048).rearrange("p (k c) -> p k c", k=16)
            mark_b = aoff[0]
            xn = abf(D)
            xbufs = [af32(D) for _ in range(2)]
            for i in range(NOWN):
                frontend(NPRE + i, lambda k, i=i: hT_own[:, k, i * 128:(i + 1) * 128], (("hTo", i, 0), ("hTo", i, 1)), xbufs, xn, i)
                gates(NPRE + i, lambda k, i=i: hT_own[:, k, i * 128:(i + 1) * 128], (("hTo", i, 0), ("hTo", i, 1)), i)
            for h in range(4):
                for b in range(2):
                    cast(ctbf[h][b][:], state[h][b][:], r=(("st", h, b),), w=(("ct", h, b),), eng="act")
            S.barrier()
            stop("B0", [(wk_o[:].rearrange("p a b -> p (a b)"), 64), (wa_o[:].rearrange("p a b -> p (a b)"), 64), (eB_o[:].rearrange("p a b -> p (a b)"), 64), (ebc_o[:].rearrange("p a b -> p (a b)"), 64), (state[0][0][:], 257)])

            aoff[0] = mark_b
            WG = abf(16 * 1280).rearrange("p (k c) -> p k c", k=16)
            qkT = [abf(4 * 512).rearrange("p (b c) -> p b c", b=4) for _ in range(2)]
            vaug1 = [abf(258)[:, 0:257] for _ in range(2)]
            kpps = [abf(256) for _ in range(2)]
            STb = [abf(128) for _ in range(2)]
            ybf = [abf(256) for _ in range(2)]
            yTt = [abf(256).rearrange("p (b c) -> p b c", b=2) for _ in range(2)]
            stg1 = [af32(1280) for _ in range(2)]
            qkpre = af32(4 * 515).rearrange("p (b c) -> p b c", b=4)
            accb = af32(512)
            sgo = af32(256)
            slz = af32(256)
            Gt = [af32(256) for _ in range(2)]
            junk = af32(256)
            w5 = w_in[:, 0:5120].rearrange("r (s c) -> r s c", c=1024)
            for vb in range(2):
                memset("pool", vaug1[vb][:, 256:257], 1.0, w=(("vaug1", vb),))
            ts("dve", mhg[:], mhg[:], 0.5, None, ALU.mult, None, r=("mhg",), w=("mhg",))
            PS0, PS1, PS2, PS3 = (ps[0], ("ps", 0)), (ps[1], ("ps", 1)), (ps[2], ("ps", 2)), (ps[3], ("ps", 3))
            for h in range(4):
                load_w(lambda k: WG[:, k, :].rearrange("p (s c) -> p s c", c=256),
                       lambda k, h=h: w5[k * 128:(k + 1) * 128, :, h * 256:(h + 1) * 256],
                       16, [s_.rearrange("p (s c) -> p s c", c=256) for s_ in stg1], "stg1", "WG")
                for blk in range(4):
                    p_, pk = (PS0, PS1)[blk % 2]
                    for k in range(16):
                        mm(p_[:, 0:3], WG[:, k, blk * 128:(blk + 1) * 128], hT_halo[:, k, 509:512], k == 0, k == 15,
                           r=("WG",), w=(pk,))
                    ts("dve", qkpre[:, blk, 0:3], p_[:, 0:3], tval[:, NPRE - 1:NPRE], None, ALU.mult, None,
                       r=(pk, "tval"), w=(("qkpre", blk),))

                def b1_proj(g, h=h):
                    qk = qkT[g % 2]
                    for blk in range(4):
                        p_, pk = (PS0, PS1)[blk % 2]
                        for k in range(16):
                            mm(p_[:, 0:512], WG[:, k, blk * 128:(blk + 1) * 128], hT_own[:, k, g * 512:(g + 1) * 512],
                               k == 0, k == 15, r=("WG",), w=(pk,))
                        cast(qkpre[:, blk, 3:515], p_[:, 0:512], r=(pk,), w=(("qkpre", blk),), eng="act")
                        cidx = (2 * h + blk) if blk < 2 else (8 + 2 * h + blk - 2)
                        conv_silu(qkpre[:, blk, :], ("qkpre", blk), cidx, accb, "accb", qk[:, blk, :], ("qkT", g % 2, blk))
                        cast(qkpre[:, blk, 0:3], qkpre[:, blk, 512:515], r=(("qkpre", blk),), w=(("qkpre", blk),), eng="dve")

                def b1_P(t, h=h):
                    vb = t % 2
                    pA, pAk = PS0
                    for k in range(16):
                        mm(pA[:, 0:512], hT_own[:, k, t * 128:(t + 1) * 128], WG[:, k, 512:1024], k == 0, k == 15,
                           r=("WG",), w=(pAk,))
                    pB, pBk = PS1
                    for k in range(16):
                        mm(pB[:, 0:256], hT_own[:, k, t * 128:(t + 1) * 128], WG[:, k, 1024:1280], k == 0, k == 15,
                           r=("WG",), w=(pBk,))
                    cast(vaug1[vb][:, 0:256], pA[:, 0:256], r=(pAk,), w=(("vaug1", vb),), eng="act")
                    act(sgo, pA[:, 256:512], AF.Tanh, r=(pAk,), w=("sgo",), scale=0.5)
                    act(slz, pB[:, 0:256], AF.Silu, r=(pBk,), w=("slz",))
                    stt("dve", Gt[vb], sgo, 1.0, slz, ALU.add, ALU.mult, r=("sgo", "slz"), w=(("Gt", vb),))
                    tt("dve", Gt[vb], Gt[vb], mhg[:, h * 256:(h + 1) * 256], ALU.mult, r=(("Gt", vb), "mhg"), w=(("Gt", vb),))

                def b1_S(t, h=h):
                    vb = t % 2
                    gp = (t // 4) % 2
                    tsl = slice((t % 4) * 128, (t % 4 + 1) * 128)
                    pS, pSk = PS2
                    for blk in range(2):
                        mm(pS[:, 0:128], qkT[gp][:, 2 + blk, tsl], qkT[gp][:, blk, tsl], blk == 0, blk == 1,
                           r=(("qkT", gp, blk), ("qkT", gp, 2 + blk)), w=(pSk,))
                    stt("dve", STb[vb], pS[:, 0:128], wa_o[:, t, h:h + 1], tri[:], ALU.mult, ALU.mult,
                        r=(pSk, ("gown", t), "tri"), w=(("STb", vb),))

                def b1_N(t, h=h):
                    vb = t % 2
                    gp = (t // 4) % 2
                    tsl = slice((t % 4) * 128, (t % 4 + 1) * 128)
                    pN, pNk = PS3
                    mm(pN[:, 0:257], STb[vb], vaug1[vb], True, False, r=(("STb", vb), ("vaug1", vb)), w=(pNk,))
                    for blk in range(2):
                        mm(pN[:, 0:257], qkT[gp][:, blk, tsl], ctbf[h][blk][:], False, blk == 1,
                           r=(("qkT", gp, blk), ("ct", h, blk)), w=(pNk,))

                def b1_Utr(t, h=h):
                    vb = t % 2
                    gp = (t // 4) % 2
                    tsl = slice((t % 4) * 128, (t % 4 + 1) * 128)
                    pt, ptk = psb[0], ("psb", 0)
                    for blk in range(2):
                        tr(pt[:, blk * 128:(blk + 1) * 128], qkT[gp][:, 2 + blk, tsl], r=(("qkT", gp, 2 + blk),), w=(ptk,))
                    ts("dve", kpps[vb], pt[:, 0:256], wk_o[:, t, h:h + 1], None, ALU.mult, None,
                       r=(ptk, ("gown", t)), w=(("kpp", vb),))

                def b1_Umm(t, h=h):
                    vb = t % 2
                    for blk in range(2):
                        pk = ("psS", blk)
                        p_ = psS[:, blk * 512:blk * 512 + 257]
                        mm(p_, kpps[vb][:, blk * 128:(blk + 1) * 128], vaug1[vb], True, True,
                           r=(("kpp", vb), ("vaug1", vb)), w=(pk,))
                        stt("dve", state[h][blk][:], state[h][blk][:], eB_o[:, t, h:h + 1], p_, ALU.mult, ALU.add,
                            r=(pk, ("gown", t), ("st", h, blk)), w=(("st", h, blk),))
                        cast(ctbf[h][blk][:], state[h][blk][:], r=(("st", h, blk),), w=(("ct", h, blk),), eng="act")

                def b1_Ychain(t, h=h):
                    vb = t % 2
                    pN, pNk = PS3
                    gk = ("gown", t)
                    tt("dve", sm[:, 44:45], pN[:, 256:257], ebc_o[:, t, h:h + 1], ALU.mult, r=(pNk, gk), w=("d1",))
                    ts("dve", sm[:, 54:55], sm[:, 44:45], -1.0, None, ALU.mult, None, r=("d1",), w=("d1n",))
                    tt("dve", sm[:, 44:45], sm[:, 44:45], sm[:, 54:55], ALU.max, r=("d1", "d1n"), w=("d1",))
                    ts("dve", sm[:, 44:45], sm[:, 44:45], 1.0, 1.0, ALU.max, ALU.mult, r=("d1",), w=("d1",))
                    S.add("dve", lambda e: e.reciprocal(sm[:, 44:45], sm[:, 44:45]), r=("d1",), w=("d1",))
                    tt("dve", sm[:, 45:46], ebc_o[:, t, h:h + 1], sm[:, 44:45], ALU.mult, r=("d1", gk), w=("rr",))
                    memset("dve", sm[:, 46:47], 0.0, w=("ss2",))
                    act(junk, pN[:, 0:256], AF.Square, r=(pNk, "rr", "ss2"), w=("junk", "ss2"), scale=sm[:, 45:46],
                        accum=sm[:, 46:47])
                    ts("dve", sm[:, 47:48], sm[:, 46:47], 1.0 / 256, EPS, ALU.mult, ALU.add, r=("ss2",), w=("r2",))
                    rsqrt_col(sm[:, 47:48], "r2")
                    tt("dve", sm[:, 47:48], sm[:, 47:48], sm[:, 45:46], ALU.mult, r=("r2", "rr"), w=("r2",))
                    stt("dve", ybf[vb], pN[:, 0:256], sm[:, 47:48], Gt[vb], ALU.mult, ALU.mult,
                        r=(pNk, "r2", ("Gt", vb)), w=(("ybf", vb),))

                def b1_Ytr(t, h=h):
                    vb = t % 2
                    pt, ptk = psb[1], ("psb", 1)
                    for blk in range(2):
                        tr(pt[:, blk * 128:(blk + 1) * 128], ybf[vb][:, blk * 128:(blk + 1) * 128], r=(("ybf", vb),), w=(ptk,))
                    cast(yTt[vb].rearrange("p b c -> p (b c)"), pt[:, 0:256], r=(ptk,), w=(("yTt", vb),), eng="act")
                    dma(dmaq(), yT_d[2 * h:2 * h + 2, :, t * 128:(t + 1) * 128].rearrange("b p t -> p b t"), yTt[vb],
                        r=(("yTt", vb),), w=(("yTd", 2 * h, t),))

                b1_proj(0)
                b1_P(0)
                b1_S(0)
                for t in range(NOWN):
                    if t + 1 < NOWN:
                        if (t + 1) % 4 == 0:
                            b1_proj((t + 1) // 4)
                        b1_P(t + 1)
                        b1_S(t + 1)
                    b1_N(t)
                    b1_Utr(t)
                    if t > 0:
                        b1_Ytr(t - 1)
                    b1_Umm(t)
                    b1_Ychain(t)
                b1_Ytr(NOWN - 1)
            S.barrier()
            stop("B1", [(state[0][0][:], 257)])

            aoff[0] = mark_b
            WG2s = [abf(16 * 512).rearrange("p (k c) -> p k c", k=16) for _ in range(2)]
            akT = abf(2560)
            aqT = abf(2048)
            Vt = abf(20 * 128).rearrange("p (t c) -> p t c", t=20)
            slzA = abf(16 * 128).rearrange("p (t c) -> p t c", t=16)
            Pb = [abf(640) for _ in range(2)]
            PT = [abf(640) for _ in range(2)]
            yab = [abf(128) for _ in range(2)]
            yaT = [abf(128) for _ in range(2)]
            stg2 = [af32(512) for _ in range(4)]
            rb = af32(640)
            bm = af32(640)
            sbuf_s = [af32(640) for _ in range(2)]
            wA = w_in[:, C_AQ:C_AQ + 4096].rearrange("r (s c) -> r s c", c=1024)
            def b2_load(h, cast_eng=None, q=None):
                W_ = WG2s[h % 2]
                load_w(lambda k: W_[:, k, :].rearrange("p (s c) -> p s c", c=128),
                       lambda k: wA[k * 128:(k + 1) * 128, :, h * 128:(h + 1) * 128],
                       16, [s_.rearrange("p (s c) -> p s c", c=128) for s_ in stg2], "stg2", ("WG2", h % 2),
                       cast_eng=cast_eng, q=q)

            b2_load(0)
            for h in range(8):
                WG2 = WG2s[h % 2]
                WK = ("WG2", h % 2)
                dma("sp", rb, relmat[h], r=(), w=("rb",))
                tt("pool", bm, rb, amask_s[:], ALU.add, r=("rb", "amask"), w=("bm",))
                for g5 in range(5):
                    src = hT_halo if g5 == 0 else hT_own[:, :, (g5 - 1) * 512:g5 * 512]
                    srk = ()
                    p_, pk = getps()
                    for k in range(16):
                        mm(p_[:, 0:512], WG2[:, k, 128:256], src[:, k, :], k == 0, k == 15, r=(WK,) + srk, w=(pk,))
                    cast(akT[:, g5 * 512:(g5 + 1) * 512], p_[:, 0:512], r=(pk,), w=(("akT", g5),), eng="act")
                    if g5 > 0:
                        p_, pk = getps()
                        for k in range(16):
                            mm(p_[:, 0:512], WG2[:, k, 0:128], src[:, k, :], k == 0, k == 15, r=(WK,) + srk, w=(pk,))
                        act(aqT[:, (g5 - 1) * 512:g5 * 512], p_[:, 0:512], AF.Copy, r=(pk,), w=(("aqT", g5 - 1),),
                            scale=float(128 ** -0.5))
                    for i in range(4):
                        ttile = g5 * 4 + i
                        p_, pk = getps()
                        for k in range(16):
                            mm(p_[:, 0:256], src[:, k, i * 128:(i + 1) * 128], WG2[:, k, 256:512], k == 0, k == 15,
                               r=(WK,) + srk, w=(pk,))
                        cast(Vt[:, ttile, :], p_[:, 0:128], r=(pk,), w=(("Vt", ttile),), eng="act")
                        if g5 > 0:
                            act(slzA[:, ttile - 4, :], p_[:, 128:256], AF.Silu, r=(pk,), w=(("slzA", ttile - 4),))
                if h + 1 < 8:
                    b2_load(h + 1, cast_eng="pool", q="sp")

                def att_s1(t, h=h):
                    vb = t % 2
                    gq = ("aqT", t // 4)
                    kk0 = tuple(("akT", x) for x in sorted({t // 4, (t + 3) // 4, (t + 4) // 4}))
                    mm(psS[:, 0:512], aqT[:, t * 128:(t + 1) * 128], akT[:, t * 128:t * 128 + 512], True, True,
                       r=(gq,) + kk0, w=("psS",))
                    mm(psS[:, 512:640], aqT[:, t * 128:(t + 1) * 128], akT[:, t * 128 + 512:t * 128 + 640], True, True,
                       r=(gq,) + kk0, w=("psS",))
                    sbt = sbuf_s[vb]
                    sk = ("sbt", vb)
                    tt("dve", sbt, psS[:, 0:640], bm, ALU.add, r=("psS", "bm"), w=(sk,))
                    for kb in range(max(0, 4 - t)):
                        lt = NPRE - 4 + t + kb
                        ts("dve", sbt[:, kb * 128:(kb + 1) * 128], sbt[:, kb * 128:(kb + 1) * 128], ntile[:, lt:lt + 1], None,
                           ALU.add, None, r=(sk, "ntile"), w=(sk,))
                    S.add("dve", lambda e, sbt=sbt: e.reduce_max(sm[:, 48:49], sbt, AX.X), r=(sk,), w=("mx",))
                    ts("dve", sm[:, 48:49], sm[:, 48:49], -1.0, None, ALU.mult, None, r=("mx",), w=("mx",))
                    rsc = sm[:, 56 + vb:57 + vb]
                    memset("dve", rsc, 0.0, w=(("rsum", vb),))
                    act(Pb[vb], sbt, AF.Exp, r=(sk, "mx", ("rsum", vb)), w=(("Pb", vb), ("rsum", vb)), bias=sm[:, 48:49],
                        accum=rsc)

                def att_s2(t, h=h):
                    vb = t % 2
                    rsc = sm[:, 56 + vb:57 + vb]
                    rrc = sm[:, 58 + vb:59 + vb]
                    pt, ptk = getpsb()
                    for kb in range(5):
                        tr(pt[:, kb * 128:(kb + 1) * 128], Pb[vb][:, kb * 128:(kb + 1) * 128], r=(("Pb", vb),), w=(ptk,))
                    cast(PT[vb], pt[:, 0:640], r=(ptk,), w=(("PT", vb),), eng="act")
                    pO, pOk = getps()
                    for kb in range(5):
                        mm(pO[:, 0:128], PT[vb][:, kb * 128:(kb + 1) * 128], Vt[:, t + kb, :], kb == 0, kb == 4,
                           r=(("PT", vb), ("Vt", t + kb)), w=(pOk,))
                    S.add("dve", lambda e: e.reciprocal(rrc, rsc), r=(("rsum", vb),), w=(("rrs", vb),))
                    stt("dve", yab[vb], pO[:, 0:128], rrc, slzA[:, t, :], ALU.mult, ALU.mult,
                        r=(pOk, ("rrs", vb), ("slzA", t)), w=(("yab", vb),))
                    pt2, pt2k = getpsb()
                    tr(pt2[:, 0:128], yab[vb], r=(("yab", vb),), w=(pt2k,))
                    cast(yaT[vb], pt2[:, 0:128], r=(pt2k,), w=(("yaT", vb),), eng="act")
                    dma("act", yT_d[8 + h, :, t * 128:(t + 1) * 128], yaT[vb], r=(("yaT", vb),), w=(("yTd", 8 + h, t),))

                att_s1(0)
                for t in range(NOWN):
                    if t + 1 < NOWN:
                        att_s1(t + 1)
                    att_s2(t)
            S.barrier()
            stop("B2", [(sm[:], 64)])

            aoff[0] = mark_b
            yT_res = abf(16 * 2048).rearrange("p (b c) -> p b c", b=16)
            hh_flat = hT_halo[:].rearrange("p k c -> p (k c)")
            wgm = [abf(16 * 128).rearrange("p (k c) -> p k c", k=16), hh_flat[:, 0:2048].rearrange("p (k c) -> p k c", k=16)]
            wga = [abf(16 * 128).rearrange("p (k c) -> p k c", k=16), hh_flat[:, 2048:4096].rearrange("p (k c) -> p k c", k=16)]
            wpm = [abf(8 * 128).rearrange("p (k c) -> p k c", k=8), hh_flat[:, 4096:5120].rearrange("p (k c) -> p k c", k=8)]
            wpa = [abf(8 * 128).rearrange("p (k c) -> p k c", k=8), hh_flat[:, 5120:6144].rearrange("p (k c) -> p k c", k=8)]
            mTb = [abf(512) for _ in range(2)]
            stg3 = [af32(8 * 128).rearrange("p (k c) -> p k c", k=8) for _ in range(2)]
            sgm = af32(512)
            sga = af32(512)
            t1b = af32(512)
            for b in range(16):
                dma(dmaq(), yT_res[:, b, :], yT_d[b], r=(), w=("yT_res",))
            si3 = [0]

            def b3_load(c):
                cbuf = c % 2
                for (dst, src, nk, nm) in ((wgm, w_in[:, C_GM + c * 128:C_GM + (c + 1) * 128], 16, "wgm"),
                                           (wga, w_in[:, C_GA + c * 128:C_GA + (c + 1) * 128], 16, "wga"),
                                           (wpm, w_pm[:, c * 128:(c + 1) * 128], 8, "wpm"),
                                           (wpa, w_pa[:, c * 128:(c + 1) * 128], 8, "wpa")):
                    for k8 in range(nk // 8):
                        st_ = stg3[si3[0] % 2]
                        sk = ("stg3", si3[0] % 2)
                        si3[0] += 1
                        for k4 in range(2):
                            r0 = (k8 * 8 + k4 * 4) * 128
                            dma(dmaq(), st_[:, 4 * k4:4 * k4 + 4, :],
                                src[r0:r0 + 512, :].rearrange("(k p) c -> p k c", p=128), r=(), w=(sk,))
                        cast(dst[cbuf][:, k8 * 8:(k8 + 1) * 8, :], st_[:], r=(sk,), w=((nm, cbuf),))

            b3_load(0)
            for c in range(16):
                cbuf = c % 2
                if c + 1 < 16:
                    b3_load(c + 1)
                for g in range(4):
                    gsl = slice(g * 512, (g + 1) * 512)
                    p1, p1k = getps()
                    for k in range(16):
                        mm(p1[:, 0:512], wgm[cbuf][:, k, :], hT_own[:, k, gsl], k == 0, k == 15, r=(("wgm", cbuf),), w=(p1k,))
                    act(sgm, p1[:, 0:512], AF.Sigmoid, r=(p1k,), w=("sgm",))
                    p2, p2k = getps()
                    for k in range(16):
                        mm(p2[:, 0:512], wga[cbuf][:, k, :], hT_own[:, k, gsl], k == 0, k == 15, r=(("wga", cbuf),), w=(p2k,))
                    act(sga, p2[:, 0:512], AF.Sigmoid, r=(p2k,), w=("sga",))
                    p3, p3k = getps()
                    for k in range(8):
                        mm(p3[:, 0:512], wpm[cbuf][:, k, :], yT_res[:, k, gsl], k == 0, k == 7,
                           r=(("wpm", cbuf), "yT_res"), w=(p3k,))
                    tt("dve", t1b, sgm, p3[:, 0:512], ALU.mult, r=("sgm", p3k), w=("t1b",))
                    p4, p4k = getps()
                    for k in range(8):
                        mm(p4[:, 0:512], wpa[cbuf][:, k, :], yT_res[:, 8 + k, gsl], k == 0, k == 7,
                           r=(("wpa", cbuf), "yT_res"), w=(p4k,))
                    tt("dve", sga, sga, p4[:, 0:512], ALU.mult, r=("sga", p4k), w=("sga",))
                    mb = mTb[(c * 4 + g) % 2]
                    mk_ = ("mTb", (c * 4 + g) % 2)
                    tt("dve", mb, t1b, sga, ALU.add, r=("t1b", "sga"), w=(mk_,))
                    dma(dmaq(), mT_d[c, :, gsl], mb, r=(mk_,), w=(("mTd", c, g),))
            S.barrier()
            stop("B3", [(sm[:], 64)])

            areset()
            Wout = abf(16 * 2048).rearrange("p (k c) -> p k c", k=16)
            mTt = [abf(16 * 128).rearrange("p (c t) -> p c t", c=16) for _ in range(2)]
            fgb = af32(D)
            stgC = [af32(D) for _ in range(2)]
            xts = [af32(D) for _ in range(2)]
            obuf = [af32(D) for _ in range(2)]
            junkC = af32(D)
            dma("sp", fgb, fg_bc, r=(), w=("fgb",))
            for k in range(16):
                st_ = stgC[k % 2]
                sk = ("stgC", k % 2)
                dma(dmaq(), st_, w_out[k * 128:(k + 1) * 128, :], r=(), w=(sk,))
                tt(("dve", "pool")[k % 2], Wout[:, k, :], st_, gate_bc[:], ALU.mult, r=(sk, "gate_bc"), w=("Wout",))
            outkeys = []
            for t in range(NOWN):
                vb = t % 2
                for c4 in range(4):
                    dma(dmaq(), mTt[vb][:, 4 * c4:4 * c4 + 4, :],
                        mT_d[4 * c4:4 * c4 + 4, :, t * 128:(t + 1) * 128].rearrange("c p t -> p c t"), r=(), w=(("mTt", vb),))
                dma(dmaq(), xts[vb], xl[(NPRE + t) * 128:(NPRE + t + 1) * 128, :], r=(), w=(("xts", vb),))
                ob = obuf[vb]
                ok = ("ob", vb)
                for cbk in range(4):
                    p_, pk = getps()
                    for c in range(16):
                        mm(p_[:, 0:512], mTt[vb][:, c, :], Wout[:, c, cbk * 512:(cbk + 1) * 512], c == 0, c == 15,
                           r=(("mTt", vb), "Wout"), w=(pk,))
                    tt("dve", ob[:, cbk * 512:(cbk + 1) * 512], p_[:, 0:512], xts[vb][:, cbk * 512:(cbk + 1) * 512], ALU.add,
                       r=(pk, ("xts", vb)), w=(ok,))
                memset("dve", sm[:, 52:53], 0.0, w=("ssC",))
                act(junkC, ob, AF.Square, r=(ok, "ssC"), w=("junkC", "ssC"), accum=sm[:, 52:53])
                ts("dve", sm[:, 53:54], sm[:, 52:53], 1.0 / D, EPS, ALU.mult, ALU.add, r=("ssC",), w=("rC",))
                rsqrt_col(sm[:, 53:54], "rC")
                stt("dve", ob, ob, sm[:, 53:54], fgb, ALU.mult, ALU.mult, r=(ok, "rC", "fgb"), w=(ok,))
                dma(dmaq(), y_out[t * 128:(t + 1) * 128, :], ob, r=(ok,), w=(("yout", t),))
                outkeys.append(("yout", t))
        try:
            body()
        except _Stop:
            pass
        S.finish(())
        S.emit()
    return nc


_NC_CACHE = {}


def _consts():
    ident = np.eye(128, dtype=np.float32).astype(ml_dtypes.bfloat16)
    s = np.arange(128)[:, None]
    t = np.arange(128)[None, :]
    tri = (s <= t).astype(np.float32)
    ones = np.ones((128, 128), np.float32)
    q = np.arange(128)[:, None]
    kap = np.arange(640)[None, :]
    cq, ck = q // 64, kap // 64
    allowed = (ck >= cq) & (ck <= cq + 8)
    amask = np.where(allowed, 0.0, NEG).astype(np.float32)
    relidx = np.clip(q + 512 - kap, -63, 128) + 63
    return ident, tri, ones, tri.copy(), amask, relidx


def kernel(x, c, w_ada, b_ada, norm_g, w_in, b_if, conv_w, conv_b, mh_norm_g, rel_bias,
           w_proj_m, w_proj_a, w_out, final_norm_g):
    f = np.float32
    x = np.asarray(x, f)
    ident, tri, ones, mst, amask, relidx = _consts()
    if "nc" not in _NC_CACHE:
        _NC_CACHE["nc"] = build_nc()
    nc = _NC_CACHE["nc"]
    rep = lambda v, n=128: np.ascontiguousarray(np.broadcast_to(np.asarray(v, f).reshape(1, -1), (n, np.asarray(v).size)))
    colT = lambda v: np.ascontiguousarray(np.asarray(v, f).reshape(16, 128).T)
    cwl = np.ascontiguousarray(np.asarray(conv_w[0], f).T.reshape(16, 128, 4).transpose(1, 0, 2))
    shared = {
        "w_ada": np.ascontiguousarray(w_ada[0], f), "b_ada": np.ascontiguousarray(b_ada[0], f).reshape(1, -1),
        "ngT": colT(norm_g[0]), "w_in": np.ascontiguousarray(w_in[0], f), "bif_bc": rep(b_if[0]),
        "convw": cwl, "convb": colT(conv_b[0]), "mhg_bc": rep(mh_norm_g[0]),
        "relmat": np.ascontiguousarray(np.asarray(rel_bias[0], f)[:, relidx]), "amask": amask,
        "w_pm": np.ascontiguousarray(w_proj_m[0], f), "w_pa": np.ascontiguousarray(w_proj_a[0], f),
        "w_out": np.ascontiguousarray(w_out[0], f), "fg_bc": rep(final_norm_g),
        "ident": ident, "tri": tri, "ones": ones, "maskst": mst,
    }
    in_maps = []
    for core in range(8):
        b, j = core // 4, core % 4
        npad = (3 - j) * 16
        xl = np.zeros((NT * 128, D), f)
        xl[npad * 128:] = x[b, 0:(j + 1) * 2048]
        valid = (np.arange(NT) >= npad).astype(f)
        m = dict(shared)
        m["xl"] = xl
        m["cT"] = colT(c[b])
        m["padneg"] = rep(np.where(valid > 0, 0.0, NEG))
        m["tilevalid"] = rep(valid)
        m["negtile"] = rep(np.where(valid > 0, 0.0, NEG))
        in_maps.append(m)
    res = run_bass_kernel_spmd(nc, in_maps, core_ids=list(range(8)))
    out = np.empty((2, 8192, D), f)
    for core in range(8):
        b, j = core // 4, core % 4
        out[b, j * 2048:(j + 1) * 2048] = res.results[core]["y"]
    return out
```

```python
import os
import numpy as np
import ml_dtypes
from contextlib import ExitStack
import concourse.bass as bass
import concourse.mybir as mybir
from concourse.bass_utils import run_bass_kernel_spmd

F32 = mybir.dt.float32
BF16 = mybir.dt.bfloat16
ALU = mybir.AluOpType
AF = mybir.ActivationFunctionType
AX = mybir.AxisListType

D = 2048
NT = 64
NPRE = 48
NOWN = 16
EPS = 1e-6
C_MQ, C_MK, C_MV, C_MO, C_MZ, C_MI, C_MF = 0, 1024, 2048, 3072, 4096, 5120, 5124
C_AQ, C_AK, C_AV, C_AZ, C_GM, C_GA = 5128, 6152, 7176, 8200, 9224, 11272
IN_COLS = 13320
NEG = -30000.0
ENGS = ("sp", "act", "dve", "pool", "pe")
KSTOP = os.environ.get("KSTOP")


class _Stop(Exception):
    pass


class Op:
    __slots__ = ("eng", "fn", "deps", "dma", "sig", "sem", "val")

    def __init__(self, eng, fn, dma):
        self.eng, self.fn, self.dma = eng, fn, dma
        self.deps, self.sig, self.sem, self.val = [], dma, None, 0


class Sched:
    ND = 40

    def __init__(self, nc, es):
        self.nc = nc
        self.eops = {e: [] for e in ENGS}
        self.lastw, self.readers = {}, {}
        self.csem = {e: es.enter_context(nc.semaphore("cs_" + e)) for e in ENGS}
        self.dsem = [es.enter_context(nc.semaphore("ds%d" % i)) for i in range(self.ND)]
        self.dlast = [None] * self.ND
        self.duse = [0] * self.ND
        self.dn = 0

    @staticmethod
    def _is_psum(k):
        return k == "psS" or (isinstance(k, tuple) and len(k) > 0 and k[0] in ("ps", "psb", "psS"))

    def add(self, eng, fn, r=(), w=(), dma=False):
        w = tuple(w) + tuple(k for k in r if self._is_psum(k))
        r = tuple(k for k in r if not self._is_psum(k))
        op = Op(eng, fn, dma)
        deps = []
        for k in r:
            if k in self.lastw:
                deps.append(self.lastw[k])
        for k in w:
            if k in self.lastw:
                deps.append(self.lastw[k])
            deps.extend(self.readers.get(k, ()))
        if dma:
            i = self.dn % self.ND
            self.dn += 1
            if self.dlast[i] is not None:
                deps.append(self.dlast[i])
            self.duse[i] += 1
            op.sem, op.val = self.dsem[i], 16 * self.duse[i]
            self.dlast[i] = op
        seen = set()
        for d in deps:
            if d is op or id(d) in seen:
                continue
            seen.add(id(d))
            if d.eng == "pe" and eng == "pe" and not d.dma:
                continue
            d.sig = True
            op.deps.append(d)
        for k in r:
            lst = self.readers.setdefault(k, [])
            if not dma:
                lst[:] = [o_ for o_ in lst if o_.dma or o_.eng != eng]
            lst.append(op)
        for k in w:
            self.lastw[k] = op
            self.readers[k] = []
        self.eops[eng].append(op)
        return op

    def barrier(self):
        lasts = []
        for e in ENGS:
            for o in reversed(self.eops[e]):
                if not o.dma and o.fn is not None:
                    lasts.append(o)
                    break
        pend = [d for d in self.dlast if d is not None]
        for e in ENGS:
            op = Op(e, None, False)
            for d in lasts + pend:
                if d.eng == e and not d.dma:
                    continue
                d.sig = True
                op.deps.append(d)
            self.eops[e].append(op)
        self.lastw, self.readers = {}, {}

    def finish(self, keys):
        op = Op("sp", None, False)
        for d in self.dlast:
            if d is not None:
                op.deps.append(d)
        self.eops["sp"].append(op)

    def emit(self):
        nc = self.nc
        for e in ENGS:
            c = 0
            for o in self.eops[e]:
                if o.dma or o.fn is None:
                    continue
                if o.sig:
                    c += 1
                    o.sem, o.val = self.csem[e], c

        def run(ename, eng):
            known = {}
            for o in self.eops[ename]:
                for d in o.deps:
                    key = id(d.sem)
                    if known.get(key, 0) >= d.val:
                        continue
                    eng.wait_ge(d.sem, d.val)
                    known[key] = d.val
                if o.fn is None:
                    continue
                ins = o.fn(eng)
                if o.sig:
                    ins.then_inc(o.sem, 16 if o.dma else 1)

        with nc.Block() as block:
            @block.sync
            def _(e):
                run("sp", e)

            @block.scalar
            def _(e):
                run("act", e)

            @block.vector
            def _(e):
                run("dve", e)

            @block.gpsimd
            def _(e):
                run("pool", e)

            @block.tensor
            def _(e):
                run("pe", e)


def build_nc():
    nc = bass.Bass("TRN2", target_bir_lowering=False)

    def din(name, shape, dt=F32):
        return nc.dram_tensor(name, list(shape), dt, kind="ExternalInput").ap()

    xl = din("xl", [NT * 128, D])
    cT = din("cT", [128, 16])
    w_ada = din("w_ada", [D, 3 * D])
    b_ada = din("b_ada", [1, 3 * D])
    ngT = din("ngT", [128, 16])
    w_in = din("w_in", [D, IN_COLS])
    bif_bc = din("bif_bc", [128, 8])
    convw = din("convw", [128, 16, 4])
    convb = din("convb", [128, 16])
    mhg_bc = din("mhg_bc", [128, 1024])
    relmat = din("relmat", [8, 128, 640])
    amask = din("amask", [128, 640])
    w_pm = din("w_pm", [1024, D])
    w_pa = din("w_pa", [1024, D])
    w_out = din("w_out", [D, D])
    fg_bc = din("fg_bc", [128, D])
    padneg = din("padneg", [128, NT])
    tilevalid = din("tilevalid", [128, NT])
    negtile = din("negtile", [128, NT])
    ident_d = din("ident", [128, 128], BF16)
    tri_d = din("tri", [128, 128])
    ones_d = din("ones", [128, 128])
    mst_d = din("maskst", [128, 128])
    y_out = nc.dram_tensor("y", [NOWN * 128, D], F32, kind="ExternalOutput").ap()
    dbg = nc.dram_tensor("dbg", [128, 8192], F32, kind="ExternalOutput").ap() if KSTOP else None
    yT_d = nc.dram_tensor("yT_d", [16, 128, NOWN * 128], BF16, kind="Internal").ap()
    mT_d = nc.dram_tensor("mT_d", [16, 128, NOWN * 128], BF16, kind="Internal").ap()

    es = ExitStack()
    with es, nc.allow_low_precision("bf16 matmul operands, fp32 accumulation"), \
            nc.allow_non_contiguous_dma("column-sliced weight loads"):
        S = Sched(nc, es)

        def sb(name, shape, dt=F32):
            return es.enter_context(nc.sbuf_tensor("s_" + name, list(shape), dt))

        def pst(name, shape, dt=F32):
            return es.enter_context(nc.psum_tensor(name, list(shape), dt))

        ident = sb("ident", [128, 128], BF16)
        tri = sb("tri", [128, 128])
        ones = sb("ones", [128, 128])
        gs = sb("gs", [128, 16])
        shift = sb("shift", [128, 16])
        ngt_s = sb("ngt_s", [128, 16])
        gate_bc = sb("gate_bc", [128, D])
        cw = sb("cw", [128, 16, 4])
        cb_ = sb("cb_", [128, 16])
        bif = sb("bif", [128, 8])
        mhg = sb("mhg", [128, 1024])
        pneg = sb("pneg", [128, NT])
        tval = sb("tval", [128, NT])
        ntile = sb("ntile", [128, NT])
        amask_s = sb("amask_s", [128, 640])
        wif = sb("wif", [128, 16, 8], BF16)
        wifs = sb("wifs", [128, 16, 8])
        state = [[sb("st%d%d" % (h, b), [128, 257]) for b in range(2)] for h in range(4)]
        ctbf = [[sb("ct%d%d" % (h, b), [128, 257], BF16) for b in range(2)] for h in range(4)]
        hT_halo = sb("hT_halo", [128, 16, 512], BF16)
        wk_o = sb("wk_o", [128, NOWN, 4])
        wa_o = sb("wa_o", [128, NOWN, 4])
        eB_o = sb("eB_o", [128, NOWN, 4])
        ebc_o = sb("ebc_o", [128, NOWN, 4])
        sm = sb("sm", [128, 64])
        ARENA_COLS = 40000
        arena = sb("arena", [128, ARENA_COLS])

        ps = [pst("ps%d" % i, [128, 512]) for i in range(4)]
        psS = pst("psS", [128, 1024])
        psb = [pst("psb%d" % i, [128, 1024], BF16) for i in range(2)]
        rot = {"ps": 0, "psb": 0, "cast": 0, "q": 0}

        def getps():
            i = rot["ps"] % 4
            rot["ps"] += 1
            return ps[i], ("ps", i)

        def getpsb():
            i = rot["psb"] % 2
            rot["psb"] += 1
            return psb[i], ("psb", i)

        aoff = [0]

        def areset():
            aoff[0] = 0

        def af32(cols):
            o = aoff[0]
            aoff[0] += cols
            assert aoff[0] <= ARENA_COLS, aoff[0]
            return arena[:, o:o + cols]

        def abf(cols):
            n = (cols + 1) // 2
            return af32(n).bitcast(BF16)[:, 0:cols]

        def dma(q, out, in_, r, w):
            S.add(q, lambda e, o=out, i=in_: e.dma_start(out=o, in_=i), r=r, w=w, dma=True)

        def dmaq():
            rot["q"] += 1
            return "sp" if rot["q"] % 2 else "pool"

        def act(out, in_, func, r, w, bias=None, scale=None, accum=None):
            kw = {}
            if bias is not None:
                kw["bias"] = bias
            if scale is not None:
                kw["scale"] = scale
            if accum is not None:
                kw["accum_out"] = accum
            S.add("act", lambda e: e.activation(out, in_, func, **kw), r=r, w=w)

        def ts(eng, out, in0, s1, s2, op0, op1, r, w):
            if s2 is None:
                s2, op1 = (1.0, ALU.mult) if op0 == ALU.add else (0.0, ALU.add)
            S.add(eng, lambda e: e.tensor_scalar(out, in0, s1, s2, op0, op1), r=r, w=w)

        def rsqrt_col(col, key):
            S.add("act", lambda e: e.activation(col, col, AF.Sqrt), r=(key,), w=(key,))
            S.add("dve", lambda e: e.reciprocal(col, col), r=(key,), w=(key,))

        def tt(eng, out, in0, in1, op, r, w):
            S.add(eng, lambda e: e.tensor_tensor(out, in0, in1, op), r=r, w=w)

        def stt(eng, out, in0, sc, in1, op0, op1, r, w):
            S.add(eng, lambda e: e.scalar_tensor_tensor(out, in0, sc, in1, op0, op1), r=r, w=w)

        def mm(out, lhsT, rhs, start, stop, r, w):
            S.add("pe", lambda e: e.matmul(out, lhsT, rhs, start=start, stop=stop), r=r, w=w)

        def tr(out, in_, r, w):
            S.add("pe", lambda e: e.transpose(out, in_, ident[:]), r=tuple(r) + ("ident",), w=w)

        def cast(out, in_, r, w, eng=None):
            if eng is None:
                rot["cast"] += 1
                eng = ("act", "pool")[rot["cast"] % 2]
            if eng == "act":
                S.add("act", lambda e: e.copy(out, in_), r=r, w=w)
            else:
                S.add(eng, lambda e: e.tensor_copy(out, in_), r=r, w=w)

        def memset(eng, ap, v, w):
            S.add(eng, lambda e: e.memset(ap, v), w=w)

        def stop(tag, dumps=()):
            if KSTOP != tag:
                return
            S.barrier()
            off = 0
            for ap_, n_ in dumps:
                dma("sp", dbg[:, off:off + n_], ap_, r=(), w=(("dbg", off),))
                off += n_
            raise _Stop()

        def body():
            for (t_, d_, k_) in ((ident, ident_d, "ident"), (tri, tri_d, "tri"), (ones, ones_d, "ones"),
                                 (ngt_s, ngT, "ngt"), (cw, convw, "cw"),
                                 (cb_, convb, "cb"), (bif, bif_bc, "bif"), (mhg, mhg_bc, "mhg"),
                                 (pneg, padneg, "pneg"), (tval, tilevalid, "tval"),
                                 (ntile, negtile, "ntile"), (amask_s, amask, "amask")):
                dma("sp", t_[:], d_, r=(), w=(k_,))
            for k4 in range(4):
                dma("sp", wifs[:, 4 * k4:4 * k4 + 4, :],
                    w_in[k4 * 512:(k4 + 1) * 512, C_MI:C_MI + 8].rearrange("(k p) c -> p k c", p=128), r=(), w=("wifs",))
            cast(wif[:], wifs[:], r=("wifs",), w=("wif",), eng="dve")
            for h in range(4):
                for b in range(2):
                    memset("dve", state[h][b][:], 0.0, w=(("st", h, b),))

            areset()
            sc_in = af32(16)
            scv = af32(16)
            modrow = af32(3 * D)[0:1, :]
            badar = af32(3 * D)[0:1, :]
            stgA = [af32(3072) for _ in range(2)]
            dma("sp", sc_in, cT, r=(), w=("sc_in",))
            dma("sp", badar, b_ada, r=(), w=("badar",))
            act(scv, sc_in, AF.Silu, r=("sc_in",), w=("scv",))
            banks = [(ps[0], ("ps", 0), 0), (ps[1], ("ps", 1), 0), (ps[2], ("ps", 2), 0),
                     (ps[3], ("ps", 3), 0), (psS, ("psS",), 0), (psS, ("psS",), 512)]
            for half in range(2):
                for k in range(16):
                    st_ = stgA[k % 2]
                    key = ("stgA", k % 2)
                    dma(dmaq(), st_, w_ada[k * 128:(k + 1) * 128, half * 3072:(half + 1) * 3072], r=(), w=(key,))
                    for cbk in range(6):
                        t_, pk, off = banks[cbk]
                        mm(t_[0:1, off:off + 512], scv[:, k:k + 1], st_[:, cbk * 512:(cbk + 1) * 512],
                           k == 0, k == 15, r=(key, "scv"), w=(pk,))
                for cbk in range(6):
                    t_, pk, off = banks[cbk]
                    c0 = half * 3072 + cbk * 512
                    tt("dve", modrow[:, c0:c0 + 512], t_[0:1, off:off + 512], badar[:, c0:c0 + 512], ALU.add,
                       r=(pk, "badar"), w=("modrow",))
            pcol, pck = getps()
            for c in range(32):
                mm(pcol[:, c:c + 1], modrow[0:1, c * 128:(c + 1) * 128], ones[0:1, 0:1], True, True,
                   r=("modrow", "ones"), w=(pck,))
            cast(shift[:], pcol[:, 0:16], r=(pck,), w=("shift",), eng="dve")
            stt("dve", gs[:], pcol[:, 16:32], 1.0, ngt_s[:], ALU.add, ALU.mult, r=(pck, "ngt"), w=("gs",))
            for cbk in range(4):
                t_, pk = getps()
                mm(t_[:, 0:512], ones[0:1, :], modrow[0:1, 2 * D + cbk * 512:2 * D + (cbk + 1) * 512], True, True,
                   r=("modrow", "ones"), w=(pk,))
                cast(gate_bc[:, cbk * 512:(cbk + 1) * 512], t_[:, 0:512], r=(pk,), w=("gate_bc",), eng="act")
            S.barrier()
            stop("S1", [(gs[:], 16), (shift[:], 16), (gate_bc[:, 0:64], 64)])

            def frontend1(tau, xbufs, xn, xnk, i2):
                xt = xbufs[i2 % len(xbufs)]
                xk = ("xt", i2 % len(xbufs))
                dma(dmaq(), xt, xl[tau * 128:(tau + 1) * 128, :], r=(), w=(xk,))
                memset("dve", sm[:, 0:1], 0.0, w=("ss",))
                act(xn, xt, AF.Square, r=(xk, "ss"), w=(xnk, "ss"), accum=sm[:, 0:1])
                ts("dve", sm[:, 1:2], sm[:, 0:1], 1.0 / D, EPS, ALU.mult, ALU.add, r=("ss",), w=("rstd",))
                rsqrt_col(sm[:, 1:2], "rstd")
                act(xn, xt, AF.Identity, r=(xk, "rstd"), w=(xnk,), scale=sm[:, 1:2])

            def frontend2(dstf, dkey, xn, xnk, halves=(0, 1)):
                for hh in halves:
                    pt, ptk = psb[hh], ("psb", hh)
                    for kk in range(8):
                        k = hh * 8 + kk
                        tr(pt[:, kk * 128:(kk + 1) * 128], xn[:, k * 128:(k + 1) * 128], r=(xnk,), w=(ptk,))
                    for kk in range(8):
                        k = hh * 8 + kk
                        if hh == 0:
                            act(dstf(k), pt[:, kk * 128:(kk + 1) * 128], AF.Identity, r=(ptk, "gs", "shift"),
                                w=(dkey[0],), bias=shift[:, k:k + 1], scale=gs[:, k:k + 1])
                        else:
                            ts("dve", dstf(k), pt[:, kk * 128:(kk + 1) * 128], gs[:, k:k + 1], shift[:, k:k + 1],
                               ALU.mult, ALU.add, r=(ptk, "gs", "shift"), w=(dkey[1],))

            def frontend(tau, dstf, dkey, xbufs, xn, i2):
                frontend1(tau, xbufs, xn, "xn", i2)
                frontend2(dstf, dkey, xn, "xn")

            def load_w(dst, src_rows, nk, stgs, skey, dkey, cast_eng=None, q=None):
                for k in range(nk):
                    st_ = stgs[k % len(stgs)]
                    key = (skey, k % len(stgs))
                    dma(q or dmaq(), st_, src_rows(k), r=(), w=(key,))
                    cast(dst(k), st_, r=(key,), w=(dkey,), eng=cast_eng)

            def gates(tau, hsrc, hkey, own_i):
                pg, pgk = getps()
                for k in range(16):
                    mm(pg[:, 0:8], hsrc(k), wif[:, k, :], k == 0, k == 15, r=tuple(hkey) + ("wif",), w=(pgk,))
                gl = sm[:, 8:16]
                tt("dve", gl, pg[:, 0:8], bif[:], ALU.add, r=(pgk, "bif"), w=("gl",))
                ts("dve", sm[:, 16:20], gl[:, 0:4], pneg[:, tau:tau + 1], None, ALU.add, None, r=("gl", "pneg"), w=("ig",))
                act(sm[:, 20:24], gl[:, 4:8], AF.Exp, r=("gl",), w=("e1",), scale=-1.0)
                ts("dve", sm[:, 20:24], sm[:, 20:24], 1.0, None, ALU.add, None, r=("e1",), w=("e1",))
                act(sm[:, 24:28], sm[:, 20:24], AF.Ln, r=("e1",), w=("lf",))
                ts("dve", sm[:, 24:28], sm[:, 24:28], -1.0, None, ALU.mult, None, r=("lf",), w=("lf",))
                pc, pckk = getps()
                mm(pc[:, 0:4], tri[:], sm[:, 24:28], True, True, r=("lf", "tri"), w=(pckk,))
                mm(pc[:, 4:8], ones[:], sm[:, 24:28], True, True, r=("lf", "ones"), w=(pckk,))
                tt("dve", sm[:, 28:32], sm[:, 16:20], pc[:, 0:4], ALU.subtract, r=("ig", pckk), w=("t1",))
                tt("dve", sm[:, 32:36], sm[:, 28:32], pc[:, 4:8], ALU.add, r=("t1", pckk), w=("t2",))
                if own_i is None:
                    wk, eB = sm[:, 36:40], sm[:, 40:44]
                    wkk, eBk = "wk", "eB"
                else:
                    wk, eB = wk_o[:, own_i, :], eB_o[:, own_i, :]
                    wkk = eBk = ("gown", own_i)
                act(wk, sm[:, 32:36], AF.Exp, r=("t2",), w=(wkk,))
                ts("dve", wk, wk, 0.0625, None, ALU.mult, None, r=(wkk,), w=(wkk,))
                act(eB, pc[:, 4:8], AF.Exp, r=(pckk,), w=(eBk,))
                if own_i is not None:
                    act(wa_o[:, own_i, :], sm[:, 28:32], AF.Exp, r=("t1",), w=(wkk,))
                    ts("dve", wa_o[:, own_i, :], wa_o[:, own_i, :], 0.0625, None, ALU.mult, None, r=(wkk,), w=(wkk,))
                    act(ebc_o[:, own_i, :], pc[:, 0:4], AF.Exp, r=(pckk,), w=(wkk,))
                return wk, eB, wkk, eBk

            def conv_silu(pre, prek, cblk, accb, acck, out, outk):
                ts("dve", accb, pre[:, 3:515], cw[:, cblk, 3:4], cb_[:, cblk:cblk + 1], ALU.mult, ALU.add,
                   r=(prek, "cw", "cb"), w=(acck,))
                for tap in range(3):
                    stt("dve", accb, pre[:, tap:tap + 512], cw[:, cblk, tap:tap + 1], accb, ALU.mult, ALU.add,
                        r=(prek, acck, "cw"), w=(acck,))
                act(out, accb, AF.Silu, r=(acck,), w=(outk,))

            def state_update(h, kT_blk, kTk, vaug, vk, wk_col, wkk, eB_col, eBk, kpp, kppk, refresh_bf):
                pt, ptk = getpsb()
                for blk in range(2):
                    tr(pt[:, blk * 128:(blk + 1) * 128], kT_blk(blk), r=(kTk[blk],), w=(ptk,))
                ts("dve", kpp, pt[:, 0:256], wk_col, None, ALU.mult, None, r=(ptk, wkk), w=(kppk,))
                for blk in range(2):
                    p_, pk = getps()
                    mm(p_[:, 0:257], kpp[:, blk * 128:(blk + 1) * 128], vaug, True, True, r=(kppk, vk), w=(pk,))
                    stt("dve", state[h][blk][:], state[h][blk][:], eB_col, p_[:, 0:257], ALU.mult, ALU.add,
                        r=(pk, eBk, ("st", h, blk)), w=(("st", h, blk),))
                    if refresh_bf:
                        cast(ctbf[h][blk][:], state[h][blk][:], r=(("st", h, blk),), w=(("ct", h, blk),), eng="act")

            areset()
            WA = abf(16 * 2048).rearrange("p (k c) -> p k c", k=16)
            hTgA = [abf(16 * 512).rearrange("p (k c) -> p k c", k=16) for _ in range(2)]
            xn = abf(D)
            xns = [xn, abf(D)]
            kT = abf(8 * 512).rearrange("p (b c) -> p b c", b=8)
            vaugs = [[abf(258)[:, 0:257] for _ in range(4)] for _ in range(4)]
            kpps = [abf(256) for _ in range(2)]
            stgs = [af32(2048) for _ in range(2)]
            xbufs = stgs
            kpre = af32(8 * 515).rearrange("p (b c) -> p b c", b=8)
            accb = af32(512)
            gtmp = af32(4 * 48).rearrange("p (i c) -> p i c", i=4)
            load_w(lambda k: WA[:, k, :], lambda k: w_in[k * 128:(k + 1) * 128, C_MK:C_MK + 2048], 16, stgs, "xt", "WA")
            memset("pool", kpre[:, :, 0:3], 0.0, w=tuple(("kpre", c) for c in range(8)))
            for i in range(4):
                for h in range(4):
                    memset("pool", vaugs[i][h][:, 256:257], 1.0, w=(("vaug", i, h),))
            NG = NPRE // 4
            fe_cnt = [0]

            def a_hT(g):
                return hT_halo if g == NG - 1 else hTgA[g % 2]

            def a_keys(g, i):
                return (("hTg", g % 2, i, 0), ("hTg", g % 2, i, 1))

            def a_gates1(g, i):
                tau = 4 * g + i
                hT = a_hT(g)
                gt = gtmp[:, i, :]
                pg, pgk = ps[2 + i % 2], ("ps", 2 + i % 2)
                for k in range(16):
                    mm(pg[:, 0:8], hT[:, k, i * 128:(i + 1) * 128], wif[:, k, :], k == 0, k == 15,
                       r=a_keys(g, i) + ("wif",), w=(pgk,))
                tt("dve", gt[:, 0:8], pg[:, 0:8], bif[:], ALU.add, r=(pgk, "bif"), w=(("g_gl", i),))
                ts("dve", gt[:, 8:12], gt[:, 0:4], pneg[:, tau:tau + 1], None, ALU.add, None, r=(("g_gl", i), "pneg"), w=(("g_ig", i),))
                act(gt[:, 12:16], gt[:, 4:8], AF.Exp, r=(("g_gl", i),), w=(("g_lf", i),), scale=-1.0)
                ts("dve", gt[:, 12:16], gt[:, 12:16], 1.0, None, ALU.add, None, r=(("g_lf", i),), w=(("g_lf", i),))
                act(gt[:, 12:16], gt[:, 12:16], AF.Ln, r=(("g_lf", i),), w=(("g_lf", i),))
                ts("dve", gt[:, 12:16], gt[:, 12:16], -1.0, None, ALU.mult, None, r=(("g_lf", i),), w=(("g_lf", i),))

            def a_gates2(g, i):
                gt = gtmp[:, i, :]
                pc, pck_ = psS[:, (i % 2) * 512:(i % 2) * 512 + 8], ("psS", i % 2)
                mm(pc[:, 0:4], tri[:], gt[:, 12:16], True, True, r=(("g_lf", i), "tri"), w=(pck_,))
                mm(pc[:, 4:8], ones[:], gt[:, 12:16], True, True, r=(("g_lf", i), "ones"), w=(pck_,))
                tt("dve", gt[:, 16:20], gt[:, 8:12], pc[:, 0:4], ALU.subtract, r=(("g_ig", i), pck_), w=(("g_t", i),))
                tt("dve", gt[:, 16:20], gt[:, 16:20], pc[:, 4:8], ALU.add, r=(("g_t", i), pck_), w=(("g_t", i),))
                act(gt[:, 20:24], gt[:, 16:20], AF.Exp, r=(("g_t", i),), w=(("g_wk", i),))
                ts("dve", gt[:, 20:24], gt[:, 20:24], 0.0625, None, ALU.mult, None, r=(("g_wk", i),), w=(("g_wk", i),))
                act(gt[:, 24:28], pc[:, 4:8], AF.Exp, r=(pck_,), w=(("g_eB", i),))

            def a_fe1(g, i):
                frontend1(4 * g + i, xbufs, xns[i % 2], ("xnA", i % 2), fe_cnt[0])
                fe_cnt[0] += 1

            def a_fe2(g, i, halves):
                hT = a_hT(g)
                frontend2(lambda k: hT[:, k, i * 128:(i + 1) * 128], a_keys(g, i), xns[i % 2], ("xnA", i % 2), halves)

            def a_U1(g, i, h):
                gt = gtmp[:, i, :]
                pt = ps[h % 2][:].bitcast(BF16)
                ptk = ("ps", h % 2)
                for blk in range(2):
                    tr(pt[:, blk * 128:(blk + 1) * 128], kT[:, 2 * h + blk, i * 128:(i + 1) * 128],
                       r=(("kT", 2 * h + blk),), w=(ptk,))
                kp, kpk = kpps[h % 2], ("kpp", h % 2)
                ts("dve", kp, pt[:, 0:256], gt[:, 20 + h:21 + h], None, ALU.mult, None, r=(ptk, ("g_wk", i)), w=(kpk,))

            def a_U2(g, i, h):
                gt = gtmp[:, i, :]
                kp, kpk = kpps[h % 2], ("kpp", h % 2)
                for blk in range(2):
                    pk = ("psS", blk)
                    p_ = psS[:, blk * 512:blk * 512 + 257]
                    mm(p_, kp[:, blk * 128:(blk + 1) * 128], vaugs[i][h], True, True, r=(kpk, ("vaug", i, h)), w=(pk,))
                    stt("dve", state[h][blk][:], state[h][blk][:], gt[:, 24 + h:25 + h], p_, ALU.mult, ALU.add,
                        r=(pk, ("g_eB", i), ("st", h, blk)), w=(("st", h, blk),))

            def a_stage3(g):
                nxt = g + 1 < NG
                if nxt:
                    a_fe1(g + 1, 0)
                for i in range(4):
                    if nxt and i + 1 < 4:
                        a_fe1(g + 1, i + 1)
                    a_U1(g, i, 0)
                    if nxt:
                        a_fe2(g + 1, i, (0,))
                    a_U1(g, i, 1)
                    a_U2(g, i, 0)
                    if nxt:
                        a_fe2(g + 1, i, (1,))
                    a_U1(g, i, 2)
                    a_U2(g, i, 1)
                    a_U1(g, i, 3)
                    a_U2(g, i, 2)
                    a_U2(g, i, 3)

            for i in range(4):
                a_fe1(0, i)
                a_fe2(0, i, (0, 1))
            for g in range(NG):
                hT = a_hT(g)
                hgk = tuple(k_ for i_ in range(4) for k_ in a_keys(g, i_))
                for cbk in range(8):
                    p_, pk = ps[cbk % 2], ("ps", cbk % 2)
                    for k in range(16):
                        mm(p_[:, 0:512], WA[:, k, cbk * 128:(cbk + 1) * 128], hT[:, k, :], k == 0, k == 15,
                           r=("WA",) + hgk, w=(pk,))
                    cast(kpre[:, cbk, 3:515], p_[:, 0:512], r=(pk,), w=(("kpre", cbk),), eng="act")
                    conv_silu(kpre[:, cbk, :], ("kpre", cbk), 8 + cbk, accb, "accb", kT[:, cbk, :], ("kT", cbk))
                    ts("dve", kpre[:, cbk, 0:3], kpre[:, cbk, 512:515], tval[:, 4 * g + 3:4 * g + 4], None, ALU.mult, None,
                       r=(("kpre", cbk), "tval"), w=(("kpre", cbk),))
                for i in range(4):
                    a_gates1(g, i)
                for i in range(4):
                    for half in range(2):
                        p_, pk = ps[2 + half], ("ps", 2 + half)
                        for k in range(16):
                            mm(p_[:, 0:512], hT[:, k, i * 128:(i + 1) * 128], WA[:, k, 1024 + half * 512:1024 + (half + 1) * 512],
                               k == 0, k == 15, r=("WA",) + a_keys(g, i), w=(pk,))
                        for hh in range(2):
                            h = 2 * half + hh
                            cast(vaugs[i][h][:, 0:256], p_[:, hh * 256:(hh + 1) * 256], r=(pk,), w=(("vaug", i, h),),
                                 eng=("act", "dve")[hh])
                for i in range(4):
                    a_gates2(g, i)
                a_stage3(g)
            S.barrier()
            stop("A", [(state[0][0][:], 257), (state[3][1][:], 257), (sm[:], 64)])

            areset()
            hT_own = abf(16 * 2048).rearrange("p (k c) -> p k c", k=16)
            mark_b = aoff[0]
            xn = abf(D)
            xbufs = [af32(D) for _ in range(2)]
            for i in range(NOWN):
                frontend(NPRE + i, lambda k, i=i: hT_own[:, k, i * 128:(i + 1) * 128], (("hTo", i, 0), ("hTo", i, 1)), xbufs, xn, i)
                gates(NPRE + i, lambda k, i=i: hT_own[:, k, i * 128:(i + 1) * 128], (("hTo", i, 0), ("hTo", i, 1)), i)
            for h in range(4):
                for b in range(2):
                    cast(ctbf[h][b][:], state[h][b][:], r=(("st", h, b),), w=(("ct", h, b),), eng="act")
            S.barrier()
            stop("B0", [(wk_o[:].rearrange("p a b -> p (a b)"), 64), (wa_o[:].rearrange("p a b -> p (a b)"), 64), (eB_o[:].rearrange("p a b -> p (a b)"), 64), (ebc_o[:].rearrange("p a b -> p (a b)"), 64), (state[0][0][:], 257)])

            aoff[0] = mark_b
            WG = abf(16 * 1280).rearrange("p (k c) -> p k c", k=16)
            qkT = [abf(4 * 512).rearrange("p (b c) -> p b c", b=4) for _ in range(2)]
            vaug1 = [abf(258)[:, 0:257] for _ in range(2)]
            kpps = [abf(256) for _ in range(2)]
            STb = [abf(128) for _ in range(2)]
            ybf = [abf(256) for _ in range(2)]
            yTt = [abf(256).rearrange("p (b c) -> p b c", b=2) for _ in range(2)]
            stg1 = [af32(1280) for _ in range(2)]
            qkpre = af32(4 * 515).rearrange("p (b c) -> p b c", b=4)
            accb = af32(512)
            sgo = af32(256)
            slz = af32(256)
            Gt = [af32(256) for _ in range(2)]
            junk = af32(256)
            w5 = w_in[:, 0:5120].rearrange("r (s c) -> r s c", c=1024)
            for vb in range(2):
                memset("pool", vaug1[vb][:, 256:257], 1.0, w=(("vaug1", vb),))
            ts("dve", mhg[:], mhg[:], 0.5, None, ALU.mult, None, r=("mhg",), w=("mhg",))
            PS0, PS1, PS2, PS3 = (ps[0], ("ps", 0)), (ps[1], ("ps", 1)), (ps[2], ("ps", 2)), (ps[3], ("ps", 3))
            for h in range(4):
                load_w(lambda k: WG[:, k, :].rearrange("p (s c) -> p s c", c=256),
                       lambda k, h=h: w5[k * 128:(k + 1) * 128, :, h * 256:(h + 1) * 256],
                       16, [s_.rearrange("p (s c) -> p s c", c=256) for s_ in stg1], "stg1", "WG")
                for blk in range(4):
                    p_, pk = (PS0, PS1)[blk % 2]
                    for k in range(16):
                        mm(p_[:, 0:3], WG[:, k, blk * 128:(blk + 1) * 128], hT_halo[:, k, 509:512], k == 0, k == 15,
                           r=("WG",), w=(pk,))
                    ts("dve", qkpre[:, blk, 0:3], p_[:, 0:3], tval[:, NPRE - 1:NPRE], None, ALU.mult, None,
                       r=(pk, "tval"), w=(("qkpre", blk),))

                def b1_proj(g, h=h):
                    qk = qkT[g % 2]
                    for blk in range(4):
                        p_, pk = (PS0, PS1)[blk % 2]
                        for k in range(16):
                            mm(p_[:, 0:512], WG[:, k, blk * 128:(blk + 1) * 128], hT_own[:, k, g * 512:(g + 1) * 512],
                               k == 0, k == 15, r=("WG",), w=(pk,))
                        cast(qkpre[:, blk, 3:515], p_[:, 0:512], r=(pk,), w=(("qkpre", blk),), eng="act")
                        cidx = (2 * h + blk) if blk < 2 else (8 + 2 * h + blk - 2)
                        conv_silu(qkpre[:, blk, :], ("qkpre", blk), cidx, accb, "accb", qk[:, blk, :], ("qkT", g % 2, blk))
                        cast(qkpre[:, blk, 0:3], qkpre[:, blk, 512:515], r=(("qkpre", blk),), w=(("qkpre", blk),), eng="dve")

                def b1_P(t, h=h):
                    vb = t % 2
                    pA, pAk = PS0
                    for k in range(16):
                        mm(pA[:, 0:512], hT_own[:, k, t * 128:(t + 1) * 128], WG[:, k, 512:1024], k == 0, k == 15,
                           r=("WG",), w=(pAk,))
                    pB, pBk = PS1
                    for k in range(16):
                        mm(pB[:, 0:256], hT_own[:, k, t * 128:(t + 1) * 128], WG[:, k, 1024:1280], k == 0, k == 15,
                           r=("WG",), w=(pBk,))
                    cast(vaug1[vb][:, 0:256], pA[:, 0:256], r=(pAk,), w=(("vaug1", vb),), eng="act")
                    act(sgo, pA[:, 256:512], AF.Tanh, r=(pAk,), w=("sgo",), scale=0.5)
                    act(slz, pB[:, 0:256], AF.Silu, r=(pBk,), w=("slz",))
                    stt("dve", Gt[vb], sgo, 1.0, slz, ALU.add, ALU.mult, r=("sgo", "slz"), w=(("Gt", vb),))
                    tt("dve", Gt[vb], Gt[vb], mhg[:, h * 256:(h + 1) * 256], ALU.mult, r=(("Gt", vb), "mhg"), w=(("Gt", vb),))

                def b1_S(t, h=h):
                    vb = t % 2
                    gp = (t // 4) % 2
                    tsl = slice((t % 4) * 128, (t % 4 + 1) * 128)
                    pS, pSk = PS2
                    for blk in range(2):
                        mm(pS[:, 0:128], qkT[gp][:, 2 + blk, tsl], qkT[gp][:, blk, tsl], blk == 0, blk == 1,
                           r=(("qkT", gp, blk), ("qkT", gp, 2 + blk)), w=(pSk,))
                    stt("dve", STb[vb], pS[:, 0:128], wa_o[:, t, h:h + 1], tri[:], ALU.mult, ALU.mult,
                        r=(pSk, ("gown", t), "tri"), w=(("STb", vb),))

                def b1_N(t, h=h):
                    vb = t % 2
                    gp = (t // 4) % 2
                    tsl = slice((t % 4) * 128, (t % 4 + 1) * 128)
                    pN, pNk = PS3
                    mm(pN[:, 0:257], STb[vb], vaug1[vb], True, False, r=(("STb", vb), ("vaug1", vb)), w=(pNk,))
                    for blk in range(2):
                        mm(pN[:, 0:257], qkT[gp][:, blk, tsl], ctbf[h][blk][:], False, blk == 1,
                           r=(("qkT", gp, blk), ("ct", h, blk)), w=(pNk,))

                def b1_Utr(t, h=h):
                    vb = t % 2
                    gp = (t // 4) % 2
                    tsl = slice((t % 4) * 128, (t % 4 + 1) * 128)
                    pt, ptk = psb[0], ("psb", 0)
                    for blk in range(2):
                        tr(pt[:, blk * 128:(blk + 1) * 128], qkT[gp][:, 2 + blk, tsl], r=(("qkT", gp, 2 + blk),), w=(ptk,))
                    ts("dve", kpps[vb], pt[:, 0:256], wk_o[:, t, h:h + 1], None, ALU.mult, None,
                       r=(ptk, ("gown", t)), w=(("kpp", vb),))

                def b1_Umm(t, h=h):
                    vb = t % 2
                    for blk in range(2):
                        pk = ("psS", blk)
                        p_ = psS[:, blk * 512:blk * 512 + 257]
                        mm(p_, kpps[vb][:, blk * 128:(blk + 1) * 128], vaug1[vb], True, True,
                           r=(("kpp", vb), ("vaug1", vb)), w=(pk,))
                        stt("dve", state[h][blk][:], state[h][blk][:], eB_o[:, t, h:h + 1], p_, ALU.mult, ALU.add,
                            r=(pk, ("gown", t), ("st", h, blk)), w=(("st", h, blk),))
                        cast(ctbf[h][blk][:], state[h][blk][:], r=(("st", h, blk),), w=(("ct", h, blk),), eng="act")

                def b1_Ychain(t, h=h):
                    vb = t % 2
                    pN, pNk = PS3
                    gk = ("gown", t)
                    tt("dve", sm[:, 44:45], pN[:, 256:257], ebc_o[:, t, h:h + 1], ALU.mult, r=(pNk, gk), w=("d1",))
                    ts("dve", sm[:, 54:55], sm[:, 44:45], -1.0, None, ALU.mult, None, r=("d1",), w=("d1n",))
                    tt("dve", sm[:, 44:45], sm[:, 44:45], sm[:, 54:55], ALU.max, r=("d1", "d1n"), w=("d1",))
                    ts("dve", sm[:, 44:45], sm[:, 44:45], 1.0, 1.0, ALU.max, ALU.mult, r=("d1",), w=("d1",))
                    S.add("dve", lambda e: e.reciprocal(sm[:, 44:45], sm[:, 44:45]), r=("d1",), w=("d1",))
                    tt("dve", sm[:, 45:46], ebc_o[:, t, h:h + 1], sm[:, 44:45], ALU.mult, r=("d1", gk), w=("rr",))
                    memset("dve", sm[:, 46:47], 0.0, w=("ss2",))
                    act(junk, pN[:, 0:256], AF.Square, r=(pNk, "rr", "ss2"), w=("junk", "ss2"), scale=sm[:, 45:46],
                        accum=sm[:, 46:47])
                    ts("dve", sm[:, 47:48], sm[:, 46:47], 1.0 / 256, EPS, ALU.mult, ALU.add, r=("ss2",), w=("r2",))
                    rsqrt_col(sm[:, 47:48], "r2")
                    tt("dve", sm[:, 47:48], sm[:, 47:48], sm[:, 45:46], ALU.mult, r=("r2", "rr"), w=("r2",))
                    stt("dve", ybf[vb], pN[:, 0:256], sm[:, 47:48], Gt[vb], ALU.mult, ALU.mult,
                        r=(pNk, "r2", ("Gt", vb)), w=(("ybf", vb),))

                def b1_Ytr(t, h=h):
                    vb = t % 2
                    pt, ptk = psb[1], ("psb", 1)
                    for blk in range(2):
                        tr(pt[:, blk * 128:(blk + 1) * 128], ybf[vb][:, blk * 128:(blk + 1) * 128], r=(("ybf", vb),), w=(ptk,))
                    cast(yTt[vb].rearrange("p b c -> p (b c)"), pt[:, 0:256], r=(ptk,), w=(("yTt", vb),), eng="act")
                    dma(dmaq(), yT_d[2 * h:2 * h + 2, :, t * 128:(t + 1) * 128].rearrange("b p t -> p b t"), yTt[vb],
                        r=(("yTt", vb),), w=(("yTd", 2 * h, t),))

                b1_proj(0)
                b1_P(0)
                b1_S(0)
                for t in range(NOWN):
                    if t + 1 < NOWN:
                        if (t + 1) % 4 == 0:
                            b1_proj((t + 1) // 4)
                        b1_P(t + 1)
                        b1_S(t + 1)
                    b1_N(t)
                    b1_Utr(t)
                    if t > 0:
                        b1_Ytr(t - 1)
                    b1_Umm(t)
                    b1_Ychain(t)
                b1_Ytr(NOWN - 1)
            S.barrier()
            stop("B1", [(state[0][0][:], 257)])

            aoff[0] = mark_b
            WG2s = [abf(16 * 512).rearrange("p (k c) -> p k c", k=16) for _ in range(2)]
            akT = abf(2560)
            aqT = abf(2048)
            Vt = abf(20 * 128).rearrange("p (t c) -> p t c", t=20)
            slzA = abf(16 * 128).rearrange("p (t c) -> p t c", t=16)
            Pb = [abf(640) for _ in range(2)]
            PT = [abf(640) for _ in range(2)]
            yab = [abf(128) for _ in range(2)]
            yaT = [abf(128) for _ in range(2)]
            stg2 = [af32(512) for _ in range(4)]
            rb = af32(640)
            bm = af32(640)
            sbuf_s = [af32(640) for _ in range(2)]
            wA = w_in[:, C_AQ:C_AQ + 4096].rearrange("r (s c) -> r s c", c=1024)
            def b2_load(h, cast_eng=None, q=None):
                W_ = WG2s[h % 2]
                load_w(lambda k: W_[:, k, :].rearrange("p (s c) -> p s c", c=128),
                       lambda k: wA[k * 128:(k + 1) * 128, :, h * 128:(h + 1) * 128],
                       16, [s_.rearrange("p (s c) -> p s c", c=128) for s_ in stg2], "stg2", ("WG2", h % 2),
                       cast_eng=cast_eng, q=q)

            b2_load(0)
            for h in range(8):
                WG2 = WG2s[h % 2]
                WK = ("WG2", h % 2)
                dma("sp", rb, relmat[h], r=(), w=("rb",))
                tt("pool", bm, rb, amask_s[:], ALU.add, r=("rb", "amask"), w=("bm",))
                for g5 in range(5):
                    src = hT_halo if g5 == 0 else hT_own[:, :, (g5 - 1) * 512:g5 * 512]
                    srk = ()
                    p_, pk = getps()
                    for k in range(16):
                        mm(p_[:, 0:512], WG2[:, k, 128:256], src[:, k, :], k == 0, k == 15, r=(WK,) + srk, w=(pk,))
                    cast(akT[:, g5 * 512:(g5 + 1) * 512], p_[:, 0:512], r=(pk,), w=(("akT", g5),), eng="act")
                    if g5 > 0:
                        p_, pk = getps()
                        for k in range(16):
                            mm(p_[:, 0:512], WG2[:, k, 0:128], src[:, k, :], k == 0, k == 15, r=(WK,) + srk, w=(pk,))
                        act(aqT[:, (g5 - 1) * 512:g5 * 512], p_[:, 0:512], AF.Copy, r=(pk,), w=(("aqT", g5 - 1),),
                            scale=float(128 ** -0.5))
                    for i in range(4):
                        ttile = g5 * 4 + i
                        p_, pk = getps()
                        for k in range(16):
                            mm(p_[:, 0:256], src[:, k, i * 128:(i + 1) * 128], WG2[:, k, 256:512], k == 0, k == 15,
                               r=(WK,) + srk, w=(pk,))
                        cast(Vt[:, ttile, :], p_[:, 0:128], r=(pk,), w=(("Vt", ttile),), eng="act")
                        if g5 > 0:
                            act(slzA[:, ttile - 4, :], p_[:, 128:256], AF.Silu, r=(pk,), w=(("slzA", ttile - 4),))
                if h + 1 < 8:
                    b2_load(h + 1, cast_eng="pool", q="sp")

                def att_s1(t, h=h):
                    vb = t % 2
                    gq = ("aqT", t // 4)
                    kk0 = tuple(("akT", x) for x in sorted({t // 4, (t + 3) // 4, (t + 4) // 4}))
                    mm(psS[:, 0:512], aqT[:, t * 128:(t + 1) * 128], akT[:, t * 128:t * 128 + 512], True, True,
                       r=(gq,) + kk0, w=("psS",))
                    mm(psS[:, 512:640], aqT[:, t * 128:(t + 1) * 128], akT[:, t * 128 + 512:t * 128 + 640], True, True,
                       r=(gq,) + kk0, w=("psS",))
                    sbt = sbuf_s[vb]
                    sk = ("sbt", vb)
                    tt("dve", sbt, psS[:, 0:640], bm, ALU.add, r=("psS", "bm"), w=(sk,))
                    for kb in range(max(0, 4 - t)):
                        lt = NPRE - 4 + t + kb
                        ts("dve", sbt[:, kb * 128:(kb + 1) * 128], sbt[:, kb * 128:(kb + 1) * 128], ntile[:, lt:lt + 1], None,
                           ALU.add, None, r=(sk, "ntile"), w=(sk,))
                    S.add("dve", lambda e, sbt=sbt: e.reduce_max(sm[:, 48:49], sbt, AX.X), r=(sk,), w=("mx",))
                    ts("dve", sm[:, 48:49], sm[:, 48:49], -1.0, None, ALU.mult, None, r=("mx",), w=("mx",))
                    rsc = sm[:, 56 + vb:57 + vb]
                    memset("dve", rsc, 0.0, w=(("rsum", vb),))
                    act(Pb[vb], sbt, AF.Exp, r=(sk, "mx", ("rsum", vb)), w=(("Pb", vb), ("rsum", vb)), bias=sm[:, 48:49],
                        accum=rsc)

                def att_s2(t, h=h):
                    vb = t % 2
                    rsc = sm[:, 56 + vb:57 + vb]
                    rrc = sm[:, 58 + vb:59 + vb]
                    pt, ptk = getpsb()
                    for kb in range(5):
                        tr(pt[:, kb * 128:(kb + 1) * 128], Pb[vb][:, kb * 128:(kb + 1) * 128], r=(("Pb", vb),), w=(ptk,))
                    cast(PT[vb], pt[:, 0:640], r=(ptk,), w=(("PT", vb),), eng="act")
                    pO, pOk = getps()
                    for kb in range(5):
                        mm(pO[:, 0:128], PT[vb][:, kb * 128:(kb + 1) * 128], Vt[:, t + kb, :], kb == 0, kb == 4,
                           r=(("PT", vb), ("Vt", t + kb)), w=(pOk,))
                    S.add("dve", lambda e: e.reciprocal(rrc, rsc), r=(("rsum", vb),), w=(("rrs", vb),))
                    stt("dve", yab[vb], pO[:, 0:128], rrc, slzA[:, t, :], ALU.mult, ALU.mult,
                        r=(pOk, ("rrs", vb), ("slzA", t)), w=(("yab", vb),))
                    pt2, pt2k = getpsb()
                    tr(pt2[:, 0:128], yab[vb], r=(("yab", vb),), w=(pt2k,))
                    cast(yaT[vb], pt2[:, 0:128], r=(pt2k,), w=(("yaT", vb),), eng="act")
                    dma("act", yT_d[8 + h, :, t * 128:(t + 1) * 128], yaT[vb], r=(("yaT", vb),), w=(("yTd", 8 + h, t),))

                att_s1(0)
                for t in range(NOWN):
                    if t + 1 < NOWN:
                        att_s1(t + 1)
                    att_s2(t)
            S.barrier()
            stop("B2", [(sm[:], 64)])

            aoff[0] = mark_b
            yT_res = abf(16 * 2048).rearrange("p (b c) -> p b c", b=16)
            hh_flat = hT_halo[:].rearrange("p k c -> p (k c)")
            wgm = [abf(16 * 128).rearrange("p (k c) -> p k c", k=16), hh_flat[:, 0:2048].rearrange("p (k c) -> p k c", k=16)]
            wga = [abf(16 * 128).rearrange("p (k c) -> p k c", k=16), hh_flat[:, 2048:4096].rearrange("p (k c) -> p k c", k=16)]
            wpm = [abf(8 * 128).rearrange("p (k c) -> p k c", k=8), hh_flat[:, 4096:5120].rearrange("p (k c) -> p k c", k=8)]
            wpa = [abf(8 * 128).rearrange("p (k c) -> p k c", k=8), hh_flat[:, 5120:6144].rearrange("p (k c) -> p k c", k=8)]
            mTb = [abf(512) for _ in range(2)]
            stg3 = [af32(8 * 128).rearrange("p (k c) -> p k c", k=8) for _ in range(2)]
            stg3.append(mhg[:].rearrange("p (k c) -> p k c", k=8))
            stg3.append(hh_flat[:, 6144:8192].bitcast(F32).rearrange("p (k c) -> p k c", k=8))
            sgm = af32(512)
            sga = af32(512)
            t1b = af32(512)
            for b in range(16):
                dma(dmaq(), yT_res[:, b, :], yT_d[b], r=(), w=("yT_res",))
            si3 = [0]

            def b3_load(c):
                cbuf = c % 2
                for (dst, src, nk, nm) in ((wgm, w_in[:, C_GM + c * 128:C_GM + (c + 1) * 128], 16, "wgm"),
                                           (wga, w_in[:, C_GA + c * 128:C_GA + (c + 1) * 128], 16, "wga"),
                                           (wpm, w_pm[:, c * 128:(c + 1) * 128], 8, "wpm"),
                                           (wpa, w_pa[:, c * 128:(c + 1) * 128], 8, "wpa")):
                    for k8 in range(nk // 8):
                        st_ = stg3[si3[0] % 4]
                        sk = ("stg3", si3[0] % 4)
                        si3[0] += 1
                        for k4 in range(2):
                            r0 = (k8 * 8 + k4 * 4) * 128
                            dma(dmaq(), st_[:, 4 * k4:4 * k4 + 4, :],
                                src[r0:r0 + 512, :].rearrange("(k p) c -> p k c", p=128), r=(), w=(sk,))
                        cast(dst[cbuf][:, k8 * 8:(k8 + 1) * 8, :], st_[:], r=(sk,), w=((nm, cbuf),), eng="pool")

            b3_load(0)
            for c in range(16):
                cbuf = c % 2
                if c + 1 < 16:
                    b3_load(c + 1)
                for g in range(4):
                    gsl = slice(g * 512, (g + 1) * 512)
                    p1, p1k = getps()
                    for k in range(16):
                        mm(p1[:, 0:512], wgm[cbuf][:, k, :], hT_own[:, k, gsl], k == 0, k == 15, r=(("wgm", cbuf),), w=(p1k,))
                    act(sgm, p1[:, 0:512], AF.Sigmoid, r=(p1k,), w=("sgm",))
                    p2, p2k = getps()
                    for k in range(16):
                        mm(p2[:, 0:512], wga[cbuf][:, k, :], hT_own[:, k, gsl], k == 0, k == 15, r=(("wga", cbuf),), w=(p2k,))
                    act(sga, p2[:, 0:512], AF.Sigmoid, r=(p2k,), w=("sga",))
                    p3, p3k = getps()
                    for k in range(8):
                        mm(p3[:, 0:512], wpm[cbuf][:, k, :], yT_res[:, k, gsl], k == 0, k == 7,
                           r=(("wpm", cbuf), "yT_res"), w=(p3k,))
                    tt("dve", t1b, sgm, p3[:, 0:512], ALU.mult, r=("sgm", p3k), w=("t1b",))
                    p4, p4k = getps()
                    for k in range(8):
                        mm(p4[:, 0:512], wpa[cbuf][:, k, :], yT_res[:, 8 + k, gsl], k == 0, k == 7,
                           r=(("wpa", cbuf), "yT_res"), w=(p4k,))
                    tt("dve", sga, sga, p4[:, 0:512], ALU.mult, r=("sga", p4k), w=("sga",))
                    mb = mTb[(c * 4 + g) % 2]
                    mk_ = ("mTb", (c * 4 + g) % 2)
                    tt("dve", mb, t1b, sga, ALU.add, r=("t1b", "sga"), w=(mk_,))
                    dma(dmaq(), mT_d[c, :, gsl], mb, r=(mk_,), w=(("mTd", c, g),))
            S.barrier()
            stop("B3", [(sm[:], 64)])

            areset()
            Wout = abf(16 * 2048).rearrange("p (k c) -> p k c", k=16)
            mTt = [abf(16 * 128).rearrange("p (c t) -> p c t", c=16) for _ in range(2)]
            fgb = af32(D)
            stgC = [af32(D) for _ in range(2)]
            xts = [af32(D) for _ in range(2)]
            obuf = [af32(D) for _ in range(2)]
            junkC = af32(D)
            dma("sp", fgb, fg_bc, r=(), w=("fgb",))
            for k in range(16):
                st_ = stgC[k % 2]
                sk = ("stgC", k % 2)
                dma(dmaq(), st_, w_out[k * 128:(k + 1) * 128, :], r=(), w=(sk,))
                tt(("dve", "pool")[k % 2], Wout[:, k, :], st_, gate_bc[:], ALU.mult, r=(sk, "gate_bc"), w=("Wout",))
            outkeys = []
            for t in range(NOWN):
                vb = t % 2
                for c4 in range(4):
                    dma(dmaq(), mTt[vb][:, 4 * c4:4 * c4 + 4, :],
                        mT_d[4 * c4:4 * c4 + 4, :, t * 128:(t + 1) * 128].rearrange("c p t -> p c t"), r=(), w=(("mTt", vb),))
                dma(dmaq(), xts[vb], xl[(NPRE + t) * 128:(NPRE + t + 1) * 128, :], r=(), w=(("xts", vb),))
                ob = obuf[vb]
                ok = ("ob", vb)
                for cbk in range(4):
                    p_, pk = getps()
                    for c in range(16):
                        mm(p_[:, 0:512], mTt[vb][:, c, :], Wout[:, c, cbk * 512:(cbk + 1) * 512], c == 0, c == 15,
                           r=(("mTt", vb), "Wout"), w=(pk,))
                    tt("dve", ob[:, cbk * 512:(cbk + 1) * 512], p_[:, 0:512], xts[vb][:, cbk * 512:(cbk + 1) * 512], ALU.add,
                       r=(pk, ("xts", vb)), w=(ok,))
                memset("dve", sm[:, 52:53], 0.0, w=("ssC",))
                act(junkC, ob, AF.Square, r=(ok, "ssC"), w=("junkC", "ssC"), accum=sm[:, 52:53])
                ts("dve", sm[:, 53:54], sm[:, 52:53], 1.0 / D, EPS, ALU.mult, ALU.add, r=("ssC",), w=("rC",))
                rsqrt_col(sm[:, 53:54], "rC")
                stt("dve", ob, ob, sm[:, 53:54], fgb, ALU.mult, ALU.mult, r=(ok, "rC", "fgb"), w=(ok,))
                dma(dmaq(), y_out[t * 128:(t + 1) * 128, :], ob, r=(ok,), w=(("yout", t),))
                outkeys.append(("yout", t))
        try:
            body()
        except _Stop:
            pass
        S.finish(())
        S.emit()
    return nc


_NC_CACHE = {}


def _consts():
    ident = np.eye(128, dtype=np.float32).astype(ml_dtypes.bfloat16)
    s = np.arange(128)[:, None]
    t = np.arange(128)[None, :]
    tri = (s <= t).astype(np.float32)
    ones = np.ones((128, 128), np.float32)
    q = np.arange(128)[:, None]
    kap = np.arange(640)[None, :]
    cq, ck = q // 64, kap // 64
    allowed = (ck >= cq) & (ck <= cq + 8)
    amask = np.where(allowed, 0.0, NEG).astype(np.float32)
    relidx = np.clip(q + 512 - kap, -63, 128) + 63
    return ident, tri, ones, tri.copy(), amask, relidx


def kernel(x, c, w_ada, b_ada, norm_g, w_in, b_if, conv_w, conv_b, mh_norm_g, rel_bias,
           w_proj_m, w_proj_a, w_out, final_norm_g):
    f = np.float32
    x = np.asarray(x, f)
    ident, tri, ones, mst, amask, relidx = _consts()
    if "nc" not in _NC_CACHE:
        _NC_CACHE["nc"] = build_nc()
    nc = _NC_CACHE["nc"]
    rep = lambda v, n=128: np.ascontiguousarray(np.broadcast_to(np.asarray(v, f).reshape(1, -1), (n, np.asarray(v).size)))
    colT = lambda v: np.ascontiguousarray(np.asarray(v, f).reshape(16, 128).T)
    cwl = np.ascontiguousarray(np.asarray(conv_w[0], f).T.reshape(16, 128, 4).transpose(1, 0, 2))
    shared = {
        "w_ada": np.ascontiguousarray(w_ada[0], f), "b_ada": np.ascontiguousarray(b_ada[0], f).reshape(1, -1),
        "ngT": colT(norm_g[0]), "w_in": np.ascontiguousarray(w_in[0], f), "bif_bc": rep(b_if[0]),
        "convw": cwl, "convb": colT(conv_b[0]), "mhg_bc": rep(mh_norm_g[0]),
        "relmat": np.ascontiguousarray(np.asarray(rel_bias[0], f)[:, relidx]), "amask": amask,
        "w_pm": np.ascontiguousarray(w_proj_m[0], f), "w_pa": np.ascontiguousarray(w_proj_a[0], f),
        "w_out": np.ascontiguousarray(w_out[0], f), "fg_bc": rep(final_norm_g),
        "ident": ident, "tri": tri, "ones": ones, "maskst": mst,
    }
    in_maps = []
    for core in range(8):
        b, j = core // 4, core % 4
        npad = (3 - j) * 16
        xl = np.zeros((NT * 128, D), f)
        xl[npad * 128:] = x[b, 0:(j + 1) * 2048]
        valid = (np.arange(NT) >= npad).astype(f)
        m = dict(shared)
        m["xl"] = xl
        m["cT"] = colT(c[b])
        m["padneg"] = rep(np.where(valid > 0, 0.0, NEG))
        m["tilevalid"] = rep(valid)
        m["negtile"] = rep(np.where(valid > 0, 0.0, NEG))
        in_maps.append(m)
    res = run_bass_kernel_spmd(nc, in_maps, core_ids=list(range(8)))
    out = np.empty((2, 8192, D), f)
    for core in range(8):
        b, j = core // 4, core % 4
        out[b, j * 2048:(j + 1) * 2048] = res.results[core]["y"]
    return out
```

```python
import os
import numpy as np
import ml_dtypes
from contextlib import ExitStack
import concourse.bass as bass
import concourse.mybir as mybir
from concourse.bass_utils import run_bass_kernel_spmd

F32 = mybir.dt.float32
BF16 = mybir.dt.bfloat16
ALU = mybir.AluOpType
AF = mybir.ActivationFunctionType
AX = mybir.AxisListType

D = 2048
NT = 64
NPRE = 48
NOWN = 16
EPS = 1e-6
C_MQ, C_MK, C_MV, C_MO, C_MZ, C_MI, C_MF = 0, 1024, 2048, 3072, 4096, 5120, 5124
C_AQ, C_AK, C_AV, C_AZ, C_GM, C_GA = 5128, 6152, 7176, 8200, 9224, 11272
IN_COLS = 13320
NEG = -30000.0
ENGS = ("sp", "act", "dve", "pool", "pe")
KSTOP = os.environ.get("KSTOP")


class _Stop(Exception):
    pass


class Op:
    __slots__ = ("eng", "fn", "deps", "dma", "sig", "sem", "val")

    def __init__(self, eng, fn, dma):
        self.eng, self.fn, self.dma = eng, fn, dma
        self.deps, self.sig, self.sem, self.val = [], dma, None, 0


class Sched:
    ND = 40

    def __init__(self, nc, es):
        self.nc = nc
        self.eops = {e: [] for e in ENGS}
        self.lastw, self.readers = {}, {}
        self.csem = {e: es.enter_context(nc.semaphore("cs_" + e)) for e in ENGS}
        self.dsem = [es.enter_context(nc.semaphore("ds%d" % i)) for i in range(self.ND)]
        self.dlast = [None] * self.ND
        self.duse = [0] * self.ND
        self.dn = 0

    @staticmethod
    def _is_psum(k):
        return k == "psS" or (isinstance(k, tuple) and len(k) > 0 and k[0] in ("ps", "psb", "psS"))

    def add(self, eng, fn, r=(), w=(), dma=False):
        w = tuple(w) + tuple(k for k in r if self._is_psum(k))
        r = tuple(k for k in r if not self._is_psum(k))
        op = Op(eng, fn, dma)
        deps = []
        for k in r:
            if k in self.lastw:
                deps.append(self.lastw[k])
        for k in w:
            if k in self.lastw:
                deps.append(self.lastw[k])
            deps.extend(self.readers.get(k, ()))
        if dma:
            i = self.dn % self.ND
            self.dn += 1
            if self.dlast[i] is not None:
                deps.append(self.dlast[i])
            self.duse[i] += 1
            op.sem, op.val = self.dsem[i], 16 * self.duse[i]
            self.dlast[i] = op
        seen = set()
        for d in deps:
            if d is op or id(d) in seen:
                continue
            seen.add(id(d))
            if d.eng == "pe" and eng == "pe" and not d.dma:
                continue
            d.sig = True
            op.deps.append(d)
        for k in r:
            lst = self.readers.setdefault(k, [])
            if not dma:
                lst[:] = [o_ for o_ in lst if o_.dma or o_.eng != eng]
            lst.append(op)
        for k in w:
            self.lastw[k] = op
            self.readers[k] = []
        self.eops[eng].append(op)
        return op

    def barrier(self):
        lasts = []
        for e in ENGS:
            for o in reversed(self.eops[e]):
                if not o.dma and o.fn is not None:
                    lasts.append(o)
                    break
        pend = [d for d in self.dlast if d is not None]
        for e in ENGS:
            op = Op(e, None, False)
            for d in lasts + pend:
                if d.eng == e and not d.dma:
                    continue
                d.sig = True
                op.deps.append(d)
            self.eops[e].append(op)
        self.lastw, self.readers = {}, {}

    def finish(self, keys):
        op = Op("sp", None, False)
        for d in self.dlast:
            if d is not None:
                op.deps.append(d)
        self.eops["sp"].append(op)

    def emit(self):
        nc = self.nc
        for e in ENGS:
            c = 0
            for o in self.eops[e]:
                if o.dma or o.fn is None:
                    continue
                if o.sig:
                    c += 1
                    o.sem, o.val = self.csem[e], c

        def run(ename, eng):
            known = {}
            for o in self.eops[ename]:
                for d in o.deps:
                    key = id(d.sem)
                    if known.get(key, 0) >= d.val:
                        continue
                    eng.wait_ge(d.sem, d.val)
                    known[key] = d.val
                if o.fn is None:
                    continue
                ins = o.fn(eng)
                if o.sig:
                    ins.then_inc(o.sem, 16 if o.dma else 1)

        with nc.Block() as block:
            @block.sync
            def _(e):
                run("sp", e)

            @block.scalar
            def _(e):
                run("act", e)

            @block.vector
            def _(e):
                run("dve", e)

            @block.gpsimd
            def _(e):
                run("pool", e)

            @block.tensor
            def _(e):
                run("pe", e)


def build_nc():
    nc = bass.Bass("TRN2", target_bir_lowering=False)

    def din(name, shape, dt=F32):
        return nc.dram_tensor(name, list(shape), dt, kind="ExternalInput").ap()

    xl = din("xl", [NT * 128, D])
    cT = din("cT", [128, 16])
    w_ada = din("w_ada", [D, 3 * D])
    b_ada = din("b_ada", [1, 3 * D])
    ngT = din("ngT", [128, 16])
    w_in = din("w_in", [D, IN_COLS])
    bif_bc = din("bif_bc", [128, 8])
    convw = din("convw", [128, 16, 4])
    convb = din("convb", [128, 16])
    mhg_bc = din("mhg_bc", [128, 1024])
    relmat = din("relmat", [8, 128, 640])
    amask = din("amask", [128, 640])
    w_pm = din("w_pm", [1024, D])
    w_pa = din("w_pa", [1024, D])
    w_out = din("w_out", [D, D])
    fg_bc = din("fg_bc", [128, D])
    padneg = din("padneg", [128, NT])
    tilevalid = din("tilevalid", [128, NT])
    negtile = din("negtile", [128, NT])
    ident_d = din("ident", [128, 128], BF16)
    tri_d = din("tri", [128, 128])
    ones_d = din("ones", [128, 128])
    mst_d = din("maskst", [128, 128])
    y_out = nc.dram_tensor("y", [NOWN * 128, D], F32, kind="ExternalOutput").ap()
    dbg = nc.dram_tensor("dbg", [128, 8192], F32, kind="ExternalOutput").ap() if KSTOP else None
    yT_d = nc.dram_tensor("yT_d", [16, 128, NOWN * 128], BF16, kind="Internal").ap()
    mT_d = nc.dram_tensor("mT_d", [16, 128, NOWN * 128], BF16, kind="Internal").ap()

    es = ExitStack()
    with es, nc.allow_low_precision("bf16 matmul operands, fp32 accumulation"), \
            nc.allow_non_contiguous_dma("column-sliced weight loads"):
        S = Sched(nc, es)

        def sb(name, shape, dt=F32):
            return es.enter_context(nc.sbuf_tensor("s_" + name, list(shape), dt))

        def pst(name, shape, dt=F32):
            return es.enter_context(nc.psum_tensor(name, list(shape), dt))

        ident = sb("ident", [128, 128], BF16)
        tri = sb("tri", [128, 128])
        ones = sb("ones", [128, 128])
        gs = sb("gs", [128, 16])
        shift = sb("shift", [128, 16])
        ngt_s = sb("ngt_s", [128, 16])
        gate_bc = sb("gate_bc", [128, D])
        cw = sb("cw", [128, 16, 4])
        cb_ = sb("cb_", [128, 16])
        bif = sb("bif", [128, 8])
        mhg = sb("mhg", [128, 1024])
        pneg = sb("pneg", [128, NT])
        tval = sb("tval", [128, NT])
        ntile = sb("ntile", [128, NT])
        amask_s = sb("amask_s", [128, 640])
        wif = sb("wif", [128, 16, 8], BF16)
        wifs = sb("wifs", [128, 16, 8])
        state = [[sb("st%d%d" % (h, b), [128, 257]) for b in range(2)] for h in range(4)]
        ctbf = [[sb("ct%d%d" % (h, b), [128, 257], BF16) for b in range(2)] for h in range(4)]
        hT_halo = sb("hT_halo", [128, 16, 512], BF16)
        wk_o = sb("wk_o", [128, NOWN, 4])
        wa_o = sb("wa_o", [128, NOWN, 4])
        eB_o = sb("eB_o", [128, NOWN, 4])
        ebc_o = sb("ebc_o", [128, NOWN, 4])
        sm = sb("sm", [128, 64])
        ARENA_COLS = 40000
        arena = sb("arena", [128, ARENA_COLS])

        ps = [pst("ps%d" % i, [128, 512]) for i in range(4)]
        psS = pst("psS", [128, 1024])
        psb = [pst("psb%d" % i, [128, 1024], BF16) for i in range(2)]
        rot = {"ps": 0, "psb": 0, "cast": 0, "q": 0}

        def getps():
            i = rot["ps"] % 4
            rot["ps"] += 1
            return ps[i], ("ps", i)

        def getpsb():
            i = rot["psb"] % 2
            rot["psb"] += 1
            return psb[i], ("psb", i)

        aoff = [0]

        def areset():
            aoff[0] = 0

        def af32(cols):
            o = aoff[0]
            aoff[0] += cols
            assert aoff[0] <= ARENA_COLS, aoff[0]
            return arena[:, o:o + cols]

        def abf(cols):
            n = (cols + 1) // 2
            return af32(n).bitcast(BF16)[:, 0:cols]

        def dma(q, out, in_, r, w):
            S.add(q, lambda e, o=out, i=in_: e.dma_start(out=o, in_=i), r=r, w=w, dma=True)

        def dmaq():
            rot["q"] += 1
            return "sp" if rot["q"] % 2 else "pool"

        def act(out, in_, func, r, w, bias=None, scale=None, accum=None):
            kw = {}
            if bias is not None:
                kw["bias"] = bias
            if scale is not None:
                kw["scale"] = scale
            if accum is not None:
                kw["accum_out"] = accum
            S.add("act", lambda e: e.activation(out, in_, func, **kw), r=r, w=w)

        def ts(eng, out, in0, s1, s2, op0, op1, r, w):
            if s2 is None:
                s2, op1 = (1.0, ALU.mult) if op0 == ALU.add else (0.0, ALU.add)
            S.add(eng, lambda e: e.tensor_scalar(out, in0, s1, s2, op0, op1), r=r, w=w)

        def rsqrt_col(col, key):
            S.add("act", lambda e: e.activation(col, col, AF.Sqrt), r=(key,), w=(key,))
            S.add("dve", lambda e: e.reciprocal(col, col), r=(key,), w=(key,))

        def tt(eng, out, in0, in1, op, r, w):
            S.add(eng, lambda e: e.tensor_tensor(out, in0, in1, op), r=r, w=w)

        def stt(eng, out, in0, sc, in1, op0, op1, r, w):
            S.add(eng, lambda e: e.scalar_tensor_tensor(out, in0, sc, in1, op0, op1), r=r, w=w)

        def mm(out, lhsT, rhs, start, stop, r, w):
            S.add("pe", lambda e: e.matmul(out, lhsT, rhs, start=start, stop=stop), r=r, w=w)

        def tr(out, in_, r, w):
            S.add("pe", lambda e: e.transpose(out, in_, ident[:]), r=tuple(r) + ("ident",), w=w)

        def cast(out, in_, r, w, eng=None):
            if eng is None:
                rot["cast"] += 1
                eng = ("act", "pool")[rot["cast"] % 2]
            if eng == "act":
                S.add("act", lambda e: e.copy(out, in_), r=r, w=w)
            else:
                S.add(eng, lambda e: e.tensor_copy(out, in_), r=r, w=w)

        def memset(eng, ap, v, w):
            S.add(eng, lambda e: e.memset(ap, v), w=w)

        def stop(tag, dumps=()):
            if KSTOP != tag:
                return
            S.barrier()
            off = 0
            for ap_, n_ in dumps:
                dma("sp", dbg[:, off:off + n_], ap_, r=(), w=(("dbg", off),))
                off += n_
            raise _Stop()

        def body():
            for (t_, d_, k_) in ((ident, ident_d, "ident"), (tri, tri_d, "tri"), (ones, ones_d, "ones"),
                                 (ngt_s, ngT, "ngt"), (cw, convw, "cw"),
                                 (cb_, convb, "cb"), (bif, bif_bc, "bif"), (mhg, mhg_bc, "mhg"),
                                 (pneg, padneg, "pneg"), (tval, tilevalid, "tval"),
                                 (ntile, negtile, "ntile"), (amask_s, amask, "amask")):
                dma("sp", t_[:], d_, r=(), w=(k_,))
            for k4 in range(4):
                dma("sp", wifs[:, 4 * k4:4 * k4 + 4, :],
                    w_in[k4 * 512:(k4 + 1) * 512, C_MI:C_MI + 8].rearrange("(k p) c -> p k c", p=128), r=(), w=("wifs",))
            cast(wif[:], wifs[:], r=("wifs",), w=("wif",), eng="dve")
            for h in range(4):
                for b in range(2):
                    memset("dve", state[h][b][:], 0.0, w=(("st", h, b),))

            areset()
            sc_in = af32(16)
            scv = af32(16)
            modrow = af32(3 * D)[0:1, :]
            badar = af32(3 * D)[0:1, :]
            stgA = [af32(3072) for _ in range(2)]
            dma("sp", sc_in, cT, r=(), w=("sc_in",))
            dma("sp", badar, b_ada, r=(), w=("badar",))
            act(scv, sc_in, AF.Silu, r=("sc_in",), w=("scv",))
            banks = [(ps[0], ("ps", 0), 0), (ps[1], ("ps", 1), 0), (ps[2], ("ps", 2), 0),
                     (ps[3], ("ps", 3), 0), (psS, ("psS",), 0), (psS, ("psS",), 512)]
            for half in range(2):
                for k in range(16):
                    st_ = stgA[k % 2]
                    key = ("stgA", k % 2)
                    dma(dmaq(), st_, w_ada[k * 128:(k + 1) * 128, half * 3072:(half + 1) * 3072], r=(), w=(key,))
                    for cbk in range(6):
                        t_, pk, off = banks[cbk]
                        mm(t_[0:1, off:off + 512], scv[:, k:k + 1], st_[:, cbk * 512:(cbk + 1) * 512],
                           k == 0, k == 15, r=(key, "scv"), w=(pk,))
                for cbk in range(6):
                    t_, pk, off = banks[cbk]
                    c0 = half * 3072 + cbk * 512
                    tt("dve", modrow[:, c0:c0 + 512], t_[0:1, off:off + 512], badar[:, c0:c0 + 512], ALU.add,
                       r=(pk, "badar"), w=("modrow",))
            pcol, pck = getps()
            for c in range(32):
                mm(pcol[:, c:c + 1], modrow[0:1, c * 128:(c + 1) * 128], ones[0:1, 0:1], True, True,
                   r=("modrow", "ones"), w=(pck,))
            cast(shift[:], pcol[:, 0:16], r=(pck,), w=("shift",), eng="dve")
            stt("dve", gs[:], pcol[:, 16:32], 1.0, ngt_s[:], ALU.add, ALU.mult, r=(pck, "ngt"), w=("gs",))
            for cbk in range(4):
                t_, pk = getps()
                mm(t_[:, 0:512], ones[0:1, :], modrow[0:1, 2 * D + cbk * 512:2 * D + (cbk + 1) * 512], True, True,
                   r=("modrow", "ones"), w=(pk,))
                cast(gate_bc[:, cbk * 512:(cbk + 1) * 512], t_[:, 0:512], r=(pk,), w=("gate_bc",), eng="act")
            S.barrier()
            stop("S1", [(gs[:], 16), (shift[:], 16), (gate_bc[:, 0:64], 64)])

            def frontend1(tau, xbufs, xn, xnk, i2):
                xt = xbufs[i2 % len(xbufs)]
                xk = ("xt", i2 % len(xbufs))
                dma(dmaq(), xt, xl[tau * 128:(tau + 1) * 128, :], r=(), w=(xk,))
                memset("dve", sm[:, 0:1], 0.0, w=("ss",))
                act(xn, xt, AF.Square, r=(xk, "ss"), w=(xnk, "ss"), accum=sm[:, 0:1])
                ts("dve", sm[:, 1:2], sm[:, 0:1], 1.0 / D, EPS, ALU.mult, ALU.add, r=("ss",), w=("rstd",))
                rsqrt_col(sm[:, 1:2], "rstd")
                act(xn, xt, AF.Identity, r=(xk, "rstd"), w=(xnk,), scale=sm[:, 1:2])

            def frontend2(dstf, dkey, xn, xnk, halves=(0, 1)):
                for hh in halves:
                    pt, ptk = psb[hh], ("psb", hh)
                    for kk in range(8):
                        k = hh * 8 + kk
                        tr(pt[:, kk * 128:(kk + 1) * 128], xn[:, k * 128:(k + 1) * 128], r=(xnk,), w=(ptk,))
                    for kk in range(8):
                        k = hh * 8 + kk
                        if hh == 0:
                            act(dstf(k), pt[:, kk * 128:(kk + 1) * 128], AF.Identity, r=(ptk, "gs", "shift"),
                                w=(dkey[0],), bias=shift[:, k:k + 1], scale=gs[:, k:k + 1])
                        else:
                            ts("dve", dstf(k), pt[:, kk * 128:(kk + 1) * 128], gs[:, k:k + 1], shift[:, k:k + 1],
                               ALU.mult, ALU.add, r=(ptk, "gs", "shift"), w=(dkey[1],))

            def frontend(tau, dstf, dkey, xbufs, xn, i2):
                frontend1(tau, xbufs, xn, "xn", i2)
                frontend2(dstf, dkey, xn, "xn")

            def load_w(dst, src_rows, nk, stgs, skey, dkey, cast_eng=None, q=None, rot_engs=None):
                for k in range(nk):
                    st_ = stgs[k % len(stgs)]
                    key = (skey, k % len(stgs))
                    dma(q or dmaq(), st_, src_rows(k), r=(), w=(key,))
                    cast(dst(k), st_, r=(key,), w=(dkey,), eng=(rot_engs[k % len(rot_engs)] if rot_engs else cast_eng))

            def gates(tau, hsrc, hkey, own_i):
                pg, pgk = getps()
                for k in range(16):
                    mm(pg[:, 0:8], hsrc(k), wif[:, k, :], k == 0, k == 15, r=tuple(hkey) + ("wif",), w=(pgk,))
                gl = sm[:, 8:16]
                tt("dve", gl, pg[:, 0:8], bif[:], ALU.add, r=(pgk, "bif"), w=("gl",))
                ts("dve", sm[:, 16:20], gl[:, 0:4], pneg[:, tau:tau + 1], None, ALU.add, None, r=("gl", "pneg"), w=("ig",))
                act(sm[:, 20:24], gl[:, 4:8], AF.Exp, r=("gl",), w=("e1",), scale=-1.0)
                ts("dve", sm[:, 20:24], sm[:, 20:24], 1.0, None, ALU.add, None, r=("e1",), w=("e1",))
                act(sm[:, 24:28], sm[:, 20:24], AF.Ln, r=("e1",), w=("lf",))
                ts("dve", sm[:, 24:28], sm[:, 24:28], -1.0, None, ALU.mult, None, r=("lf",), w=("lf",))
                pc, pckk = getps()
                mm(pc[:, 0:4], tri[:], sm[:, 24:28], True, True, r=("lf", "tri"), w=(pckk,))
                mm(pc[:, 4:8], ones[:], sm[:, 24:28], True, True, r=("lf", "ones"), w=(pckk,))
                tt("dve", sm[:, 28:32], sm[:, 16:20], pc[:, 0:4], ALU.subtract, r=("ig", pckk), w=("t1",))
                tt("dve", sm[:, 32:36], sm[:, 28:32], pc[:, 4:8], ALU.add, r=("t1", pckk), w=("t2",))
                if own_i is None:
                    wk, eB = sm[:, 36:40], sm[:, 40:44]
                    wkk, eBk = "wk", "eB"
                else:
                    wk, eB = wk_o[:, own_i, :], eB_o[:, own_i, :]
                    wkk = eBk = ("gown", own_i)
                act(wk, sm[:, 32:36], AF.Exp, r=("t2",), w=(wkk,))
                ts("dve", wk, wk, 0.0625, None, ALU.mult, None, r=(wkk,), w=(wkk,))
                act(eB, pc[:, 4:8], AF.Exp, r=(pckk,), w=(eBk,))
                if own_i is not None:
                    act(wa_o[:, own_i, :], sm[:, 28:32], AF.Exp, r=("t1",), w=(wkk,))
                    ts("dve", wa_o[:, own_i, :], wa_o[:, own_i, :], 0.0625, None, ALU.mult, None, r=(wkk,), w=(wkk,))
                    act(ebc_o[:, own_i, :], pc[:, 0:4], AF.Exp, r=(pckk,), w=(wkk,))
                return wk, eB, wkk, eBk

            def conv_silu(pre, prek, cblk, accb, acck, out, outk):
                ts("dve", accb, pre[:, 3:515], cw[:, cblk, 3:4], cb_[:, cblk:cblk + 1], ALU.mult, ALU.add,
                   r=(prek, "cw", "cb"), w=(acck,))
                for tap in range(3):
                    stt("dve", accb, pre[:, tap:tap + 512], cw[:, cblk, tap:tap + 1], accb, ALU.mult, ALU.add,
                        r=(prek, acck, "cw"), w=(acck,))
                act(out, accb, AF.Silu, r=(acck,), w=(outk,))

            def state_update(h, kT_blk, kTk, vaug, vk, wk_col, wkk, eB_col, eBk, kpp, kppk, refresh_bf):
                pt, ptk = getpsb()
                for blk in range(2):
                    tr(pt[:, blk * 128:(blk + 1) * 128], kT_blk(blk), r=(kTk[blk],), w=(ptk,))
                ts("dve", kpp, pt[:, 0:256], wk_col, None, ALU.mult, None, r=(ptk, wkk), w=(kppk,))
                for blk in range(2):
                    p_, pk = getps()
                    mm(p_[:, 0:257], kpp[:, blk * 128:(blk + 1) * 128], vaug, True, True, r=(kppk, vk), w=(pk,))
                    stt("dve", state[h][blk][:], state[h][blk][:], eB_col, p_[:, 0:257], ALU.mult, ALU.add,
                        r=(pk, eBk, ("st", h, blk)), w=(("st", h, blk),))
                    if refresh_bf:
                        cast(ctbf[h][blk][:], state[h][blk][:], r=(("st", h, blk),), w=(("ct", h, blk),), eng="act")

            areset()
            WA = abf(16 * 2048).rearrange("p (k c) -> p k c", k=16)
            hTgA = [abf(16 * 512).rearrange("p (k c) -> p k c", k=16) for _ in range(2)]
            xn = abf(D)
            xns = [xn, abf(D)]
            kT = abf(8 * 512).rearrange("p (b c) -> p b c", b=8)
            vaugs = [[abf(258)[:, 0:257] for _ in range(4)] for _ in range(4)]
            kpps = [abf(256) for _ in range(2)]
            stgs = [af32(2048) for _ in range(2)]
            xbufs = stgs
            kpre = af32(8 * 515).rearrange("p (b c) -> p b c", b=8)
            accb = af32(512)
            gtmp = af32(4 * 48).rearrange("p (i c) -> p i c", i=4)
            load_w(lambda k: WA[:, k, :], lambda k: w_in[k * 128:(k + 1) * 128, C_MK:C_MK + 2048], 16, stgs, "xt", "WA", rot_engs=("act", "dve"))
            memset("pool", kpre[:, :, 0:3], 0.0, w=tuple(("kpre", c) for c in range(8)))
            for i in range(4):
                for h in range(4):
                    memset("pool", vaugs[i][h][:, 256:257], 1.0, w=(("vaug", i, h),))
            NG = NPRE // 4
            fe_cnt = [0]

            def a_hT(g):
                return hT_halo if g == NG - 1 else hTgA[g % 2]

            def a_keys(g, i):
                return (("hTg", g % 2, i, 0), ("hTg", g % 2, i, 1))

            def a_gates1(g, i):
                tau = 4 * g + i
                hT = a_hT(g)
                gt = gtmp[:, i, :]
                pg, pgk = ps[2 + i % 2], ("ps", 2 + i % 2)
                for k in range(16):
                    mm(pg[:, 0:8], hT[:, k, i * 128:(i + 1) * 128], wif[:, k, :], k == 0, k == 15,
                       r=a_keys(g, i) + ("wif",), w=(pgk,))
                tt("dve", gt[:, 0:8], pg[:, 0:8], bif[:], ALU.add, r=(pgk, "bif"), w=(("g_gl", i),))
                ts("dve", gt[:, 8:12], gt[:, 0:4], pneg[:, tau:tau + 1], None, ALU.add, None, r=(("g_gl", i), "pneg"), w=(("g_ig", i),))
                act(gt[:, 12:16], gt[:, 4:8], AF.Exp, r=(("g_gl", i),), w=(("g_lf", i),), scale=-1.0)
                ts("dve", gt[:, 12:16], gt[:, 12:16], 1.0, None, ALU.add, None, r=(("g_lf", i),), w=(("g_lf", i),))
                act(gt[:, 12:16], gt[:, 12:16], AF.Ln, r=(("g_lf", i),), w=(("g_lf", i),))
                ts("dve", gt[:, 12:16], gt[:, 12:16], -1.0, None, ALU.mult, None, r=(("g_lf", i),), w=(("g_lf", i),))

            def a_gates2(g, i):
                gt = gtmp[:, i, :]
                pc, pck_ = psS[:, (i % 2) * 512:(i % 2) * 512 + 8], ("psS", i % 2)
                mm(pc[:, 0:4], tri[:], gt[:, 12:16], True, True, r=(("g_lf", i), "tri"), w=(pck_,))
                mm(pc[:, 4:8], ones[:], gt[:, 12:16], True, True, r=(("g_lf", i), "ones"), w=(pck_,))
                tt("dve", gt[:, 16:20], gt[:, 8:12], pc[:, 0:4], ALU.subtract, r=(("g_ig", i), pck_), w=(("g_t", i),))
                tt("dve", gt[:, 16:20], gt[:, 16:20], pc[:, 4:8], ALU.add, r=(("g_t", i), pck_), w=(("g_t", i),))
                act(gt[:, 20:24], gt[:, 16:20], AF.Exp, r=(("g_t", i),), w=(("g_wk", i),))
                ts("dve", gt[:, 20:24], gt[:, 20:24], 0.0625, None, ALU.mult, None, r=(("g_wk", i),), w=(("g_wk", i),))
                act(gt[:, 24:28], pc[:, 4:8], AF.Exp, r=(pck_,), w=(("g_eB", i),))

            def a_fe1(g, i):
                frontend1(4 * g + i, xbufs, xns[i % 2], ("xnA", i % 2), fe_cnt[0])
                fe_cnt[0] += 1

            def a_fe2(g, i, halves):
                hT = a_hT(g)
                frontend2(lambda k: hT[:, k, i * 128:(i + 1) * 128], a_keys(g, i), xns[i % 2], ("xnA", i % 2), halves)

            def a_U1(g, i, h):
                gt = gtmp[:, i, :]
                pt = ps[h % 2][:].bitcast(BF16)
                ptk = ("ps", h % 2)
                for blk in range(2):
                    tr(pt[:, blk * 128:(blk + 1) * 128], kT[:, 2 * h + blk, i * 128:(i + 1) * 128],
                       r=(("kT", 2 * h + blk),), w=(ptk,))
                kp, kpk = kpps[h % 2], ("kpp", h % 2)
                ts("dve", kp, pt[:, 0:256], gt[:, 20 + h:21 + h], None, ALU.mult, None, r=(ptk, ("g_wk", i)), w=(kpk,))

            def a_U2(g, i, h):
                gt = gtmp[:, i, :]
                kp, kpk = kpps[h % 2], ("kpp", h % 2)
                for blk in range(2):
                    pk = ("psS", blk)
                    p_ = psS[:, blk * 512:blk * 512 + 257]
                    mm(p_, kp[:, blk * 128:(blk + 1) * 128], vaugs[i][h], True, True, r=(kpk, ("vaug", i, h)), w=(pk,))
                    stt("dve", state[h][blk][:], state[h][blk][:], gt[:, 24 + h:25 + h], p_, ALU.mult, ALU.add,
                        r=(pk, ("g_eB", i), ("st", h, blk)), w=(("st", h, blk),))

            def a_stage3(g):
                nxt = g + 1 < NG
                if nxt:
                    a_fe1(g + 1, 0)
                for i in range(4):
                    if nxt and i + 1 < 4:
                        a_fe1(g + 1, i + 1)
                    a_U1(g, i, 0)
                    if nxt:
                        a_fe2(g + 1, i, (0,))
                    a_U1(g, i, 1)
                    a_U2(g, i, 0)
                    if nxt:
                        a_fe2(g + 1, i, (1,))
                    a_U1(g, i, 2)
                    a_U2(g, i, 1)
                    a_U1(g, i, 3)
                    a_U2(g, i, 2)
                    a_U2(g, i, 3)

            for i in range(4):
                a_fe1(0, i)
                a_fe2(0, i, (0, 1))
            for g in range(NG):
                hT = a_hT(g)
                hgk = tuple(k_ for i_ in range(4) for k_ in a_keys(g, i_))
                for cbk in range(8):
                    p_, pk = ps[cbk % 2], ("ps", cbk % 2)
                    for k in range(16):
                        mm(p_[:, 0:512], WA[:, k, cbk * 128:(cbk + 1) * 128], hT[:, k, :], k == 0, k == 15,
                           r=("WA",) + hgk, w=(pk,))
                    cast(kpre[:, cbk, 3:515], p_[:, 0:512], r=(pk,), w=(("kpre", cbk),), eng="act")
                    conv_silu(kpre[:, cbk, :], ("kpre", cbk), 8 + cbk, accb, "accb", kT[:, cbk, :], ("kT", cbk))
                    ts("dve", kpre[:, cbk, 0:3], kpre[:, cbk, 512:515], tval[:, 4 * g + 3:4 * g + 4], None, ALU.mult, None,
                       r=(("kpre", cbk), "tval"), w=(("kpre", cbk),))
                for i in range(4):
                    a_gates1(g, i)
                for i in range(4):
                    for half in range(2):
                        p_, pk = ps[2 + half], ("ps", 2 + half)
                        for k in range(16):
                            mm(p_[:, 0:512], hT[:, k, i * 128:(i + 1) * 128], WA[:, k, 1024 + half * 512:1024 + (half + 1) * 512],
                               k == 0, k == 15, r=("WA",) + a_keys(g, i), w=(pk,))
                        for hh in range(2):
                            h = 2 * half + hh
                            cast(vaugs[i][h][:, 0:256], p_[:, hh * 256:(hh + 1) * 256], r=(pk,), w=(("vaug", i, h),),
                                 eng=("act", "dve")[hh])
                for i in range(4):
                    a_gates2(g, i)
                a_stage3(g)
            S.barrier()
            stop("A", [(state[0][0][:], 257), (state[3][1][:], 257), (sm[:], 64)])

            areset()
            hT_own = abf(16 * 2048).rearrange("p (k c) -> p k c", k=16)
            mark_b = aoff[0]
            xn = abf(D)
            xbufs = [af32(D) for _ in range(2)]
            xns0 = [xn, abf(D)]
            frontend1(NPRE, xbufs, xns0[0], ("xnB", 0), 0)
            for i in range(NOWN):
                if i + 1 < NOWN:
                    frontend1(NPRE + i + 1, xbufs, xns0[(i + 1) % 2], ("xnB", (i + 1) % 2), i + 1)
                frontend2(lambda k, i=i: hT_own[:, k, i * 128:(i + 1) * 128], (("hTo", i, 0), ("hTo", i, 1)),
                          xns0[i % 2], ("xnB", i % 2))
                gates(NPRE + i, lambda k, i=i: hT_own[:, k, i * 128:(i + 1) * 128], (("hTo", i, 0), ("hTo", i, 1)), i)
            for h in range(4):
                for b in range(2):
                    cast(ctbf[h][b][:], state[h][b][:], r=(("st", h, b),), w=(("ct", h, b),), eng="act")
            S.barrier()
            stop("B0", [(wk_o[:].rearrange("p a b -> p (a b)"), 64), (wa_o[:].rearrange("p a b -> p (a b)"), 64), (eB_o[:].rearrange("p a b -> p (a b)"), 64), (ebc_o[:].rearrange("p a b -> p (a b)"), 64), (state[0][0][:], 257)])

            aoff[0] = mark_b
            WG = abf(16 * 1280).rearrange("p (k c) -> p k c", k=16)
            qkT = [abf(4 * 512).rearrange("p (b c) -> p b c", b=4) for _ in range(2)]
            vaug1 = [abf(258)[:, 0:257] for _ in range(2)]
            kpps = [abf(256) for _ in range(2)]
            STb = [abf(128) for _ in range(2)]
            ybf = [abf(256) for _ in range(2)]
            yTt = [abf(256).rearrange("p (b c) -> p b c", b=2) for _ in range(2)]
            stg1 = [af32(1280) for _ in range(2)]
            qkpre = af32(4 * 515).rearrange("p (b c) -> p b c", b=4)
            accb = af32(512)
            sgo = af32(256)
            slz = af32(256)
            Gt = [af32(256) for _ in range(2)]
            junk = af32(256)
            w5 = w_in[:, 0:5120].rearrange("r (s c) -> r s c", c=1024)
            for vb in range(2):
                memset("pool", vaug1[vb][:, 256:257], 1.0, w=(("vaug1", vb),))
            ts("dve", mhg[:], mhg[:], 0.5, None, ALU.mult, None, r=("mhg",), w=("mhg",))
            PS0, PS1, PS2, PS3 = (ps[0], ("ps", 0)), (ps[1], ("ps", 1)), (ps[2], ("ps", 2)), (ps[3], ("ps", 3))
            for h in range(4):
                load_w(lambda k: WG[:, k, :].rearrange("p (s c) -> p s c", c=256),
                       lambda k, h=h: w5[k * 128:(k + 1) * 128, :, h * 256:(h + 1) * 256],
                       16, [s_.rearrange("p (s c) -> p s c", c=256) for s_ in stg1], "stg1", "WG", rot_engs=("act", "dve"))
                for blk in range(4):
                    p_, pk = (PS0, PS1)[blk % 2]
                    for k in range(16):
                        mm(p_[:, 0:3], WG[:, k, blk * 128:(blk + 1) * 128], hT_halo[:, k, 509:512], k == 0, k == 15,
                           r=("WG",), w=(pk,))
                    ts("dve", qkpre[:, blk, 0:3], p_[:, 0:3], tval[:, NPRE - 1:NPRE], None, ALU.mult, None,
                       r=(pk, "tval"), w=(("qkpre", blk),))

                def b1_proj(g, h=h):
                    qk = qkT[g % 2]
                    for blk in range(4):
                        p_, pk = (PS0, PS1)[blk % 2]
                        for k in range(16):
                            mm(p_[:, 0:512], WG[:, k, blk * 128:(blk + 1) * 128], hT_own[:, k, g * 512:(g + 1) * 512],
                               k == 0, k == 15, r=("WG",), w=(pk,))
                        cast(qkpre[:, blk, 3:515], p_[:, 0:512], r=(pk,), w=(("qkpre", blk),), eng="act")
                        cidx = (2 * h + blk) if blk < 2 else (8 + 2 * h + blk - 2)
                        conv_silu(qkpre[:, blk, :], ("qkpre", blk), cidx, accb, "accb", qk[:, blk, :], ("qkT", g % 2, blk))
                        cast(qkpre[:, blk, 0:3], qkpre[:, blk, 512:515], r=(("qkpre", blk),), w=(("qkpre", blk),), eng="dve")

                def b1_P(t, h=h):
                    vb = t % 2
                    pA, pAk = PS0
                    for k in range(16):
                        mm(pA[:, 0:512], hT_own[:, k, t * 128:(t + 1) * 128], WG[:, k, 512:1024], k == 0, k == 15,
                           r=("WG",), w=(pAk,))
                    pB, pBk = PS1
                    for k in range(16):
                        mm(pB[:, 0:256], hT_own[:, k, t * 128:(t + 1) * 128], WG[:, k, 1024:1280], k == 0, k == 15,
                           r=("WG",), w=(pBk,))
                    cast(vaug1[vb][:, 0:256], pA[:, 0:256], r=(pAk,), w=(("vaug1", vb),), eng="act")
                    act(sgo, pA[:, 256:512], AF.Tanh, r=(pAk,), w=("sgo",), scale=0.5)
                    act(slz, pB[:, 0:256], AF.Silu, r=(pBk,), w=("slz",))
                    stt("dve", Gt[vb], sgo, 1.0, slz, ALU.add, ALU.mult, r=("sgo", "slz"), w=(("Gt", vb),))
                    tt("dve", Gt[vb], Gt[vb], mhg[:, h * 256:(h + 1) * 256], ALU.mult, r=(("Gt", vb), "mhg"), w=(("Gt", vb),))

                def b1_S(t, h=h):
                    vb = t % 2
                    gp = (t // 4) % 2
                    tsl = slice((t % 4) * 128, (t % 4 + 1) * 128)
                    pS, pSk = PS2
                    for blk in range(2):
                        mm(pS[:, 0:128], qkT[gp][:, 2 + blk, tsl], qkT[gp][:, blk, tsl], blk == 0, blk == 1,
                           r=(("qkT", gp, blk), ("qkT", gp, 2 + blk)), w=(pSk,))
                    stt("dve", STb[vb], pS[:, 0:128], wa_o[:, t, h:h + 1], tri[:], ALU.mult, ALU.mult,
                        r=(pSk, ("gown", t), "tri"), w=(("STb", vb),))

                def b1_N(t, h=h):
                    vb = t % 2
                    gp = (t // 4) % 2
                    tsl = slice((t % 4) * 128, (t % 4 + 1) * 128)
                    pN, pNk = PS3
                    mm(pN[:, 0:257], STb[vb], vaug1[vb], True, False, r=(("STb", vb), ("vaug1", vb)), w=(pNk,))
                    for blk in range(2):
                        mm(pN[:, 0:257], qkT[gp][:, blk, tsl], ctbf[h][blk][:], False, blk == 1,
                           r=(("qkT", gp, blk), ("ct", h, blk)), w=(pNk,))

                def b1_Utr(t, h=h):
                    vb = t % 2
                    gp = (t // 4) % 2
                    tsl = slice((t % 4) * 128, (t % 4 + 1) * 128)
                    pt, ptk = psb[0], ("psb", 0)
                    for blk in range(2):
                        tr(pt[:, blk * 128:(blk + 1) * 128], qkT[gp][:, 2 + blk, tsl], r=(("qkT", gp, 2 + blk),), w=(ptk,))
                    ts("dve", kpps[vb], pt[:, 0:256], wk_o[:, t, h:h + 1], None, ALU.mult, None,
                       r=(ptk, ("gown", t)), w=(("kpp", vb),))

                def b1_Umm(t, h=h):
                    vb = t % 2
                    for blk in range(2):
                        pk = ("psS", blk)
                        p_ = psS[:, blk * 512:blk * 512 + 257]
                        mm(p_, kpps[vb][:, blk * 128:(blk + 1) * 128], vaug1[vb], True, True,
                           r=(("kpp", vb), ("vaug1", vb)), w=(pk,))
                        stt("dve", state[h][blk][:], state[h][blk][:], eB_o[:, t, h:h + 1], p_, ALU.mult, ALU.add,
                            r=(pk, ("gown", t), ("st", h, blk)), w=(("st", h, blk),))
                        cast(ctbf[h][blk][:], state[h][blk][:], r=(("st", h, blk),), w=(("ct", h, blk),), eng="act")

                def b1_Ychain(t, h=h):
                    vb = t % 2
                    pN, pNk = PS3
                    gk = ("gown", t)
                    tt("dve", sm[:, 44:45], pN[:, 256:257], ebc_o[:, t, h:h + 1], ALU.mult, r=(pNk, gk), w=("d1",))
                    ts("dve", sm[:, 54:55], sm[:, 44:45], -1.0, None, ALU.mult, None, r=("d1",), w=("d1n",))
                    tt("dve", sm[:, 44:45], sm[:, 44:45], sm[:, 54:55], ALU.max, r=("d1", "d1n"), w=("d1",))
                    ts("dve", sm[:, 44:45], sm[:, 44:45], 1.0, 1.0, ALU.max, ALU.mult, r=("d1",), w=("d1",))
                    S.add("dve", lambda e: e.reciprocal(sm[:, 44:45], sm[:, 44:45]), r=("d1",), w=("d1",))
                    tt("dve", sm[:, 45:46], ebc_o[:, t, h:h + 1], sm[:, 44:45], ALU.mult, r=("d1", gk), w=("rr",))
                    memset("dve", sm[:, 46:47], 0.0, w=("ss2",))
                    act(junk, pN[:, 0:256], AF.Square, r=(pNk, "rr", "ss2"), w=("junk", "ss2"), scale=sm[:, 45:46],
                        accum=sm[:, 46:47])
                    ts("dve", sm[:, 47:48], sm[:, 46:47], 1.0 / 256, EPS, ALU.mult, ALU.add, r=("ss2",), w=("r2",))
                    rsqrt_col(sm[:, 47:48], "r2")
                    tt("dve", sm[:, 47:48], sm[:, 47:48], sm[:, 45:46], ALU.mult, r=("r2", "rr"), w=("r2",))
                    stt("dve", ybf[vb], pN[:, 0:256], sm[:, 47:48], Gt[vb], ALU.mult, ALU.mult,
                        r=(pNk, "r2", ("Gt", vb)), w=(("ybf", vb),))

                def b1_Ytr(t, h=h):
                    vb = t % 2
                    pt, ptk = psb[1], ("psb", 1)
                    for blk in range(2):
                        tr(pt[:, blk * 128:(blk + 1) * 128], ybf[vb][:, blk * 128:(blk + 1) * 128], r=(("ybf", vb),), w=(ptk,))
                    cast(yTt[vb].rearrange("p b c -> p (b c)"), pt[:, 0:256], r=(ptk,), w=(("yTt", vb),), eng="act")
                    dma(dmaq(), yT_d[2 * h:2 * h + 2, :, t * 128:(t + 1) * 128].rearrange("b p t -> p b t"), yTt[vb],
                        r=(("yTt", vb),), w=(("yTd", 2 * h, t),))

                b1_proj(0)
                b1_P(0)
                b1_S(0)
                for t in range(NOWN):
                    if t + 1 < NOWN:
                        if (t + 1) % 4 == 0:
                            b1_proj((t + 1) // 4)
                        b1_P(t + 1)
                        b1_S(t + 1)
                    b1_N(t)
                    b1_Utr(t)
                    if t > 0:
                        b1_Ytr(t - 1)
                    b1_Umm(t)
                    b1_Ychain(t)
                b1_Ytr(NOWN - 1)
            S.barrier()
            stop("B1", [(state[0][0][:], 257)])

            aoff[0] = mark_b
            WG2s = [abf(16 * 512).rearrange("p (k c) -> p k c", k=16) for _ in range(2)]
            akT = abf(2560)
            aqT = abf(2048)
            Vt = abf(20 * 128).rearrange("p (t c) -> p t c", t=20)
            slzA = abf(16 * 128).rearrange("p (t c) -> p t c", t=16)
            Pb = [abf(640) for _ in range(2)]
            PT = [abf(640) for _ in range(2)]
            yab = [abf(128) for _ in range(2)]
            yaT = [abf(128) for _ in range(2)]
            stg2 = [af32(512) for _ in range(4)]
            rb = af32(640)
            bm = af32(640)
            sbuf_s = [af32(640) for _ in range(2)]
            wA = w_in[:, C_AQ:C_AQ + 4096].rearrange("r (s c) -> r s c", c=1024)
            def b2_load(h, cast_eng=None, q=None):
                W_ = WG2s[h % 2]
                load_w(lambda k: W_[:, k, :].rearrange("p (s c) -> p s c", c=128),
                       lambda k: wA[k * 128:(k + 1) * 128, :, h * 128:(h + 1) * 128],
                       16, [s_.rearrange("p (s c) -> p s c", c=128) for s_ in stg2], "stg2", ("WG2", h % 2),
                       cast_eng=cast_eng, q=q)

            b2_load(0)
            for h in range(8):
                WG2 = WG2s[h % 2]
                WK = ("WG2", h % 2)
                dma("sp", rb, relmat[h], r=(), w=("rb",))
                tt("pool", bm, rb, amask_s[:], ALU.add, r=("rb", "amask"), w=("bm",))
                for g5 in range(5):
                    src = hT_halo if g5 == 0 else hT_own[:, :, (g5 - 1) * 512:g5 * 512]
                    srk = ()
                    p_, pk = getps()
                    for k in range(16):
                        mm(p_[:, 0:512], WG2[:, k, 128:256], src[:, k, :], k == 0, k == 15, r=(WK,) + srk, w=(pk,))
                    cast(akT[:, g5 * 512:(g5 + 1) * 512], p_[:, 0:512], r=(pk,), w=(("akT", g5),), eng="act")
                    if g5 > 0:
                        p_, pk = getps()
                        for k in range(16):
                            mm(p_[:, 0:512], WG2[:, k, 0:128], src[:, k, :], k == 0, k == 15, r=(WK,) + srk, w=(pk,))
                        act(aqT[:, (g5 - 1) * 512:g5 * 512], p_[:, 0:512], AF.Copy, r=(pk,), w=(("aqT", g5 - 1),),
                            scale=float(128 ** -0.5))
                    for i in range(4):
                        ttile = g5 * 4 + i
                        p_, pk = getps()
                        for k in range(16):
                            mm(p_[:, 0:256], src[:, k, i * 128:(i + 1) * 128], WG2[:, k, 256:512], k == 0, k == 15,
                               r=(WK,) + srk, w=(pk,))
                        cast(Vt[:, ttile, :], p_[:, 0:128], r=(pk,), w=(("Vt", ttile),), eng="act")
                        if g5 > 0:
                            act(slzA[:, ttile - 4, :], p_[:, 128:256], AF.Silu, r=(pk,), w=(("slzA", ttile - 4),))
                if h + 1 < 8:
                    b2_load(h + 1, cast_eng="pool", q="sp")

                RS = [sm[:, 56:57], sm[:, 57:58], sm[:, 60:61]]

                def att_s1(t, h=h):
                    vb = t % 2
                    gq = ("aqT", t // 4)
                    kk0 = tuple(("akT", x) for x in sorted({t // 4, (t + 3) // 4, (t + 4) // 4}))
                    mm(psS[:, 0:512], aqT[:, t * 128:(t + 1) * 128], akT[:, t * 128:t * 128 + 512], True, True,
                       r=(gq,) + kk0, w=("psS",))
                    mm(psS[:, 512:640], aqT[:, t * 128:(t + 1) * 128], akT[:, t * 128 + 512:t * 128 + 640], True, True,
                       r=(gq,) + kk0, w=("psS",))
                    sbt = sbuf_s[vb]
                    sk = ("sbt", vb)
                    tt("dve", sbt, psS[:, 0:640], bm, ALU.add, r=("psS", "bm"), w=(sk,))
                    for kb in range(max(0, 4 - t)):
                        lt = NPRE - 4 + t + kb
                        ts("dve", sbt[:, kb * 128:(kb + 1) * 128], sbt[:, kb * 128:(kb + 1) * 128], ntile[:, lt:lt + 1], None,
                           ALU.add, None, r=(sk, "ntile"), w=(sk,))
                    S.add("dve", lambda e, sbt=sbt: e.reduce_max(sm[:, 48:49], sbt, AX.X), r=(sk,), w=("mx",))
                    ts("dve", sm[:, 48:49], sm[:, 48:49], -1.0, None, ALU.mult, None, r=("mx",), w=("mx",))
                    rsc = RS[t % 3]
                    memset("dve", rsc, 0.0, w=(("rsum", t % 3),))
                    act(Pb[vb], sbt, AF.Exp, r=(sk, "mx", ("rsum", t % 3)), w=(("Pb", vb), ("rsum", t % 3)), bias=sm[:, 48:49],
                        accum=rsc)

                def att_s2(t, h=h):
                    vb = t % 2
                    pt, ptk = psb[0], ("psb", 0)
                    for kb in range(5):
                        tr(pt[:, kb * 128:(kb + 1) * 128], Pb[vb][:, kb * 128:(kb + 1) * 128], r=(("Pb", vb),), w=(ptk,))
                    cast(PT[vb], pt[:, 0:640], r=(ptk,), w=(("PT", vb),), eng="act")

                def att_s3(t, h=h):
                    vb = t % 2
                    rsc = RS[t % 3]
                    rrc = sm[:, 58 + vb:59 + vb]
                    pO, pOk = getps()
                    for kb in range(5):
                        mm(pO[:, 0:128], PT[vb][:, kb * 128:(kb + 1) * 128], Vt[:, t + kb, :], kb == 0, kb == 4,
                           r=(("PT", vb), ("Vt", t + kb)), w=(pOk,))
                    S.add("dve", lambda e: e.reciprocal(rrc, rsc), r=(("rsum", t % 3),), w=(("rrs", vb),))
                    stt("dve", yab[vb], pO[:, 0:128], rrc, slzA[:, t, :], ALU.mult, ALU.mult,
                        r=(pOk, ("rrs", vb), ("slzA", t)), w=(("yab", vb),))
                    pt2, pt2k = psb[1], ("psb", 1)
                    tr(pt2[:, 0:128], yab[vb], r=(("yab", vb),), w=(pt2k,))
                    cast(yaT[vb], pt2[:, 0:128], r=(pt2k,), w=(("yaT", vb),), eng="act")
                    dma("act", yT_d[8 + h, :, t * 128:(t + 1) * 128], yaT[vb], r=(("yaT", vb),), w=(("yTd", 8 + h, t),))

                att_s1(0)
                att_s1(1)
                att_s2(0)
                for t in range(NOWN):
                    if t + 2 < NOWN:
                        att_s1(t + 2)
                    if t + 1 < NOWN:
                        att_s2(t + 1)
                    att_s3(t)
            S.barrier()
            stop("B2", [(sm[:], 64)])

            aoff[0] = mark_b
            yT_res = abf(16 * 2048).rearrange("p (b c) -> p b c", b=16)
            hh_flat = hT_halo[:].rearrange("p k c -> p (k c)")
            wgm = [abf(16 * 128).rearrange("p (k c) -> p k c", k=16), hh_flat[:, 0:2048].rearrange("p (k c) -> p k c", k=16)]
            wga = [abf(16 * 128).rearrange("p (k c) -> p k c", k=16), hh_flat[:, 2048:4096].rearrange("p (k c) -> p k c", k=16)]
            wpm = [abf(8 * 128).rearrange("p (k c) -> p k c", k=8), hh_flat[:, 4096:5120].rearrange("p (k c) -> p k c", k=8)]
            wpa = [abf(8 * 128).rearrange("p (k c) -> p k c", k=8), hh_flat[:, 5120:6144].rearrange("p (k c) -> p k c", k=8)]
            mTb = [abf(512) for _ in range(2)]
            stg3 = [af32(8 * 128).rearrange("p (k c) -> p k c", k=8) for _ in range(2)]
            stg3.append(mhg[:].rearrange("p (k c) -> p k c", k=8))
            stg3.append(hh_flat[:, 6144:8192].bitcast(F32).rearrange("p (k c) -> p k c", k=8))
            sgm = af32(512)
            sga = af32(512)
            t1b = af32(512)
            for b in range(16):
                dma(dmaq(), yT_res[:, b, :], yT_d[b], r=(), w=("yT_res",))
            si3 = [0]

            def b3_load(c):
                cbuf = c % 2
                for (dst, src, nk, nm) in ((wgm, w_in[:, C_GM + c * 128:C_GM + (c + 1) * 128], 16, "wgm"),
                                           (wga, w_in[:, C_GA + c * 128:C_GA + (c + 1) * 128], 16, "wga"),
                                           (wpm, w_pm[:, c * 128:(c + 1) * 128], 8, "wpm"),
                                           (wpa, w_pa[:, c * 128:(c + 1) * 128], 8, "wpa")):
                    for k8 in range(nk // 8):
                        st_ = stg3[si3[0] % 4]
                        sk = ("stg3", si3[0] % 4)
                        si3[0] += 1
                        for k4 in range(2):
                            r0 = (k8 * 8 + k4 * 4) * 128
                            dma(dmaq(), st_[:, 4 * k4:4 * k4 + 4, :],
                                src[r0:r0 + 512, :].rearrange("(k p) c -> p k c", p=128), r=(), w=(sk,))
                        cast(dst[cbuf][:, k8 * 8:(k8 + 1) * 8, :], st_[:], r=(sk,), w=((nm, cbuf),), eng="pool")

            b3_load(0)
            for c in range(16):
                cbuf = c % 2
                if c + 1 < 16:
                    b3_load(c + 1)
                for g in range(4):
                    gsl = slice(g * 512, (g + 1) * 512)
                    p1, p1k = getps()
                    for k in range(16):
                        mm(p1[:, 0:512], wgm[cbuf][:, k, :], hT_own[:, k, gsl], k == 0, k == 15, r=(("wgm", cbuf),), w=(p1k,))
                    act(sgm, p1[:, 0:512], AF.Sigmoid, r=(p1k,), w=("sgm",))
                    p2, p2k = getps()
                    for k in range(16):
                        mm(p2[:, 0:512], wga[cbuf][:, k, :], hT_own[:, k, gsl], k == 0, k == 15, r=(("wga", cbuf),), w=(p2k,))
                    act(sga, p2[:, 0:512], AF.Sigmoid, r=(p2k,), w=("sga",))
                    p3, p3k = getps()
                    for k in range(8):
                        mm(p3[:, 0:512], wpm[cbuf][:, k, :], yT_res[:, k, gsl], k == 0, k == 7,
                           r=(("wpm", cbuf), "yT_res"), w=(p3k,))
                    tt("dve", t1b, sgm, p3[:, 0:512], ALU.mult, r=("sgm", p3k), w=("t1b",))
                    p4, p4k = getps()
                    for k in range(8):
                        mm(p4[:, 0:512], wpa[cbuf][:, k, :], yT_res[:, 8 + k, gsl], k == 0, k == 7,
                           r=(("wpa", cbuf), "yT_res"), w=(p4k,))
                    tt("dve", sga, sga, p4[:, 0:512], ALU.mult, r=("sga", p4k), w=("sga",))
                    mb = mTb[(c * 4 + g) % 2]
                    mk_ = ("mTb", (c * 4 + g) % 2)
                    tt("dve", mb, t1b, sga, ALU.add, r=("t1b", "sga"), w=(mk_,))
                    dma(dmaq(), mT_d[c, :, gsl], mb, r=(mk_,), w=(("mTd", c, g),))
            S.barrier()
            stop("B3", [(sm[:], 64)])

            areset()
            Wout = abf(16 * 2048).rearrange("p (k c) -> p k c", k=16)
            mTt = [abf(16 * 128).rearrange("p (c t) -> p c t", c=16) for _ in range(2)]
            fgb = af32(D)
            stgC = [af32(D) for _ in range(2)]
            xts = [af32(D) for _ in range(2)]
            obuf = [af32(D) for _ in range(2)]
            junkC = af32(D)
            dma("sp", fgb, fg_bc, r=(), w=("fgb",))
            for k in range(16):
                st_ = stgC[k % 2]
                sk = ("stgC", k % 2)
                dma(dmaq(), st_, w_out[k * 128:(k + 1) * 128, :], r=(), w=(sk,))
                tt("dve", Wout[:, k, :], st_, gate_bc[:], ALU.mult, r=(sk, "gate_bc"), w=("Wout",))
            outkeys = []
            for t in range(NOWN):
                vb = t % 2
                for c4 in range(4):
                    dma(dmaq(), mTt[vb][:, 4 * c4:4 * c4 + 4, :],
                        mT_d[4 * c4:4 * c4 + 4, :, t * 128:(t + 1) * 128].rearrange("c p t -> p c t"), r=(), w=(("mTt", vb),))
                dma(dmaq(), xts[vb], xl[(NPRE + t) * 128:(NPRE + t + 1) * 128, :], r=(), w=(("xts", vb),))
                ob = obuf[vb]
                ok = ("ob", vb)
                for cbk in range(4):
                    p_, pk = getps()
                    for c in range(16):
                        mm(p_[:, 0:512], mTt[vb][:, c, :], Wout[:, c, cbk * 512:(cbk + 1) * 512], c == 0, c == 15,
                           r=(("mTt", vb), "Wout"), w=(pk,))
                    tt("dve", ob[:, cbk * 512:(cbk + 1) * 512], p_[:, 0:512], xts[vb][:, cbk * 512:(cbk + 1) * 512], ALU.add,
                       r=(pk, ("xts", vb)), w=(ok,))
                memset("dve", sm[:, 52:53], 0.0, w=("ssC",))
                act(junkC, ob, AF.Square, r=(ok, "ssC"), w=("junkC", "ssC"), accum=sm[:, 52:53])
                ts("dve", sm[:, 53:54], sm[:, 52:53], 1.0 / D, EPS, ALU.mult, ALU.add, r=("ssC",), w=("rC",))
                rsqrt_col(sm[:, 53:54], "rC")
                stt("dve", ob, ob, sm[:, 53:54], fgb, ALU.mult, ALU.mult, r=(ok, "rC", "fgb"), w=(ok,))
                dma(dmaq(), y_out[t * 128:(t + 1) * 128, :], ob, r=(ok,), w=(("yout", t),))
                outkeys.append(("yout", t))
        try:
            body()
        except _Stop:
            pass
        S.finish(())
        S.emit()
    return nc


_NC_CACHE = {}


def _consts():
    ident = np.eye(128, dtype=np.float32).astype(ml_dtypes.bfloat16)
    s = np.arange(128)[:, None]
    t = np.arange(128)[None, :]
    tri = (s <= t).astype(np.float32)
    ones = np.ones((128, 128), np.float32)
    q = np.arange(128)[:, None]
    kap = np.arange(640)[None, :]
    cq, ck = q // 64, kap // 64
    allowed = (ck >= cq) & (ck <= cq + 8)
    amask = np.where(allowed, 0.0, NEG).astype(np.float32)
    relidx = np.clip(q + 512 - kap, -63, 128) + 63
    return ident, tri, ones, tri.copy(), amask, relidx


def kernel(x, c, w_ada, b_ada, norm_g, w_in, b_if, conv_w, conv_b, mh_norm_g, rel_bias,
           w_proj_m, w_proj_a, w_out, final_norm_g):
    f = np.float32
    x = np.asarray(x, f)
    ident, tri, ones, mst, amask, relidx = _consts()
    if "nc" not in _NC_CACHE:
        _NC_CACHE["nc"] = build_nc()
    nc = _NC_CACHE["nc"]
    rep = lambda v, n=128: np.ascontiguousarray(np.broadcast_to(np.asarray(v, f).reshape(1, -1), (n, np.asarray(v).size)))
    colT = lambda v: np.ascontiguousarray(np.asarray(v, f).reshape(16, 128).T)
    cwl = np.ascontiguousarray(np.asarray(conv_w[0], f).T.reshape(16, 128, 4).transpose(1, 0, 2))
    shared = {
        "w_ada": np.ascontiguousarray(w_ada[0], f), "b_ada": np.ascontiguousarray(b_ada[0], f).reshape(1, -1),
        "ngT": colT(norm_g[0]), "w_in": np.ascontiguousarray(w_in[0], f), "bif_bc": rep(b_if[0]),
        "convw": cwl, "convb": colT(conv_b[0]), "mhg_bc": rep(mh_norm_g[0]),
        "relmat": np.ascontiguousarray(np.asarray(rel_bias[0], f)[:, relidx]), "amask": amask,
        "w_pm": np.ascontiguousarray(w_proj_m[0], f), "w_pa": np.ascontiguousarray(w_proj_a[0], f),
        "w_out": np.ascontiguousarray(w_out[0], f), "fg_bc": rep(final_norm_g),
        "ident": ident, "tri": tri, "ones": ones, "maskst": mst,
    }
    in_maps = []
    for core in range(8):
        b, j = core // 4, core % 4
        npad = (3 - j) * 16
        xl = np.zeros((NT * 128, D), f)
        xl[npad * 128:] = x[b, 0:(j + 1) * 2048]
        valid = (np.arange(NT) >= npad).astype(f)
        m = dict(shared)
        m["xl"] = xl
        m["cT"] = colT(c[b])
        m["padneg"] = rep(np.where(valid > 0, 0.0, NEG))
        m["tilevalid"] = rep(valid)
        m["negtile"] = rep(np.where(valid > 0, 0.0, NEG))
        in_maps.append(m)
    res = run_bass_kernel_spmd(nc, in_maps, core_ids=list(range(8)))
    out = np.empty((2, 8192, D), f)
    for core in range(8):
        b, j = core // 4, core % 4
        out[b, j * 2048:(j + 1) * 2048] = res.results[core]["y"]
    return out
```

```python
import os
import numpy as np
import ml_dtypes
from contextlib import ExitStack
import concourse.bass as bass
import concourse.mybir as mybir
from concourse.bass_utils import run_bass_kernel_spmd

F32 = mybir.dt.float32
BF16 = mybir.dt.bfloat16
ALU = mybir.AluOpType
AF = mybir.ActivationFunctionType
AX = mybir.AxisListType

D = 2048
NT = 64
NPRE = 48
NOWN = 16
EPS = 1e-6
C_MQ, C_MK, C_MV, C_MO, C_MZ, C_MI, C_MF = 0, 1024, 2048, 3072, 4096, 5120, 5124
C_AQ, C_AK, C_AV, C_AZ, C_GM, C_GA = 5128, 6152, 7176, 8200, 9224, 11272
IN_COLS = 13320
NEG = -30000.0
ENGS = ("sp", "act", "dve", "pool", "pe")
KSTOP = os.environ.get("KSTOP")


class _Stop(Exception):
    pass


class Op:
    __slots__ = ("eng", "fn", "deps", "dma", "sig", "sem", "val")

    def __init__(self, eng, fn, dma):
        self.eng, self.fn, self.dma = eng, fn, dma
        self.deps, self.sig, self.sem, self.val = [], dma, None, 0


class Sched:
    ND = 40

    def __init__(self, nc, es):
        self.nc = nc
        self.eops = {e: [] for e in ENGS}
        self.lastw, self.readers = {}, {}
        self.csem = {e: es.enter_context(nc.semaphore("cs_" + e)) for e in ENGS}
        self.dsem = [es.enter_context(nc.semaphore("ds%d" % i)) for i in range(self.ND)]
        self.dlast = [None] * self.ND
        self.duse = [0] * self.ND
        self.dn = 0

    @staticmethod
    def _is_psum(k):
        return k == "psS" or (isinstance(k, tuple) and len(k) > 0 and k[0] in ("ps", "psb", "psS"))

    def add(self, eng, fn, r=(), w=(), dma=False):
        w = tuple(w) + tuple(k for k in r if self._is_psum(k))
        r = tuple(k for k in r if not self._is_psum(k))
        op = Op(eng, fn, dma)
        deps = []
        for k in r:
            if k in self.lastw:
                deps.append(self.lastw[k])
        for k in w:
            if k in self.lastw:
                deps.append(self.lastw[k])
            deps.extend(self.readers.get(k, ()))
        if dma:
            i = self.dn % self.ND
            self.dn += 1
            if self.dlast[i] is not None:
                deps.append(self.dlast[i])
            self.duse[i] += 1
            op.sem, op.val = self.dsem[i], 16 * self.duse[i]
            self.dlast[i] = op
        seen = set()
        for d in deps:
            if d is op or id(d) in seen:
                continue
            seen.add(id(d))
            if d.eng == "pe" and eng == "pe" and not d.dma:
                continue
            d.sig = True
            op.deps.append(d)
        for k in r:
            lst = self.readers.setdefault(k, [])
            if not dma:
                lst[:] = [o_ for o_ in lst if o_.dma or o_.eng != eng]
            lst.append(op)
        for k in w:
            self.lastw[k] = op
            self.readers[k] = []
        self.eops[eng].append(op)
        return op

    def barrier(self):
        lasts = []
        for e in ENGS:
            for o in reversed(self.eops[e]):
                if not o.dma and o.fn is not None:
                    lasts.append(o)
                    break
        pend = [d for d in self.dlast if d is not None]
        for e in ENGS:
            op = Op(e, None, False)
            for d in lasts + pend:
                if d.eng == e and not d.dma:
                    continue
                d.sig = True
                op.deps.append(d)
            self.eops[e].append(op)
        self.lastw, self.readers = {}, {}

    def finish(self, keys):
        op = Op("sp", None, False)
        for d in self.dlast:
            if d is not None:
                op.deps.append(d)
        self.eops["sp"].append(op)

    def emit(self):
        nc = self.nc
        for e in ENGS:
            c = 0
            for o in self.eops[e]:
                if o.dma or o.fn is None:
                    continue
                if o.sig:
                    c += 1
                    o.sem, o.val = self.csem[e], c

        def run(ename, eng):
            known = {}
            for o in self.eops[ename]:
                for d in o.deps:
                    key = id(d.sem)
                    if known.get(key, 0) >= d.val:
                        continue
                    eng.wait_ge(d.sem, d.val)
                    known[key] = d.val
                if o.fn is None:
                    continue
                ins = o.fn(eng)
                if o.sig:
                    ins.then_inc(o.sem, 16 if o.dma else 1)

        with nc.Block() as block:
            @block.sync
            def _(e):
                run("sp", e)

            @block.scalar
            def _(e):
                run("act", e)

            @block.vector
            def _(e):
                run("dve", e)

            @block.gpsimd
            def _(e):
                run("pool", e)

            @block.tensor
            def _(e):
                run("pe", e)


def build_nc():
    nc = bass.Bass("TRN2", target_bir_lowering=False)

    def din(name, shape, dt=F32):
        return nc.dram_tensor(name, list(shape), dt, kind="ExternalInput").ap()

    xl = din("xl", [NT * 128, D])
    cT = din("cT", [128, 16])
    w_ada = din("w_ada", [D, 3 * D])
    b_ada = din("b_ada", [1, 3 * D])
    ngT = din("ngT", [128, 16])
    w_in = din("w_in", [D, IN_COLS])
    bif_bc = din("bif_bc", [128, 8])
    convw = din("convw", [128, 16, 4])
    convb = din("convb", [128, 16])
    mhg_bc = din("mhg_bc", [128, 1024])
    relmat = din("relmat", [8, 128, 640])
    amask = din("amask", [128, 640])
    w_pm = din("w_pm", [1024, D])
    w_pa = din("w_pa", [1024, D])
    w_out = din("w_out", [D, D])
    fg_bc = din("fg_bc", [128, D])
    padneg = din("padneg", [128, NT])
    tilevalid = din("tilevalid", [128, NT])
    negtile = din("negtile", [128, NT])
    ident_d = din("ident", [128, 128], BF16)
    tri_d = din("tri", [128, 128])
    ones_d = din("ones", [128, 128])
    mst_d = din("maskst", [128, 128])
    y_out = nc.dram_tensor("y", [NOWN * 128, D], F32, kind="ExternalOutput").ap()
    dbg = nc.dram_tensor("dbg", [128, 8192], F32, kind="ExternalOutput").ap() if KSTOP else None
    yT_d = nc.dram_tensor("yT_d", [16, 128, NOWN * 128], BF16, kind="Internal").ap()
    mT_d = nc.dram_tensor("mT_d", [16, 128, NOWN * 128], BF16, kind="Internal").ap()

    es = ExitStack()
    with es, nc.allow_low_precision("bf16 matmul operands, fp32 accumulation"), \
            nc.allow_non_contiguous_dma("column-sliced weight loads"):
        S = Sched(nc, es)

        def sb(name, shape, dt=F32):
            return es.enter_context(nc.sbuf_tensor("s_" + name, list(shape), dt))

        def pst(name, shape, dt=F32):
            return es.enter_context(nc.psum_tensor(name, list(shape), dt))

        ident = sb("ident", [128, 128], BF16)
        tri = sb("tri", [128, 128])
        ones = sb("ones", [128, 128])
        gs = sb("gs", [128, 16])
        shift = sb("shift", [128, 16])
        ngt_s = sb("ngt_s", [128, 16])
        gate_bc = sb("gate_bc", [128, D])
        cw = sb("cw", [128, 16, 4])
        cb_ = sb("cb_", [128, 16])
        bif = sb("bif", [128, 8])
        mhg = sb("mhg", [128, 1024])
        pneg = sb("pneg", [128, NT])
        tval = sb("tval", [128, NT])
        ntile = sb("ntile", [128, NT])
        amask_s = sb("amask_s", [128, 640])
        wif = sb("wif", [128, 16, 8], BF16)
        wifs = sb("wifs", [128, 16, 8])
        state = [[sb("st%d%d" % (h, b), [128, 257]) for b in range(2)] for h in range(4)]
        ctbf = [[sb("ct%d%d" % (h, b), [128, 257], BF16) for b in range(2)] for h in range(4)]
        hT_halo = sb("hT_halo", [128, 16, 512], BF16)
        wk_o = sb("wk_o", [128, NOWN, 4])
        wa_o = sb("wa_o", [128, NOWN, 4])
        eB_o = sb("eB_o", [128, NOWN, 4])
        ebc_o = sb("ebc_o", [128, NOWN, 4])
        sm = sb("sm", [128, 64])
        ARENA_COLS = 40000
        arena = sb("arena", [128, ARENA_COLS])

        ps = [pst("ps%d" % i, [128, 512]) for i in range(4)]
        psS = pst("psS", [128, 1024])
        psb = [pst("psb%d" % i, [128, 1024], BF16) for i in range(2)]
        rot = {"ps": 0, "psb": 0, "cast": 0, "q": 0}

        def getps():
            i = rot["ps"] % 4
            rot["ps"] += 1
            return ps[i], ("ps", i)

        def getpsb():
            i = rot["psb"] % 2
            rot["psb"] += 1
            return psb[i], ("psb", i)

        aoff = [0]

        def areset():
            aoff[0] = 0

        def af32(cols):
            o = aoff[0]
            aoff[0] += cols
            assert aoff[0] <= ARENA_COLS, aoff[0]
            return arena[:, o:o + cols]

        def abf(cols):
            n = (cols + 1) // 2
            return af32(n).bitcast(BF16)[:, 0:cols]

        def dma(q, out, in_, r, w):
            S.add(q, lambda e, o=out, i=in_: e.dma_start(out=o, in_=i), r=r, w=w, dma=True)

        def dmaq():
            rot["q"] += 1
            return "sp" if rot["q"] % 2 else "pool"

        def act(out, in_, func, r, w, bias=None, scale=None, accum=None):
            kw = {}
            if bias is not None:
                kw["bias"] = bias
            if scale is not None:
                kw["scale"] = scale
            if accum is not None:
                kw["accum_out"] = accum
            S.add("act", lambda e: e.activation(out, in_, func, **kw), r=r, w=w)

        def ts(eng, out, in0, s1, s2, op0, op1, r, w):
            if s2 is None:
                s2, op1 = (1.0, ALU.mult) if op0 == ALU.add else (0.0, ALU.add)
            S.add(eng, lambda e: e.tensor_scalar(out, in0, s1, s2, op0, op1), r=r, w=w)

        def rsqrt_col(col, key):
            S.add("act", lambda e: e.activation(col, col, AF.Sqrt), r=(key,), w=(key,))
            S.add("dve", lambda e: e.reciprocal(col, col), r=(key,), w=(key,))

        def tt(eng, out, in0, in1, op, r, w):
            S.add(eng, lambda e: e.tensor_tensor(out, in0, in1, op), r=r, w=w)

        def stt(eng, out, in0, sc, in1, op0, op1, r, w):
            S.add(eng, lambda e: e.scalar_tensor_tensor(out, in0, sc, in1, op0, op1), r=r, w=w)

        def mm(out, lhsT, rhs, start, stop, r, w):
            S.add("pe", lambda e: e.matmul(out, lhsT, rhs, start=start, stop=stop), r=r, w=w)

        def tr(out, in_, r, w):
            S.add("pe", lambda e: e.transpose(out, in_, ident[:]), r=tuple(r) + ("ident",), w=w)

        def cast(out, in_, r, w, eng=None):
            if eng is None:
                rot["cast"] += 1
                eng = ("act", "pool")[rot["cast"] % 2]
            if eng == "act":
                S.add("act", lambda e: e.copy(out, in_), r=r, w=w)
            else:
                S.add(eng, lambda e: e.tensor_copy(out, in_), r=r, w=w)

        def memset(eng, ap, v, w):
            S.add(eng, lambda e: e.memset(ap, v), w=w)

        def stop(tag, dumps=()):
            if KSTOP != tag:
                return
            S.barrier()
            off = 0
            for ap_, n_ in dumps:
                dma("sp", dbg[:, off:off + n_], ap_, r=(), w=(("dbg", off),))
                off += n_
            raise _Stop()

        def body():
            for (t_, d_, k_) in ((ident, ident_d, "ident"), (tri, tri_d, "tri"), (ones, ones_d, "ones"),
                                 (ngt_s, ngT, "ngt"), (cw, convw, "cw"),
                                 (cb_, convb, "cb"), (bif, bif_bc, "bif"), (mhg, mhg_bc, "mhg"),
                                 (pneg, padneg, "pneg"), (tval, tilevalid, "tval"),
                                 (ntile, negtile, "ntile"), (amask_s, amask, "amask")):
                dma("sp", t_[:], d_, r=(), w=(k_,))
            for k4 in range(4):
                dma("sp", wifs[:, 4 * k4:4 * k4 + 4, :],
                    w_in[k4 * 512:(k4 + 1) * 512, C_MI:C_MI + 8].rearrange("(k p) c -> p k c", p=128), r=(), w=("wifs",))
            cast(wif[:], wifs[:], r=("wifs",), w=("wif",), eng="dve")
            for h in range(4):
                for b in range(2):
                    memset("dve", state[h][b][:], 0.0, w=(("st", h, b),))

            areset()
            sc_in = af32(16)
            scv = af32(16)
            modrow = af32(3 * D)[0:1, :]
            badar = af32(3 * D)[0:1, :]
            stgA = [af32(3072) for _ in range(2)]
            dma("sp", sc_in, cT, r=(), w=("sc_in",))
            dma("sp", badar, b_ada, r=(), w=("badar",))
            act(scv, sc_in, AF.Silu, r=("sc_in",), w=("scv",))
            banks = [(ps[0], ("ps", 0), 0), (ps[1], ("ps", 1), 0), (ps[2], ("ps", 2), 0),
                     (ps[3], ("ps", 3), 0), (psS, ("psS",), 0), (psS, ("psS",), 512)]
            for half in range(2):
                for k in range(16):
                    st_ = stgA[k % 2]
                    key = ("stgA", k % 2)
                    dma(dmaq(), st_, w_ada[k * 128:(k + 1) * 128, half * 3072:(half + 1) * 3072], r=(), w=(key,))
                    for cbk in range(6):
                        t_, pk, off = banks[cbk]
                        mm(t_[0:1, off:off + 512], scv[:, k:k + 1], st_[:, cbk * 512:(cbk + 1) * 512],
                           k == 0, k == 15, r=(key, "scv"), w=(pk,))
                for cbk in range(6):
                    t_, pk, off = banks[cbk]
                    c0 = half * 3072 + cbk * 512
                    tt("dve", modrow[:, c0:c0 + 512], t_[0:1, off:off + 512], badar[:, c0:c0 + 512], ALU.add,
                       r=(pk, "badar"), w=("modrow",))
            pcol, pck = getps()
            for c in range(32):
                mm(pcol[:, c:c + 1], modrow[0:1, c * 128:(c + 1) * 128], ones[0:1, 0:1], True, True,
                   r=("modrow", "ones"), w=(pck,))
            cast(shift[:], pcol[:, 0:16], r=(pck,), w=("shift",), eng="dve")
            stt("dve", gs[:], pcol[:, 16:32], 1.0, ngt_s[:], ALU.add, ALU.mult, r=(pck, "ngt"), w=("gs",))
            for cbk in range(4):
                t_, pk = getps()
                mm(t_[:, 0:512], ones[0:1, :], modrow[0:1, 2 * D + cbk * 512:2 * D + (cbk + 1) * 512], True, True,
                   r=("modrow", "ones"), w=(pk,))
                cast(gate_bc[:, cbk * 512:(cbk + 1) * 512], t_[:, 0:512], r=(pk,), w=("gate_bc",), eng="act")
            S.barrier()
            stop("S1", [(gs[:], 16), (shift[:], 16), (gate_bc[:, 0:64], 64)])

            def frontend1(tau, xbufs, xn, xnk, i2):
                xt = xbufs[i2 % len(xbufs)]
                xk = ("xt", i2 % len(xbufs))
                dma(dmaq(), xt, xl[tau * 128:(tau + 1) * 128, :], r=(), w=(xk,))
                memset("dve", sm[:, 0:1], 0.0, w=("ss",))
                act(xn, xt, AF.Square, r=(xk, "ss"), w=(xnk, "ss"), accum=sm[:, 0:1])
                ts("dve", sm[:, 1:2], sm[:, 0:1], 1.0 / D, EPS, ALU.mult, ALU.add, r=("ss",), w=("rstd",))
                rsqrt_col(sm[:, 1:2], "rstd")
                act(xn, xt, AF.Identity, r=(xk, "rstd"), w=(xnk,), scale=sm[:, 1:2])

            def frontend2(dstf, dkey, xn, xnk, halves=(0, 1), act_halves=(0,)):
                for hh in halves:
                    pt, ptk = psb[hh], ("psb", hh)
                    for kk in range(8):
                        k = hh * 8 + kk
                        tr(pt[:, kk * 128:(kk + 1) * 128], xn[:, k * 128:(k + 1) * 128], r=(xnk,), w=(ptk,))
                    for kk in range(8):
                        k = hh * 8 + kk
                        if hh in act_halves:
                            act(dstf(k), pt[:, kk * 128:(kk + 1) * 128], AF.Identity, r=(ptk, "gs", "shift"),
                                w=(dkey[0],), bias=shift[:, k:k + 1], scale=gs[:, k:k + 1])
                        else:
                            ts("dve", dstf(k), pt[:, kk * 128:(kk + 1) * 128], gs[:, k:k + 1], shift[:, k:k + 1],
                               ALU.mult, ALU.add, r=(ptk, "gs", "shift"), w=(dkey[1],))

            def frontend(tau, dstf, dkey, xbufs, xn, i2):
                frontend1(tau, xbufs, xn, "xn", i2)
                frontend2(dstf, dkey, xn, "xn")

            def load_w(dst, src_rows, nk, stgs, skey, dkey, cast_eng=None, q=None, rot_engs=None):
                for k in range(nk):
                    st_ = stgs[k % len(stgs)]
                    key = (skey, k % len(stgs))
                    dma(q or dmaq(), st_, src_rows(k), r=(), w=(key,))
                    cast(dst(k), st_, r=(key,), w=(dkey,), eng=(rot_engs[k % len(rot_engs)] if rot_engs else cast_eng))

            def gates(tau, hsrc, hkey, own_i):
                pg, pgk = getps()
                for k in range(16):
                    mm(pg[:, 0:8], hsrc(k), wif[:, k, :], k == 0, k == 15, r=tuple(hkey) + ("wif",), w=(pgk,))
                gl = sm[:, 8:16]
                tt("dve", gl, pg[:, 0:8], bif[:], ALU.add, r=(pgk, "bif"), w=("gl",))
                ts("dve", sm[:, 16:20], gl[:, 0:4], pneg[:, tau:tau + 1], None, ALU.add, None, r=("gl", "pneg"), w=("ig",))
                act(sm[:, 20:24], gl[:, 4:8], AF.Exp, r=("gl",), w=("e1",), scale=-1.0)
                ts("dve", sm[:, 20:24], sm[:, 20:24], 1.0, None, ALU.add, None, r=("e1",), w=("e1",))
                act(sm[:, 24:28], sm[:, 20:24], AF.Ln, r=("e1",), w=("lf",))
                ts("dve", sm[:, 24:28], sm[:, 24:28], -1.0, None, ALU.mult, None, r=("lf",), w=("lf",))
                pc, pckk = getps()
                mm(pc[:, 0:4], tri[:], sm[:, 24:28], True, True, r=("lf", "tri"), w=(pckk,))
                mm(pc[:, 4:8], ones[:], sm[:, 24:28], True, True, r=("lf", "ones"), w=(pckk,))
                tt("dve", sm[:, 28:32], sm[:, 16:20], pc[:, 0:4], ALU.subtract, r=("ig", pckk), w=("t1",))
                tt("dve", sm[:, 32:36], sm[:, 28:32], pc[:, 4:8], ALU.add, r=("t1", pckk), w=("t2",))
                if own_i is None:
                    wk, eB = sm[:, 36:40], sm[:, 40:44]
                    wkk, eBk = "wk", "eB"
                else:
                    wk, eB = wk_o[:, own_i, :], eB_o[:, own_i, :]
                    wkk = eBk = ("gown", own_i)
                act(wk, sm[:, 32:36], AF.Exp, r=("t2",), w=(wkk,))
                ts("dve", wk, wk, 0.0625, None, ALU.mult, None, r=(wkk,), w=(wkk,))
                act(eB, pc[:, 4:8], AF.Exp, r=(pckk,), w=(eBk,))
                if own_i is not None:
                    act(wa_o[:, own_i, :], sm[:, 28:32], AF.Exp, r=("t1",), w=(wkk,))
                    ts("dve", wa_o[:, own_i, :], wa_o[:, own_i, :], 0.0625, None, ALU.mult, None, r=(wkk,), w=(wkk,))
                    act(ebc_o[:, own_i, :], pc[:, 0:4], AF.Exp, r=(pckk,), w=(wkk,))
                return wk, eB, wkk, eBk

            def conv_silu(pre, prek, cblk, accb, acck, out, outk):
                ts("dve", accb, pre[:, 3:515], cw[:, cblk, 3:4], cb_[:, cblk:cblk + 1], ALU.mult, ALU.add,
                   r=(prek, "cw", "cb"), w=(acck,))
                for tap in range(3):
                    stt("dve", accb, pre[:, tap:tap + 512], cw[:, cblk, tap:tap + 1], accb, ALU.mult, ALU.add,
                        r=(prek, acck, "cw"), w=(acck,))
                act(out, accb, AF.Silu, r=(acck,), w=(outk,))

            def state_update(h, kT_blk, kTk, vaug, vk, wk_col, wkk, eB_col, eBk, kpp, kppk, refresh_bf):
                pt, ptk = getpsb()
                for blk in range(2):
                    tr(pt[:, blk * 128:(blk + 1) * 128], kT_blk(blk), r=(kTk[blk],), w=(ptk,))
                ts("dve", kpp, pt[:, 0:256], wk_col, None, ALU.mult, None, r=(ptk, wkk), w=(kppk,))
                for blk in range(2):
                    p_, pk = getps()
                    mm(p_[:, 0:257], kpp[:, blk * 128:(blk + 1) * 128], vaug, True, True, r=(kppk, vk), w=(pk,))
                    stt("dve", state[h][blk][:], state[h][blk][:], eB_col, p_[:, 0:257], ALU.mult, ALU.add,
                        r=(pk, eBk, ("st", h, blk)), w=(("st", h, blk),))
                    if refresh_bf:
                        cast(ctbf[h][blk][:], state[h][blk][:], r=(("st", h, blk),), w=(("ct", h, blk),), eng="act")

            areset()
            WA = abf(16 * 2048).rearrange("p (k c) -> p k c", k=16)
            hTgA = [abf(16 * 512).rearrange("p (k c) -> p k c", k=16) for _ in range(2)]
            xn = abf(D)
            xns = [xn, abf(D)]
            kT = abf(8 * 512).rearrange("p (b c) -> p b c", b=8)
            vaugs = [[abf(258)[:, 0:257] for _ in range(4)] for _ in range(4)]
            kpps = [abf(256) for _ in range(2)]
            stgs = [af32(2048) for _ in range(2)]
            xbufs = stgs
            kpre = af32(8 * 515).rearrange("p (b c) -> p b c", b=8)
            accb = af32(512)
            gtmp = af32(4 * 48).rearrange("p (i c) -> p i c", i=4)
            load_w(lambda k: WA[:, k, :], lambda k: w_in[k * 128:(k + 1) * 128, C_MK:C_MK + 2048], 16, stgs, "xt", "WA", rot_engs=("act", "dve"))
            memset("pool", kpre[:, :, 0:3], 0.0, w=tuple(("kpre", c) for c in range(8)))
            for i in range(4):
                for h in range(4):
                    memset("pool", vaugs[i][h][:, 256:257], 1.0, w=(("vaug", i, h),))
            NG = NPRE // 4
            fe_cnt = [0]

            def a_hT(g):
                return hT_halo if g == NG - 1 else hTgA[g % 2]

            def a_keys(g, i):
                return (("hTg", g % 2, i, 0), ("hTg", g % 2, i, 1))

            def a_gates1(g, i):
                tau = 4 * g + i
                hT = a_hT(g)
                gt = gtmp[:, i, :]
                pg, pgk = ps[2 + i % 2], ("ps", 2 + i % 2)
                for k in range(16):
                    mm(pg[:, 0:8], hT[:, k, i * 128:(i + 1) * 128], wif[:, k, :], k == 0, k == 15,
                       r=a_keys(g, i) + ("wif",), w=(pgk,))
                tt("dve", gt[:, 0:8], pg[:, 0:8], bif[:], ALU.add, r=(pgk, "bif"), w=(("g_gl", i),))
                ts("dve", gt[:, 8:12], gt[:, 0:4], pneg[:, tau:tau + 1], None, ALU.add, None, r=(("g_gl", i), "pneg"), w=(("g_ig", i),))
                act(gt[:, 12:16], gt[:, 4:8], AF.Exp, r=(("g_gl", i),), w=(("g_lf", i),), scale=-1.0)
                ts("dve", gt[:, 12:16], gt[:, 12:16], 1.0, None, ALU.add, None, r=(("g_lf", i),), w=(("g_lf", i),))
                act(gt[:, 12:16], gt[:, 12:16], AF.Ln, r=(("g_lf", i),), w=(("g_lf", i),))
                ts("dve", gt[:, 12:16], gt[:, 12:16], -1.0, None, ALU.mult, None, r=(("g_lf", i),), w=(("g_lf", i),))

            def a_gates2(g, i):
                gt = gtmp[:, i, :]
                pc, pck_ = psS[:, (i % 2) * 512:(i % 2) * 512 + 8], ("psS", i % 2)
                mm(pc[:, 0:4], tri[:], gt[:, 12:16], True, True, r=(("g_lf", i), "tri"), w=(pck_,))
                mm(pc[:, 4:8], ones[:], gt[:, 12:16], True, True, r=(("g_lf", i), "ones"), w=(pck_,))
                tt("dve", gt[:, 16:20], gt[:, 8:12], pc[:, 0:4], ALU.subtract, r=(("g_ig", i), pck_), w=(("g_t", i),))
                tt("dve", gt[:, 16:20], gt[:, 16:20], pc[:, 4:8], ALU.add, r=(("g_t", i), pck_), w=(("g_t", i),))
                act(gt[:, 20:24], gt[:, 16:20], AF.Exp, r=(("g_t", i),), w=(("g_wk", i),))
                ts("dve", gt[:, 20:24], gt[:, 20:24], 0.0625, None, ALU.mult, None, r=(("g_wk", i),), w=(("g_wk", i),))
                act(gt[:, 24:28], pc[:, 4:8], AF.Exp, r=(pck_,), w=(("g_eB", i),))

            def a_fe1(g, i):
                frontend1(4 * g + i, xbufs, xns[i % 2], ("xnA", i % 2), fe_cnt[0])
                fe_cnt[0] += 1

            def a_fe2(g, i, halves):
                hT = a_hT(g)
                frontend2(lambda k: hT[:, k, i * 128:(i + 1) * 128], a_keys(g, i), xns[i % 2], ("xnA", i % 2), halves)

            def a_U1(g, i, h):
                gt = gtmp[:, i, :]
                pt = ps[h % 2][:].bitcast(BF16)
                ptk = ("ps", h % 2)
                for blk in range(2):
                    tr(pt[:, blk * 128:(blk + 1) * 128], kT[:, 2 * h + blk, i * 128:(i + 1) * 128],
                       r=(("kT", 2 * h + blk),), w=(ptk,))
                kp, kpk = kpps[h % 2], ("kpp", h % 2)
                ts("dve", kp, pt[:, 0:256], gt[:, 20 + h:21 + h], None, ALU.mult, None, r=(ptk, ("g_wk", i)), w=(kpk,))

            def a_U2(g, i, h):
                gt = gtmp[:, i, :]
                kp, kpk = kpps[h % 2], ("kpp", h % 2)
                for blk in range(2):
                    pk = ("psS", blk)
                    p_ = psS[:, blk * 512:blk * 512 + 257]
                    mm(p_, kp[:, blk * 128:(blk + 1) * 128], vaugs[i][h], True, True, r=(kpk, ("vaug", i, h)), w=(pk,))
                    stt("dve", state[h][blk][:], state[h][blk][:], gt[:, 24 + h:25 + h], p_, ALU.mult, ALU.add,
                        r=(pk, ("g_eB", i), ("st", h, blk)), w=(("st", h, blk),))

            def a_stage3(g):
                nxt = g + 1 < NG
                if nxt:
                    a_fe1(g + 1, 0)
                for i in range(4):
                    if nxt and i + 1 < 4:
                        a_fe1(g + 1, i + 1)
                    a_U1(g, i, 0)
                    if nxt:
                        a_fe2(g + 1, i, (0,))
                    a_U1(g, i, 1)
                    a_U2(g, i, 0)
                    if nxt:
                        a_fe2(g + 1, i, (1,))
                    a_U1(g, i, 2)
                    a_U2(g, i, 1)
                    a_U1(g, i, 3)
                    a_U2(g, i, 2)
                    a_U2(g, i, 3)

            for i in range(4):
                a_fe1(0, i)
                a_fe2(0, i, (0, 1))
            for g in range(NG):
                hT = a_hT(g)
                hgk = tuple(k_ for i_ in range(4) for k_ in a_keys(g, i_))
                for i in range(4):
                    a_gates1(g, i)
                for cbk in range(8):
                    p_, pk = ps[cbk % 2], ("ps", cbk % 2)
                    for k in range(16):
                        mm(p_[:, 0:512], WA[:, k, cbk * 128:(cbk + 1) * 128], hT[:, k, :], k == 0, k == 15,
                           r=("WA",) + hgk, w=(pk,))
                    cast(kpre[:, cbk, 3:515], p_[:, 0:512], r=(pk,), w=(("kpre", cbk),), eng="act")
                    conv_silu(kpre[:, cbk, :], ("kpre", cbk), 8 + cbk, accb, "accb", kT[:, cbk, :], ("kT", cbk))
                    ts("dve", kpre[:, cbk, 0:3], kpre[:, cbk, 512:515], tval[:, 4 * g + 3:4 * g + 4], None, ALU.mult, None,
                       r=(("kpre", cbk), "tval"), w=(("kpre", cbk),))
                for i in range(4):
                    for half in range(2):
                        p_, pk = ps[2 + half], ("ps", 2 + half)
                        for k in range(16):
                            mm(p_[:, 0:512], hT[:, k, i * 128:(i + 1) * 128], WA[:, k, 1024 + half * 512:1024 + (half + 1) * 512],
                               k == 0, k == 15, r=("WA",) + a_keys(g, i), w=(pk,))
                        for hh in range(2):
                            h = 2 * half + hh
                            cast(vaugs[i][h][:, 0:256], p_[:, hh * 256:(hh + 1) * 256], r=(pk,), w=(("vaug", i, h),),
                                 eng="act")
                for i in range(4):
                    a_gates2(g, i)
                a_stage3(g)
            S.barrier()
            stop("A", [(state[0][0][:], 257), (state[3][1][:], 257), (sm[:], 64)])

            areset()
            hT_own = abf(16 * 2048).rearrange("p (k c) -> p k c", k=16)
            mark_b = aoff[0]
            xn = abf(D)
            xbufs = [af32(D) for _ in range(2)]
            xns0 = [xn, abf(D)]
            frontend1(NPRE, xbufs, xns0[0], ("xnB", 0), 0)
            for i in range(NOWN):
                if i + 1 < NOWN:
                    frontend1(NPRE + i + 1, xbufs, xns0[(i + 1) % 2], ("xnB", (i + 1) % 2), i + 1)
                frontend2(lambda k, i=i: hT_own[:, k, i * 128:(i + 1) * 128], (("hTo", i, 0), ("hTo", i, 1)),
                          xns0[i % 2], ("xnB", i % 2), act_halves=())
                gates(NPRE + i, lambda k, i=i: hT_own[:, k, i * 128:(i + 1) * 128], (("hTo", i, 0), ("hTo", i, 1)), i)
            for h in range(4):
                for b in range(2):
                    cast(ctbf[h][b][:], state[h][b][:], r=(("st", h, b),), w=(("ct", h, b),), eng="act")
            S.barrier()
            stop("B0", [(wk_o[:].rearrange("p a b -> p (a b)"), 64), (wa_o[:].rearrange("p a b -> p (a b)"), 64), (eB_o[:].rearrange("p a b -> p (a b)"), 64), (ebc_o[:].rearrange("p a b -> p (a b)"), 64), (state[0][0][:], 257)])

            aoff[0] = mark_b
            WG = abf(16 * 1280).rearrange("p (k c) -> p k c", k=16)
            qkT = [abf(4 * 512).rearrange("p (b c) -> p b c", b=4) for _ in range(2)]
            vaug1 = [abf(258)[:, 0:257] for _ in range(2)]
            kpps = [abf(256) for _ in range(2)]
            STb = [abf(128) for _ in range(2)]
            ybf = [abf(256) for _ in range(2)]
            yTt = [abf(256).rearrange("p (b c) -> p b c", b=2) for _ in range(2)]
            stg1 = [af32(1280) for _ in range(4)]
            qkpre = af32(4 * 515).rearrange("p (b c) -> p b c", b=4)
            accb = af32(512)
            sgo = af32(256)
            slz = af32(256)
            Gt = [af32(256) for _ in range(2)]
            junk = af32(256)
            w5 = w_in[:, 0:5120].rearrange("r (s c) -> r s c", c=1024)
            for vb in range(2):
                memset("pool", vaug1[vb][:, 256:257], 1.0, w=(("vaug1", vb),))
            ts("dve", mhg[:], mhg[:], 0.5, None, ALU.mult, None, r=("mhg",), w=("mhg",))
            PS0, PS1, PS2, PS3 = (ps[0], ("ps", 0)), (ps[1], ("ps", 1)), (ps[2], ("ps", 2)), (ps[3], ("ps", 3))
            for h in range(4):
                load_w(lambda k: WG[:, k, :].rearrange("p (s c) -> p s c", c=256),
                       lambda k, h=h: w5[k * 128:(k + 1) * 128, :, h * 256:(h + 1) * 256],
                       16, [s_.rearrange("p (s c) -> p s c", c=256) for s_ in stg1], "stg1", "WG", rot_engs=("act", "dve"))
                for blk in range(4):
                    p_, pk = (PS0, PS1)[blk % 2]
                    for k in range(16):
                        mm(p_[:, 0:3], WG[:, k, blk * 128:(blk + 1) * 128], hT_halo[:, k, 509:512], k == 0, k == 15,
                           r=("WG",), w=(pk,))
                    ts("dve", qkpre[:, blk, 0:3], p_[:, 0:3], tval[:, NPRE - 1:NPRE], None, ALU.mult, None,
                       r=(pk, "tval"), w=(("qkpre", blk),))

                def b1_proj(g, h=h):
                    qk = qkT[g % 2]
                    for blk in range(4):
                        p_, pk = (PS0, PS1)[blk % 2]
                        for k in range(16):
                            mm(p_[:, 0:512], WG[:, k, blk * 128:(blk + 1) * 128], hT_own[:, k, g * 512:(g + 1) * 512],
                               k == 0, k == 15, r=("WG",), w=(pk,))
                        cast(qkpre[:, blk, 3:515], p_[:, 0:512], r=(pk,), w=(("qkpre", blk),), eng="act")
                        cidx = (2 * h + blk) if blk < 2 else (8 + 2 * h + blk - 2)
                        conv_silu(qkpre[:, blk, :], ("qkpre", blk), cidx, accb, "accb", qk[:, blk, :], ("qkT", g % 2, blk))
                        cast(qkpre[:, blk, 0:3], qkpre[:, blk, 512:515], r=(("qkpre", blk),), w=(("qkpre", blk),), eng="dve")

                def b1_P(t, h=h):
                    vb = t % 2
                    pA, pAk = PS0
                    for k in range(16):
                        mm(pA[:, 0:512], hT_own[:, k, t * 128:(t + 1) * 128], WG[:, k, 512:1024], k == 0, k == 15,
                           r=("WG",), w=(pAk,))
                    pB, pBk = PS1
                    for k in range(16):
                        mm(pB[:, 0:256], hT_own[:, k, t * 128:(t + 1) * 128], WG[:, k, 1024:1280], k == 0, k == 15,
                           r=("WG",), w=(pBk,))
                    cast(vaug1[vb][:, 0:256], pA[:, 0:256], r=(pAk,), w=(("vaug1", vb),), eng="act")
                    act(sgo, pA[:, 256:512], AF.Tanh, r=(pAk,), w=("sgo",), scale=0.5)
                    act(slz, pB[:, 0:256], AF.Silu, r=(pBk,), w=("slz",))
                    stt("dve", Gt[vb], sgo, 1.0, slz, ALU.add, ALU.mult, r=("sgo", "slz"), w=(("Gt", vb),))
                    tt("dve", Gt[vb], Gt[vb], mhg[:, h * 256:(h + 1) * 256], ALU.mult, r=(("Gt", vb), "mhg"), w=(("Gt", vb),))

                def b1_S(t, h=h):
                    vb = t % 2
                    gp = (t // 4) % 2
                    tsl = slice((t % 4) * 128, (t % 4 + 1) * 128)
                    pS, pSk = PS2
                    for blk in range(2):
                        mm(pS[:, 0:128], qkT[gp][:, 2 + blk, tsl], qkT[gp][:, blk, tsl], blk == 0, blk == 1,
                           r=(("qkT", gp, blk), ("qkT", gp, 2 + blk)), w=(pSk,))
                    stt("dve", STb[vb], pS[:, 0:128], wa_o[:, t, h:h + 1], tri[:], ALU.mult, ALU.mult,
                        r=(pSk, ("gown", t), "tri"), w=(("STb", vb),))

                def b1_N(t, h=h):
                    vb = t % 2
                    gp = (t // 4) % 2
                    tsl = slice((t % 4) * 128, (t % 4 + 1) * 128)
                    pN, pNk = PS3
                    mm(pN[:, 0:257], STb[vb], vaug1[vb], True, False, r=(("STb", vb), ("vaug1", vb)), w=(pNk,))
                    for blk in range(2):
                        mm(pN[:, 0:257], qkT[gp][:, blk, tsl], ctbf[h][blk][:], False, blk == 1,
                           r=(("qkT", gp, blk), ("ct", h, blk)), w=(pNk,))

                def b1_Utr(t, h=h):
                    vb = t % 2
                    gp = (t // 4) % 2
                    tsl = slice((t % 4) * 128, (t % 4 + 1) * 128)
                    pt, ptk = psb[0], ("psb", 0)
                    for blk in range(2):
                        tr(pt[:, blk * 128:(blk + 1) * 128], qkT[gp][:, 2 + blk, tsl], r=(("qkT", gp, 2 + blk),), w=(ptk,))
                    ts("dve", kpps[vb], pt[:, 0:256], wk_o[:, t, h:h + 1], None, ALU.mult, None,
                       r=(ptk, ("gown", t)), w=(("kpp", vb),))

                def b1_Umm(t, h=h):
                    vb = t % 2
                    for blk in range(2):
                        pk = ("psS", blk)
                        p_ = psS[:, blk * 512:blk * 512 + 257]
                        mm(p_, kpps[vb][:, blk * 128:(blk + 1) * 128], vaug1[vb], True, True,
                           r=(("kpp", vb), ("vaug1", vb)), w=(pk,))
                        stt("dve", state[h][blk][:], state[h][blk][:], eB_o[:, t, h:h + 1], p_, ALU.mult, ALU.add,
                            r=(pk, ("gown", t), ("st", h, blk)), w=(("st", h, blk),))
                        cast(ctbf[h][blk][:], state[h][blk][:], r=(("st", h, blk),), w=(("ct", h, blk),), eng="act")

                def b1_Ychain(t, h=h):
                    vb = t % 2
                    pN, pNk = PS3
                    gk = ("gown", t)
                    tt("dve", sm[:, 44:45], pN[:, 256:257], ebc_o[:, t, h:h + 1], ALU.mult, r=(pNk, gk), w=("d1",))
                    ts("dve", sm[:, 54:55], sm[:, 44:45], -1.0, None, ALU.mult, None, r=("d1",), w=("d1n",))
                    tt("dve", sm[:, 44:45], sm[:, 44:45], sm[:, 54:55], ALU.max, r=("d1", "d1n"), w=("d1",))
                    ts("dve", sm[:, 44:45], sm[:, 44:45], 1.0, 1.0, ALU.max, ALU.mult, r=("d1",), w=("d1",))
                    S.add("dve", lambda e: e.reciprocal(sm[:, 44:45], sm[:, 44:45]), r=("d1",), w=("d1",))
                    tt("dve", sm[:, 45:46], ebc_o[:, t, h:h + 1], sm[:, 44:45], ALU.mult, r=("d1", gk), w=("rr",))
                    memset("dve", sm[:, 46:47], 0.0, w=("ss2",))
                    act(junk, pN[:, 0:256], AF.Square, r=(pNk, "rr", "ss2"), w=("junk", "ss2"), scale=sm[:, 45:46],
                        accum=sm[:, 46:47])
                    ts("dve", sm[:, 47:48], sm[:, 46:47], 1.0 / 256, EPS, ALU.mult, ALU.add, r=("ss2",), w=("r2",))
                    rsqrt_col(sm[:, 47:48], "r2")
                    tt("dve", sm[:, 47:48], sm[:, 47:48], sm[:, 45:46], ALU.mult, r=("r2", "rr"), w=("r2",))
                    stt("dve", ybf[vb], pN[:, 0:256], sm[:, 47:48], Gt[vb], ALU.mult, ALU.mult,
                        r=(pNk, "r2", ("Gt", vb)), w=(("ybf", vb),))

                def b1_Ytr(t, h=h):
                    vb = t % 2
                    pt, ptk = psb[1], ("psb", 1)
                    for blk in range(2):
                        tr(pt[:, blk * 128:(blk + 1) * 128], ybf[vb][:, blk * 128:(blk + 1) * 128], r=(("ybf", vb),), w=(ptk,))
                    cast(yTt[vb].rearrange("p b c -> p (b c)"), pt[:, 0:256], r=(ptk,), w=(("yTt", vb),), eng="act")
                    dma(dmaq(), yT_d[2 * h:2 * h + 2, :, t * 128:(t + 1) * 128].rearrange("b p t -> p b t"), yTt[vb],
                        r=(("yTt", vb),), w=(("yTd", 2 * h, t),))

                b1_proj(0)
                b1_P(0)
                b1_S(0)
                for t in range(NOWN):
                    if t + 1 < NOWN:
                        if (t + 1) % 4 == 0:
                            b1_proj((t + 1) // 4)
                        b1_P(t + 1)
                        b1_S(t + 1)
                    b1_N(t)
                    b1_Utr(t)
                    if t > 0:
                        b1_Ytr(t - 1)
                    b1_Umm(t)
                    b1_Ychain(t)
                b1_Ytr(NOWN - 1)
            S.barrier()
            stop("B1", [(state[0][0][:], 257)])

            aoff[0] = mark_b
            WG2s = [abf(16 * 512).rearrange("p (k c) -> p k c", k=16) for _ in range(2)]
            akT = abf(2560)
            aqT = abf(2048)
            Vt = abf(20 * 128).rearrange("p (t c) -> p t c", t=20)
            slzA = abf(16 * 128).rearrange("p (t c) -> p t c", t=16)
            Pb = [abf(640) for _ in range(2)]
            PT = [abf(640) for _ in range(2)]
            yab = [abf(128) for _ in range(2)]
            yaT = [abf(128) for _ in range(2)]
            stg2 = [af32(512) for _ in range(4)]
            rb = af32(640)
            bm = af32(640)
            sbuf_s = [af32(640) for _ in range(2)]
            wA = w_in[:, C_AQ:C_AQ + 4096].rearrange("r (s c) -> r s c", c=1024)
            def b2_load(h, cast_eng=None, q=None):
                W_ = WG2s[h % 2]
                load_w(lambda k: W_[:, k, :].rearrange("p (s c) -> p s c", c=128),
                       lambda k: wA[k * 128:(k + 1) * 128, :, h * 128:(h + 1) * 128],
                       16, [s_.rearrange("p (s c) -> p s c", c=128) for s_ in stg2], "stg2", ("WG2", h % 2),
                       cast_eng=cast_eng, q=q)

            b2_load(0)
            for h in range(8):
                WG2 = WG2s[h % 2]
                WK = ("WG2", h % 2)
                dma("sp", rb, relmat[h], r=(), w=("rb",))
                tt("pool", bm, rb, amask_s[:], ALU.add, r=("rb", "amask"), w=("bm",))
                for g5 in range(5):
                    src = hT_halo if g5 == 0 else hT_own[:, :, (g5 - 1) * 512:g5 * 512]
                    srk = ()
                    p_, pk = getps()
                    for k in range(16):
                        mm(p_[:, 0:512], WG2[:, k, 128:256], src[:, k, :], k == 0, k == 15, r=(WK,) + srk, w=(pk,))
                    cast(akT[:, g5 * 512:(g5 + 1) * 512], p_[:, 0:512], r=(pk,), w=(("akT", g5),), eng="act")
                    if g5 > 0:
                        p_, pk = getps()
                        for k in range(16):
                            mm(p_[:, 0:512], WG2[:, k, 0:128], src[:, k, :], k == 0, k == 15, r=(WK,) + srk, w=(pk,))
                        act(aqT[:, (g5 - 1) * 512:g5 * 512], p_[:, 0:512], AF.Copy, r=(pk,), w=(("aqT", g5 - 1),),
                            scale=float(128 ** -0.5))
                    for i in range(4):
                        ttile = g5 * 4 + i
                        p_, pk = getps()
                        for k in range(16):
                            mm(p_[:, 0:256], src[:, k, i * 128:(i + 1) * 128], WG2[:, k, 256:512], k == 0, k == 15,
                               r=(WK,) + srk, w=(pk,))
                        cast(Vt[:, ttile, :], p_[:, 0:128], r=(pk,), w=(("Vt", ttile),), eng="act")
                        if g5 > 0:
                            act(slzA[:, ttile - 4, :], p_[:, 128:256], AF.Silu, r=(pk,), w=(("slzA", ttile - 4),))
                if h + 1 < 8:
                    b2_load(h + 1, cast_eng="pool", q="sp")

                RS = [sm[:, 56:57], sm[:, 57:58], sm[:, 60:61]]

                def att_s1(t, h=h):
                    vb = t % 2
                    gq = ("aqT", t // 4)
                    kk0 = tuple(("akT", x) for x in sorted({t // 4, (t + 3) // 4, (t + 4) // 4}))
                    mm(psS[:, 0:512], aqT[:, t * 128:(t + 1) * 128], akT[:, t * 128:t * 128 + 512], True, True,
                       r=(gq,) + kk0, w=("psS",))
                    mm(psS[:, 512:640], aqT[:, t * 128:(t + 1) * 128], akT[:, t * 128 + 512:t * 128 + 640], True, True,
                       r=(gq,) + kk0, w=("psS",))
                    sbt = sbuf_s[vb]
                    sk = ("sbt", vb)
                    tt("dve", sbt, psS[:, 0:640], bm, ALU.add, r=("psS", "bm"), w=(sk,))
                    for kb in range(max(0, 4 - t)):
                        lt = NPRE - 4 + t + kb
                        ts("dve", sbt[:, kb * 128:(kb + 1) * 128], sbt[:, kb * 128:(kb + 1) * 128], ntile[:, lt:lt + 1], None,
                           ALU.add, None, r=(sk, "ntile"), w=(sk,))
                    S.add("dve", lambda e, sbt=sbt: e.reduce_max(sm[:, 48:49], sbt, AX.X), r=(sk,), w=("mx",))
                    ts("dve", sm[:, 48:49], sm[:, 48:49], -1.0, None, ALU.mult, None, r=("mx",), w=("mx",))
                    rsc = RS[t % 3]
                    memset("dve", rsc, 0.0, w=(("rsum", t % 3),))
                    act(Pb[vb], sbt, AF.Exp, r=(sk, "mx", ("rsum", t % 3)), w=(("Pb", vb), ("rsum", t % 3)), bias=sm[:, 48:49],
                        accum=rsc)

                def att_s2(t, h=h):
                    vb = t % 2
                    pt, ptk = psb[0], ("psb", 0)
                    for kb in range(5):
                        tr(pt[:, kb * 128:(kb + 1) * 128], Pb[vb][:, kb * 128:(kb + 1) * 128], r=(("Pb", vb),), w=(ptk,))
                    cast(PT[vb], pt[:, 0:640], r=(ptk,), w=(("PT", vb),), eng="act")

                def att_s3(t, h=h):
                    vb = t % 2
                    rsc = RS[t % 3]
                    rrc = sm[:, 58 + vb:59 + vb]
                    pO, pOk = getps()
                    for kb in range(5):
                        mm(pO[:, 0:128], PT[vb][:, kb * 128:(kb + 1) * 128], Vt[:, t + kb, :], kb == 0, kb == 4,
                           r=(("PT", vb), ("Vt", t + kb)), w=(pOk,))
                    S.add("dve", lambda e: e.reciprocal(rrc, rsc), r=(("rsum", t % 3),), w=(("rrs", vb),))
                    stt("dve", yab[vb], pO[:, 0:128], rrc, slzA[:, t, :], ALU.mult, ALU.mult,
                        r=(pOk, ("rrs", vb), ("slzA", t)), w=(("yab", vb),))
                    pt2, pt2k = psb[1], ("psb", 1)
                    tr(pt2[:, 0:128], yab[vb], r=(("yab", vb),), w=(pt2k,))
                    cast(yaT[vb], pt2[:, 0:128], r=(pt2k,), w=(("yaT", vb),), eng="act")
                    dma("act", yT_d[8 + h, :, t * 128:(t + 1) * 128], yaT[vb], r=(("yaT", vb),), w=(("yTd", 8 + h, t),))

                att_s1(0)
                att_s1(1)
                att_s2(0)
                for t in range(NOWN):
                    if t + 2 < NOWN:
                        att_s1(t + 2)
                    if t + 1 < NOWN:
                        att_s2(t + 1)
                    att_s3(t)
            S.barrier()
            stop("B2", [(sm[:], 64)])

            aoff[0] = mark_b
            yT_res = abf(16 * 2048).rearrange("p (b c) -> p b c", b=16)
            hh_flat = hT_halo[:].rearrange("p k c -> p (k c)")
            wgm = [abf(16 * 128).rearrange("p (k c) -> p k c", k=16), hh_flat[:, 0:2048].rearrange("p (k c) -> p k c", k=16)]
            wga = [abf(16 * 128).rearrange("p (k c) -> p k c", k=16), hh_flat[:, 2048:4096].rearrange("p (k c) -> p k c", k=16)]
            wpm = [abf(8 * 128).rearrange("p (k c) -> p k c", k=8), hh_flat[:, 4096:5120].rearrange("p (k c) -> p k c", k=8)]
            wpa = [abf(8 * 128).rearrange("p (k c) -> p k c", k=8), hh_flat[:, 5120:6144].rearrange("p (k c) -> p k c", k=8)]
            mTb = [abf(512) for _ in range(2)]
            stg3 = [af32(8 * 128).rearrange("p (k c) -> p k c", k=8) for _ in range(2)]
            stg3.append(mhg[:].rearrange("p (k c) -> p k c", k=8))
            stg3.append(hh_flat[:, 6144:8192].bitcast(F32).rearrange("p (k c) -> p k c", k=8))
            sgm = af32(512)
            sga = af32(512)
            t1b = af32(512)
            for b in range(16):
                dma(dmaq(), yT_res[:, b, :], yT_d[b], r=(), w=("yT_res",))
            si3 = [0]

            def b3_load(c):
                cbuf = c % 2
                for (dst, src, nk, nm) in ((wgm, w_in[:, C_GM + c * 128:C_GM + (c + 1) * 128], 16, "wgm"),
                                           (wga, w_in[:, C_GA + c * 128:C_GA + (c + 1) * 128], 16, "wga"),
                                           (wpm, w_pm[:, c * 128:(c + 1) * 128], 8, "wpm"),
                                           (wpa, w_pa[:, c * 128:(c + 1) * 128], 8, "wpa")):
                    for k8 in range(nk // 8):
                        st_ = stg3[si3[0] % 4]
                        sk = ("stg3", si3[0] % 4)
                        si3[0] += 1
                        for k4 in range(2):
                            r0 = (k8 * 8 + k4 * 4) * 128
                            dma(dmaq(), st_[:, 4 * k4:4 * k4 + 4, :],
                                src[r0:r0 + 512, :].rearrange("(k p) c -> p k c", p=128), r=(), w=(sk,))
                        cast(dst[cbuf][:, k8 * 8:(k8 + 1) * 8, :], st_[:], r=(sk,), w=((nm, cbuf),), eng="pool")

            b3_load(0)
            for c in range(16):
                cbuf = c % 2
                if c + 1 < 16:
                    b3_load(c + 1)
                for g in range(4):
                    gsl = slice(g * 512, (g + 1) * 512)
                    p1, p1k = getps()
                    for k in range(16):
                        mm(p1[:, 0:512], wgm[cbuf][:, k, :], hT_own[:, k, gsl], k == 0, k == 15, r=(("wgm", cbuf),), w=(p1k,))
                    act(sgm, p1[:, 0:512], AF.Sigmoid, r=(p1k,), w=("sgm",))
                    p2, p2k = getps()
                    for k in range(16):
                        mm(p2[:, 0:512], wga[cbuf][:, k, :], hT_own[:, k, gsl], k == 0, k == 15, r=(("wga", cbuf),), w=(p2k,))
                    act(sga, p2[:, 0:512], AF.Sigmoid, r=(p2k,), w=("sga",))
                    p3, p3k = getps()
                    for k in range(8):
                        mm(p3[:, 0:512], wpm[cbuf][:, k, :], yT_res[:, k, gsl], k == 0, k == 7,
                           r=(("wpm", cbuf), "yT_res"), w=(p3k,))
                    tt("dve", t1b, sgm, p3[:, 0:512], ALU.mult, r=("sgm", p3k), w=("t1b",))
                    p4, p4k = getps()
                    for k in range(8):
                        mm(p4[:, 0:512], wpa[cbuf][:, k, :], yT_res[:, 8 + k, gsl], k == 0, k == 7,
                           r=(("wpa", cbuf), "yT_res"), w=(p4k,))
                    tt("dve", sga, sga, p4[:, 0:512], ALU.mult, r=("sga", p4k), w=("sga",))
                    mb = mTb[(c * 4 + g) % 2]
                    mk_ = ("mTb", (c * 4 + g) % 2)
                    tt("dve", mb, t1b, sga, ALU.add, r=("t1b", "sga"), w=(mk_,))
                    dma(dmaq(), mT_d[c, :, gsl], mb, r=(mk_,), w=(("mTd", c, g),))
            S.barrier()
            stop("B3", [(sm[:], 64)])

            areset()
            Wout = abf(16 * 2048).rearrange("p (k c) -> p k c", k=16)
            mTt = [abf(16 * 128).rearrange("p (c t) -> p c t", c=16) for _ in range(2)]
            fgb = af32(D)
            stgC = [af32(D) for _ in range(2)]
            xts = [af32(D) for _ in range(2)]
            obuf = [af32(D) for _ in range(2)]
            junkC = af32(D)
            dma("sp", fgb, fg_bc, r=(), w=("fgb",))
            for k in range(16):
                st_ = stgC[k % 2]
                sk = ("stgC", k % 2)
                dma(dmaq(), st_, w_out[k * 128:(k + 1) * 128, :], r=(), w=(sk,))
                tt("dve", Wout[:, k, :], st_, gate_bc[:], ALU.mult, r=(sk, "gate_bc"), w=("Wout",))
            outkeys = []

            def c_loads(t):
                vb = t % 2
                for c4 in range(4):
                    dma(("sp", "pool")[c4 % 2], mTt[vb][:, 4 * c4:4 * c4 + 4, :],
                        mT_d[4 * c4:4 * c4 + 4, :, t * 128:(t + 1) * 128].rearrange("c p t -> p c t"), r=(), w=(("mTt", vb),))
                dma(("sp", "pool")[t % 2], xts[vb], xl[(NPRE + t) * 128:(NPRE + t + 1) * 128, :], r=(), w=(("xts", vb),))

            c_loads(0)
            for t in range(NOWN):
                vb = t % 2
                if t + 1 < NOWN:
                    c_loads(t + 1)
                ob = obuf[vb]
                ok = ("ob", vb)
                for cbk in range(4):
                    p_, pk = getps()
                    for c in range(16):
                        mm(p_[:, 0:512], mTt[vb][:, c, :], Wout[:, c, cbk * 512:(cbk + 1) * 512], c == 0, c == 15,
                           r=(("mTt", vb), "Wout"), w=(pk,))
                    tt("dve", ob[:, cbk * 512:(cbk + 1) * 512], p_[:, 0:512], xts[vb][:, cbk * 512:(cbk + 1) * 512], ALU.add,
                       r=(pk, ("xts", vb)), w=(ok,))
                memset("dve", sm[:, 52:53], 0.0, w=("ssC",))
                act(junkC, ob, AF.Square, r=(ok, "ssC"), w=("junkC", "ssC"), accum=sm[:, 52:53])
                ts("dve", sm[:, 53:54], sm[:, 52:53], 1.0 / D, EPS, ALU.mult, ALU.add, r=("ssC",), w=("rC",))
                rsqrt_col(sm[:, 53:54], "rC")
                stt("dve", ob, ob, sm[:, 53:54], fgb, ALU.mult, ALU.mult, r=(ok, "rC", "fgb"), w=(ok,))
                dma("act", y_out[t * 128:(t + 1) * 128, :], ob, r=(ok,), w=(("yout", t),))
                outkeys.append(("yout", t))
        try:
            body()
        except _Stop:
            pass
        S.finish(())
        S.emit()
    return nc


_NC_CACHE = {}


def _consts():
    ident = np.eye(128, dtype=np.float32).astype(ml_dtypes.bfloat16)
    s = np.arange(128)[:, None]
    t = np.arange(128)[None, :]
    tri = (s <= t).astype(np.float32)
    ones = np.ones((128, 128), np.float32)
    q = np.arange(128)[:, None]
    kap = np.arange(640)[None, :]
    cq, ck = q // 64, kap // 64
    allowed = (ck >= cq) & (ck <= cq + 8)
    amask = np.where(allowed, 0.0, NEG).astype(np.float32)
    relidx = np.clip(q + 512 - kap, -63, 128) + 63
    return ident, tri, ones, tri.copy(), amask, relidx


def kernel(x, c, w_ada, b_ada, norm_g, w_in, b_if, conv_w, conv_b, mh_norm_g, rel_bias,
           w_proj_m, w_proj_a, w_out, final_norm_g):
    f = np.float32
    x = np.asarray(x, f)
    ident, tri, ones, mst, amask, relidx = _consts()
    if "nc" not in _NC_CACHE:
        _NC_CACHE["nc"] = build_nc()
    nc = _NC_CACHE["nc"]
    rep = lambda v, n=128: np.ascontiguousarray(np.broadcast_to(np.asarray(v, f).reshape(1, -1), (n, np.asarray(v).size)))
    colT = lambda v: np.ascontiguousarray(np.asarray(v, f).reshape(16, 128).T)
    cwl = np.ascontiguousarray(np.asarray(conv_w[0], f).T.reshape(16, 128, 4).transpose(1, 0, 2))
    shared = {
        "w_ada": np.ascontiguousarray(w_ada[0], f), "b_ada": np.ascontiguousarray(b_ada[0], f).reshape(1, -1),
        "ngT": colT(norm_g[0]), "w_in": np.ascontiguousarray(w_in[0], f), "bif_bc": rep(b_if[0]),
        "convw": cwl, "convb": colT(conv_b[0]), "mhg_bc": rep(mh_norm_g[0]),
        "relmat": np.ascontiguousarray(np.asarray(rel_bias[0], f)[:, relidx]), "amask": amask,
        "w_pm": np.ascontiguousarray(w_proj_m[0], f), "w_pa": np.ascontiguousarray(w_proj_a[0], f),
        "w_out": np.ascontiguousarray(w_out[0], f), "fg_bc": rep(final_norm_g),
        "ident": ident, "tri": tri, "ones": ones, "maskst": mst,
    }
    in_maps = []
    for core in range(8):
        b, j = core // 4, core % 4
        npad = (3 - j) * 16
        xl = np.zeros((NT * 128, D), f)
        xl[npad * 128:] = x[b, 0:(j + 1) * 2048]
        valid = (np.arange(NT) >= npad).astype(f)
        m = dict(shared)
        m["xl"] = xl
        m["cT"] = colT(c[b])
        m["padneg"] = rep(np.where(valid > 0, 0.0, NEG))
        m["tilevalid"] = rep(valid)
        m["negtile"] = rep(np.where(valid > 0, 0.0, NEG))
        in_maps.append(m)
    res = run_bass_kernel_spmd(nc, in_maps, core_ids=list(range(8)))
    out = np.empty((2, 8192, D), f)
    for core in range(8):
        b, j = core // 4, core % 4
        out[b, j * 2048:(j + 1) * 2048] = res.results[core]["y"]
    return out
```

```python
import os
import numpy as np
import ml_dtypes
from contextlib import ExitStack
import concourse.bass as bass
import concourse.mybir as mybir
from concourse.bass_utils import run_bass_kernel_spmd

F32 = mybir.dt.float32
BF16 = mybir.dt.bfloat16
ALU = mybir.AluOpType
AF = mybir.ActivationFunctionType
AX = mybir.AxisListType

D = 2048
NT = 64
NPRE = 48
NOWN = 16
EPS = 1e-6
C_MQ, C_MK, C_MV, C_MO, C_MZ, C_MI, C_MF = 0, 1024, 2048, 3072, 4096, 5120, 5124
C_AQ, C_AK, C_AV, C_AZ, C_GM, C_GA = 5128, 6152, 7176, 8200, 9224, 11272
IN_COLS = 13320
NEG = -30000.0
ENGS = ("sp", "act", "dve", "pool", "pe")
KSTOP = os.environ.get("KSTOP")


class _Stop(Exception):
    pass


class Op:
    __slots__ = ("eng", "fn", "deps", "dma", "sig", "sem", "val")

    def __init__(self, eng, fn, dma):
        self.eng, self.fn, self.dma = eng, fn, dma
        self.deps, self.sig, self.sem, self.val = [], dma, None, 0


class Sched:
    ND = 40

    def __init__(self, nc, es):
        self.nc = nc
        self.eops = {e: [] for e in ENGS}
        self.lastw, self.readers = {}, {}
        self.csem = {e: es.enter_context(nc.semaphore("cs_" + e)) for e in ENGS}
        self.dsem = [es.enter_context(nc.semaphore("ds%d" % i)) for i in range(self.ND)]
        self.dlast = [None] * self.ND
        self.duse = [0] * self.ND
        self.dn = 0

    @staticmethod
    def _is_psum(k):
        return k == "psS" or (isinstance(k, tuple) and len(k) > 0 and k[0] in ("ps", "psb", "psS"))

    def add(self, eng, fn, r=(), w=(), dma=False):
        w = tuple(w) + tuple(k for k in r if self._is_psum(k))
        r = tuple(k for k in r if not self._is_psum(k))
        op = Op(eng, fn, dma)
        deps = []
        for k in r:
            if k in self.lastw:
                deps.append(self.lastw[k])
        for k in w:
            if k in self.lastw:
                deps.append(self.lastw[k])
            deps.extend(self.readers.get(k, ()))
        if dma:
            i = self.dn % self.ND
            self.dn += 1
            if self.dlast[i] is not None:
                deps.append(self.dlast[i])
            self.duse[i] += 1
            op.sem, op.val = self.dsem[i], 16 * self.duse[i]
            self.dlast[i] = op
        seen = set()
        for d in deps:
            if d is op or id(d) in seen:
                continue
            seen.add(id(d))
            if d.eng == "pe" and eng == "pe" and not d.dma:
                continue
            d.sig = True
            op.deps.append(d)
        for k in r:
            lst = self.readers.setdefault(k, [])
            if not dma:
                lst[:] = [o_ for o_ in lst if o_.dma or o_.eng != eng]
            lst.append(op)
        for k in w:
            self.lastw[k] = op
            self.readers[k] = []
        self.eops[eng].append(op)
        return op

    def barrier(self):
        lasts = []
        for e in ENGS:
            for o in reversed(self.eops[e]):
                if not o.dma and o.fn is not None:
                    lasts.append(o)
                    break
        pend = [d for d in self.dlast if d is not None]
        for e in ENGS:
            op = Op(e, None, False)
            for d in lasts + pend:
                if d.eng == e and not d.dma:
                    continue
                d.sig = True
                op.deps.append(d)
            self.eops[e].append(op)
        self.lastw, self.readers = {}, {}

    def finish(self, keys):
        op = Op("sp", None, False)
        for d in self.dlast:
            if d is not None:
                op.deps.append(d)
        self.eops["sp"].append(op)

    def emit(self):
        nc = self.nc
        for e in ENGS:
            c = 0
            for o in self.eops[e]:
                if o.dma or o.fn is None:
                    continue
                if o.sig:
                    c += 1
                    o.sem, o.val = self.csem[e], c

        def run(ename, eng):
            known = {}
            for o in self.eops[ename]:
                for d in o.deps:
                    key = id(d.sem)
                    if known.get(key, 0) >= d.val:
                        continue
                    eng.wait_ge(d.sem, d.val)
                    known[key] = d.val
                if o.fn is None:
                    continue
                ins = o.fn(eng)
                if o.sig:
                    ins.then_inc(o.sem, 16 if o.dma else 1)

        with nc.Block() as block:
            @block.sync
            def _(e):
                run("sp", e)

            @block.scalar
            def _(e):
                run("act", e)

            @block.vector
            def _(e):
                run("dve", e)

            @block.gpsimd
            def _(e):
                run("pool", e)

            @block.tensor
            def _(e):
                run("pe", e)


def build_nc():
    nc = bass.Bass("TRN2", target_bir_lowering=False)

    def din(name, shape, dt=F32):
        return nc.dram_tensor(name, list(shape), dt, kind="ExternalInput").ap()

    xl = din("xl", [NT * 128, D])
    cT = din("cT", [128, 16])
    w_ada = din("w_ada", [D, 3 * D])
    b_ada = din("b_ada", [1, 3 * D])
    ngT = din("ngT", [128, 16])
    w_in = din("w_in", [D, IN_COLS])
    bif_bc = din("bif_bc", [128, 8])
    convw = din("convw", [128, 16, 4])
    convb = din("convb", [128, 16])
    mhg_bc = din("mhg_bc", [128, 1024])
    relmat = din("relmat", [8, 128, 640])
    amask = din("amask", [128, 640])
    w_pm = din("w_pm", [1024, D])
    w_pa = din("w_pa", [1024, D])
    w_out = din("w_out", [D, D])
    fg_bc = din("fg_bc", [128, D])
    padneg = din("padneg", [128, NT])
    tilevalid = din("tilevalid", [128, NT])
    negtile = din("negtile", [128, NT])
    ident_d = din("ident", [128, 128], BF16)
    tri_d = din("tri", [128, 128])
    ones_d = din("ones", [128, 128])
    mst_d = din("maskst", [128, 128])
    y_out = nc.dram_tensor("y", [NOWN * 128, D], F32, kind="ExternalOutput").ap()
    dbg = nc.dram_tensor("dbg", [128, 8192], F32, kind="ExternalOutput").ap() if KSTOP else None
    yT_d = nc.dram_tensor("yT_d", [16, 128, NOWN * 128], BF16, kind="Internal").ap()
    mT_d = nc.dram_tensor("mT_d", [16, 128, NOWN * 128], BF16, kind="Internal").ap()

    es = ExitStack()
    with es, nc.allow_low_precision("bf16 matmul operands, fp32 accumulation"), \
            nc.allow_non_contiguous_dma("column-sliced weight loads"):
        S = Sched(nc, es)

        def sb(name, shape, dt=F32):
            return es.enter_context(nc.sbuf_tensor("s_" + name, list(shape), dt))

        def pst(name, shape, dt=F32):
            return es.enter_context(nc.psum_tensor(name, list(shape), dt))

        ident = sb("ident", [128, 128], BF16)
        tri = sb("tri", [128, 128])
        ones = sb("ones", [128, 128])
        gs = sb("gs", [128, 16])
        shift = sb("shift", [128, 16])
        ngt_s = sb("ngt_s", [128, 16])
        gate_bc = sb("gate_bc", [128, D])
        cw = sb("cw", [128, 16, 4])
        cb_ = sb("cb_", [128, 16])
        bif = sb("bif", [128, 8])
        mhg = sb("mhg", [128, 1024])
        pneg = sb("pneg", [128, NT])
        tval = sb("tval", [128, NT])
        ntile = sb("ntile", [128, NT])
        amask_s = sb("amask_s", [128, 640])
        wif = sb("wif", [128, 16, 8], BF16)
        wifs = sb("wifs", [128, 16, 8])
        state = [[sb("st%d%d" % (h, b), [128, 257]) for b in range(2)] for h in range(4)]
        ctbf = [[sb("ct%d%d" % (h, b), [128, 257], BF16) for b in range(2)] for h in range(4)]
        hT_halo = sb("hT_halo", [128, 16, 512], BF16)
        wk_o = sb("wk_o", [128, NOWN, 4])
        wa_o = sb("wa_o", [128, NOWN, 4])
        eB_o = sb("eB_o", [128, NOWN, 4])
        ebc_o = sb("ebc_o", [128, NOWN, 4])
        sm = sb("sm", [128, 64])
        ARENA_COLS = 40000
        arena = sb("arena", [128, ARENA_COLS])

        ps = [pst("ps%d" % i, [128, 512]) for i in range(4)]
        psS = pst("psS", [128, 1024])
        psb = [pst("psb%d" % i, [128, 1024], BF16) for i in range(2)]
        rot = {"ps": 0, "psb": 0, "cast": 0, "q": 0}

        def getps():
            i = rot["ps"] % 4
            rot["ps"] += 1
            return ps[i], ("ps", i)

        def getpsb():
            i = rot["psb"] % 2
            rot["psb"] += 1
            return psb[i], ("psb", i)

        aoff = [0]

        def areset():
            aoff[0] = 0

        def af32(cols):
            o = aoff[0]
            aoff[0] += cols
            assert aoff[0] <= ARENA_COLS, aoff[0]
            return arena[:, o:o + cols]

        def abf(cols):
            n = (cols + 1) // 2
            return af32(n).bitcast(BF16)[:, 0:cols]

        def dma(q, out, in_, r, w):
            S.add(q, lambda e, o=out, i=in_: e.dma_start(out=o, in_=i), r=r, w=w, dma=True)

        def dmaq():
            rot["q"] += 1
            return "sp" if rot["q"] % 2 else "pool"

        def act(out, in_, func, r, w, bias=None, scale=None, accum=None):
            kw = {}
            if bias is not None:
                kw["bias"] = bias
            if scale is not None:
                kw["scale"] = scale
            if accum is not None:
                kw["accum_out"] = accum
            S.add("act", lambda e: e.activation(out, in_, func, **kw), r=r, w=w)

        def ts(eng, out, in0, s1, s2, op0, op1, r, w):
            if s2 is None:
                s2, op1 = (1.0, ALU.mult) if op0 == ALU.add else (0.0, ALU.add)
            S.add(eng, lambda e: e.tensor_scalar(out, in0, s1, s2, op0, op1), r=r, w=w)

        def rsqrt_col(col, key):
            S.add("act", lambda e: e.activation(col, col, AF.Sqrt), r=(key,), w=(key,))
            S.add("dve", lambda e: e.reciprocal(col, col), r=(key,), w=(key,))

        def tt(eng, out, in0, in1, op, r, w):
            S.add(eng, lambda e: e.tensor_tensor(out, in0, in1, op), r=r, w=w)

        def stt(eng, out, in0, sc, in1, op0, op1, r, w):
            S.add(eng, lambda e: e.scalar_tensor_tensor(out, in0, sc, in1, op0, op1), r=r, w=w)

        def mm(out, lhsT, rhs, start, stop, r, w):
            S.add("pe", lambda e: e.matmul(out, lhsT, rhs, start=start, stop=stop), r=r, w=w)

        def tr(out, in_, r, w):
            S.add("pe", lambda e: e.transpose(out, in_, ident[:]), r=tuple(r) + ("ident",), w=w)

        def cast(out, in_, r, w, eng=None):
            if eng is None:
                rot["cast"] += 1
                eng = ("act", "pool")[rot["cast"] % 2]
            if eng == "act":
                S.add("act", lambda e: e.copy(out, in_), r=r, w=w)
            else:
                S.add(eng, lambda e: e.tensor_copy(out, in_), r=r, w=w)

        def memset(eng, ap, v, w):
            S.add(eng, lambda e: e.memset(ap, v), w=w)

        def stop(tag, dumps=()):
            if KSTOP != tag:
                return
            S.barrier()
            off = 0
            for ap_, n_ in dumps:
                dma("sp", dbg[:, off:off + n_], ap_, r=(), w=(("dbg", off),))
                off += n_
            raise _Stop()

        def body():
            for (t_, d_, k_) in ((ident, ident_d, "ident"), (tri, tri_d, "tri"), (ones, ones_d, "ones"),
                                 (ngt_s, ngT, "ngt"), (cw, convw, "cw"),
                                 (cb_, convb, "cb"), (bif, bif_bc, "bif"), (mhg, mhg_bc, "mhg"),
                                 (pneg, padneg, "pneg"), (tval, tilevalid, "tval"),
                                 (ntile, negtile, "ntile"), (amask_s, amask, "amask")):
                dma("sp", t_[:], d_, r=(), w=(k_,))
            for k4 in range(4):
                dma("sp", wifs[:, 4 * k4:4 * k4 + 4, :],
                    w_in[k4 * 512:(k4 + 1) * 512, C_MI:C_MI + 8].rearrange("(k p) c -> p k c", p=128), r=(), w=("wifs",))
            cast(wif[:], wifs[:], r=("wifs",), w=("wif",), eng="dve")
            for h in range(4):
                for b in range(2):
                    memset("dve", state[h][b][:], 0.0, w=(("st", h, b),))

            areset()
            sc_in = af32(16)
            scv = af32(16)
            modrow = af32(3 * D)[0:1, :]
            badar = af32(3 * D)[0:1, :]
            stgA = [af32(3072) for _ in range(2)]
            dma("sp", sc_in, cT, r=(), w=("sc_in",))
            dma("sp", badar, b_ada, r=(), w=("badar",))
            act(scv, sc_in, AF.Silu, r=("sc_in",), w=("scv",))
            banks = [(ps[0], ("ps", 0), 0), (ps[1], ("ps", 1), 0), (ps[2], ("ps", 2), 0),
                     (ps[3], ("ps", 3), 0), (psS, ("psS",), 0), (psS, ("psS",), 512)]
            for half in range(2):
                for k in range(16):
                    st_ = stgA[k % 2]
                    key = ("stgA", k % 2)
                    dma(dmaq(), st_, w_ada[k * 128:(k + 1) * 128, half * 3072:(half + 1) * 3072], r=(), w=(key,))
                    for cbk in range(6):
                        t_, pk, off = banks[cbk]
                        mm(t_[0:1, off:off + 512], scv[:, k:k + 1], st_[:, cbk * 512:(cbk + 1) * 512],
                           k == 0, k == 15, r=(key, "scv"), w=(pk,))
                for cbk in range(6):
                    t_, pk, off = banks[cbk]
                    c0 = half * 3072 + cbk * 512
                    tt("dve", modrow[:, c0:c0 + 512], t_[0:1, off:off + 512], badar[:, c0:c0 + 512], ALU.add,
                       r=(pk, "badar"), w=("modrow",))
            pcol, pck = getps()
            for c in range(32):
                mm(pcol[:, c:c + 1], modrow[0:1, c * 128:(c + 1) * 128], ones[0:1, 0:1], True, True,
                   r=("modrow", "ones"), w=(pck,))
            cast(shift[:], pcol[:, 0:16], r=(pck,), w=("shift",), eng="dve")
            stt("dve", gs[:], pcol[:, 16:32], 1.0, ngt_s[:], ALU.add, ALU.mult, r=(pck, "ngt"), w=("gs",))
            for cbk in range(4):
                t_, pk = getps()
                mm(t_[:, 0:512], ones[0:1, :], modrow[0:1, 2 * D + cbk * 512:2 * D + (cbk + 1) * 512], True, True,
                   r=("modrow", "ones"), w=(pk,))
                cast(gate_bc[:, cbk * 512:(cbk + 1) * 512], t_[:, 0:512], r=(pk,), w=("gate_bc",), eng="act")
            S.barrier()
            stop("S1", [(gs[:], 16), (shift[:], 16), (gate_bc[:, 0:64], 64)])

            def frontend1(tau, xbufs, xn, xnk, i2):
                xt = xbufs[i2 % len(xbufs)]
                xk = ("xt", i2 % len(xbufs))
                dma(dmaq(), xt, xl[tau * 128:(tau + 1) * 128, :], r=(), w=(xk,))
                memset("dve", sm[:, 0:1], 0.0, w=("ss",))
                act(xn, xt, AF.Square, r=(xk, "ss"), w=(xnk, "ss"), accum=sm[:, 0:1])
                ts("dve", sm[:, 1:2], sm[:, 0:1], 1.0 / D, EPS, ALU.mult, ALU.add, r=("ss",), w=("rstd",))
                rsqrt_col(sm[:, 1:2], "rstd")
                act(xn, xt, AF.Identity, r=(xk, "rstd"), w=(xnk,), scale=sm[:, 1:2])

            def frontend2(dstf, dkey, xn, xnk, halves=(0, 1), act_halves=(0,)):
                for hh in halves:
                    pt, ptk = psb[hh], ("psb", hh)
                    for kk in range(8):
                        k = hh * 8 + kk
                        tr(pt[:, kk * 128:(kk + 1) * 128], xn[:, k * 128:(k + 1) * 128], r=(xnk,), w=(ptk,))
                    for kk in range(8):
                        k = hh * 8 + kk
                        if hh in act_halves:
                            act(dstf(k), pt[:, kk * 128:(kk + 1) * 128], AF.Identity, r=(ptk, "gs", "shift"),
                                w=(dkey[0],), bias=shift[:, k:k + 1], scale=gs[:, k:k + 1])
                        else:
                            ts("dve", dstf(k), pt[:, kk * 128:(kk + 1) * 128], gs[:, k:k + 1], shift[:, k:k + 1],
                               ALU.mult, ALU.add, r=(ptk, "gs", "shift"), w=(dkey[1],))

            def frontend(tau, dstf, dkey, xbufs, xn, i2):
                frontend1(tau, xbufs, xn, "xn", i2)
                frontend2(dstf, dkey, xn, "xn")

            def load_w(dst, src_rows, nk, stgs, skey, dkey, cast_eng=None, q=None, rot_engs=None):
                for k in range(nk):
                    st_ = stgs[k % len(stgs)]
                    key = (skey, k % len(stgs))
                    dma(q or dmaq(), st_, src_rows(k), r=(), w=(key,))
                    cast(dst(k), st_, r=(key,), w=(dkey,), eng=(rot_engs[k % len(rot_engs)] if rot_engs else cast_eng))

            def gates(tau, hsrc, hkey, own_i):
                pg, pgk = getps()
                for k in range(16):
                    mm(pg[:, 0:8], hsrc(k), wif[:, k, :], k == 0, k == 15, r=tuple(hkey) + ("wif",), w=(pgk,))
                gl = sm[:, 8:16]
                tt("dve", gl, pg[:, 0:8], bif[:], ALU.add, r=(pgk, "bif"), w=("gl",))
                ts("dve", sm[:, 16:20], gl[:, 0:4], pneg[:, tau:tau + 1], None, ALU.add, None, r=("gl", "pneg"), w=("ig",))
                act(sm[:, 20:24], gl[:, 4:8], AF.Exp, r=("gl",), w=("e1",), scale=-1.0)
                ts("dve", sm[:, 20:24], sm[:, 20:24], 1.0, None, ALU.add, None, r=("e1",), w=("e1",))
                act(sm[:, 24:28], sm[:, 20:24], AF.Ln, r=("e1",), w=("lf",))
                ts("dve", sm[:, 24:28], sm[:, 24:28], -1.0, None, ALU.mult, None, r=("lf",), w=("lf",))
                pc, pckk = getps()
                mm(pc[:, 0:4], tri[:], sm[:, 24:28], True, True, r=("lf", "tri"), w=(pckk,))
                mm(pc[:, 4:8], ones[:], sm[:, 24:28], True, True, r=("lf", "ones"), w=(pckk,))
                tt("dve", sm[:, 28:32], sm[:, 16:20], pc[:, 0:4], ALU.subtract, r=("ig", pckk), w=("t1",))
                tt("dve", sm[:, 32:36], sm[:, 28:32], pc[:, 4:8], ALU.add, r=("t1", pckk), w=("t2",))
                if own_i is None:
                    wk, eB = sm[:, 36:40], sm[:, 40:44]
                    wkk, eBk = "wk", "eB"
                else:
                    wk, eB = wk_o[:, own_i, :], eB_o[:, own_i, :]
                    wkk = eBk = ("gown", own_i)
                act(wk, sm[:, 32:36], AF.Exp, r=("t2",), w=(wkk,))
                ts("dve", wk, wk, 0.0625, None, ALU.mult, None, r=(wkk,), w=(wkk,))
                act(eB, pc[:, 4:8], AF.Exp, r=(pckk,), w=(eBk,))
                if own_i is not None:
                    act(wa_o[:, own_i, :], sm[:, 28:32], AF.Exp, r=("t1",), w=(wkk,))
                    ts("dve", wa_o[:, own_i, :], wa_o[:, own_i, :], 0.0625, None, ALU.mult, None, r=(wkk,), w=(wkk,))
                    act(ebc_o[:, own_i, :], pc[:, 0:4], AF.Exp, r=(pckk,), w=(wkk,))
                return wk, eB, wkk, eBk

            def conv_silu(pre, prek, cblk, accb, acck, out, outk):
                ts("dve", accb, pre[:, 3:515], cw[:, cblk, 3:4], cb_[:, cblk:cblk + 1], ALU.mult, ALU.add,
                   r=(prek, "cw", "cb"), w=(acck,))
                for tap in range(3):
                    stt("dve", accb, pre[:, tap:tap + 512], cw[:, cblk, tap:tap + 1], accb, ALU.mult, ALU.add,
                        r=(prek, acck, "cw"), w=(acck,))
                act(out, accb, AF.Silu, r=(acck,), w=(outk,))

            def state_update(h, kT_blk, kTk, vaug, vk, wk_col, wkk, eB_col, eBk, kpp, kppk, refresh_bf):
                pt, ptk = getpsb()
                for blk in range(2):
                    tr(pt[:, blk * 128:(blk + 1) * 128], kT_blk(blk), r=(kTk[blk],), w=(ptk,))
                ts("dve", kpp, pt[:, 0:256], wk_col, None, ALU.mult, None, r=(ptk, wkk), w=(kppk,))
                for blk in range(2):
                    p_, pk = getps()
                    mm(p_[:, 0:257], kpp[:, blk * 128:(blk + 1) * 128], vaug, True, True, r=(kppk, vk), w=(pk,))
                    stt("dve", state[h][blk][:], state[h][blk][:], eB_col, p_[:, 0:257], ALU.mult, ALU.add,
                        r=(pk, eBk, ("st", h, blk)), w=(("st", h, blk),))
                    if refresh_bf:
                        cast(ctbf[h][blk][:], state[h][blk][:], r=(("st", h, blk),), w=(("ct", h, blk),), eng="act")

            areset()
            WA = abf(16 * 2048).rearrange("p (k c) -> p k c", k=16)
            hTgA = [abf(16 * 512).rearrange("p (k c) -> p k c", k=16) for _ in range(2)]
            xn = abf(D)
            xns = [xn, abf(D)]
            kT = abf(8 * 512).rearrange("p (b c) -> p b c", b=8)
            vaugs = [[abf(258)[:, 0:257] for _ in range(4)] for _ in range(4)]
            kpps = [abf(256) for _ in range(2)]
            stgs = [af32(2048) for _ in range(2)]
            xbufs = stgs
            kpre = af32(8 * 515).rearrange("p (b c) -> p b c", b=8)
            accb = af32(512)
            gtmp = af32(4 * 48).rearrange("p (i c) -> p i c", i=4)
            load_w(lambda k: WA[:, k, :], lambda k: w_in[k * 128:(k + 1) * 128, C_MK:C_MK + 2048], 16, stgs, "xt", "WA", rot_engs=("act", "dve"))
            memset("pool", kpre[:, :, 0:3], 0.0, w=tuple(("kpre", c) for c in range(8)))
            for i in range(4):
                for h in range(4):
                    memset("pool", vaugs[i][h][:, 256:257], 1.0, w=(("vaug", i, h),))
            NG = NPRE // 4
            fe_cnt = [0]

            def a_hT(g):
                return hT_halo if g == NG - 1 else hTgA[g % 2]

            def a_keys(g, i):
                return (("hTg", g % 2, i, 0), ("hTg", g % 2, i, 1))

            def a_gates1(g, i):
                tau = 4 * g + i
                hT = a_hT(g)
                gt = gtmp[:, i, :]
                pg, pgk = ps[2 + i % 2], ("ps", 2 + i % 2)
                for k in range(16):
                    mm(pg[:, 0:8], hT[:, k, i * 128:(i + 1) * 128], wif[:, k, :], k == 0, k == 15,
                       r=a_keys(g, i) + ("wif",), w=(pgk,))
                tt("dve", gt[:, 0:8], pg[:, 0:8], bif[:], ALU.add, r=(pgk, "bif"), w=(("g_gl", i),))
                ts("dve", gt[:, 8:12], gt[:, 0:4], pneg[:, tau:tau + 1], None, ALU.add, None, r=(("g_gl", i), "pneg"), w=(("g_ig", i),))
                act(gt[:, 12:16], gt[:, 4:8], AF.Exp, r=(("g_gl", i),), w=(("g_lf", i),), scale=-1.0)
                ts("dve", gt[:, 12:16], gt[:, 12:16], 1.0, None, ALU.add, None, r=(("g_lf", i),), w=(("g_lf", i),))
                act(gt[:, 12:16], gt[:, 12:16], AF.Ln, r=(("g_lf", i),), w=(("g_lf", i),))
                ts("dve", gt[:, 12:16], gt[:, 12:16], -1.0, None, ALU.mult, None, r=(("g_lf", i),), w=(("g_lf", i),))

            def a_gates2(g, i):
                gt = gtmp[:, i, :]
                pc, pck_ = psS[:, (i % 2) * 512:(i % 2) * 512 + 8], ("psS", i % 2)
                mm(pc[:, 0:4], tri[:], gt[:, 12:16], True, True, r=(("g_lf", i), "tri"), w=(pck_,))
                mm(pc[:, 4:8], ones[:], gt[:, 12:16], True, True, r=(("g_lf", i), "ones"), w=(pck_,))
                tt("dve", gt[:, 16:20], gt[:, 8:12], pc[:, 0:4], ALU.subtract, r=(("g_ig", i), pck_), w=(("g_t", i),))
                tt("dve", gt[:, 16:20], gt[:, 16:20], pc[:, 4:8], ALU.add, r=(("g_t", i), pck_), w=(("g_t", i),))
                act(gt[:, 20:24], gt[:, 16:20], AF.Exp, r=(("g_t", i),), w=(("g_wk", i),))
                ts("dve", gt[:, 20:24], gt[:, 20:24], 0.0625, None, ALU.mult, None, r=(("g_wk", i),), w=(("g_wk", i),))
                act(gt[:, 24:28], pc[:, 4:8], AF.Exp, r=(pck_,), w=(("g_eB", i),))

            def a_fe1(g, i):
                frontend1(4 * g + i, xbufs, xns[i % 2], ("xnA", i % 2), fe_cnt[0])
                fe_cnt[0] += 1

            def a_fe2(g, i, halves):
                hT = a_hT(g)
                frontend2(lambda k: hT[:, k, i * 128:(i + 1) * 128], a_keys(g, i), xns[i % 2], ("xnA", i % 2), halves)

            def a_U1(g, i, h):
                gt = gtmp[:, i, :]
                pt = ps[h % 2][:].bitcast(BF16)
                ptk = ("ps", h % 2)
                for blk in range(2):
                    tr(pt[:, blk * 128:(blk + 1) * 128], kT[:, 2 * h + blk, i * 128:(i + 1) * 128],
                       r=(("kT", 2 * h + blk),), w=(ptk,))
                kp, kpk = kpps[h % 2], ("kpp", h % 2)
                ts("dve", kp, pt[:, 0:256], gt[:, 20 + h:21 + h], None, ALU.mult, None, r=(ptk, ("g_wk", i)), w=(kpk,))

            def a_U2(g, i, h):
                gt = gtmp[:, i, :]
                kp, kpk = kpps[h % 2], ("kpp", h % 2)
                for blk in range(2):
                    pk = ("psS", blk)
                    p_ = psS[:, blk * 512:blk * 512 + 257]
                    mm(p_, kp[:, blk * 128:(blk + 1) * 128], vaugs[i][h], True, True, r=(kpk, ("vaug", i, h)), w=(pk,))
                    stt("dve", state[h][blk][:], state[h][blk][:], gt[:, 24 + h:25 + h], p_, ALU.mult, ALU.add,
                        r=(pk, ("g_eB", i), ("st", h, blk)), w=(("st", h, blk),))

            def a_stage3(g):
                nxt = g + 1 < NG
                if nxt:
                    a_fe1(g + 1, 0)
                for i in range(4):
                    if nxt and i + 1 < 4:
                        a_fe1(g + 1, i + 1)
                    a_U1(g, i, 0)
                    if nxt:
                        a_fe2(g + 1, i, (0,))
                    a_U1(g, i, 1)
                    a_U2(g, i, 0)
                    if nxt:
                        a_fe2(g + 1, i, (1,))
                    a_U1(g, i, 2)
                    a_U2(g, i, 1)
                    a_U1(g, i, 3)
                    a_U2(g, i, 2)
                    a_U2(g, i, 3)

            for i in range(4):
                a_fe1(0, i)
                a_fe2(0, i, (0, 1))
            for g in range(NG):
                hT = a_hT(g)
                hgk = tuple(k_ for i_ in range(4) for k_ in a_keys(g, i_))
                for i in range(4):
                    a_gates1(g, i)
                for cbk in range(8):
                    p_, pk = ps[cbk % 2], ("ps", cbk % 2)
                    for k in range(16):
                        mm(p_[:, 0:512], WA[:, k, cbk * 128:(cbk + 1) * 128], hT[:, k, :], k == 0, k == 15,
                           r=("WA",) + hgk, w=(pk,))
                    cast(kpre[:, cbk, 3:515], p_[:, 0:512], r=(pk,), w=(("kpre", cbk),), eng="act")
                    conv_silu(kpre[:, cbk, :], ("kpre", cbk), 8 + cbk, accb, "accb", kT[:, cbk, :], ("kT", cbk))
                    ts("dve", kpre[:, cbk, 0:3], kpre[:, cbk, 512:515], tval[:, 4 * g + 3:4 * g + 4], None, ALU.mult, None,
                       r=(("kpre", cbk), "tval"), w=(("kpre", cbk),))
                for i in range(4):
                    for half in range(2):
                        p_, pk = ps[2 + half], ("ps", 2 + half)
                        for k in range(16):
                            mm(p_[:, 0:512], hT[:, k, i * 128:(i + 1) * 128], WA[:, k, 1024 + half * 512:1024 + (half + 1) * 512],
                               k == 0, k == 15, r=("WA",) + a_keys(g, i), w=(pk,))
                        for hh in range(2):
                            h = 2 * half + hh
                            cast(vaugs[i][h][:, 0:256], p_[:, hh * 256:(hh + 1) * 256], r=(pk,), w=(("vaug", i, h),),
                                 eng="act")
                for i in range(4):
                    a_gates2(g, i)
                a_stage3(g)
            S.barrier()
            stop("A", [(state[0][0][:], 257), (state[3][1][:], 257), (sm[:], 64)])

            areset()
            hT_own = abf(16 * 2048).rearrange("p (k c) -> p k c", k=16)
            mark_b = aoff[0]
            xn = abf(D)
            xbufs = [af32(D) for _ in range(2)]
            xns0 = [xn, abf(D)]
            frontend1(NPRE, xbufs, xns0[0], ("xnB", 0), 0)
            for i in range(NOWN):
                if i + 1 < NOWN:
                    frontend1(NPRE + i + 1, xbufs, xns0[(i + 1) % 2], ("xnB", (i + 1) % 2), i + 1)
                frontend2(lambda k, i=i: hT_own[:, k, i * 128:(i + 1) * 128], (("hTo", i, 0), ("hTo", i, 1)),
                          xns0[i % 2], ("xnB", i % 2), act_halves=())
                gates(NPRE + i, lambda k, i=i: hT_own[:, k, i * 128:(i + 1) * 128], (("hTo", i, 0), ("hTo", i, 1)), i)
            for h in range(4):
                for b in range(2):
                    cast(ctbf[h][b][:], state[h][b][:], r=(("st", h, b),), w=(("ct", h, b),), eng="act")
            S.barrier()
            stop("B0", [(wk_o[:].rearrange("p a b -> p (a b)"), 64), (wa_o[:].rearrange("p a b -> p (a b)"), 64), (eB_o[:].rearrange("p a b -> p (a b)"), 64), (ebc_o[:].rearrange("p a b -> p (a b)"), 64), (state[0][0][:], 257)])

            aoff[0] = mark_b
            WG = abf(16 * 1280).rearrange("p (k c) -> p k c", k=16)
            qkT = [abf(4 * 512).rearrange("p (b c) -> p b c", b=4) for _ in range(2)]
            vaug1 = [abf(258)[:, 0:257] for _ in range(2)]
            kpps = [abf(256) for _ in range(2)]
            STb = [abf(128) for _ in range(2)]
            ybf = [abf(256) for _ in range(2)]
            yTt = [abf(256).rearrange("p (b c) -> p b c", b=2) for _ in range(2)]
            stg1 = [af32(1280) for _ in range(4)]
            qkpre = af32(4 * 515).rearrange("p (b c) -> p b c", b=4)
            accb = af32(512)
            sgo = af32(256)
            slz = af32(256)
            Gt = [af32(256) for _ in range(2)]
            junk = af32(256)
            w5 = w_in[:, 0:5120].rearrange("r (s c) -> r s c", c=1024)
            for vb in range(2):
                memset("pool", vaug1[vb][:, 256:257], 1.0, w=(("vaug1", vb),))
            ts("dve", mhg[:], mhg[:], 0.5, None, ALU.mult, None, r=("mhg",), w=("mhg",))
            PS0, PS1, PS2, PS3 = (ps[0], ("ps", 0)), (ps[1], ("ps", 1)), (ps[2], ("ps", 2)), (ps[3], ("ps", 3))
            for h in range(4):
                load_w(lambda k: WG[:, k, :].rearrange("p (s c) -> p s c", c=256),
                       lambda k, h=h: w5[k * 128:(k + 1) * 128, :, h * 256:(h + 1) * 256],
                       16, [s_.rearrange("p (s c) -> p s c", c=256) for s_ in stg1], "stg1", "WG", rot_engs=("act", "dve"))
                for blk in range(4):
                    p_, pk = (PS0, PS1)[blk % 2]
                    for k in range(16):
                        mm(p_[:, 0:3], WG[:, k, blk * 128:(blk + 1) * 128], hT_halo[:, k, 509:512], k == 0, k == 15,
                           r=("WG",), w=(pk,))
                    ts("dve", qkpre[:, blk, 0:3], p_[:, 0:3], tval[:, NPRE - 1:NPRE], None, ALU.mult, None,
                       r=(pk, "tval"), w=(("qkpre", blk),))

                def b1_proj(g, h=h):
                    qk = qkT[g % 2]
                    for blk in range(4):
                        p_, pk = (PS0, PS1)[blk % 2]
                        for k in range(16):
                            mm(p_[:, 0:512], WG[:, k, blk * 128:(blk + 1) * 128], hT_own[:, k, g * 512:(g + 1) * 512],
                               k == 0, k == 15, r=("WG",), w=(pk,))
                        cast(qkpre[:, blk, 3:515], p_[:, 0:512], r=(pk,), w=(("qkpre", blk),), eng="act")
                        cidx = (2 * h + blk) if blk < 2 else (8 + 2 * h + blk - 2)
                        conv_silu(qkpre[:, blk, :], ("qkpre", blk), cidx, accb, "accb", qk[:, blk, :], ("qkT", g % 2, blk))
                        cast(qkpre[:, blk, 0:3], qkpre[:, blk, 512:515], r=(("qkpre", blk),), w=(("qkpre", blk),), eng="dve")

                def b1_P(t, h=h):
                    vb = t % 2
                    pA, pAk = PS0
                    for k in range(16):
                        mm(pA[:, 0:512], hT_own[:, k, t * 128:(t + 1) * 128], WG[:, k, 512:1024], k == 0, k == 15,
                           r=("WG",), w=(pAk,))
                    pB, pBk = PS1
                    for k in range(16):
                        mm(pB[:, 0:256], hT_own[:, k, t * 128:(t + 1) * 128], WG[:, k, 1024:1280], k == 0, k == 15,
                           r=("WG",), w=(pBk,))
                    cast(vaug1[vb][:, 0:256], pA[:, 0:256], r=(pAk,), w=(("vaug1", vb),), eng="act")
                    act(sgo, pA[:, 256:512], AF.Tanh, r=(pAk,), w=("sgo",), scale=0.5)
                    act(slz, pB[:, 0:256], AF.Silu, r=(pBk,), w=("slz",))
                    stt("dve", Gt[vb], sgo, 1.0, slz, ALU.add, ALU.mult, r=("sgo", "slz"), w=(("Gt", vb),))
                    tt("dve", Gt[vb], Gt[vb], mhg[:, h * 256:(h + 1) * 256], ALU.mult, r=(("Gt", vb), "mhg"), w=(("Gt", vb),))

                def b1_S(t, h=h):
                    vb = t % 2
                    gp = (t // 4) % 2
                    tsl = slice((t % 4) * 128, (t % 4 + 1) * 128)
                    pS, pSk = PS2
                    for blk in range(2):
                        mm(pS[:, 0:128], qkT[gp][:, 2 + blk, tsl], qkT[gp][:, blk, tsl], blk == 0, blk == 1,
                           r=(("qkT", gp, blk), ("qkT", gp, 2 + blk)), w=(pSk,))
                    stt("dve", STb[vb], pS[:, 0:128], wa_o[:, t, h:h + 1], tri[:], ALU.mult, ALU.mult,
                        r=(pSk, ("gown", t), "tri"), w=(("STb", vb),))

                def b1_N(t, h=h):
                    vb = t % 2
                    gp = (t // 4) % 2
                    tsl = slice((t % 4) * 128, (t % 4 + 1) * 128)
                    pN, pNk = PS3
                    mm(pN[:, 0:257], STb[vb], vaug1[vb], True, False, r=(("STb", vb), ("vaug1", vb)), w=(pNk,))
                    for blk in range(2):
                        mm(pN[:, 0:257], qkT[gp][:, blk, tsl], ctbf[h][blk][:], False, blk == 1,
                           r=(("qkT", gp, blk), ("ct", h, blk)), w=(pNk,))

                def b1_Utr(t, h=h):
                    vb = t % 2
                    gp = (t // 4) % 2
                    tsl = slice((t % 4) * 128, (t % 4 + 1) * 128)
                    pt, ptk = psb[0], ("psb", 0)
                    for blk in range(2):
                        tr(pt[:, blk * 128:(blk + 1) * 128], qkT[gp][:, 2 + blk, tsl], r=(("qkT", gp, 2 + blk),), w=(ptk,))
                    ts("dve", kpps[vb], pt[:, 0:256], wk_o[:, t, h:h + 1], None, ALU.mult, None,
                       r=(ptk, ("gown", t)), w=(("kpp", vb),))

                def b1_Umm(t, h=h):
                    vb = t % 2
                    for blk in range(2):
                        pk = ("psS", blk)
                        p_ = psS[:, blk * 512:blk * 512 + 257]
                        mm(p_, kpps[vb][:, blk * 128:(blk + 1) * 128], vaug1[vb], True, True,
                           r=(("kpp", vb), ("vaug1", vb)), w=(pk,))
                        stt("dve", state[h][blk][:], state[h][blk][:], eB_o[:, t, h:h + 1], p_, ALU.mult, ALU.add,
                            r=(pk, ("gown", t), ("st", h, blk)), w=(("st", h, blk),))
                        cast(ctbf[h][blk][:], state[h][blk][:], r=(("st", h, blk),), w=(("ct", h, blk),), eng="act")

                def b1_Ychain(t, h=h):
                    vb = t % 2
                    pN, pNk = PS3
                    gk = ("gown", t)
                    tt("dve", sm[:, 44:45], pN[:, 256:257], ebc_o[:, t, h:h + 1], ALU.mult, r=(pNk, gk), w=("d1",))
                    ts("dve", sm[:, 54:55], sm[:, 44:45], -1.0, None, ALU.mult, None, r=("d1",), w=("d1n",))
                    tt("dve", sm[:, 44:45], sm[:, 44:45], sm[:, 54:55], ALU.max, r=("d1", "d1n"), w=("d1",))
                    ts("dve", sm[:, 44:45], sm[:, 44:45], 1.0, 1.0, ALU.max, ALU.mult, r=("d1",), w=("d1",))
                    S.add("dve", lambda e: e.reciprocal(sm[:, 44:45], sm[:, 44:45]), r=("d1",), w=("d1",))
                    tt("dve", sm[:, 45:46], ebc_o[:, t, h:h + 1], sm[:, 44:45], ALU.mult, r=("d1", gk), w=("rr",))
                    memset("dve", sm[:, 46:47], 0.0, w=("ss2",))
                    act(junk, pN[:, 0:256], AF.Square, r=(pNk, "rr", "ss2"), w=("junk", "ss2"), scale=sm[:, 45:46],
                        accum=sm[:, 46:47])
                    ts("dve", sm[:, 47:48], sm[:, 46:47], 1.0 / 256, EPS, ALU.mult, ALU.add, r=("ss2",), w=("r2",))
                    rsqrt_col(sm[:, 47:48], "r2")
                    tt("dve", sm[:, 47:48], sm[:, 47:48], sm[:, 45:46], ALU.mult, r=("r2", "rr"), w=("r2",))
                    stt("dve", ybf[vb], pN[:, 0:256], sm[:, 47:48], Gt[vb], ALU.mult, ALU.mult,
                        r=(pNk, "r2", ("Gt", vb)), w=(("ybf", vb),))

                def b1_Ytr(t, h=h):
                    vb = t % 2
                    pt, ptk = psb[1], ("psb", 1)
                    for blk in range(2):
                        tr(pt[:, blk * 128:(blk + 1) * 128], ybf[vb][:, blk * 128:(blk + 1) * 128], r=(("ybf", vb),), w=(ptk,))
                    cast(yTt[vb].rearrange("p b c -> p (b c)"), pt[:, 0:256], r=(ptk,), w=(("yTt", vb),), eng="act")
                    dma(dmaq(), yT_d[2 * h:2 * h + 2, :, t * 128:(t + 1) * 128].rearrange("b p t -> p b t"), yTt[vb],
                        r=(("yTt", vb),), w=(("yTd", 2 * h, t),))

                b1_proj(0)
                b1_P(0)
                b1_S(0)
                for t in range(NOWN):
                    if t + 1 < NOWN:
                        if (t + 1) % 4 == 0:
                            b1_proj((t + 1) // 4)
                        b1_P(t + 1)
                        b1_S(t + 1)
                    b1_N(t)
                    b1_Utr(t)
                    if t > 0:
                        b1_Ytr(t - 1)
                    b1_Umm(t)
                    b1_Ychain(t)
                b1_Ytr(NOWN - 1)
            S.barrier()
            stop("B1", [(state[0][0][:], 257)])

            aoff[0] = mark_b
            WG2s = [abf(16 * 512).rearrange("p (k c) -> p k c", k=16) for _ in range(2)]
            akT = abf(2560)
            aqT = abf(2048)
            Vt = abf(20 * 128).rearrange("p (t c) -> p t c", t=20)
            slzA = abf(16 * 128).rearrange("p (t c) -> p t c", t=16)
            Pb = [abf(640) for _ in range(2)]
            PT = [abf(640) for _ in range(2)]
            yab = [abf(128) for _ in range(2)]
            yaT = [abf(128) for _ in range(2)]
            stg2 = [af32(512) for _ in range(4)]
            rb = af32(640)
            bm = af32(640)
            sbuf_s = [af32(640) for _ in range(2)]
            wA = w_in[:, C_AQ:C_AQ + 4096].rearrange("r (s c) -> r s c", c=1024)
            def b2_load(h, cast_eng=None, q=None):
                W_ = WG2s[h % 2]
                load_w(lambda k: W_[:, k, :].rearrange("p (s c) -> p s c", c=128),
                       lambda k: wA[k * 128:(k + 1) * 128, :, h * 128:(h + 1) * 128],
                       16, [s_.rearrange("p (s c) -> p s c", c=128) for s_ in stg2], "stg2", ("WG2", h % 2),
                       cast_eng=cast_eng, q=q)

            b2_load(0)
            for h in range(8):
                WG2 = WG2s[h % 2]
                WK = ("WG2", h % 2)
                dma("sp", rb, relmat[h], r=(), w=("rb",))
                tt("pool", bm, rb, amask_s[:], ALU.add, r=("rb", "amask"), w=("bm",))
                for g5 in range(5):
                    src = hT_halo if g5 == 0 else hT_own[:, :, (g5 - 1) * 512:g5 * 512]
                    srk = ()
                    p_, pk = getps()
                    for k in range(16):
                        mm(p_[:, 0:512], WG2[:, k, 128:256], src[:, k, :], k == 0, k == 15, r=(WK,) + srk, w=(pk,))
                    cast(akT[:, g5 * 512:(g5 + 1) * 512], p_[:, 0:512], r=(pk,), w=(("akT", g5),), eng="act")
                    if g5 > 0:
                        p_, pk = getps()
                        for k in range(16):
                            mm(p_[:, 0:512], WG2[:, k, 0:128], src[:, k, :], k == 0, k == 15, r=(WK,) + srk, w=(pk,))
                        act(aqT[:, (g5 - 1) * 512:g5 * 512], p_[:, 0:512], AF.Copy, r=(pk,), w=(("aqT", g5 - 1),),
                            scale=float(128 ** -0.5))
                    for i in range(4):
                        ttile = g5 * 4 + i
                        p_, pk = getps()
                        for k in range(16):
                            mm(p_[:, 0:256], src[:, k, i * 128:(i + 1) * 128], WG2[:, k, 256:512], k == 0, k == 15,
                               r=(WK,) + srk, w=(pk,))
                        cast(Vt[:, ttile, :], p_[:, 0:128], r=(pk,), w=(("Vt", ttile),), eng="act")
                        if g5 > 0:
                            act(slzA[:, ttile - 4, :], p_[:, 128:256], AF.Silu, r=(pk,), w=(("slzA", ttile - 4),))
                if h + 1 < 8:
                    b2_load(h + 1, cast_eng="pool", q="sp")

                RS = [sm[:, 56:57], sm[:, 57:58], sm[:, 60:61]]

                def att_s1(t, h=h):
                    vb = t % 2
                    gq = ("aqT", t // 4)
                    kk0 = tuple(("akT", x) for x in sorted({t // 4, (t + 3) // 4, (t + 4) // 4}))
                    mm(psS[:, 0:512], aqT[:, t * 128:(t + 1) * 128], akT[:, t * 128:t * 128 + 512], True, True,
                       r=(gq,) + kk0, w=("psS",))
                    mm(psS[:, 512:640], aqT[:, t * 128:(t + 1) * 128], akT[:, t * 128 + 512:t * 128 + 640], True, True,
                       r=(gq,) + kk0, w=("psS",))
                    sbt = sbuf_s[vb]
                    sk = ("sbt", vb)
                    tt("dve", sbt, psS[:, 0:640], bm, ALU.add, r=("psS", "bm"), w=(sk,))
                    for kb in range(max(0, 4 - t)):
                        lt = NPRE - 4 + t + kb
                        ts("dve", sbt[:, kb * 128:(kb + 1) * 128], sbt[:, kb * 128:(kb + 1) * 128], ntile[:, lt:lt + 1], None,
                           ALU.add, None, r=(sk, "ntile"), w=(sk,))
                    S.add("dve", lambda e, sbt=sbt: e.reduce_max(sm[:, 48:49], sbt, AX.X), r=(sk,), w=("mx",))
                    ts("dve", sm[:, 48:49], sm[:, 48:49], -1.0, None, ALU.mult, None, r=("mx",), w=("mx",))
                    rsc = RS[t % 3]
                    memset("dve", rsc, 0.0, w=(("rsum", t % 3),))
                    act(Pb[vb], sbt, AF.Exp, r=(sk, "mx", ("rsum", t % 3)), w=(("Pb", vb), ("rsum", t % 3)), bias=sm[:, 48:49],
                        accum=rsc)

                def att_s2(t, h=h):
                    vb = t % 2
                    pt, ptk = psb[0], ("psb", 0)
                    for kb in range(5):
                        tr(pt[:, kb * 128:(kb + 1) * 128], Pb[vb][:, kb * 128:(kb + 1) * 128], r=(("Pb", vb),), w=(ptk,))
                    cast(PT[vb], pt[:, 0:640], r=(ptk,), w=(("PT", vb),), eng="act")

                def att_s3(t, h=h):
                    vb = t % 2
                    rsc = RS[t % 3]
                    rrc = sm[:, 58 + vb:59 + vb]
                    pO, pOk = getps()
                    for kb in range(5):
                        mm(pO[:, 0:128], PT[vb][:, kb * 128:(kb + 1) * 128], Vt[:, t + kb, :], kb == 0, kb == 4,
                           r=(("PT", vb), ("Vt", t + kb)), w=(pOk,))
                    S.add("dve", lambda e: e.reciprocal(rrc, rsc), r=(("rsum", t % 3),), w=(("rrs", vb),))
                    stt("dve", yab[vb], pO[:, 0:128], rrc, slzA[:, t, :], ALU.mult, ALU.mult,
                        r=(pOk, ("rrs", vb), ("slzA", t)), w=(("yab", vb),))
                    pt2, pt2k = psb[1], ("psb", 1)
                    tr(pt2[:, 0:128], yab[vb], r=(("yab", vb),), w=(pt2k,))
                    cast(yaT[vb], pt2[:, 0:128], r=(pt2k,), w=(("yaT", vb),), eng="act")
                    dma("act", yT_d[8 + h, :, t * 128:(t + 1) * 128], yaT[vb], r=(("yaT", vb),), w=(("yTd", 8 + h, t),))

                att_s1(0)
                att_s1(1)
                att_s2(0)
                for t in range(NOWN):
                    if t + 2 < NOWN:
                        att_s1(t + 2)
                    if t + 1 < NOWN:
                        att_s2(t + 1)
                    att_s3(t)
            S.barrier()
            stop("B2", [(sm[:], 64)])

            aoff[0] = mark_b
            yT_res = abf(16 * 2048).rearrange("p (b c) -> p b c", b=16)
            hh_flat = hT_halo[:].rearrange("p k c -> p (k c)")
            wgm = [abf(16 * 128).rearrange("p (k c) -> p k c", k=16), hh_flat[:, 0:2048].rearrange("p (k c) -> p k c", k=16)]
            wga = [abf(16 * 128).rearrange("p (k c) -> p k c", k=16), hh_flat[:, 2048:4096].rearrange("p (k c) -> p k c", k=16)]
            wpm = [abf(8 * 128).rearrange("p (k c) -> p k c", k=8), hh_flat[:, 4096:5120].rearrange("p (k c) -> p k c", k=8)]
            wpa = [abf(8 * 128).rearrange("p (k c) -> p k c", k=8), hh_flat[:, 5120:6144].rearrange("p (k c) -> p k c", k=8)]
            mTb = [abf(512) for _ in range(2)]
            stg3 = [af32(8 * 128).rearrange("p (k c) -> p k c", k=8) for _ in range(2)]
            stg3.append(mhg[:].rearrange("p (k c) -> p k c", k=8))
            stg3.append(hh_flat[:, 6144:8192].bitcast(F32).rearrange("p (k c) -> p k c", k=8))
            sgm = af32(512)
            sga = af32(512)
            t1b = af32(512)
            for b in range(16):
                dma(dmaq(), yT_res[:, b, :], yT_d[b], r=(), w=("yT_res",))
            si3 = [0]

            def b3_load(c):
                cbuf = c % 2
                for (dst, src, nk, nm) in ((wgm, w_in[:, C_GM + c * 128:C_GM + (c + 1) * 128], 16, "wgm"),
                                           (wga, w_in[:, C_GA + c * 128:C_GA + (c + 1) * 128], 16, "wga"),
                                           (wpm, w_pm[:, c * 128:(c + 1) * 128], 8, "wpm"),
                                           (wpa, w_pa[:, c * 128:(c + 1) * 128], 8, "wpa")):
                    for k8 in range(nk // 8):
                        st_ = stg3[si3[0] % 4]
                        sk = ("stg3", si3[0] % 4)
                        si3[0] += 1
                        for k4 in range(2):
                            r0 = (k8 * 8 + k4 * 4) * 128
                            dma("sp", st_[:, 4 * k4:4 * k4 + 4, :],
                                src[r0:r0 + 512, :].rearrange("(k p) c -> p k c", p=128), r=(), w=(sk,))
                        cast(dst[cbuf][:, k8 * 8:(k8 + 1) * 8, :], st_[:], r=(sk,), w=((nm, cbuf),), eng="pool")

            b3_load(0)
            for c in range(16):
                cbuf = c % 2
                if c + 1 < 16:
                    b3_load(c + 1)
                for g in range(4):
                    gsl = slice(g * 512, (g + 1) * 512)
                    p1, p1k = getps()
                    for k in range(16):
                        mm(p1[:, 0:512], wgm[cbuf][:, k, :], hT_own[:, k, gsl], k == 0, k == 15, r=(("wgm", cbuf),), w=(p1k,))
                    act(sgm, p1[:, 0:512], AF.Sigmoid, r=(p1k,), w=("sgm",))
                    p2, p2k = getps()
                    for k in range(16):
                        mm(p2[:, 0:512], wga[cbuf][:, k, :], hT_own[:, k, gsl], k == 0, k == 15, r=(("wga", cbuf),), w=(p2k,))
                    act(sga, p2[:, 0:512], AF.Sigmoid, r=(p2k,), w=("sga",))
                    p3, p3k = getps()
                    for k in range(8):
                        mm(p3[:, 0:512], wpm[cbuf][:, k, :], yT_res[:, k, gsl], k == 0, k == 7,
                           r=(("wpm", cbuf), "yT_res"), w=(p3k,))
                    tt("dve", t1b, sgm, p3[:, 0:512], ALU.mult, r=("sgm", p3k), w=("t1b",))
                    p4, p4k = getps()
                    for k in range(8):
                        mm(p4[:, 0:512], wpa[cbuf][:, k, :], yT_res[:, 8 + k, gsl], k == 0, k == 7,
                           r=(("wpa", cbuf), "yT_res"), w=(p4k,))
                    tt("dve", sga, sga, p4[:, 0:512], ALU.mult, r=("sga", p4k), w=("sga",))
                    mb = mTb[(c * 4 + g) % 2]
                    mk_ = ("mTb", (c * 4 + g) % 2)
                    tt("dve", mb, t1b, sga, ALU.add, r=("t1b", "sga"), w=(mk_,))
                    dma("act", mT_d[c, :, gsl], mb, r=(mk_,), w=(("mTd", c, g),))
            S.barrier()
            stop("B3", [(sm[:], 64)])

            areset()
            Wout = abf(16 * 2048).rearrange("p (k c) -> p k c", k=16)
            mTt = [abf(16 * 128).rearrange("p (c t) -> p c t", c=16) for _ in range(2)]
            fgb = af32(D)
            stgC = [af32(D) for _ in range(2)]
            xts = [af32(D) for _ in range(2)]
            obuf = [af32(D) for _ in range(2)]
            junkC = af32(D)
            dma("sp", fgb, fg_bc, r=(), w=("fgb",))
            for k in range(16):
                st_ = stgC[k % 2]
                sk = ("stgC", k % 2)
                dma(dmaq(), st_, w_out[k * 128:(k + 1) * 128, :], r=(), w=(sk,))
                tt("dve", Wout[:, k, :], st_, gate_bc[:], ALU.mult, r=(sk, "gate_bc"), w=("Wout",))
            outkeys = []

            def c_loads(t):
                vb = t % 2
                for c4 in range(4):
                    dma(("sp", "pool")[c4 % 2], mTt[vb][:, 4 * c4:4 * c4 + 4, :],
                        mT_d[4 * c4:4 * c4 + 4, :, t * 128:(t + 1) * 128].rearrange("c p t -> p c t"), r=(), w=(("mTt", vb),))
                dma(("sp", "pool")[t % 2], xts[vb], xl[(NPRE + t) * 128:(NPRE + t + 1) * 128, :], r=(), w=(("xts", vb),))

            c_loads(0)
            for t in range(NOWN):
                vb = t % 2
                if t + 1 < NOWN:
                    c_loads(t + 1)
                ob = obuf[vb]
                ok = ("ob", vb)
                for cbk in range(4):
                    p_, pk = getps()
                    for c in range(16):
                        mm(p_[:, 0:512], mTt[vb][:, c, :], Wout[:, c, cbk * 512:(cbk + 1) * 512], c == 0, c == 15,
                           r=(("mTt", vb), "Wout"), w=(pk,))
                    tt("dve", ob[:, cbk * 512:(cbk + 1) * 512], p_[:, 0:512], xts[vb][:, cbk * 512:(cbk + 1) * 512], ALU.add,
                       r=(pk, ("xts", vb)), w=(ok,))
                memset("dve", sm[:, 52:53], 0.0, w=("ssC",))
                act(junkC, ob, AF.Square, r=(ok, "ssC"), w=("junkC", "ssC"), accum=sm[:, 52:53])
                ts("dve", sm[:, 53:54], sm[:, 52:53], 1.0 / D, EPS, ALU.mult, ALU.add, r=("ssC",), w=("rC",))
                rsqrt_col(sm[:, 53:54], "rC")
                stt("dve", ob, ob, sm[:, 53:54], fgb, ALU.mult, ALU.mult, r=(ok, "rC", "fgb"), w=(ok,))
                dma("act", y_out[t * 128:(t + 1) * 128, :], ob, r=(ok,), w=(("yout", t),))
                outkeys.append(("yout", t))
        try:
            body()
        except _Stop:
            pass
        S.finish(())
        S.emit()
    return nc


_NC_CACHE = {}


def _consts():
    ident = np.eye(128, dtype=np.float32).astype(ml_dtypes.bfloat16)
    s = np.arange(128)[:, None]
    t = np.arange(128)[None, :]
    tri = (s <= t).astype(np.float32)
    ones = np.ones((128, 128), np.float32)
    q = np.arange(128)[:, None]
    kap = np.arange(640)[None, :]
    cq, ck = q // 64, kap // 64
    allowed = (ck >= cq) & (ck <= cq + 8)
    amask = np.where(allowed, 0.0, NEG).astype(np.float32)
    relidx = np.clip(q + 512 - kap, -63, 128) + 63
    return ident, tri, ones, tri.copy(), amask, relidx


def kernel(x, c, w_ada, b_ada, norm_g, w_in, b_if, conv_w, conv_b, mh_norm_g, rel_bias,
           w_proj_m, w_proj_a, w_out, final_norm_g):
    f = np.float32
    x = np.asarray(x, f)
    ident, tri, ones, mst, amask, relidx = _consts()
    if "nc" not in _NC_CACHE:
        _NC_CACHE["nc"] = build_nc()
    nc = _NC_CACHE["nc"]
    rep = lambda v, n=128: np.ascontiguousarray(np.broadcast_to(np.asarray(v, f).reshape(1, -1), (n, np.asarray(v).size)))
    colT = lambda v: np.ascontiguousarray(np.asarray(v, f).reshape(16, 128).T)
    cwl = np.ascontiguousarray(np.asarray(conv_w[0], f).T.reshape(16, 128, 4).transpose(1, 0, 2))
    shared = {
        "w_ada": np.ascontiguousarray(w_ada[0], f), "b_ada": np.ascontiguousarray(b_ada[0], f).reshape(1, -1),
        "ngT": colT(norm_g[0]), "w_in": np.ascontiguousarray(w_in[0], f), "bif_bc": rep(b_if[0]),
        "convw": cwl, "convb": colT(conv_b[0]), "mhg_bc": rep(mh_norm_g[0]),
        "relmat": np.ascontiguousarray(np.asarray(rel_bias[0], f)[:, relidx]), "amask": amask,
        "w_pm": np.ascontiguousarray(w_proj_m[0], f), "w_pa": np.ascontiguousarray(w_proj_a[0], f),
        "w_out": np.ascontiguousarray(w_out[0], f), "fg_bc": rep(final_norm_g),
        "ident": ident, "tri": tri, "ones": ones, "maskst": mst,
    }
    in_maps = []
    for core in range(8):
        b, j = core // 4, core % 4
        npad = (3 - j) * 16
        xl = np.zeros((NT * 128, D), f)
        xl[npad * 128:] = x[b, 0:(j + 1) * 2048]
        valid = (np.arange(NT) >= npad).astype(f)
        m = dict(shared)
        m["xl"] = xl
        m["cT"] = colT(c[b])
        m["padneg"] = rep(np.where(valid > 0, 0.0, NEG))
        m["tilevalid"] = rep(valid)
        m["negtile"] = rep(np.where(valid > 0, 0.0, NEG))
        in_maps.append(m)
    res = run_bass_kernel_spmd(nc, in_maps, core_ids=list(range(8)))
    out = np.empty((2, 8192, D), f)
    for core in range(8):
        b, j = core // 4, core % 4
        out[b, j * 2048:(j + 1) * 2048] = res.results[core]["y"]
    return out
```

```python
import os
import numpy as np
import ml_dtypes
from contextlib import ExitStack
import concourse.bass as bass
import concourse.mybir as mybir
from concourse.bass_utils import run_bass_kernel_spmd

F32 = mybir.dt.float32
BF16 = mybir.dt.bfloat16
ALU = mybir.AluOpType
AF = mybir.ActivationFunctionType
AX = mybir.AxisListType

D = 2048
NT = 64
NPRE = 48
NOWN = 16
EPS = 1e-6
C_MQ, C_MK, C_MV, C_MO, C_MZ, C_MI, C_MF = 0, 1024, 2048, 3072, 4096, 5120, 5124
C_AQ, C_AK, C_AV, C_AZ, C_GM, C_GA = 5128, 6152, 7176, 8200, 9224, 11272
IN_COLS = 13320
NEG = -30000.0
ENGS = ("sp", "act", "dve", "pool", "pe")
KSTOP = os.environ.get("KSTOP")


class _Stop(Exception):
    pass


class Op:
    __slots__ = ("eng", "fn", "deps", "dma", "sig", "sem", "val")

    def __init__(self, eng, fn, dma):
        self.eng, self.fn, self.dma = eng, fn, dma
        self.deps, self.sig, self.sem, self.val = [], dma, None, 0


class Sched:
    ND = 40

    def __init__(self, nc, es):
        self.nc = nc
        self.eops = {e: [] for e in ENGS}
        self.lastw, self.readers = {}, {}
        self.csem = {e: es.enter_context(nc.semaphore("cs_" + e)) for e in ENGS}
        self.dsem = [es.enter_context(nc.semaphore("ds%d" % i)) for i in range(self.ND)]
        self.dlast = [None] * self.ND
        self.duse = [0] * self.ND
        self.dn = 0

    @staticmethod
    def _is_psum(k):
        return k == "psS" or (isinstance(k, tuple) and len(k) > 0 and k[0] in ("ps", "psb", "psS"))

    def add(self, eng, fn, r=(), w=(), dma=False):
        w = tuple(w) + tuple(k for k in r if self._is_psum(k))
        r = tuple(k for k in r if not self._is_psum(k))
        op = Op(eng, fn, dma)
        deps = []
        for k in r:
            if k in self.lastw:
                deps.append(self.lastw[k])
        for k in w:
            if k in self.lastw:
                deps.append(self.lastw[k])
            deps.extend(self.readers.get(k, ()))
        if dma:
            i = self.dn % self.ND
            self.dn += 1
            if self.dlast[i] is not None:
                deps.append(self.dlast[i])
            self.duse[i] += 1
            op.sem, op.val = self.dsem[i], 16 * self.duse[i]
            self.dlast[i] = op
        seen = set()
        for d in deps:
            if d is op or id(d) in seen:
                continue
            seen.add(id(d))
            if d.eng == "pe" and eng == "pe" and not d.dma:
                continue
            d.sig = True
            op.deps.append(d)
        for k in r:
            lst = self.readers.setdefault(k, [])
            if not dma:
                lst[:] = [o_ for o_ in lst if o_.dma or o_.eng != eng]
            lst.append(op)
        for k in w:
            self.lastw[k] = op
            self.readers[k] = []
        self.eops[eng].append(op)
        return op

    def barrier(self):
        lasts = []
        for e in ENGS:
            for o in reversed(self.eops[e]):
                if not o.dma and o.fn is not None:
                    lasts.append(o)
                    break
        pend = [d for d in self.dlast if d is not None]
        for e in ENGS:
            op = Op(e, None, False)
            for d in lasts + pend:
                if d.eng == e and not d.dma:
                    continue
                d.sig = True
                op.deps.append(d)
            self.eops[e].append(op)
        self.lastw, self.readers = {}, {}

    def finish(self, keys):
        op = Op("sp", None, False)
        for d in self.dlast:
            if d is not None:
                op.deps.append(d)
        self.eops["sp"].append(op)

    def emit(self):
        nc = self.nc
        for e in ENGS:
            c = 0
            for o in self.eops[e]:
                if o.dma or o.fn is None:
                    continue
                if o.sig:
                    c += 1
                    o.sem, o.val = self.csem[e], c

        def run(ename, eng):
            known = {}
            for o in self.eops[ename]:
                for d in o.deps:
                    key = id(d.sem)
                    if known.get(key, 0) >= d.val:
                        continue
                    eng.wait_ge(d.sem, d.val)
                    known[key] = d.val
                if o.fn is None:
                    continue
                ins = o.fn(eng)
                if o.sig:
                    ins.then_inc(o.sem, 16 if o.dma else 1)

        with nc.Block() as block:
            @block.sync
            def _(e):
                run("sp", e)

            @block.scalar
            def _(e):
                run("act", e)

            @block.vector
            def _(e):
                run("dve", e)

            @block.gpsimd
            def _(e):
                run("pool", e)

            @block.tensor
            def _(e):
                run("pe", e)


def build_nc():
    nc = bass.Bass("TRN2", target_bir_lowering=False)

    def din(name, shape, dt=F32):
        return nc.dram_tensor(name, list(shape), dt, kind="ExternalInput").ap()

    xl = din("xl", [NT * 128, D])
    cT = din("cT", [128, 16])
    w_ada = din("w_ada", [D, 3 * D])
    b_ada = din("b_ada", [1, 3 * D])
    ngT = din("ngT", [128, 16])
    w_in = din("w_in", [D, IN_COLS])
    bif_bc = din("bif_bc", [128, 8])
    convw = din("convw", [128, 16, 4])
    convb = din("convb", [128, 16])
    mhg_bc = din("mhg_bc", [128, 1024])
    relmat = din("relmat", [8, 128, 640])
    amask = din("amask", [128, 640])
    w_pm = din("w_pm", [1024, D])
    w_pa = din("w_pa", [1024, D])
    w_out = din("w_out", [D, D])
    fg_bc = din("fg_bc", [128, D])
    padneg = din("padneg", [128, NT])
    tilevalid = din("tilevalid", [128, NT])
    negtile = din("negtile", [128, NT])
    ident_d = din("ident", [128, 128], BF16)
    tri_d = din("tri", [128, 128])
    ones_d = din("ones", [128, 128])
    mst_d = din("maskst", [128, 128])
    y_out = nc.dram_tensor("y", [NOWN * 128, D], F32, kind="ExternalOutput").ap()
    dbg = nc.dram_tensor("dbg", [128, 8192], F32, kind="ExternalOutput").ap() if KSTOP else None
    yT_d = nc.dram_tensor("yT_d", [16, 128, NOWN * 128], BF16, kind="Internal").ap()
    mT_d = nc.dram_tensor("mT_d", [16, 128, NOWN * 128], BF16, kind="Internal").ap()

    es = ExitStack()
    with es, nc.allow_low_precision("bf16 matmul operands, fp32 accumulation"), \
            nc.allow_non_contiguous_dma("column-sliced weight loads"):
        S = Sched(nc, es)

        def sb(name, shape, dt=F32):
            return es.enter_context(nc.sbuf_tensor("s_" + name, list(shape), dt))

        def pst(name, shape, dt=F32):
            return es.enter_context(nc.psum_tensor(name, list(shape), dt))

        ident = sb("ident", [128, 128], BF16)
        tri = sb("tri", [128, 128])
        ones = sb("ones", [128, 128])
        gs = sb("gs", [128, 16])
        shift = sb("shift", [128, 16])
        ngt_s = sb("ngt_s", [128, 16])
        gate_bc = sb("gate_bc", [128, D])
        cw = sb("cw", [128, 16, 4])
        cb_ = sb("cb_", [128, 16])
        bif = sb("bif", [128, 8])
        mhg = sb("mhg", [128, 1024])
        pneg = sb("pneg", [128, NT])
        tval = sb("tval", [128, NT])
        ntile = sb("ntile", [128, NT])
        amask_s = sb("amask_s", [128, 640])
        wif = sb("wif", [128, 16, 8], BF16)
        wifs = sb("wifs", [128, 16, 8])
        state = [[sb("st%d%d" % (h, b), [128, 257]) for b in range(2)] for h in range(4)]
        ctbf = [[sb("ct%d%d" % (h, b), [128, 257], BF16) for b in range(2)] for h in range(4)]
        hT_halo = sb("hT_halo", [128, 16, 512], BF16)
        wk_o = sb("wk_o", [128, NOWN, 4])
        wa_o = sb("wa_o", [128, NOWN, 4])
        eB_o = sb("eB_o", [128, NOWN, 4])
        ebc_o = sb("ebc_o", [128, NOWN, 4])
        sm = sb("sm", [128, 64])
        ARENA_COLS = 40000
        arena = sb("arena", [128, ARENA_COLS])

        ps = [pst("ps%d" % i, [128, 512]) for i in range(4)]
        psS = pst("psS", [128, 1024])
        psb = [pst("psb%d" % i, [128, 1024], BF16) for i in range(2)]
        rot = {"ps": 0, "psb": 0, "cast": 0, "q": 0}

        def getps():
            i = rot["ps"] % 4
            rot["ps"] += 1
            return ps[i], ("ps", i)

        def getpsb():
            i = rot["psb"] % 2
            rot["psb"] += 1
            return psb[i], ("psb", i)

        aoff = [0]

        def areset():
            aoff[0] = 0

        def af32(cols):
            o = aoff[0]
            aoff[0] += cols
            assert aoff[0] <= ARENA_COLS, aoff[0]
            return arena[:, o:o + cols]

        def abf(cols):
            n = (cols + 1) // 2
            return af32(n).bitcast(BF16)[:, 0:cols]

        def dma(q, out, in_, r, w):
            S.add(q, lambda e, o=out, i=in_: e.dma_start(out=o, in_=i), r=r, w=w, dma=True)

        def dmaq():
            rot["q"] += 1
            return "sp" if rot["q"] % 2 else "pool"

        def act(out, in_, func, r, w, bias=None, scale=None, accum=None):
            kw = {}
            if bias is not None:
                kw["bias"] = bias
            if scale is not None:
                kw["scale"] = scale
            if accum is not None:
                kw["accum_out"] = accum
            S.add("act", lambda e: e.activation(out, in_, func, **kw), r=r, w=w)

        def ts(eng, out, in0, s1, s2, op0, op1, r, w):
            if s2 is None:
                s2, op1 = (1.0, ALU.mult) if op0 == ALU.add else (0.0, ALU.add)
            S.add(eng, lambda e: e.tensor_scalar(out, in0, s1, s2, op0, op1), r=r, w=w)

        def rsqrt_col(col, key):
            S.add("act", lambda e: e.activation(col, col, AF.Sqrt), r=(key,), w=(key,))
            S.add("dve", lambda e: e.reciprocal(col, col), r=(key,), w=(key,))

        def tt(eng, out, in0, in1, op, r, w):
            S.add(eng, lambda e: e.tensor_tensor(out, in0, in1, op), r=r, w=w)

        def stt(eng, out, in0, sc, in1, op0, op1, r, w):
            S.add(eng, lambda e: e.scalar_tensor_tensor(out, in0, sc, in1, op0, op1), r=r, w=w)

        def mm(out, lhsT, rhs, start, stop, r, w):
            S.add("pe", lambda e: e.matmul(out, lhsT, rhs, start=start, stop=stop), r=r, w=w)

        def tr(out, in_, r, w):
            S.add("pe", lambda e: e.transpose(out, in_, ident[:]), r=tuple(r) + ("ident",), w=w)

        def cast(out, in_, r, w, eng=None):
            if eng is None:
                rot["cast"] += 1
                eng = ("act", "pool")[rot["cast"] % 2]
            if eng == "act":
                S.add("act", lambda e: e.copy(out, in_), r=r, w=w)
            else:
                S.add(eng, lambda e: e.tensor_copy(out, in_), r=r, w=w)

        def memset(eng, ap, v, w):
            S.add(eng, lambda e: e.memset(ap, v), w=w)

        def stop(tag, dumps=()):
            if KSTOP != tag:
                return
            S.barrier()
            off = 0
            for ap_, n_ in dumps:
                dma("sp", dbg[:, off:off + n_], ap_, r=(), w=(("dbg", off),))
                off += n_
            raise _Stop()

        def body():
            for (t_, d_, k_) in ((ident, ident_d, "ident"), (tri, tri_d, "tri"), (ones, ones_d, "ones"),
                                 (ngt_s, ngT, "ngt"), (cw, convw, "cw"),
                                 (cb_, convb, "cb"), (bif, bif_bc, "bif"), (mhg, mhg_bc, "mhg"),
                                 (pneg, padneg, "pneg"), (tval, tilevalid, "tval"),
                                 (ntile, negtile, "ntile"), (amask_s, amask, "amask")):
                dma("sp", t_[:], d_, r=(), w=(k_,))
            for k4 in range(4):
                dma("sp", wifs[:, 4 * k4:4 * k4 + 4, :],
                    w_in[k4 * 512:(k4 + 1) * 512, C_MI:C_MI + 8].rearrange("(k p) c -> p k c", p=128), r=(), w=("wifs",))
            cast(wif[:], wifs[:], r=("wifs",), w=("wif",), eng="dve")
            for h in range(4):
                for b in range(2):
                    memset("dve", state[h][b][:], 0.0, w=(("st", h, b),))

            areset()
            sc_in = af32(16)
            scv = af32(16)
            modrow = af32(3 * D)[0:1, :]
            badar = af32(3 * D)[0:1, :]
            stgA = [af32(3072) for _ in range(2)]
            dma("sp", sc_in, cT, r=(), w=("sc_in",))
            dma("sp", badar, b_ada, r=(), w=("badar",))
            act(scv, sc_in, AF.Silu, r=("sc_in",), w=("scv",))
            banks = [(ps[0], ("ps", 0), 0), (ps[1], ("ps", 1), 0), (ps[2], ("ps", 2), 0),
                     (ps[3], ("ps", 3), 0), (psS, ("psS",), 0), (psS, ("psS",), 512)]
            for half in range(2):
                for k in range(16):
                    st_ = stgA[k % 2]
                    key = ("stgA", k % 2)
                    dma(dmaq(), st_, w_ada[k * 128:(k + 1) * 128, half * 3072:(half + 1) * 3072], r=(), w=(key,))
                    for cbk in range(6):
                        t_, pk, off = banks[cbk]
                        mm(t_[0:1, off:off + 512], scv[:, k:k + 1], st_[:, cbk * 512:(cbk + 1) * 512],
                           k == 0, k == 15, r=(key, "scv"), w=(pk,))
                for cbk in range(6):
                    t_, pk, off = banks[cbk]
                    c0 = half * 3072 + cbk * 512
                    tt("dve", modrow[:, c0:c0 + 512], t_[0:1, off:off + 512], badar[:, c0:c0 + 512], ALU.add,
                       r=(pk, "badar"), w=("modrow",))
            pcol, pck = getps()
            for c in range(32):
                mm(pcol[:, c:c + 1], modrow[0:1, c * 128:(c + 1) * 128], ones[0:1, 0:1], True, True,
                   r=("modrow", "ones"), w=(pck,))
            cast(shift[:], pcol[:, 0:16], r=(pck,), w=("shift",), eng="dve")
            stt("dve", gs[:], pcol[:, 16:32], 1.0, ngt_s[:], ALU.add, ALU.mult, r=(pck, "ngt"), w=("gs",))
            for cbk in range(4):
                t_, pk = getps()
                mm(t_[:, 0:512], ones[0:1, :], modrow[0:1, 2 * D + cbk * 512:2 * D + (cbk + 1) * 512], True, True,
                   r=("modrow", "ones"), w=(pk,))
                cast(gate_bc[:, cbk * 512:(cbk + 1) * 512], t_[:, 0:512], r=(pk,), w=("gate_bc",), eng="act")
            S.barrier()
            stop("S1", [(gs[:], 16), (shift[:], 16), (gate_bc[:, 0:64], 64)])

            def frontend1(tau, xbufs, xn, xnk, i2, ident_eng="act"):
                xt = xbufs[i2 % len(xbufs)]
                xk = ("xt", i2 % len(xbufs))
                dma(dmaq(), xt, xl[tau * 128:(tau + 1) * 128, :], r=(), w=(xk,))
                memset("dve", sm[:, 0:1], 0.0, w=("ss",))
                act(xn, xt, AF.Square, r=(xk, "ss"), w=(xnk, "ss"), accum=sm[:, 0:1])
                ts("dve", sm[:, 1:2], sm[:, 0:1], 1.0 / D, EPS, ALU.mult, ALU.add, r=("ss",), w=("rstd",))
                rsqrt_col(sm[:, 1:2], "rstd")
                if ident_eng == "act":
                    act(xn, xt, AF.Identity, r=(xk, "rstd"), w=(xnk,), scale=sm[:, 1:2])
                else:
                    ts("dve", xn, xt, sm[:, 1:2], None, ALU.mult, None, r=(xk, "rstd"), w=(xnk,))

            def frontend2(dstf, dkey, xn, xnk, halves=(0, 1), act_halves=(0,)):
                for hh in halves:
                    pt, ptk = psb[hh], ("psb", hh)
                    for kk in range(8):
                        k = hh * 8 + kk
                        tr(pt[:, kk * 128:(kk + 1) * 128], xn[:, k * 128:(k + 1) * 128], r=(xnk,), w=(ptk,))
                    for kk in range(8):
                        k = hh * 8 + kk
                        if hh in act_halves:
                            act(dstf(k), pt[:, kk * 128:(kk + 1) * 128], AF.Identity, r=(ptk, "gs", "shift"),
                                w=(dkey[0],), bias=shift[:, k:k + 1], scale=gs[:, k:k + 1])
                        else:
                            ts("dve", dstf(k), pt[:, kk * 128:(kk + 1) * 128], gs[:, k:k + 1], shift[:, k:k + 1],
                               ALU.mult, ALU.add, r=(ptk, "gs", "shift"), w=(dkey[1],))

            def frontend(tau, dstf, dkey, xbufs, xn, i2):
                frontend1(tau, xbufs, xn, "xn", i2)
                frontend2(dstf, dkey, xn, "xn")

            def load_w(dst, src_rows, nk, stgs, skey, dkey, cast_eng=None, q=None, rot_engs=None):
                for k in range(nk):
                    st_ = stgs[k % len(stgs)]
                    key = (skey, k % len(stgs))
                    dma(q or dmaq(), st_, src_rows(k), r=(), w=(key,))
                    cast(dst(k), st_, r=(key,), w=(dkey,), eng=(rot_engs[k % len(rot_engs)] if rot_engs else cast_eng))

            def gates(tau, hsrc, hkey, own_i):
                pg, pgk = getps()
                for k in range(16):
                    mm(pg[:, 0:8], hsrc(k), wif[:, k, :], k == 0, k == 15, r=tuple(hkey) + ("wif",), w=(pgk,))
                gl = sm[:, 8:16]
                tt("dve", gl, pg[:, 0:8], bif[:], ALU.add, r=(pgk, "bif"), w=("gl",))
                ts("dve", sm[:, 16:20], gl[:, 0:4], pneg[:, tau:tau + 1], None, ALU.add, None, r=("gl", "pneg"), w=("ig",))
                act(sm[:, 20:24], gl[:, 4:8], AF.Exp, r=("gl",), w=("e1",), scale=-1.0)
                ts("dve", sm[:, 20:24], sm[:, 20:24], 1.0, None, ALU.add, None, r=("e1",), w=("e1",))
                act(sm[:, 24:28], sm[:, 20:24], AF.Ln, r=("e1",), w=("lf",))
                ts("dve", sm[:, 24:28], sm[:, 24:28], -1.0, None, ALU.mult, None, r=("lf",), w=("lf",))
                pc, pckk = getps()
                mm(pc[:, 0:4], tri[:], sm[:, 24:28], True, True, r=("lf", "tri"), w=(pckk,))
                mm(pc[:, 4:8], ones[:], sm[:, 24:28], True, True, r=("lf", "ones"), w=(pckk,))
                tt("dve", sm[:, 28:32], sm[:, 16:20], pc[:, 0:4], ALU.subtract, r=("ig", pckk), w=("t1",))
                tt("dve", sm[:, 32:36], sm[:, 28:32], pc[:, 4:8], ALU.add, r=("t1", pckk), w=("t2",))
                if own_i is None:
                    wk, eB = sm[:, 36:40], sm[:, 40:44]
                    wkk, eBk = "wk", "eB"
                else:
                    wk, eB = wk_o[:, own_i, :], eB_o[:, own_i, :]
                    wkk = eBk = ("gown", own_i)
                act(wk, sm[:, 32:36], AF.Exp, r=("t2",), w=(wkk,))
                ts("dve", wk, wk, 0.0625, None, ALU.mult, None, r=(wkk,), w=(wkk,))
                act(eB, pc[:, 4:8], AF.Exp, r=(pckk,), w=(eBk,))
                if own_i is not None:
                    act(wa_o[:, own_i, :], sm[:, 28:32], AF.Exp, r=("t1",), w=(wkk,))
                    ts("dve", wa_o[:, own_i, :], wa_o[:, own_i, :], 0.0625, None, ALU.mult, None, r=(wkk,), w=(wkk,))
                    act(ebc_o[:, own_i, :], pc[:, 0:4], AF.Exp, r=(pckk,), w=(wkk,))
                return wk, eB, wkk, eBk

            def conv_silu(pre, prek, cblk, accb, acck, out, outk):
                ts("dve", accb, pre[:, 3:515], cw[:, cblk, 3:4], cb_[:, cblk:cblk + 1], ALU.mult, ALU.add,
                   r=(prek, "cw", "cb"), w=(acck,))
                for tap in range(3):
                    stt("dve", accb, pre[:, tap:tap + 512], cw[:, cblk, tap:tap + 1], accb, ALU.mult, ALU.add,
                        r=(prek, acck, "cw"), w=(acck,))
                act(out, accb, AF.Silu, r=(acck,), w=(outk,))

            def state_update(h, kT_blk, kTk, vaug, vk, wk_col, wkk, eB_col, eBk, kpp, kppk, refresh_bf):
                pt, ptk = getpsb()
                for blk in range(2):
                    tr(pt[:, blk * 128:(blk + 1) * 128], kT_blk(blk), r=(kTk[blk],), w=(ptk,))
                ts("dve", kpp, pt[:, 0:256], wk_col, None, ALU.mult, None, r=(ptk, wkk), w=(kppk,))
                for blk in range(2):
                    p_, pk = getps()
                    mm(p_[:, 0:257], kpp[:, blk * 128:(blk + 1) * 128], vaug, True, True, r=(kppk, vk), w=(pk,))
                    stt("dve", state[h][blk][:], state[h][blk][:], eB_col, p_[:, 0:257], ALU.mult, ALU.add,
                        r=(pk, eBk, ("st", h, blk)), w=(("st", h, blk),))
                    if refresh_bf:
                        cast(ctbf[h][blk][:], state[h][blk][:], r=(("st", h, blk),), w=(("ct", h, blk),), eng="act")

            areset()
            WA = abf(16 * 2048).rearrange("p (k c) -> p k c", k=16)
            hTgA = [abf(16 * 512).rearrange("p (k c) -> p k c", k=16) for _ in range(2)]
            xn = abf(D)
            xns = [xn, abf(D)]
            kT = abf(8 * 512).rearrange("p (b c) -> p b c", b=8)
            vaugs = [[abf(258)[:, 0:257] for _ in range(4)] for _ in range(4)]
            kpps = [abf(256) for _ in range(2)]
            stgs = [af32(2048) for _ in range(2)]
            xbufs = stgs
            kpre = af32(8 * 515).rearrange("p (b c) -> p b c", b=8)
            accb = af32(512)
            gtmp = af32(4 * 48).rearrange("p (i c) -> p i c", i=4)
            load_w(lambda k: WA[:, k, :], lambda k: w_in[k * 128:(k + 1) * 128, C_MK:C_MK + 2048], 16, stgs, "xt", "WA", rot_engs=("act", "dve"))
            memset("pool", kpre[:, :, 0:3], 0.0, w=tuple(("kpre", c) for c in range(8)))
            for i in range(4):
                for h in range(4):
                    memset("pool", vaugs[i][h][:, 256:257], 1.0, w=(("vaug", i, h),))
            NG = NPRE // 4
            fe_cnt = [0]

            def a_hT(g):
                return hT_halo if g == NG - 1 else hTgA[g % 2]

            def a_keys(g, i):
                return (("hTg", g % 2, i, 0), ("hTg", g % 2, i, 1))

            def a_gates1(g, i):
                tau = 4 * g + i
                hT = a_hT(g)
                gt = gtmp[:, i, :]
                pg, pgk = ps[2 + i % 2], ("ps", 2 + i % 2)
                for k in range(16):
                    mm(pg[:, 0:8], hT[:, k, i * 128:(i + 1) * 128], wif[:, k, :], k == 0, k == 15,
                       r=a_keys(g, i) + ("wif",), w=(pgk,))
                tt("dve", gt[:, 0:8], pg[:, 0:8], bif[:], ALU.add, r=(pgk, "bif"), w=(("g_gl", i),))
                ts("dve", gt[:, 8:12], gt[:, 0:4], pneg[:, tau:tau + 1], None, ALU.add, None, r=(("g_gl", i), "pneg"), w=(("g_ig", i),))
                act(gt[:, 12:16], gt[:, 4:8], AF.Exp, r=(("g_gl", i),), w=(("g_lf", i),), scale=-1.0)
                ts("dve", gt[:, 12:16], gt[:, 12:16], 1.0, None, ALU.add, None, r=(("g_lf", i),), w=(("g_lf", i),))
                act(gt[:, 12:16], gt[:, 12:16], AF.Ln, r=(("g_lf", i),), w=(("g_lf", i),))
                ts("dve", gt[:, 12:16], gt[:, 12:16], -1.0, None, ALU.mult, None, r=(("g_lf", i),), w=(("g_lf", i),))

            def a_gates2(g, i):
                gt = gtmp[:, i, :]
                pc, pck_ = psS[:, (i % 2) * 512:(i % 2) * 512 + 8], ("psS", i % 2)
                mm(pc[:, 0:4], tri[:], gt[:, 12:16], True, True, r=(("g_lf", i), "tri"), w=(pck_,))
                mm(pc[:, 4:8], ones[:], gt[:, 12:16], True, True, r=(("g_lf", i), "ones"), w=(pck_,))
                tt("dve", gt[:, 16:20], gt[:, 8:12], pc[:, 0:4], ALU.subtract, r=(("g_ig", i), pck_), w=(("g_t", i),))
                tt("dve", gt[:, 16:20], gt[:, 16:20], pc[:, 4:8], ALU.add, r=(("g_t", i), pck_), w=(("g_t", i),))
                act(gt[:, 20:24], gt[:, 16:20], AF.Exp, r=(("g_t", i),), w=(("g_wk", i),))
                ts("dve", gt[:, 20:24], gt[:, 20:24], 0.0625, None, ALU.mult, None, r=(("g_wk", i),), w=(("g_wk", i),))
                act(gt[:, 24:28], pc[:, 4:8], AF.Exp, r=(pck_,), w=(("g_eB", i),))

            def a_fe1(g, i):
                frontend1(4 * g + i, xbufs, xns[i % 2], ("xnA", i % 2), fe_cnt[0])
                fe_cnt[0] += 1

            def a_fe2(g, i, halves):
                hT = a_hT(g)
                frontend2(lambda k: hT[:, k, i * 128:(i + 1) * 128], a_keys(g, i), xns[i % 2], ("xnA", i % 2), halves)

            def a_U1(g, i, h):
                gt = gtmp[:, i, :]
                pt = ps[h % 2][:].bitcast(BF16)
                ptk = ("ps", h % 2)
                for blk in range(2):
                    tr(pt[:, blk * 128:(blk + 1) * 128], kT[:, 2 * h + blk, i * 128:(i + 1) * 128],
                       r=(("kT", 2 * h + blk),), w=(ptk,))
                kp, kpk = kpps[h % 2], ("kpp", h % 2)
                ts("dve", kp, pt[:, 0:256], gt[:, 20 + h:21 + h], None, ALU.mult, None, r=(ptk, ("g_wk", i)), w=(kpk,))

            def a_U2(g, i, h):
                gt = gtmp[:, i, :]
                kp, kpk = kpps[h % 2], ("kpp", h % 2)
                for blk in range(2):
                    pk = ("psS", blk)
                    p_ = psS[:, blk * 512:blk * 512 + 257]
                    mm(p_, kp[:, blk * 128:(blk + 1) * 128], vaugs[i][h], True, True, r=(kpk, ("vaug", i, h)), w=(pk,))
                    stt("dve", state[h][blk][:], state[h][blk][:], gt[:, 24 + h:25 + h], p_, ALU.mult, ALU.add,
                        r=(pk, ("g_eB", i), ("st", h, blk)), w=(("st", h, blk),))

            def a_stage3(g):
                nxt = g + 1 < NG
                if nxt:
                    a_fe1(g + 1, 0)
                for i in range(4):
                    if nxt and i + 1 < 4:
                        a_fe1(g + 1, i + 1)
                    a_U1(g, i, 0)
                    if nxt:
                        a_fe2(g + 1, i, (0,))
                    a_U1(g, i, 1)
                    a_U2(g, i, 0)
                    if nxt:
                        a_fe2(g + 1, i, (1,))
                    a_U1(g, i, 2)
                    a_U2(g, i, 1)
                    a_U1(g, i, 3)
                    a_U2(g, i, 2)
                    a_U2(g, i, 3)

            for i in range(4):
                a_fe1(0, i)
                a_fe2(0, i, (0, 1))
            for g in range(NG):
                hT = a_hT(g)
                hgk = tuple(k_ for i_ in range(4) for k_ in a_keys(g, i_))
                for i in range(4):
                    a_gates1(g, i)
                for cbk in range(8):
                    p_, pk = ps[cbk % 2], ("ps", cbk % 2)
                    for k in range(16):
                        mm(p_[:, 0:512], WA[:, k, cbk * 128:(cbk + 1) * 128], hT[:, k, :], k == 0, k == 15,
                           r=("WA",) + hgk, w=(pk,))
                    cast(kpre[:, cbk, 3:515], p_[:, 0:512], r=(pk,), w=(("kpre", cbk),), eng="act")
                    conv_silu(kpre[:, cbk, :], ("kpre", cbk), 8 + cbk, accb, "accb", kT[:, cbk, :], ("kT", cbk))
                    ts("dve", kpre[:, cbk, 0:3], kpre[:, cbk, 512:515], tval[:, 4 * g + 3:4 * g + 4], None, ALU.mult, None,
                       r=(("kpre", cbk), "tval"), w=(("kpre", cbk),))
                for i in range(4):
                    for half in range(2):
                        p_, pk = ps[2 + half], ("ps", 2 + half)
                        for k in range(16):
                            mm(p_[:, 0:512], hT[:, k, i * 128:(i + 1) * 128], WA[:, k, 1024 + half * 512:1024 + (half + 1) * 512],
                               k == 0, k == 15, r=("WA",) + a_keys(g, i), w=(pk,))
                        for hh in range(2):
                            h = 2 * half + hh
                            cast(vaugs[i][h][:, 0:256], p_[:, hh * 256:(hh + 1) * 256], r=(pk,), w=(("vaug", i, h),),
                                 eng="act")
                for i in range(4):
                    a_gates2(g, i)
                a_stage3(g)
            S.barrier()
            stop("A", [(state[0][0][:], 257), (state[3][1][:], 257), (sm[:], 64)])

            areset()
            hT_own = abf(16 * 2048).rearrange("p (k c) -> p k c", k=16)
            mark_b = aoff[0]
            xn = abf(D)
            xbufs = [af32(D) for _ in range(2)]
            xns0 = [xn, abf(D)]
            frontend1(NPRE, xbufs, xns0[0], ("xnB", 0), 0, ident_eng="dve")
            for i in range(NOWN):
                if i + 1 < NOWN:
                    frontend1(NPRE + i + 1, xbufs, xns0[(i + 1) % 2], ("xnB", (i + 1) % 2), i + 1, ident_eng="dve")
                frontend2(lambda k, i=i: hT_own[:, k, i * 128:(i + 1) * 128], (("hTo", i, 0), ("hTo", i, 1)),
                          xns0[i % 2], ("xnB", i % 2), act_halves=())
                gates(NPRE + i, lambda k, i=i: hT_own[:, k, i * 128:(i + 1) * 128], (("hTo", i, 0), ("hTo", i, 1)), i)
            for h in range(4):
                for b in range(2):
                    cast(ctbf[h][b][:], state[h][b][:], r=(("st", h, b),), w=(("ct", h, b),), eng="act")
            S.barrier()
            stop("B0", [(wk_o[:].rearrange("p a b -> p (a b)"), 64), (wa_o[:].rearrange("p a b -> p (a b)"), 64), (eB_o[:].rearrange("p a b -> p (a b)"), 64), (ebc_o[:].rearrange("p a b -> p (a b)"), 64), (state[0][0][:], 257)])

            aoff[0] = mark_b
            WG = abf(16 * 1280).rearrange("p (k c) -> p k c", k=16)
            qkT = [abf(4 * 512).rearrange("p (b c) -> p b c", b=4) for _ in range(2)]
            vaug1 = [abf(258)[:, 0:257] for _ in range(2)]
            kpps = [abf(256) for _ in range(2)]
            STb = [abf(128) for _ in range(2)]
            ybf = [abf(256) for _ in range(2)]
            yTt = [abf(256).rearrange("p (b c) -> p b c", b=2) for _ in range(2)]
            stg1 = [af32(1280) for _ in range(4)]
            qkpre = af32(4 * 515).rearrange("p (b c) -> p b c", b=4)
            accb = af32(512)
            sgo = af32(256)
            slz = af32(256)
            Gt = [af32(256) for _ in range(2)]
            junk = af32(256)
            w5 = w_in[:, 0:5120].rearrange("r (s c) -> r s c", c=1024)
            for vb in range(2):
                memset("pool", vaug1[vb][:, 256:257], 1.0, w=(("vaug1", vb),))
            ts("dve", mhg[:], mhg[:], 0.5, None, ALU.mult, None, r=("mhg",), w=("mhg",))
            PS0, PS1, PS2, PS3 = (ps[0], ("ps", 0)), (ps[1], ("ps", 1)), (ps[2], ("ps", 2)), (ps[3], ("ps", 3))
            for h in range(4):
                load_w(lambda k: WG[:, k, :].rearrange("p (s c) -> p s c", c=256),
                       lambda k, h=h: w5[k * 128:(k + 1) * 128, :, h * 256:(h + 1) * 256],
                       16, [s_.rearrange("p (s c) -> p s c", c=256) for s_ in stg1], "stg1", "WG", rot_engs=("act", "dve"))
                for blk in range(4):
                    p_, pk = (PS0, PS1)[blk % 2]
                    for k in range(16):
                        mm(p_[:, 0:3], WG[:, k, blk * 128:(blk + 1) * 128], hT_halo[:, k, 509:512], k == 0, k == 15,
                           r=("WG",), w=(pk,))
                    ts("dve", qkpre[:, blk, 0:3], p_[:, 0:3], tval[:, NPRE - 1:NPRE], None, ALU.mult, None,
                       r=(pk, "tval"), w=(("qkpre", blk),))

                def b1_proj(g, h=h):
                    qk = qkT[g % 2]
                    for blk in range(4):
                        p_, pk = (PS0, PS1)[blk % 2]
                        for k in range(16):
                            mm(p_[:, 0:512], WG[:, k, blk * 128:(blk + 1) * 128], hT_own[:, k, g * 512:(g + 1) * 512],
                               k == 0, k == 15, r=("WG",), w=(pk,))
                        cast(qkpre[:, blk, 3:515], p_[:, 0:512], r=(pk,), w=(("qkpre", blk),), eng="act")
                        cidx = (2 * h + blk) if blk < 2 else (8 + 2 * h + blk - 2)
                        conv_silu(qkpre[:, blk, :], ("qkpre", blk), cidx, accb, "accb", qk[:, blk, :], ("qkT", g % 2, blk))
                        cast(qkpre[:, blk, 0:3], qkpre[:, blk, 512:515], r=(("qkpre", blk),), w=(("qkpre", blk),), eng="dve")

                def b1_P(t, h=h):
                    vb = t % 2
                    pA, pAk = PS0
                    for k in range(16):
                        mm(pA[:, 0:512], hT_own[:, k, t * 128:(t + 1) * 128], WG[:, k, 512:1024], k == 0, k == 15,
                           r=("WG",), w=(pAk,))
                    pB, pBk = PS1
                    for k in range(16):
                        mm(pB[:, 0:256], hT_own[:, k, t * 128:(t + 1) * 128], WG[:, k, 1024:1280], k == 0, k == 15,
                           r=("WG",), w=(pBk,))
                    cast(vaug1[vb][:, 0:256], pA[:, 0:256], r=(pAk,), w=(("vaug1", vb),), eng="act")
                    act(sgo, pA[:, 256:512], AF.Tanh, r=(pAk,), w=("sgo",), scale=0.5)
                    act(slz, pB[:, 0:256], AF.Silu, r=(pBk,), w=("slz",))
                    stt("dve", Gt[vb], sgo, 1.0, slz, ALU.add, ALU.mult, r=("sgo", "slz"), w=(("Gt", vb),))
                    tt("dve", Gt[vb], Gt[vb], mhg[:, h * 256:(h + 1) * 256], ALU.mult, r=(("Gt", vb), "mhg"), w=(("Gt", vb),))

                def b1_S(t, h=h):
                    vb = t % 2
                    gp = (t // 4) % 2
                    tsl = slice((t % 4) * 128, (t % 4 + 1) * 128)
                    pS, pSk = PS2
                    for blk in range(2):
                        mm(pS[:, 0:128], qkT[gp][:, 2 + blk, tsl], qkT[gp][:, blk, tsl], blk == 0, blk == 1,
                           r=(("qkT", gp, blk), ("qkT", gp, 2 + blk)), w=(pSk,))
                    stt("dve", STb[vb], pS[:, 0:128], wa_o[:, t, h:h + 1], tri[:], ALU.mult, ALU.mult,
                        r=(pSk, ("gown", t), "tri"), w=(("STb", vb),))

                def b1_N(t, h=h):
                    vb = t % 2
                    gp = (t // 4) % 2
                    tsl = slice((t % 4) * 128, (t % 4 + 1) * 128)
                    pN, pNk = PS3
                    mm(pN[:, 0:257], STb[vb], vaug1[vb], True, False, r=(("STb", vb), ("vaug1", vb)), w=(pNk,))
                    for blk in range(2):
                        mm(pN[:, 0:257], qkT[gp][:, blk, tsl], ctbf[h][blk][:], False, blk == 1,
                           r=(("qkT", gp, blk), ("ct", h, blk)), w=(pNk,))

                def b1_Utr(t, h=h):
                    vb = t % 2
                    gp = (t // 4) % 2
                    tsl = slice((t % 4) * 128, (t % 4 + 1) * 128)
                    pt, ptk = psb[0], ("psb", 0)
                    for blk in range(2):
                        tr(pt[:, blk * 128:(blk + 1) * 128], qkT[gp][:, 2 + blk, tsl], r=(("qkT", gp, 2 + blk),), w=(ptk,))
                    ts("dve", kpps[vb], pt[:, 0:256], wk_o[:, t, h:h + 1], None, ALU.mult, None,
                       r=(ptk, ("gown", t)), w=(("kpp", vb),))

                def b1_Umm(t, h=h):
                    vb = t % 2
                    for blk in range(2):
                        pk = ("psS", blk)
                        p_ = psS[:, blk * 512:blk * 512 + 257]
                        mm(p_, kpps[vb][:, blk * 128:(blk + 1) * 128], vaug1[vb], True, True,
                           r=(("kpp", vb), ("vaug1", vb)), w=(pk,))
                        stt("dve", state[h][blk][:], state[h][blk][:], eB_o[:, t, h:h + 1], p_, ALU.mult, ALU.add,
                            r=(pk, ("gown", t), ("st", h, blk)), w=(("st", h, blk),))
                        cast(ctbf[h][blk][:], state[h][blk][:], r=(("st", h, blk),), w=(("ct", h, blk),), eng="act")

                def b1_Ychain(t, h=h):
                    vb = t % 2
                    pN, pNk = PS3
                    gk = ("gown", t)
                    tt("dve", sm[:, 44:45], pN[:, 256:257], ebc_o[:, t, h:h + 1], ALU.mult, r=(pNk, gk), w=("d1",))
                    ts("dve", sm[:, 54:55], sm[:, 44:45], -1.0, None, ALU.mult, None, r=("d1",), w=("d1n",))
                    tt("dve", sm[:, 44:45], sm[:, 44:45], sm[:, 54:55], ALU.max, r=("d1", "d1n"), w=("d1",))
                    ts("dve", sm[:, 44:45], sm[:, 44:45], 1.0, 1.0, ALU.max, ALU.mult, r=("d1",), w=("d1",))
                    S.add("dve", lambda e: e.reciprocal(sm[:, 44:45], sm[:, 44:45]), r=("d1",), w=("d1",))
                    tt("dve", sm[:, 45:46], ebc_o[:, t, h:h + 1], sm[:, 44:45], ALU.mult, r=("d1", gk), w=("rr",))
                    memset("dve", sm[:, 46:47], 0.0, w=("ss2",))
                    act(junk, pN[:, 0:256], AF.Square, r=(pNk, "rr", "ss2"), w=("junk", "ss2"), scale=sm[:, 45:46],
                        accum=sm[:, 46:47])
                    ts("dve", sm[:, 47:48], sm[:, 46:47], 1.0 / 256, EPS, ALU.mult, ALU.add, r=("ss2",), w=("r2",))
                    rsqrt_col(sm[:, 47:48], "r2")
                    tt("dve", sm[:, 47:48], sm[:, 47:48], sm[:, 45:46], ALU.mult, r=("r2", "rr"), w=("r2",))
                    stt("dve", ybf[vb], pN[:, 0:256], sm[:, 47:48], Gt[vb], ALU.mult, ALU.mult,
                        r=(pNk, "r2", ("Gt", vb)), w=(("ybf", vb),))

                def b1_Ytr(t, h=h):
                    vb = t % 2
                    pt, ptk = psb[1], ("psb", 1)
                    for blk in range(2):
                        tr(pt[:, blk * 128:(blk + 1) * 128], ybf[vb][:, blk * 128:(blk + 1) * 128], r=(("ybf", vb),), w=(ptk,))
                    cast(yTt[vb].rearrange("p b c -> p (b c)"), pt[:, 0:256], r=(ptk,), w=(("yTt", vb),), eng="act")
                    dma(dmaq(), yT_d[2 * h:2 * h + 2, :, t * 128:(t + 1) * 128].rearrange("b p t -> p b t"), yTt[vb],
                        r=(("yTt", vb),), w=(("yTd", 2 * h, t),))

                b1_proj(0)
                b1_P(0)
                b1_S(0)
                for t in range(NOWN):
                    if t + 1 < NOWN:
                        if (t + 1) % 4 == 0:
                            b1_proj((t + 1) // 4)
                        b1_P(t + 1)
                        b1_S(t + 1)
                    b1_N(t)
                    b1_Utr(t)
                    if t > 0:
                        b1_Ytr(t - 1)
                    b1_Umm(t)
                    b1_Ychain(t)
                b1_Ytr(NOWN - 1)
            S.barrier()
            stop("B1", [(state[0][0][:], 257)])

            aoff[0] = mark_b
            WG2s = [abf(16 * 512).rearrange("p (k c) -> p k c", k=16) for _ in range(2)]
            akT = abf(2560)
            aqT = abf(2048)
            Vt = abf(20 * 128).rearrange("p (t c) -> p t c", t=20)
            slzA = abf(16 * 128).rearrange("p (t c) -> p t c", t=16)
            Pb = [abf(640) for _ in range(2)]
            PT = [abf(640) for _ in range(2)]
            yab = [abf(128) for _ in range(2)]
            yaT = [abf(128) for _ in range(2)]
            stg2 = [af32(512) for _ in range(4)]
            rb = af32(640)
            bm = af32(640)
            sbuf_s = [af32(640) for _ in range(2)]
            wA = w_in[:, C_AQ:C_AQ + 4096].rearrange("r (s c) -> r s c", c=1024)
            def b2_load(h, cast_eng=None, q=None):
                W_ = WG2s[h % 2]
                load_w(lambda k: W_[:, k, :].rearrange("p (s c) -> p s c", c=128),
                       lambda k: wA[k * 128:(k + 1) * 128, :, h * 128:(h + 1) * 128],
                       16, [s_.rearrange("p (s c) -> p s c", c=128) for s_ in stg2], "stg2", ("WG2", h % 2),
                       cast_eng=cast_eng, q=q)

            b2_load(0)
            for h in range(8):
                WG2 = WG2s[h % 2]
                WK = ("WG2", h % 2)
                dma("sp", rb, relmat[h], r=(), w=("rb",))
                tt("pool", bm, rb, amask_s[:], ALU.add, r=("rb", "amask"), w=("bm",))
                for g5 in range(5):
                    src = hT_halo if g5 == 0 else hT_own[:, :, (g5 - 1) * 512:g5 * 512]
                    srk = ()
                    p_, pk = getps()
                    for k in range(16):
                        mm(p_[:, 0:512], WG2[:, k, 128:256], src[:, k, :], k == 0, k == 15, r=(WK,) + srk, w=(pk,))
                    cast(akT[:, g5 * 512:(g5 + 1) * 512], p_[:, 0:512], r=(pk,), w=(("akT", g5),), eng="act")
                    if g5 > 0:
                        p_, pk = getps()
                        for k in range(16):
                            mm(p_[:, 0:512], WG2[:, k, 0:128], src[:, k, :], k == 0, k == 15, r=(WK,) + srk, w=(pk,))
                        act(aqT[:, (g5 - 1) * 512:g5 * 512], p_[:, 0:512], AF.Copy, r=(pk,), w=(("aqT", g5 - 1),),
                            scale=float(128 ** -0.5))
                    for i in range(4):
                        ttile = g5 * 4 + i
                        p_, pk = getps()
                        for k in range(16):
                            mm(p_[:, 0:256], src[:, k, i * 128:(i + 1) * 128], WG2[:, k, 256:512], k == 0, k == 15,
                               r=(WK,) + srk, w=(pk,))
                        cast(Vt[:, ttile, :], p_[:, 0:128], r=(pk,), w=(("Vt", ttile),), eng="act")
                        if g5 > 0:
                            act(slzA[:, ttile - 4, :], p_[:, 128:256], AF.Silu, r=(pk,), w=(("slzA", ttile - 4),))
                if h + 1 < 8:
                    b2_load(h + 1, cast_eng="pool", q="sp")

                RS = [sm[:, 56:57], sm[:, 57:58], sm[:, 60:61]]

                def att_s1(t, h=h):
                    vb = t % 2
                    gq = ("aqT", t // 4)
                    kk0 = tuple(("akT", x) for x in sorted({t // 4, (t + 3) // 4, (t + 4) // 4}))
                    mm(psS[:, 0:512], aqT[:, t * 128:(t + 1) * 128], akT[:, t * 128:t * 128 + 512], True, True,
                       r=(gq,) + kk0, w=("psS",))
                    mm(psS[:, 512:640], aqT[:, t * 128:(t + 1) * 128], akT[:, t * 128 + 512:t * 128 + 640], True, True,
                       r=(gq,) + kk0, w=("psS",))
                    sbt = sbuf_s[vb]
                    sk = ("sbt", vb)
                    tt("dve", sbt, psS[:, 0:640], bm, ALU.add, r=("psS", "bm"), w=(sk,))
                    for kb in range(max(0, 4 - t)):
                        lt = NPRE - 4 + t + kb
                        ts("dve", sbt[:, kb * 128:(kb + 1) * 128], sbt[:, kb * 128:(kb + 1) * 128], ntile[:, lt:lt + 1], None,
                           ALU.add, None, r=(sk, "ntile"), w=(sk,))
                    S.add("dve", lambda e, sbt=sbt: e.reduce_max(sm[:, 48:49], sbt, AX.X), r=(sk,), w=("mx",))
                    ts("dve", sm[:, 48:49], sm[:, 48:49], -1.0, None, ALU.mult, None, r=("mx",), w=("mx",))
                    rsc = RS[t % 3]
                    memset("dve", rsc, 0.0, w=(("rsum", t % 3),))
                    act(Pb[vb], sbt, AF.Exp, r=(sk, "mx", ("rsum", t % 3)), w=(("Pb", vb), ("rsum", t % 3)), bias=sm[:, 48:49],
                        accum=rsc)

                def att_s2(t, h=h):
                    vb = t % 2
                    pt, ptk = psb[0], ("psb", 0)
                    for kb in range(5):
                        tr(pt[:, kb * 128:(kb + 1) * 128], Pb[vb][:, kb * 128:(kb + 1) * 128], r=(("Pb", vb),), w=(ptk,))
                    cast(PT[vb], pt[:, 0:640], r=(ptk,), w=(("PT", vb),), eng="act")

                def att_s3(t, h=h):
                    vb = t % 2
                    rsc = RS[t % 3]
                    rrc = sm[:, 58 + vb:59 + vb]
                    pO, pOk = getps()
                    for kb in range(5):
                        mm(pO[:, 0:128], PT[vb][:, kb * 128:(kb + 1) * 128], Vt[:, t + kb, :], kb == 0, kb == 4,
                           r=(("PT", vb), ("Vt", t + kb)), w=(pOk,))
                    S.add("dve", lambda e: e.reciprocal(rrc, rsc), r=(("rsum", t % 3),), w=(("rrs", vb),))
                    stt("dve", yab[vb], pO[:, 0:128], rrc, slzA[:, t, :], ALU.mult, ALU.mult,
                        r=(pOk, ("rrs", vb), ("slzA", t)), w=(("yab", vb),))
                    pt2, pt2k = psb[1], ("psb", 1)
                    tr(pt2[:, 0:128], yab[vb], r=(("yab", vb),), w=(pt2k,))
                    cast(yaT[vb], pt2[:, 0:128], r=(pt2k,), w=(("yaT", vb),), eng="act")
                    dma("act", yT_d[8 + h, :, t * 128:(t + 1) * 128], yaT[vb], r=(("yaT", vb),), w=(("yTd", 8 + h, t),))

                att_s1(0)
                att_s1(1)
                att_s2(0)
                for t in range(NOWN):
                    if t + 2 < NOWN:
                        att_s1(t + 2)
                    if t + 1 < NOWN:
                        att_s2(t + 1)
                    att_s3(t)
            S.barrier()
            stop("B2", [(sm[:], 64)])

            aoff[0] = mark_b
            yT_res = abf(16 * 2048).rearrange("p (b c) -> p b c", b=16)
            hh_flat = hT_halo[:].rearrange("p k c -> p (k c)")
            wgm = [abf(16 * 128).rearrange("p (k c) -> p k c", k=16), hh_flat[:, 0:2048].rearrange("p (k c) -> p k c", k=16)]
            wga = [abf(16 * 128).rearrange("p (k c) -> p k c", k=16), hh_flat[:, 2048:4096].rearrange("p (k c) -> p k c", k=16)]
            wpm = [abf(8 * 128).rearrange("p (k c) -> p k c", k=8), hh_flat[:, 4096:5120].rearrange("p (k c) -> p k c", k=8)]
            wpa = [abf(8 * 128).rearrange("p (k c) -> p k c", k=8), hh_flat[:, 5120:6144].rearrange("p (k c) -> p k c", k=8)]
            mTb = [abf(512) for _ in range(2)]
            stg3 = [af32(8 * 128).rearrange("p (k c) -> p k c", k=8) for _ in range(2)]
            stg3.append(mhg[:].rearrange("p (k c) -> p k c", k=8))
            stg3.append(hh_flat[:, 6144:8192].bitcast(F32).rearrange("p (k c) -> p k c", k=8))
            sgm = af32(512)
            sga = af32(512)
            t1b = af32(512)
            for b in range(16):
                dma(dmaq(), yT_res[:, b, :], yT_d[b], r=(), w=("yT_res",))
            si3 = [0]

            def b3_load(c):
                cbuf = c % 2
                for (dst, src, nk, nm) in ((wgm, w_in[:, C_GM + c * 128:C_GM + (c + 1) * 128], 16, "wgm"),
                                           (wga, w_in[:, C_GA + c * 128:C_GA + (c + 1) * 128], 16, "wga"),
                                           (wpm, w_pm[:, c * 128:(c + 1) * 128], 8, "wpm"),
                                           (wpa, w_pa[:, c * 128:(c + 1) * 128], 8, "wpa")):
                    for k8 in range(nk // 8):
                        st_ = stg3[si3[0] % 4]
                        sk = ("stg3", si3[0] % 4)
                        si3[0] += 1
                        for k4 in range(2):
                            r0 = (k8 * 8 + k4 * 4) * 128
                            dma("sp", st_[:, 4 * k4:4 * k4 + 4, :],
                                src[r0:r0 + 512, :].rearrange("(k p) c -> p k c", p=128), r=(), w=(sk,))
                        cast(dst[cbuf][:, k8 * 8:(k8 + 1) * 8, :], st_[:], r=(sk,), w=((nm, cbuf),), eng="pool")

            b3_load(0)
            for c in range(16):
                cbuf = c % 2
                if c + 1 < 16:
                    b3_load(c + 1)
                for g in range(4):
                    gsl = slice(g * 512, (g + 1) * 512)
                    p1, p1k = getps()
                    for k in range(16):
                        mm(p1[:, 0:512], wgm[cbuf][:, k, :], hT_own[:, k, gsl], k == 0, k == 15, r=(("wgm", cbuf),), w=(p1k,))
                    act(sgm, p1[:, 0:512], AF.Sigmoid, r=(p1k,), w=("sgm",))
                    p2, p2k = getps()
                    for k in range(16):
                        mm(p2[:, 0:512], wga[cbuf][:, k, :], hT_own[:, k, gsl], k == 0, k == 15, r=(("wga", cbuf),), w=(p2k,))
                    act(sga, p2[:, 0:512], AF.Sigmoid, r=(p2k,), w=("sga",))
                    p3, p3k = getps()
                    for k in range(8):
                        mm(p3[:, 0:512], wpm[cbuf][:, k, :], yT_res[:, k, gsl], k == 0, k == 7,
                           r=(("wpm", cbuf), "yT_res"), w=(p3k,))
                    tt("dve", t1b, sgm, p3[:, 0:512], ALU.mult, r=("sgm", p3k), w=("t1b",))
                    p4, p4k = getps()
                    for k in range(8):
                        mm(p4[:, 0:512], wpa[cbuf][:, k, :], yT_res[:, 8 + k, gsl], k == 0, k == 7,
                           r=(("wpa", cbuf), "yT_res"), w=(p4k,))
                    tt("dve", sga, sga, p4[:, 0:512], ALU.mult, r=("sga", p4k), w=("sga",))
                    mb = mTb[(c * 4 + g) % 2]
                    mk_ = ("mTb", (c * 4 + g) % 2)
                    tt("dve", mb, t1b, sga, ALU.add, r=("t1b", "sga"), w=(mk_,))
                    dma("act", mT_d[c, :, gsl], mb, r=(mk_,), w=(("mTd", c, g),))
            S.barrier()
            stop("B3", [(sm[:], 64)])

            areset()
            Wout = abf(16 * 2048).rearrange("p (k c) -> p k c", k=16)
            mTt = [abf(16 * 128).rearrange("p (c t) -> p c t", c=16) for _ in range(2)]
            fgb = af32(D)
            stgC = [af32(D) for _ in range(2)]
            xts = [af32(D) for _ in range(2)]
            obuf = [af32(D) for _ in range(2)]
            junkC = af32(D)
            dma("sp", fgb, fg_bc, r=(), w=("fgb",))
            for k in range(16):
                st_ = stgC[k % 2]
                sk = ("stgC", k % 2)
                dma(dmaq(), st_, w_out[k * 128:(k + 1) * 128, :], r=(), w=(sk,))
                tt("dve", Wout[:, k, :], st_, gate_bc[:], ALU.mult, r=(sk, "gate_bc"), w=("Wout",))
            outkeys = []

            def c_loads(t):
                vb = t % 2
                for c4 in range(4):
                    dma(("sp", "pool")[c4 % 2], mTt[vb][:, 4 * c4:4 * c4 + 4, :],
                        mT_d[4 * c4:4 * c4 + 4, :, t * 128:(t + 1) * 128].rearrange("c p t -> p c t"), r=(), w=(("mTt", vb),))
                dma(("sp", "pool")[t % 2], xts[vb], xl[(NPRE + t) * 128:(NPRE + t + 1) * 128, :], r=(), w=(("xts", vb),))

            c_loads(0)
            for t in range(NOWN):
                vb = t % 2
                if t + 1 < NOWN:
                    c_loads(t + 1)
                ob = obuf[vb]
                ok = ("ob", vb)
                for cbk in range(4):
                    p_, pk = getps()
                    for c in range(16):
                        mm(p_[:, 0:512], mTt[vb][:, c, :], Wout[:, c, cbk * 512:(cbk + 1) * 512], c == 0, c == 15,
                           r=(("mTt", vb), "Wout"), w=(pk,))
                    tt("dve", ob[:, cbk * 512:(cbk + 1) * 512], p_[:, 0:512], xts[vb][:, cbk * 512:(cbk + 1) * 512], ALU.add,
                       r=(pk, ("xts", vb)), w=(ok,))
                memset("dve", sm[:, 52:53], 0.0, w=("ssC",))
                act(junkC, ob, AF.Square, r=(ok, "ssC"), w=("junkC", "ssC"), accum=sm[:, 52:53])
                ts("dve", sm[:, 53:54], sm[:, 52:53], 1.0 / D, EPS, ALU.mult, ALU.add, r=("ssC",), w=("rC",))
                rsqrt_col(sm[:, 53:54], "rC")
                stt("dve", ob, ob, sm[:, 53:54], fgb, ALU.mult, ALU.mult, r=(ok, "rC", "fgb"), w=(ok,))
                dma("act", y_out[t * 128:(t + 1) * 128, :], ob, r=(ok,), w=(("yout", t),))
                outkeys.append(("yout", t))
        try:
            body()
        except _Stop:
            pass
        S.finish(())
        S.emit()
    return nc


_NC_CACHE = {}


def _consts():
    ident = np.eye(128, dtype=np.float32).astype(ml_dtypes.bfloat16)
    s = np.arange(128)[:, None]
    t = np.arange(128)[None, :]
    tri = (s <= t).astype(np.float32)
    ones = np.ones((128, 128), np.float32)
    q = np.arange(128)[:, None]
    kap = np.arange(640)[None, :]
    cq, ck = q // 64, kap // 64
    allowed = (ck >= cq) & (ck <= cq + 8)
    amask = np.where(allowed, 0.0, NEG).astype(np.float32)
    relidx = np.clip(q + 512 - kap, -63, 128) + 63
    return ident, tri, ones, tri.copy(), amask, relidx


def kernel(x, c, w_ada, b_ada, norm_g, w_in, b_if, conv_w, conv_b, mh_norm_g, rel_bias,
           w_proj_m, w_proj_a, w_out, final_norm_g):
    f = np.float32
    x = np.asarray(x, f)
    ident, tri, ones, mst, amask, relidx = _consts()
    if "nc" not in _NC_CACHE:
        _NC_CACHE["nc"] = build_nc()
    nc = _NC_CACHE["nc"]
    rep = lambda v, n=128: np.ascontiguousarray(np.broadcast_to(np.asarray(v, f).reshape(1, -1), (n, np.asarray(v).size)))
    colT = lambda v: np.ascontiguousarray(np.asarray(v, f).reshape(16, 128).T)
    cwl = np.ascontiguousarray(np.asarray(conv_w[0], f).T.reshape(16, 128, 4).transpose(1, 0, 2))
    shared = {
        "w_ada": np.ascontiguousarray(w_ada[0], f), "b_ada": np.ascontiguousarray(b_ada[0], f).reshape(1, -1),
        "ngT": colT(norm_g[0]), "w_in": np.ascontiguousarray(w_in[0], f), "bif_bc": rep(b_if[0]),
        "convw": cwl, "convb": colT(conv_b[0]), "mhg_bc": rep(mh_norm_g[0]),
        "relmat": np.ascontiguousarray(np.asarray(rel_bias[0], f)[:, relidx]), "amask": amask,
        "w_pm": np.ascontiguousarray(w_proj_m[0], f), "w_pa": np.ascontiguousarray(w_proj_a[0], f),
        "w_out": np.ascontiguousarray(w_out[0], f), "fg_bc": rep(final_norm_g),
        "ident": ident, "tri": tri, "ones": ones, "maskst": mst,
    }
    in_maps = []
    for core in range(8):
        b, j = core // 4, core % 4
        npad = (3 - j) * 16
        xl = np.zeros((NT * 128, D), f)
        xl[npad * 128:] = x[b, 0:(j + 1) * 2048]
        valid = (np.arange(NT) >= npad).astype(f)
        m = dict(shared)
        m["xl"] = xl
        m["cT"] = colT(c[b])
        m["padneg"] = rep(np.where(valid > 0, 0.0, NEG))
        m["tilevalid"] = rep(valid)
        m["negtile"] = rep(np.where(valid > 0, 0.0, NEG))
        in_maps.append(m)
    res = run_bass_kernel_spmd(nc, in_maps, core_ids=list(range(8)))
    out = np.empty((2, 8192, D), f)
    for core in range(8):
        b, j = core // 4, core % 4
        out[b, j * 2048:(j + 1) * 2048] = res.results[core]["y"]
    return out
```

```python
import os
import numpy as np
import ml_dtypes
from contextlib import ExitStack
import concourse.bass as bass
import concourse.mybir as mybir
from concourse.bass_utils import run_bass_kernel_spmd

F32 = mybir.dt.float32
BF16 = mybir.dt.bfloat16
ALU = mybir.AluOpType
AF = mybir.ActivationFunctionType
AX = mybir.AxisListType

D = 2048
NT = 64
NPRE = 48
NOWN = 16
EPS = 1e-6
C_MQ, C_MK, C_MV, C_MO, C_MZ, C_MI, C_MF = 0, 1024, 2048, 3072, 4096, 5120, 5124
C_AQ, C_AK, C_AV, C_AZ, C_GM, C_GA = 5128, 6152, 7176, 8200, 9224, 11272
IN_COLS = 13320
NEG = -30000.0
ENGS = ("sp", "act", "dve", "pool", "pe")
KSTOP = os.environ.get("KSTOP")


class _Stop(Exception):
    pass


class Op:
    __slots__ = ("eng", "fn", "deps", "dma", "sig", "sem", "val")

    def __init__(self, eng, fn, dma):
        self.eng, self.fn, self.dma = eng, fn, dma
        self.deps, self.sig, self.sem, self.val = [], dma, None, 0


class Sched:
    ND = 40

    def __init__(self, nc, es):
        self.nc = nc
        self.eops = {e: [] for e in ENGS}
        self.lastw, self.readers = {}, {}
        self.csem = {e: es.enter_context(nc.semaphore("cs_" + e)) for e in ENGS}
        self.dsem = [es.enter_context(nc.semaphore("ds%d" % i)) for i in range(self.ND)]
        self.dlast = [None] * self.ND
        self.duse = [0] * self.ND
        self.dn = 0

    @staticmethod
    def _is_psum(k):
        return k == "psS" or (isinstance(k, tuple) and len(k) > 0 and k[0] in ("ps", "psb", "psS"))

    def add(self, eng, fn, r=(), w=(), dma=False):
        w = tuple(w) + tuple(k for k in r if self._is_psum(k))
        r = tuple(k for k in r if not self._is_psum(k))
        op = Op(eng, fn, dma)
        deps = []
        for k in r:
            if k in self.lastw:
                deps.append(self.lastw[k])
        for k in w:
            if k in self.lastw:
                deps.append(self.lastw[k])
            deps.extend(self.readers.get(k, ()))
        if dma:
            i = self.dn % self.ND
            self.dn += 1
            if self.dlast[i] is not None:
                deps.append(self.dlast[i])
            self.duse[i] += 1
            op.sem, op.val = self.dsem[i], 16 * self.duse[i]
            self.dlast[i] = op
        seen = set()
        for d in deps:
            if d is op or id(d) in seen:
                continue
            seen.add(id(d))
            if d.eng == "pe" and eng == "pe" and not d.dma:
                continue
            d.sig = True
            op.deps.append(d)
        for k in r:
            lst = self.readers.setdefault(k, [])
            if not dma:
                lst[:] = [o_ for o_ in lst if o_.dma or o_.eng != eng]
            lst.append(op)
        for k in w:
            self.lastw[k] = op
            self.readers[k] = []
        self.eops[eng].append(op)
        return op

    def barrier(self):
        lasts = []
        for e in ENGS:
            for o in reversed(self.eops[e]):
                if not o.dma and o.fn is not None:
                    lasts.append(o)
                    break
        pend = [d for d in self.dlast if d is not None]
        for e in ENGS:
            op = Op(e, None, False)
            for d in lasts + pend:
                if d.eng == e and not d.dma:
                    continue
                d.sig = True
                op.deps.append(d)
            self.eops[e].append(op)
        self.lastw, self.readers = {}, {}

    def finish(self, keys):
        op = Op("sp", None, False)
        for d in self.dlast:
            if d is not None:
                op.deps.append(d)
        self.eops["sp"].append(op)

    def emit(self):
        nc = self.nc
        for e in ENGS:
            c = 0
            for o in self.eops[e]:
                if o.dma or o.fn is None:
                    continue
                if o.sig:
                    c += 1
                    o.sem, o.val = self.csem[e], c

        def run(ename, eng):
            known = {}
            for o in self.eops[ename]:
                for d in o.deps:
                    key = id(d.sem)
                    if known.get(key, 0) >= d.val:
                        continue
                    eng.wait_ge(d.sem, d.val)
                    known[key] = d.val
                if o.fn is None:
                    continue
                ins = o.fn(eng)
                if o.sig:
                    ins.then_inc(o.sem, 16 if o.dma else 1)

        with nc.Block() as block:
            @block.sync
            def _(e):
                run("sp", e)

            @block.scalar
            def _(e):
                run("act", e)

            @block.vector
            def _(e):
                run("dve", e)

            @block.gpsimd
            def _(e):
                run("pool", e)

            @block.tensor
            def _(e):
                run("pe", e)


def build_nc():
    nc = bass.Bass("TRN2", target_bir_lowering=False)

    def din(name, shape, dt=F32):
        return nc.dram_tensor(name, list(shape), dt, kind="ExternalInput").ap()

    xl = din("xl", [NT * 128, D])
    cT = din("cT", [128, 16])
    w_ada = din("w_ada", [D, 3 * D])
    b_ada = din("b_ada", [1, 3 * D])
    ngT = din("ngT", [128, 16])
    w_in = din("w_in", [D, IN_COLS])
    bif_bc = din("bif_bc", [128, 8])
    convw = din("convw", [128, 16, 4])
    convb = din("convb", [128, 16])
    mhg_bc = din("mhg_bc", [128, 1024])
    relmat = din("relmat", [8, 128, 640])
    amask = din("amask", [128, 640])
    w_pm = din("w_pm", [1024, D])
    w_pa = din("w_pa", [1024, D])
    w_out = din("w_out", [D, D])
    fg_bc = din("fg_bc", [128, D])
    padneg = din("padneg", [128, NT])
    tilevalid = din("tilevalid", [128, NT])
    negtile = din("negtile", [128, NT])
    ident_d = din("ident", [128, 128], BF16)
    tri_d = din("tri", [128, 128])
    ones_d = din("ones", [128, 128])
    mst_d = din("maskst", [128, 128])
    y_out = nc.dram_tensor("y", [NOWN * 128, D], F32, kind="ExternalOutput").ap()
    dbg = nc.dram_tensor("dbg", [128, 8192], F32, kind="ExternalOutput").ap() if KSTOP else None
    yT_d = nc.dram_tensor("yT_d", [16, 128, NOWN * 128], BF16, kind="Internal").ap()
    mT_d = nc.dram_tensor("mT_d", [16, 128, NOWN * 128], BF16, kind="Internal").ap()

    es = ExitStack()
    with es, nc.allow_low_precision("bf16 matmul operands, fp32 accumulation"), \
            nc.allow_non_contiguous_dma("column-sliced weight loads"):
        S = Sched(nc, es)

        def sb(name, shape, dt=F32):
            return es.enter_context(nc.sbuf_tensor("s_" + name, list(shape), dt))

        def pst(name, shape, dt=F32):
            return es.enter_context(nc.psum_tensor(name, list(shape), dt))

        ident = sb("ident", [128, 128], BF16)
        tri = sb("tri", [128, 128])
        ones = sb("ones", [128, 128])
        gs = sb("gs", [128, 16])
        shift = sb("shift", [128, 16])
        ngt_s = sb("ngt_s", [128, 16])
        gate_bc = sb("gate_bc", [128, D])
        cw = sb("cw", [128, 16, 4])
        cb_ = sb("cb_", [128, 16])
        bif = sb("bif", [128, 8])
        mhg = sb("mhg", [128, 1024])
        pneg = sb("pneg", [128, NT])
        tval = sb("tval", [128, NT])
        ntile = sb("ntile", [128, NT])
        amask_s = sb("amask_s", [128, 640])
        wif = sb("wif", [128, 16, 8], BF16)
        wifs = sb("wifs", [128, 16, 8])
        state = [[sb("st%d%d" % (h, b), [128, 257]) for b in range(2)] for h in range(4)]
        ctbf = [[sb("ct%d%d" % (h, b), [128, 257], BF16) for b in range(2)] for h in range(4)]
        hT_halo = sb("hT_halo", [128, 16, 512], BF16)
        wk_o = sb("wk_o", [128, NOWN, 4])
        wa_o = sb("wa_o", [128, NOWN, 4])
        eB_o = sb("eB_o", [128, NOWN, 4])
        ebc_o = sb("ebc_o", [128, NOWN, 4])
        sm = sb("sm", [128, 64])
        ARENA_COLS = 40000
        arena = sb("arena", [128, ARENA_COLS])

        ps = [pst("ps%d" % i, [128, 512]) for i in range(4)]
        psS = pst("psS", [128, 1024])
        psb = [pst("psb%d" % i, [128, 1024], BF16) for i in range(2)]
        rot = {"ps": 0, "psb": 0, "cast": 0, "q": 0}

        def getps():
            i = rot["ps"] % 4
            rot["ps"] += 1
            return ps[i], ("ps", i)

        def getpsb():
            i = rot["psb"] % 2
            rot["psb"] += 1
            return psb[i], ("psb", i)

        aoff = [0]

        def areset():
            aoff[0] = 0

        def af32(cols):
            o = aoff[0]
            aoff[0] += cols
            assert aoff[0] <= ARENA_COLS, aoff[0]
            return arena[:, o:o + cols]

        def abf(cols):
            n = (cols + 1) // 2
            return af32(n).bitcast(BF16)[:, 0:cols]

        def dma(q, out, in_, r, w):
            S.add(q, lambda e, o=out, i=in_: e.dma_start(out=o, in_=i), r=r, w=w, dma=True)

        def dmaq():
            rot["q"] += 1
            return "sp" if rot["q"] % 2 else "pool"

        def act(out, in_, func, r, w, bias=None, scale=None, accum=None):
            kw = {}
            if bias is not None:
                kw["bias"] = bias
            if scale is not None:
                kw["scale"] = scale
            if accum is not None:
                kw["accum_out"] = accum
            S.add("act", lambda e: e.activation(out, in_, func, **kw), r=r, w=w)

        def ts(eng, out, in0, s1, s2, op0, op1, r, w):
            if s2 is None:
                s2, op1 = (1.0, ALU.mult) if op0 == ALU.add else (0.0, ALU.add)
            S.add(eng, lambda e: e.tensor_scalar(out, in0, s1, s2, op0, op1), r=r, w=w)

        def rsqrt_col(col, key):
            S.add("act", lambda e: e.activation(col, col, AF.Sqrt), r=(key,), w=(key,))
            S.add("dve", lambda e: e.reciprocal(col, col), r=(key,), w=(key,))

        def tt(eng, out, in0, in1, op, r, w):
            S.add(eng, lambda e: e.tensor_tensor(out, in0, in1, op), r=r, w=w)

        def stt(eng, out, in0, sc, in1, op0, op1, r, w):
            S.add(eng, lambda e: e.scalar_tensor_tensor(out, in0, sc, in1, op0, op1), r=r, w=w)

        def mm(out, lhsT, rhs, start, stop, r, w):
            S.add("pe", lambda e: e.matmul(out, lhsT, rhs, start=start, stop=stop), r=r, w=w)

        def tr(out, in_, r, w):
            S.add("pe", lambda e: e.transpose(out, in_, ident[:]), r=tuple(r) + ("ident",), w=w)

        def cast(out, in_, r, w, eng=None):
            if eng is None:
                rot["cast"] += 1
                eng = ("act", "pool")[rot["cast"] % 2]
            if eng == "act":
                S.add("act", lambda e: e.copy(out, in_), r=r, w=w)
            else:
                S.add(eng, lambda e: e.tensor_copy(out, in_), r=r, w=w)

        def memset(eng, ap, v, w):
            S.add(eng, lambda e: e.memset(ap, v), w=w)

        def stop(tag, dumps=()):
            if KSTOP != tag:
                return
            S.barrier()
            off = 0
            for ap_, n_ in dumps:
                dma("sp", dbg[:, off:off + n_], ap_, r=(), w=(("dbg", off),))
                off += n_
            raise _Stop()

        def body():
            for (t_, d_, k_) in ((ident, ident_d, "ident"), (tri, tri_d, "tri"), (ones, ones_d, "ones"),
                                 (ngt_s, ngT, "ngt"), (cw, convw, "cw"),
                                 (cb_, convb, "cb"), (bif, bif_bc, "bif"), (mhg, mhg_bc, "mhg"),
                                 (pneg, padneg, "pneg"), (tval, tilevalid, "tval"),
                                 (ntile, negtile, "ntile"), (amask_s, amask, "amask")):
                dma("sp", t_[:], d_, r=(), w=(k_,))
            for k4 in range(4):
                dma("sp", wifs[:, 4 * k4:4 * k4 + 4, :],
                    w_in[k4 * 512:(k4 + 1) * 512, C_MI:C_MI + 8].rearrange("(k p) c -> p k c", p=128), r=(), w=("wifs",))
            cast(wif[:], wifs[:], r=("wifs",), w=("wif",), eng="dve")
            for h in range(4):
                for b in range(2):
                    memset("dve", state[h][b][:], 0.0, w=(("st", h, b),))

            areset()
            sc_in = af32(16)
            scv = af32(16)
            modrow = af32(3 * D)[0:1, :]
            badar = af32(3 * D)[0:1, :]
            stgA = [af32(3072) for _ in range(2)]
            dma("sp", sc_in, cT, r=(), w=("sc_in",))
            dma("sp", badar, b_ada, r=(), w=("badar",))
            act(scv, sc_in, AF.Silu, r=("sc_in",), w=("scv",))
            banks = [(ps[0], ("ps", 0), 0), (ps[1], ("ps", 1), 0), (ps[2], ("ps", 2), 0),
                     (ps[3], ("ps", 3), 0), (psS, ("psS",), 0), (psS, ("psS",), 512)]
            for half in range(2):
                for k in range(16):
                    st_ = stgA[k % 2]
                    key = ("stgA", k % 2)
                    dma(dmaq(), st_, w_ada[k * 128:(k + 1) * 128, half * 3072:(half + 1) * 3072], r=(), w=(key,))
                    for cbk in range(6):
                        t_, pk, off = banks[cbk]
                        mm(t_[0:1, off:off + 512], scv[:, k:k + 1], st_[:, cbk * 512:(cbk + 1) * 512],
                           k == 0, k == 15, r=(key, "scv"), w=(pk,))
                for cbk in range(6):
                    t_, pk, off = banks[cbk]
                    c0 = half * 3072 + cbk * 512
                    tt("dve", modrow[:, c0:c0 + 512], t_[0:1, off:off + 512], badar[:, c0:c0 + 512], ALU.add,
                       r=(pk, "badar"), w=("modrow",))
            pcol, pck = getps()
            for c in range(32):
                mm(pcol[:, c:c + 1], modrow[0:1, c * 128:(c + 1) * 128], ones[0:1, 0:1], True, True,
                   r=("modrow", "ones"), w=(pck,))
            cast(shift[:], pcol[:, 0:16], r=(pck,), w=("shift",), eng="dve")
            stt("dve", gs[:], pcol[:, 16:32], 1.0, ngt_s[:], ALU.add, ALU.mult, r=(pck, "ngt"), w=("gs",))
            for cbk in range(4):
                t_, pk = getps()
                mm(t_[:, 0:512], ones[0:1, :], modrow[0:1, 2 * D + cbk * 512:2 * D + (cbk + 1) * 512], True, True,
                   r=("modrow", "ones"), w=(pk,))
                cast(gate_bc[:, cbk * 512:(cbk + 1) * 512], t_[:, 0:512], r=(pk,), w=("gate_bc",), eng="act")
            S.barrier()
            stop("S1", [(gs[:], 16), (shift[:], 16), (gate_bc[:, 0:64], 64)])

            def frontend1(tau, xbufs, xn, xnk, i2, ident_eng="act"):
                xt = xbufs[i2 % len(xbufs)]
                xk = ("xt", i2 % len(xbufs))
                dma(dmaq(), xt, xl[tau * 128:(tau + 1) * 128, :], r=(), w=(xk,))
                memset("dve", sm[:, 0:1], 0.0, w=("ss",))
                act(xn, xt, AF.Square, r=(xk, "ss"), w=(xnk, "ss"), accum=sm[:, 0:1])
                ts("dve", sm[:, 1:2], sm[:, 0:1], 1.0 / D, EPS, ALU.mult, ALU.add, r=("ss",), w=("rstd",))
                rsqrt_col(sm[:, 1:2], "rstd")
                if ident_eng == "act":
                    act(xn, xt, AF.Identity, r=(xk, "rstd"), w=(xnk,), scale=sm[:, 1:2])
                else:
                    ts("dve", xn, xt, sm[:, 1:2], None, ALU.mult, None, r=(xk, "rstd"), w=(xnk,))

            def frontend2(dstf, dkey, xn, xnk, halves=(0, 1), act_halves=(0,)):
                for hh in halves:
                    pt, ptk = psb[hh], ("psb", hh)
                    for kk in range(8):
                        k = hh * 8 + kk
                        tr(pt[:, kk * 128:(kk + 1) * 128], xn[:, k * 128:(k + 1) * 128], r=(xnk,), w=(ptk,))
                    for kk in range(8):
                        k = hh * 8 + kk
                        if hh in act_halves:
                            act(dstf(k), pt[:, kk * 128:(kk + 1) * 128], AF.Identity, r=(ptk, "gs", "shift"),
                                w=(dkey[0],), bias=shift[:, k:k + 1], scale=gs[:, k:k + 1])
                        else:
                            ts("dve", dstf(k), pt[:, kk * 128:(kk + 1) * 128], gs[:, k:k + 1], shift[:, k:k + 1],
                               ALU.mult, ALU.add, r=(ptk, "gs", "shift"), w=(dkey[1],))

            def frontend(tau, dstf, dkey, xbufs, xn, i2):
                frontend1(tau, xbufs, xn, "xn", i2)
                frontend2(dstf, dkey, xn, "xn")

            def load_w(dst, src_rows, nk, stgs, skey, dkey, cast_eng=None, q=None, rot_engs=None):
                for k in range(nk):
                    st_ = stgs[k % len(stgs)]
                    key = (skey, k % len(stgs))
                    dma(q or dmaq(), st_, src_rows(k), r=(), w=(key,))
                    cast(dst(k), st_, r=(key,), w=(dkey,), eng=(rot_engs[k % len(rot_engs)] if rot_engs else cast_eng))

            def gates(tau, hsrc, hkey, own_i):
                pg, pgk = getps()
                for k in range(16):
                    mm(pg[:, 0:8], hsrc(k), wif[:, k, :], k == 0, k == 15, r=tuple(hkey) + ("wif",), w=(pgk,))
                gl = sm[:, 8:16]
                tt("dve", gl, pg[:, 0:8], bif[:], ALU.add, r=(pgk, "bif"), w=("gl",))
                ts("dve", sm[:, 16:20], gl[:, 0:4], pneg[:, tau:tau + 1], None, ALU.add, None, r=("gl", "pneg"), w=("ig",))
                act(sm[:, 20:24], gl[:, 4:8], AF.Exp, r=("gl",), w=("e1",), scale=-1.0)
                ts("dve", sm[:, 20:24], sm[:, 20:24], 1.0, None, ALU.add, None, r=("e1",), w=("e1",))
                act(sm[:, 24:28], sm[:, 20:24], AF.Ln, r=("e1",), w=("lf",))
                ts("dve", sm[:, 24:28], sm[:, 24:28], -1.0, None, ALU.mult, None, r=("lf",), w=("lf",))
                pc, pckk = getps()
                mm(pc[:, 0:4], tri[:], sm[:, 24:28], True, True, r=("lf", "tri"), w=(pckk,))
                mm(pc[:, 4:8], ones[:], sm[:, 24:28], True, True, r=("lf", "ones"), w=(pckk,))
                tt("dve", sm[:, 28:32], sm[:, 16:20], pc[:, 0:4], ALU.subtract, r=("ig", pckk), w=("t1",))
                tt("dve", sm[:, 32:36], sm[:, 28:32], pc[:, 4:8], ALU.add, r=("t1", pckk), w=("t2",))
                if own_i is None:
                    wk, eB = sm[:, 36:40], sm[:, 40:44]
                    wkk, eBk = "wk", "eB"
                else:
                    wk, eB = wk_o[:, own_i, :], eB_o[:, own_i, :]
                    wkk = eBk = ("gown", own_i)
                act(wk, sm[:, 32:36], AF.Exp, r=("t2",), w=(wkk,))
                ts("dve", wk, wk, 0.0625, None, ALU.mult, None, r=(wkk,), w=(wkk,))
                act(eB, pc[:, 4:8], AF.Exp, r=(pckk,), w=(eBk,))
                if own_i is not None:
                    act(wa_o[:, own_i, :], sm[:, 28:32], AF.Exp, r=("t1",), w=(wkk,))
                    ts("dve", wa_o[:, own_i, :], wa_o[:, own_i, :], 0.0625, None, ALU.mult, None, r=(wkk,), w=(wkk,))
                    act(ebc_o[:, own_i, :], pc[:, 0:4], AF.Exp, r=(pckk,), w=(wkk,))
                return wk, eB, wkk, eBk

            def conv_silu(pre, prek, cblk, accb, acck, out, outk):
                ts("dve", accb, pre[:, 3:515], cw[:, cblk, 3:4], cb_[:, cblk:cblk + 1], ALU.mult, ALU.add,
                   r=(prek, "cw", "cb"), w=(acck,))
                for tap in range(3):
                    stt("dve", accb, pre[:, tap:tap + 512], cw[:, cblk, tap:tap + 1], accb, ALU.mult, ALU.add,
                        r=(prek, acck, "cw"), w=(acck,))
                act(out, accb, AF.Silu, r=(acck,), w=(outk,))

            def state_update(h, kT_blk, kTk, vaug, vk, wk_col, wkk, eB_col, eBk, kpp, kppk, refresh_bf):
                pt, ptk = getpsb()
                for blk in range(2):
                    tr(pt[:, blk * 128:(blk + 1) * 128], kT_blk(blk), r=(kTk[blk],), w=(ptk,))
                ts("dve", kpp, pt[:, 0:256], wk_col, None, ALU.mult, None, r=(ptk, wkk), w=(kppk,))
                for blk in range(2):
                    p_, pk = getps()
                    mm(p_[:, 0:257], kpp[:, blk * 128:(blk + 1) * 128], vaug, True, True, r=(kppk, vk), w=(pk,))
                    stt("dve", state[h][blk][:], state[h][blk][:], eB_col, p_[:, 0:257], ALU.mult, ALU.add,
                        r=(pk, eBk, ("st", h, blk)), w=(("st", h, blk),))
                    if refresh_bf:
                        cast(ctbf[h][blk][:], state[h][blk][:], r=(("st", h, blk),), w=(("ct", h, blk),), eng="act")

            areset()
            WA = abf(16 * 2048).rearrange("p (k c) -> p k c", k=16)
            hTgA = [abf(16 * 512).rearrange("p (k c) -> p k c", k=16) for _ in range(2)]
            xn = abf(D)
            xns = [xn, abf(D)]
            kT = abf(8 * 512).rearrange("p (b c) -> p b c", b=8)
            vaugs = [[abf(258)[:, 0:257] for _ in range(4)] for _ in range(4)]
            kpps = [abf(256) for _ in range(2)]
            stgs = [af32(2048) for _ in range(2)]
            xbufs = stgs
            kpre = af32(8 * 515).rearrange("p (b c) -> p b c", b=8)
            accb = af32(512)
            gtmp = af32(4 * 48).rearrange("p (i c) -> p i c", i=4)
            load_w(lambda k: WA[:, k, :], lambda k: w_in[k * 128:(k + 1) * 128, C_MK:C_MK + 2048], 16, stgs, "xt", "WA", rot_engs=("act", "dve"))
            memset("pool", kpre[:, :, 0:3], 0.0, w=tuple(("kpre", c) for c in range(8)))
            for i in range(4):
                for h in range(4):
                    memset("pool", vaugs[i][h][:, 256:257], 1.0, w=(("vaug", i, h),))
            NG = NPRE // 4
            fe_cnt = [0]

            def a_hT(g):
                return hT_halo if g == NG - 1 else hTgA[g % 2]

            def a_keys(g, i):
                return (("hTg", g % 2, i, 0), ("hTg", g % 2, i, 1))

            def a_gates1(g, i):
                tau = 4 * g + i
                hT = a_hT(g)
                gt = gtmp[:, i, :]
                pg, pgk = ps[2 + i % 2], ("ps", 2 + i % 2)
                for k in range(16):
                    mm(pg[:, 0:8], hT[:, k, i * 128:(i + 1) * 128], wif[:, k, :], k == 0, k == 15,
                       r=a_keys(g, i) + ("wif",), w=(pgk,))
                tt("dve", gt[:, 0:8], pg[:, 0:8], bif[:], ALU.add, r=(pgk, "bif"), w=(("g_gl", i),))
                ts("dve", gt[:, 8:12], gt[:, 0:4], pneg[:, tau:tau + 1], None, ALU.add, None, r=(("g_gl", i), "pneg"), w=(("g_ig", i),))
                act(gt[:, 12:16], gt[:, 4:8], AF.Exp, r=(("g_gl", i),), w=(("g_lf", i),), scale=-1.0)
                ts("dve", gt[:, 12:16], gt[:, 12:16], 1.0, None, ALU.add, None, r=(("g_lf", i),), w=(("g_lf", i),))
                act(gt[:, 12:16], gt[:, 12:16], AF.Ln, r=(("g_lf", i),), w=(("g_lf", i),))
                ts("dve", gt[:, 12:16], gt[:, 12:16], -1.0, None, ALU.mult, None, r=(("g_lf", i),), w=(("g_lf", i),))

            def a_gates2(g, i):
                gt = gtmp[:, i, :]
                pc, pck_ = psS[:, (i % 2) * 512:(i % 2) * 512 + 8], ("psS", i % 2)
                mm(pc[:, 0:4], tri[:], gt[:, 12:16], True, True, r=(("g_lf", i), "tri"), w=(pck_,))
                mm(pc[:, 4:8], ones[:], gt[:, 12:16], True, True, r=(("g_lf", i), "ones"), w=(pck_,))
                tt("dve", gt[:, 16:20], gt[:, 8:12], pc[:, 0:4], ALU.subtract, r=(("g_ig", i), pck_), w=(("g_t", i),))
                tt("dve", gt[:, 16:20], gt[:, 16:20], pc[:, 4:8], ALU.add, r=(("g_t", i), pck_), w=(("g_t", i),))
                act(gt[:, 20:24], gt[:, 16:20], AF.Exp, r=(("g_t", i),), w=(("g_wk", i),))
                ts("dve", gt[:, 20:24], gt[:, 20:24], 0.0625, None, ALU.mult, None, r=(("g_wk", i),), w=(("g_wk", i),))
                act(gt[:, 24:28], pc[:, 4:8], AF.Exp, r=(pck_,), w=(("g_eB", i),))

            def a_fe1(g, i):
                frontend1(4 * g + i, xbufs, xns[i % 2], ("xnA", i % 2), fe_cnt[0])
                fe_cnt[0] += 1

            def a_fe2(g, i, halves):
                hT = a_hT(g)
                frontend2(lambda k: hT[:, k, i * 128:(i + 1) * 128], a_keys(g, i), xns[i % 2], ("xnA", i % 2), halves)

            def a_U1(g, i, h):
                gt = gtmp[:, i, :]
                pt = ps[h % 2][:].bitcast(BF16)
                ptk = ("ps", h % 2)
                for blk in range(2):
                    tr(pt[:, blk * 128:(blk + 1) * 128], kT[:, 2 * h + blk, i * 128:(i + 1) * 128],
                       r=(("kT", 2 * h + blk),), w=(ptk,))
                kp, kpk = kpps[h % 2], ("kpp", h % 2)
                ts("dve", kp, pt[:, 0:256], gt[:, 20 + h:21 + h], None, ALU.mult, None, r=(ptk, ("g_wk", i)), w=(kpk,))

            def a_U2(g, i, h):
                gt = gtmp[:, i, :]
                kp, kpk = kpps[h % 2], ("kpp", h % 2)
                for blk in range(2):
                    pk = ("psS", blk)
                    p_ = psS[:, blk * 512:blk * 512 + 257]
                    mm(p_, kp[:, blk * 128:(blk + 1) * 128], vaugs[i][h], True, True, r=(kpk, ("vaug", i, h)), w=(pk,))
                    stt("dve", state[h][blk][:], state[h][blk][:], gt[:, 24 + h:25 + h], p_, ALU.mult, ALU.add,
                        r=(pk, ("g_eB", i), ("st", h, blk)), w=(("st", h, blk),))

            def a_stage3(g):
                nxt = g + 1 < NG
                if nxt:
                    a_fe1(g + 1, 0)
                for i in range(4):
                    if nxt and i + 1 < 4:
                        a_fe1(g + 1, i + 1)
                    a_U1(g, i, 0)
                    if nxt:
                        a_fe2(g + 1, i, (0,))
                    a_U1(g, i, 1)
                    a_U2(g, i, 0)
                    if nxt:
                        a_fe2(g + 1, i, (1,))
                    a_U1(g, i, 2)
                    a_U2(g, i, 1)
                    a_U1(g, i, 3)
                    a_U2(g, i, 2)
                    a_U2(g, i, 3)

            for i in range(4):
                a_fe1(0, i)
                a_fe2(0, i, (0, 1))
            for g in range(NG):
                hT = a_hT(g)
                hgk = tuple(k_ for i_ in range(4) for k_ in a_keys(g, i_))
                for i in range(4):
                    a_gates1(g, i)
                for cbk in range(8):
                    p_, pk = ps[cbk % 2], ("ps", cbk % 2)
                    for k in range(16):
                        mm(p_[:, 0:512], WA[:, k, cbk * 128:(cbk + 1) * 128], hT[:, k, :], k == 0, k == 15,
                           r=("WA",) + hgk, w=(pk,))
                    cast(kpre[:, cbk, 3:515], p_[:, 0:512], r=(pk,), w=(("kpre", cbk),), eng="act")
                    conv_silu(kpre[:, cbk, :], ("kpre", cbk), 8 + cbk, accb, "accb", kT[:, cbk, :], ("kT", cbk))
                    ts("dve", kpre[:, cbk, 0:3], kpre[:, cbk, 512:515], tval[:, 4 * g + 3:4 * g + 4], None, ALU.mult, None,
                       r=(("kpre", cbk), "tval"), w=(("kpre", cbk),))
                for i in range(4):
                    for half in range(2):
                        p_, pk = ps[2 + half], ("ps", 2 + half)
                        for k in range(16):
                            mm(p_[:, 0:512], hT[:, k, i * 128:(i + 1) * 128], WA[:, k, 1024 + half * 512:1024 + (half + 1) * 512],
                               k == 0, k == 15, r=("WA",) + a_keys(g, i), w=(pk,))
                        for hh in range(2):
                            h = 2 * half + hh
                            cast(vaugs[i][h][:, 0:256], p_[:, hh * 256:(hh + 1) * 256], r=(pk,), w=(("vaug", i, h),),
                                 eng="act")
                for i in range(4):
                    a_gates2(g, i)
                a_stage3(g)
            S.barrier()
            stop("A", [(state[0][0][:], 257), (state[3][1][:], 257), (sm[:], 64)])

            areset()
            hT_own = abf(16 * 2048).rearrange("p (k c) -> p k c", k=16)
            mark_b = aoff[0]
            xn = abf(D)
            xbufs = [af32(D) for _ in range(2)]
            xns0 = [xn, abf(D)]
            frontend1(NPRE, xbufs, xns0[0], ("xnB", 0), 0, ident_eng="dve")
            for i in range(NOWN):
                if i + 1 < NOWN:
                    frontend1(NPRE + i + 1, xbufs, xns0[(i + 1) % 2], ("xnB", (i + 1) % 2), i + 1, ident_eng="dve")
                frontend2(lambda k, i=i: hT_own[:, k, i * 128:(i + 1) * 128], (("hTo", i, 0), ("hTo", i, 1)),
                          xns0[i % 2], ("xnB", i % 2), act_halves=())
                gates(NPRE + i, lambda k, i=i: hT_own[:, k, i * 128:(i + 1) * 128], (("hTo", i, 0), ("hTo", i, 1)), i)
            for h in range(4):
                for b in range(2):
                    cast(ctbf[h][b][:], state[h][b][:], r=(("st", h, b),), w=(("ct", h, b),), eng="act")
            S.barrier()
            stop("B0", [(wk_o[:].rearrange("p a b -> p (a b)"), 64), (wa_o[:].rearrange("p a b -> p (a b)"), 64), (eB_o[:].rearrange("p a b -> p (a b)"), 64), (ebc_o[:].rearrange("p a b -> p (a b)"), 64), (state[0][0][:], 257)])

            aoff[0] = mark_b
            WG = abf(16 * 1280).rearrange("p (k c) -> p k c", k=16)
            qkT = [abf(4 * 512).rearrange("p (b c) -> p b c", b=4) for _ in range(2)]
            vaug1 = [abf(258)[:, 0:257] for _ in range(2)]
            kpps = [abf(256) for _ in range(2)]
            STb = [abf(128) for _ in range(2)]
            ybf = [abf(256) for _ in range(2)]
            yTt = [abf(256).rearrange("p (b c) -> p b c", b=2) for _ in range(2)]
            stg1 = [af32(1280) for _ in range(4)]
            qkpre = af32(4 * 515).rearrange("p (b c) -> p b c", b=4)
            accb = af32(512)
            sgo = af32(256)
            slz = af32(256)
            Gt = [af32(256) for _ in range(2)]
            junk = af32(256)
            w5 = w_in[:, 0:5120].rearrange("r (s c) -> r s c", c=1024)
            for vb in range(2):
                memset("pool", vaug1[vb][:, 256:257], 1.0, w=(("vaug1", vb),))
            ts("dve", mhg[:], mhg[:], 0.5, None, ALU.mult, None, r=("mhg",), w=("mhg",))
            PS0, PS1, PS2, PS3 = (ps[0], ("ps", 0)), (ps[1], ("ps", 1)), (ps[2], ("ps", 2)), (ps[3], ("ps", 3))
            for h in range(4):
                load_w(lambda k: WG[:, k, :].rearrange("p (s c) -> p s c", c=256),
                       lambda k, h=h: w5[k * 128:(k + 1) * 128, :, h * 256:(h + 1) * 256],
                       16, [s_.rearrange("p (s c) -> p s c", c=256) for s_ in stg1], "stg1", "WG", rot_engs=("act", "dve"))
                for blk in range(4):
                    p_, pk = (PS0, PS1)[blk % 2]
                    for k in range(16):
                        mm(p_[:, 0:3], WG[:, k, blk * 128:(blk + 1) * 128], hT_halo[:, k, 509:512], k == 0, k == 15,
                           r=("WG",), w=(pk,))
                    ts("dve", qkpre[:, blk, 0:3], p_[:, 0:3], tval[:, NPRE - 1:NPRE], None, ALU.mult, None,
                       r=(pk, "tval"), w=(("qkpre", blk),))

                def b1_proj(g, h=h):
                    qk = qkT[g % 2]
                    for blk in range(4):
                        p_, pk = (PS0, PS1)[blk % 2]
                        for k in range(16):
                            mm(p_[:, 0:512], WG[:, k, blk * 128:(blk + 1) * 128], hT_own[:, k, g * 512:(g + 1) * 512],
                               k == 0, k == 15, r=("WG",), w=(pk,))
                        cast(qkpre[:, blk, 3:515], p_[:, 0:512], r=(pk,), w=(("qkpre", blk),), eng="act")
                        cidx = (2 * h + blk) if blk < 2 else (8 + 2 * h + blk - 2)
                        conv_silu(qkpre[:, blk, :], ("qkpre", blk), cidx, accb, "accb", qk[:, blk, :], ("qkT", g % 2, blk))
                        cast(qkpre[:, blk, 0:3], qkpre[:, blk, 512:515], r=(("qkpre", blk),), w=(("qkpre", blk),), eng="dve")

                def b1_P(t, h=h):
                    vb = t % 2
                    pA, pAk = PS0
                    for k in range(16):
                        mm(pA[:, 0:512], hT_own[:, k, t * 128:(t + 1) * 128], WG[:, k, 512:1024], k == 0, k == 15,
                           r=("WG",), w=(pAk,))
                    pB, pBk = PS1
                    for k in range(16):
                        mm(pB[:, 0:256], hT_own[:, k, t * 128:(t + 1) * 128], WG[:, k, 1024:1280], k == 0, k == 15,
                           r=("WG",), w=(pBk,))
                    cast(vaug1[vb][:, 0:256], pA[:, 0:256], r=(pAk,), w=(("vaug1", vb),), eng="act")
                    act(sgo, pA[:, 256:512], AF.Tanh, r=(pAk,), w=("sgo",), scale=0.5)
                    act(slz, pB[:, 0:256], AF.Silu, r=(pBk,), w=("slz",))
                    stt("dve", Gt[vb], sgo, 1.0, slz, ALU.add, ALU.mult, r=("sgo", "slz"), w=(("Gt", vb),))
                    tt("dve", Gt[vb], Gt[vb], mhg[:, h * 256:(h + 1) * 256], ALU.mult, r=(("Gt", vb), "mhg"), w=(("Gt", vb),))

                def b1_S(t, h=h):
                    vb = t % 2
                    gp = (t // 4) % 2
                    tsl = slice((t % 4) * 128, (t % 4 + 1) * 128)
                    pS, pSk = PS2
                    for blk in range(2):
                        mm(pS[:, 0:128], qkT[gp][:, 2 + blk, tsl], qkT[gp][:, blk, tsl], blk == 0, blk == 1,
                           r=(("qkT", gp, blk), ("qkT", gp, 2 + blk)), w=(pSk,))
                    stt("dve", STb[vb], pS[:, 0:128], wa_o[:, t, h:h + 1], tri[:], ALU.mult, ALU.mult,
                        r=(pSk, ("gown", t), "tri"), w=(("STb", vb),))

                def b1_N(t, h=h):
                    vb = t % 2
                    gp = (t // 4) % 2
                    tsl = slice((t % 4) * 128, (t % 4 + 1) * 128)
                    pN, pNk = PS3
                    mm(pN[:, 0:257], STb[vb], vaug1[vb], True, False, r=(("STb", vb), ("vaug1", vb)), w=(pNk,))
                    for blk in range(2):
                        mm(pN[:, 0:257], qkT[gp][:, blk, tsl], ctbf[h][blk][:], False, blk == 1,
                           r=(("qkT", gp, blk), ("ct", h, blk)), w=(pNk,))

                def b1_Utr(t, h=h):
                    vb = t % 2
                    gp = (t // 4) % 2
                    tsl = slice((t % 4) * 128, (t % 4 + 1) * 128)
                    pt, ptk = psb[0], ("psb", 0)
                    for blk in range(2):
                        tr(pt[:, blk * 128:(blk + 1) * 128], qkT[gp][:, 2 + blk, tsl], r=(("qkT", gp, 2 + blk),), w=(ptk,))
                    ts("dve", kpps[vb], pt[:, 0:256], wk_o[:, t, h:h + 1], None, ALU.mult, None,
                       r=(ptk, ("gown", t)), w=(("kpp", vb),))

                def b1_Umm(t, h=h):
                    vb = t % 2
                    for blk in range(2):
                        pk = ("psS", blk)
                        p_ = psS[:, blk * 512:blk * 512 + 257]
                        mm(p_, kpps[vb][:, blk * 128:(blk + 1) * 128], vaug1[vb], True, True,
                           r=(("kpp", vb), ("vaug1", vb)), w=(pk,))
                        stt("dve", state[h][blk][:], state[h][blk][:], eB_o[:, t, h:h + 1], p_, ALU.mult, ALU.add,
                            r=(pk, ("gown", t), ("st", h, blk)), w=(("st", h, blk),))
                        cast(ctbf[h][blk][:], state[h][blk][:], r=(("st", h, blk),), w=(("ct", h, blk),), eng="act")

                def b1_Ychain(t, h=h):
                    vb = t % 2
                    pN, pNk = PS3
                    gk = ("gown", t)
                    tt("dve", sm[:, 44:45], pN[:, 256:257], ebc_o[:, t, h:h + 1], ALU.mult, r=(pNk, gk), w=("d1",))
                    ts("dve", sm[:, 54:55], sm[:, 44:45], -1.0, None, ALU.mult, None, r=("d1",), w=("d1n",))
                    tt("dve", sm[:, 44:45], sm[:, 44:45], sm[:, 54:55], ALU.max, r=("d1", "d1n"), w=("d1",))
                    ts("dve", sm[:, 44:45], sm[:, 44:45], 1.0, 1.0, ALU.max, ALU.mult, r=("d1",), w=("d1",))
                    S.add("dve", lambda e: e.reciprocal(sm[:, 44:45], sm[:, 44:45]), r=("d1",), w=("d1",))
                    tt("dve", sm[:, 45:46], ebc_o[:, t, h:h + 1], sm[:, 44:45], ALU.mult, r=("d1", gk), w=("rr",))
                    memset("dve", sm[:, 46:47], 0.0, w=("ss2",))
                    act(junk, pN[:, 0:256], AF.Square, r=(pNk, "rr", "ss2"), w=("junk", "ss2"), scale=sm[:, 45:46],
                        accum=sm[:, 46:47])
                    ts("dve", sm[:, 47:48], sm[:, 46:47], 1.0 / 256, EPS, ALU.mult, ALU.add, r=("ss2",), w=("r2",))
                    rsqrt_col(sm[:, 47:48], "r2")
                    tt("dve", sm[:, 47:48], sm[:, 47:48], sm[:, 45:46], ALU.mult, r=("r2", "rr"), w=("r2",))
                    stt("dve", ybf[vb], pN[:, 0:256], sm[:, 47:48], Gt[vb], ALU.mult, ALU.mult,
                        r=(pNk, "r2", ("Gt", vb)), w=(("ybf", vb),))

                def b1_Ytr(t, h=h):
                    vb = t % 2
                    pt, ptk = psb[1], ("psb", 1)
                    for blk in range(2):
                        tr(pt[:, blk * 128:(blk + 1) * 128], ybf[vb][:, blk * 128:(blk + 1) * 128], r=(("ybf", vb),), w=(ptk,))
                    cast(yTt[vb].rearrange("p b c -> p (b c)"), pt[:, 0:256], r=(ptk,), w=(("yTt", vb),), eng="act")
                    dma("act", yT_d[2 * h:2 * h + 2, :, t * 128:(t + 1) * 128].rearrange("b p t -> p b t"), yTt[vb],
                        r=(("yTt", vb),), w=(("yTd", 2 * h, t),))

                b1_proj(0)
                b1_P(0)
                b1_S(0)
                for t in range(NOWN):
                    if t + 1 < NOWN:
                        if (t + 1) % 4 == 0:
                            b1_proj((t + 1) // 4)
                        b1_P(t + 1)
                        b1_S(t + 1)
                    b1_N(t)
                    b1_Utr(t)
                    if t > 0:
                        b1_Ytr(t - 1)
                    b1_Umm(t)
                    b1_Ychain(t)
                b1_Ytr(NOWN - 1)
            S.barrier()
            stop("B1", [(state[0][0][:], 257)])

            aoff[0] = mark_b
            WG2s = [abf(16 * 512).rearrange("p (k c) -> p k c", k=16) for _ in range(2)]
            akT = abf(2560)
            aqT = abf(2048)
            Vt = abf(20 * 128).rearrange("p (t c) -> p t c", t=20)
            slzA = abf(16 * 128).rearrange("p (t c) -> p t c", t=16)
            Pb = [abf(640) for _ in range(2)]
            PT = [abf(640) for _ in range(2)]
            yab = [abf(128) for _ in range(2)]
            yaT = [abf(128) for _ in range(2)]
            stg2 = [af32(512) for _ in range(4)]
            rb = af32(640)
            bm = af32(640)
            sbuf_s = [af32(640) for _ in range(2)]
            wA = w_in[:, C_AQ:C_AQ + 4096].rearrange("r (s c) -> r s c", c=1024)
            def b2_load(h, cast_eng=None, q=None):
                W_ = WG2s[h % 2]
                load_w(lambda k: W_[:, k, :].rearrange("p (s c) -> p s c", c=128),
                       lambda k: wA[k * 128:(k + 1) * 128, :, h * 128:(h + 1) * 128],
                       16, [s_.rearrange("p (s c) -> p s c", c=128) for s_ in stg2], "stg2", ("WG2", h % 2),
                       cast_eng=cast_eng, q=q)

            b2_load(0)
            for h in range(8):
                WG2 = WG2s[h % 2]
                WK = ("WG2", h % 2)
                dma("sp", rb, relmat[h], r=(), w=("rb",))
                tt("pool", bm, rb, amask_s[:], ALU.add, r=("rb", "amask"), w=("bm",))
                for g5 in range(5):
                    src = hT_halo if g5 == 0 else hT_own[:, :, (g5 - 1) * 512:g5 * 512]
                    srk = ()
                    p_, pk = getps()
                    for k in range(16):
                        mm(p_[:, 0:512], WG2[:, k, 128:256], src[:, k, :], k == 0, k == 15, r=(WK,) + srk, w=(pk,))
                    cast(akT[:, g5 * 512:(g5 + 1) * 512], p_[:, 0:512], r=(pk,), w=(("akT", g5),), eng="act")
                    if g5 > 0:
                        p_, pk = getps()
                        for k in range(16):
                            mm(p_[:, 0:512], WG2[:, k, 0:128], src[:, k, :], k == 0, k == 15, r=(WK,) + srk, w=(pk,))
                        act(aqT[:, (g5 - 1) * 512:g5 * 512], p_[:, 0:512], AF.Copy, r=(pk,), w=(("aqT", g5 - 1),),
                            scale=float(128 ** -0.5))
                    for i in range(4):
                        ttile = g5 * 4 + i
                        p_, pk = getps()
                        for k in range(16):
                            mm(p_[:, 0:256], src[:, k, i * 128:(i + 1) * 128], WG2[:, k, 256:512], k == 0, k == 15,
                               r=(WK,) + srk, w=(pk,))
                        cast(Vt[:, ttile, :], p_[:, 0:128], r=(pk,), w=(("Vt", ttile),), eng="act")
                        if g5 > 0:
                            act(slzA[:, ttile - 4, :], p_[:, 128:256], AF.Silu, r=(pk,), w=(("slzA", ttile - 4),))
                if h + 1 < 8:
                    b2_load(h + 1, cast_eng="pool", q="sp")

                RS = [sm[:, 56:57], sm[:, 57:58], sm[:, 60:61]]

                def att_s1(t, h=h):
                    vb = t % 2
                    gq = ("aqT", t // 4)
                    kk0 = tuple(("akT", x) for x in sorted({t // 4, (t + 3) // 4, (t + 4) // 4}))
                    mm(psS[:, 0:512], aqT[:, t * 128:(t + 1) * 128], akT[:, t * 128:t * 128 + 512], True, True,
                       r=(gq,) + kk0, w=("psS",))
                    mm(psS[:, 512:640], aqT[:, t * 128:(t + 1) * 128], akT[:, t * 128 + 512:t * 128 + 640], True, True,
                       r=(gq,) + kk0, w=("psS",))
                    sbt = sbuf_s[vb]
                    sk = ("sbt", vb)
                    tt("dve", sbt, psS[:, 0:640], bm, ALU.add, r=("psS", "bm"), w=(sk,))
                    for kb in range(max(0, 4 - t)):
                        lt = NPRE - 4 + t + kb
                        ts("dve", sbt[:, kb * 128:(kb + 1) * 128], sbt[:, kb * 128:(kb + 1) * 128], ntile[:, lt:lt + 1], None,
                           ALU.add, None, r=(sk, "ntile"), w=(sk,))
                    S.add("dve", lambda e, sbt=sbt: e.reduce_max(sm[:, 48:49], sbt, AX.X), r=(sk,), w=("mx",))
                    ts("dve", sm[:, 48:49], sm[:, 48:49], -1.0, None, ALU.mult, None, r=("mx",), w=("mx",))
                    rsc = RS[t % 3]
                    memset("dve", rsc, 0.0, w=(("rsum", t % 3),))
                    act(Pb[vb], sbt, AF.Exp, r=(sk, "mx", ("rsum", t % 3)), w=(("Pb", vb), ("rsum", t % 3)), bias=sm[:, 48:49],
                        accum=rsc)

                def att_s2(t, h=h):
                    vb = t % 2
                    pt, ptk = psb[0], ("psb", 0)
                    for kb in range(5):
                        tr(pt[:, kb * 128:(kb + 1) * 128], Pb[vb][:, kb * 128:(kb + 1) * 128], r=(("Pb", vb),), w=(ptk,))
                    cast(PT[vb], pt[:, 0:640], r=(ptk,), w=(("PT", vb),), eng="act")

                def att_s3(t, h=h):
                    vb = t % 2
                    rsc = RS[t % 3]
                    rrc = sm[:, 58 + vb:59 + vb]
                    pO, pOk = getps()
                    for kb in range(5):
                        mm(pO[:, 0:128], PT[vb][:, kb * 128:(kb + 1) * 128], Vt[:, t + kb, :], kb == 0, kb == 4,
                           r=(("PT", vb), ("Vt", t + kb)), w=(pOk,))
                    S.add("dve", lambda e: e.reciprocal(rrc, rsc), r=(("rsum", t % 3),), w=(("rrs", vb),))
                    stt("dve", yab[vb], pO[:, 0:128], rrc, slzA[:, t, :], ALU.mult, ALU.mult,
                        r=(pOk, ("rrs", vb), ("slzA", t)), w=(("yab", vb),))
                    pt2, pt2k = psb[1], ("psb", 1)
                    tr(pt2[:, 0:128], yab[vb], r=(("yab", vb),), w=(pt2k,))
                    cast(yaT[vb], pt2[:, 0:128], r=(pt2k,), w=(("yaT", vb),), eng="act")
                    dma("act", yT_d[8 + h, :, t * 128:(t + 1) * 128], yaT[vb], r=(("yaT", vb),), w=(("yTd", 8 + h, t),))

                att_s1(0)
                att_s1(1)
                att_s2(0)
                for t in range(NOWN):
                    if t + 2 < NOWN:
                        att_s1(t + 2)
                    if t + 1 < NOWN:
                        att_s2(t + 1)
                    att_s3(t)
            S.barrier()
            stop("B2", [(sm[:], 64)])

            aoff[0] = mark_b
            yT_res = abf(16 * 2048).rearrange("p (b c) -> p b c", b=16)
            hh_flat = hT_halo[:].rearrange("p k c -> p (k c)")
            wgm = [abf(16 * 128).rearrange("p (k c) -> p k c", k=16), hh_flat[:, 0:2048].rearrange("p (k c) -> p k c", k=16)]
            wga = [abf(16 * 128).rearrange("p (k c) -> p k c", k=16), hh_flat[:, 2048:4096].rearrange("p (k c) -> p k c", k=16)]
            wpm = [abf(8 * 128).rearrange("p (k c) -> p k c", k=8), hh_flat[:, 4096:5120].rearrange("p (k c) -> p k c", k=8)]
            wpa = [abf(8 * 128).rearrange("p (k c) -> p k c", k=8), hh_flat[:, 5120:6144].rearrange("p (k c) -> p k c", k=8)]
            mTb = [abf(512) for _ in range(2)]
            stg3 = [af32(8 * 128).rearrange("p (k c) -> p k c", k=8) for _ in range(2)]
            stg3.append(mhg[:].rearrange("p (k c) -> p k c", k=8))
            stg3.append(hh_flat[:, 6144:8192].bitcast(F32).rearrange("p (k c) -> p k c", k=8))
            sgm = af32(512)
            sga = af32(512)
            t1b = af32(512)
            for b in range(16):
                dma(dmaq(), yT_res[:, b, :], yT_d[b], r=(), w=("yT_res",))
            si3 = [0]

            def b3_load(c):
                cbuf = c % 2
                for (dst, src, nk, nm) in ((wgm, w_in[:, C_GM + c * 128:C_GM + (c + 1) * 128], 16, "wgm"),
                                           (wga, w_in[:, C_GA + c * 128:C_GA + (c + 1) * 128], 16, "wga"),
                                           (wpm, w_pm[:, c * 128:(c + 1) * 128], 8, "wpm"),
                                           (wpa, w_pa[:, c * 128:(c + 1) * 128], 8, "wpa")):
                    for k8 in range(nk // 8):
                        st_ = stg3[si3[0] % 4]
                        sk = ("stg3", si3[0] % 4)
                        si3[0] += 1
                        for k4 in range(2):
                            r0 = (k8 * 8 + k4 * 4) * 128
                            dma("sp", st_[:, 4 * k4:4 * k4 + 4, :],
                                src[r0:r0 + 512, :].rearrange("(k p) c -> p k c", p=128), r=(), w=(sk,))
                        cast(dst[cbuf][:, k8 * 8:(k8 + 1) * 8, :], st_[:], r=(sk,), w=((nm, cbuf),), eng="pool")

            b3_load(0)
            for c in range(16):
                cbuf = c % 2
                if c + 1 < 16:
                    b3_load(c + 1)
                for g in range(4):
                    gsl = slice(g * 512, (g + 1) * 512)
                    p1, p1k = getps()
                    for k in range(16):
                        mm(p1[:, 0:512], wgm[cbuf][:, k, :], hT_own[:, k, gsl], k == 0, k == 15, r=(("wgm", cbuf),), w=(p1k,))
                    act(sgm, p1[:, 0:512], AF.Sigmoid, r=(p1k,), w=("sgm",))
                    p2, p2k = getps()
                    for k in range(16):
                        mm(p2[:, 0:512], wga[cbuf][:, k, :], hT_own[:, k, gsl], k == 0, k == 15, r=(("wga", cbuf),), w=(p2k,))
                    act(sga, p2[:, 0:512], AF.Sigmoid, r=(p2k,), w=("sga",))
                    p3, p3k = getps()
                    for k in range(8):
                        mm(p3[:, 0:512], wpm[cbuf][:, k, :], yT_res[:, k, gsl], k == 0, k == 7,
                           r=(("wpm", cbuf), "yT_res"), w=(p3k,))
                    tt("dve", t1b, sgm, p3[:, 0:512], ALU.mult, r=("sgm", p3k), w=("t1b",))
                    p4, p4k = getps()
                    for k in range(8):
                        mm(p4[:, 0:512], wpa[cbuf][:, k, :], yT_res[:, 8 + k, gsl], k == 0, k == 7,
                           r=(("wpa", cbuf), "yT_res"), w=(p4k,))
                    tt("dve", sga, sga, p4[:, 0:512], ALU.mult, r=("sga", p4k), w=("sga",))
                    mb = mTb[(c * 4 + g) % 2]
                    mk_ = ("mTb", (c * 4 + g) % 2)
                    tt("dve", mb, t1b, sga, ALU.add, r=("t1b", "sga"), w=(mk_,))
                    dma("act", mT_d[c, :, gsl], mb, r=(mk_,), w=(("mTd", c, g),))
            S.barrier()
            stop("B3", [(sm[:], 64)])

            areset()
            Wout = abf(16 * 2048).rearrange("p (k c) -> p k c", k=16)
            mTt = [abf(16 * 128).rearrange("p (c t) -> p c t", c=16) for _ in range(2)]
            fgb = af32(D)
            stgC = [af32(D) for _ in range(2)]
            xts = [af32(D) for _ in range(2)]
            obuf = [af32(D) for _ in range(2)]
            junkC = af32(D)
            dma("sp", fgb, fg_bc, r=(), w=("fgb",))
            for k in range(16):
                st_ = stgC[k % 2]
                sk = ("stgC", k % 2)
                dma(dmaq(), st_, w_out[k * 128:(k + 1) * 128, :], r=(), w=(sk,))
                tt("dve", Wout[:, k, :], st_, gate_bc[:], ALU.mult, r=(sk, "gate_bc"), w=("Wout",))
            outkeys = []

            def c_loads(t):
                vb = t % 2
                for c4 in range(4):
                    dma(("sp", "pool")[c4 % 2], mTt[vb][:, 4 * c4:4 * c4 + 4, :],
                        mT_d[4 * c4:4 * c4 + 4, :, t * 128:(t + 1) * 128].rearrange("c p t -> p c t"), r=(), w=(("mTt", vb),))
                dma(("sp", "pool")[t % 2], xts[vb], xl[(NPRE + t) * 128:(NPRE + t + 1) * 128, :], r=(), w=(("xts", vb),))

            c_loads(0)
            for t in range(NOWN):
                vb = t % 2
                if t + 1 < NOWN:
                    c_loads(t + 1)
                ob = obuf[vb]
                ok = ("ob", vb)
                for cbk in range(4):
                    p_, pk = getps()
                    for c in range(16):
                        mm(p_[:, 0:512], mTt[vb][:, c, :], Wout[:, c, cbk * 512:(cbk + 1) * 512], c == 0, c == 15,
                           r=(("mTt", vb), "Wout"), w=(pk,))
                    tt("dve", ob[:, cbk * 512:(cbk + 1) * 512], p_[:, 0:512], xts[vb][:, cbk * 512:(cbk + 1) * 512], ALU.add,
                       r=(pk, ("xts", vb)), w=(ok,))
                memset("dve", sm[:, 52:53], 0.0, w=("ssC",))
                act(junkC, ob, AF.Square, r=(ok, "ssC"), w=("junkC", "ssC"), accum=sm[:, 52:53])
                ts("dve", sm[:, 53:54], sm[:, 52:53], 1.0 / D, EPS, ALU.mult, ALU.add, r=("ssC",), w=("rC",))
                rsqrt_col(sm[:, 53:54], "rC")
                stt("dve", ob, ob, sm[:, 53:54], fgb, ALU.mult, ALU.mult, r=(ok, "rC", "fgb"), w=(ok,))
                dma("act", y_out[t * 128:(t + 1) * 128, :], ob, r=(ok,), w=(("yout", t),))
                outkeys.append(("yout", t))
        try:
            body()
        except _Stop:
            pass
        S.finish(())
        S.emit()
    return nc


_NC_CACHE = {}


def _consts():
    ident = np.eye(128, dtype=np.float32).astype(ml_dtypes.bfloat16)
    s = np.arange(128)[:, None]
    t = np.arange(128)[None, :]
    tri = (s <= t).astype(np.float32)
    ones = np.ones((128, 128), np.float32)
    q = np.arange(128)[:, None]
    kap = np.arange(640)[None, :]
    cq, ck = q // 64, kap // 64
    allowed = (ck >= cq) & (ck <= cq + 8)
    amask = np.where(allowed, 0.0, NEG).astype(np.float32)
    relidx = np.clip(q + 512 - kap, -63, 128) + 63
    return ident, tri, ones, tri.copy(), amask, relidx


def kernel(x, c, w_ada, b_ada, norm_g, w_in, b_if, conv_w, conv_b, mh_norm_g, rel_bias,
           w_proj_m, w_proj_a, w_out, final_norm_g):
    f = np.float32
    x = np.asarray(x, f)
    ident, tri, ones, mst, amask, relidx = _consts()
    if "nc" not in _NC_CACHE:
        _NC_CACHE["nc"] = build_nc()
    nc = _NC_CACHE["nc"]
    rep = lambda v, n=128: np.ascontiguousarray(np.broadcast_to(np.asarray(v, f).reshape(1, -1), (n, np.asarray(v).size)))
    colT = lambda v: np.ascontiguousarray(np.asarray(v, f).reshape(16, 128).T)
    cwl = np.ascontiguousarray(np.asarray(conv_w[0], f).T.reshape(16, 128, 4).transpose(1, 0, 2))
    shared = {
        "w_ada": np.ascontiguousarray(w_ada[0], f), "b_ada": np.ascontiguousarray(b_ada[0], f).reshape(1, -1),
        "ngT": colT(norm_g[0]), "w_in": np.ascontiguousarray(w_in[0], f), "bif_bc": rep(b_if[0]),
        "convw": cwl, "convb": colT(conv_b[0]), "mhg_bc": rep(mh_norm_g[0]),
        "relmat": np.ascontiguousarray(np.asarray(rel_bias[0], f)[:, relidx]), "amask": amask,
        "w_pm": np.ascontiguousarray(w_proj_m[0], f), "w_pa": np.ascontiguousarray(w_proj_a[0], f),
        "w_out": np.ascontiguousarray(w_out[0], f), "fg_bc": rep(final_norm_g),
        "ident": ident, "tri": tri, "ones": ones, "maskst": mst,
    }
    in_maps = []
    for core in range(8):
        b, j = core // 4, core % 4
        npad = (3 - j) * 16
        xl = np.zeros((NT * 128, D), f)
        xl[npad * 128:] = x[b, 0:(j + 1) * 2048]
        valid = (np.arange(NT) >= npad).astype(f)
        m = dict(shared)
        m["xl"] = xl
        m["cT"] = colT(c[b])
        m["padneg"] = rep(np.where(valid > 0, 0.0, NEG))
        m["tilevalid"] = rep(valid)
        m["negtile"] = rep(np.where(valid > 0, 0.0, NEG))
        in_maps.append(m)
    res = run_bass_kernel_spmd(nc, in_maps, core_ids=list(range(8)))
    out = np.empty((2, 8192, D), f)
    for core in range(8):
        b, j = core // 4, core % 4
        out[b, j * 2048:(j + 1) * 2048] = res.results[core]["y"]
    return out
```
